# Optimizing a Trainium2 kernel written in Bass

```python
import jax
import jax.numpy as jnp
from jax import lax
import numpy as np


D_MODEL = 1024
BATCH = 8
SEQ = 2048
DEPTH = 4

GRID_W = 64
CTX_LEN = 256
N_MIXERS = 3
N_HEADS = 16
N_KV_HEADS = 4
HEAD_DIM = D_MODEL // N_HEADS
Q_GROUP = N_HEADS // N_KV_HEADS
ROPE_FREQS = HEAD_DIM // 4
ROPE_THETA = 10000.0
Q_BLOCK = 128
CONV_WIDTH = 31
D_RNN = D_MODEL
N_LRU_BLOCKS = 4
LRU_BLOCK = D_RNN // N_LRU_BLOCKS
LRU_CONV_WIDTH = 4
LRU_C = 8.0
D_FF = 4 * D_MODEL
EPS = 1e-6
N_A = (DEPTH + N_MIXERS - 1) // N_MIXERS
N_B = (DEPTH + N_MIXERS - 2) // N_MIXERS
N_C = (DEPTH + N_MIXERS - 3) // N_MIXERS

kernel_name = 'hybrid_interleaved_dit_block'

F32 = jnp.float32


def _rms_f32(x, g):
    xf = x.astype(F32)
    return xf * lax.rsqrt(jnp.mean(xf * xf, axis=-1, keepdims=True) + EPS) * g.astype(F32)


def rms_norm(x, g):
    return _rms_f32(x, g).astype(x.dtype)


def layer_norm(x, g, b):
    xf = x.astype(F32)
    xc = xf - jnp.mean(xf, axis=-1, keepdims=True)
    var = jnp.mean(xc * xc, axis=-1, keepdims=True)
    return (xc * lax.rsqrt(var + EPS) * g.astype(F32) + b.astype(F32)).astype(x.dtype)


def ada_mod(cond, w, b):
    m = jax.nn.silu(cond) @ w + b
    return jnp.split(m[:, None, :], 6, axis=-1)


def modulate(h, shift, scale):
    return h * (1.0 + scale) + shift


def depthwise_conv(x, w, b, pad):
    y = lax.conv_general_dilated(x, w[:, None, :].astype(x.dtype), window_strides=(1,), padding=[pad],
                                 dimension_numbers=('NWC', 'WIO', 'NWC'), feature_group_count=x.shape[-1])
    return y + b


def axial_rope_tables(n_tokens):
    rows = n_tokens // GRID_W
    row = jnp.repeat(jnp.arange(rows, dtype=jnp.int32), GRID_W)
    col = jnp.tile(jnp.arange(GRID_W, dtype=jnp.int32), rows)
    pos = jnp.stack([row, col], axis=-1).astype(F32)
    inv = ROPE_THETA ** (-jnp.arange(ROPE_FREQS, dtype=F32) / ROPE_FREQS)
    ang = pos[:, :, None] * inv
    return jnp.cos(ang), jnp.sin(ang)


def apply_axial_rope(x, cos, sin):
    b, n, h, _ = x.shape
    xr = x.reshape(b, n, h, 2, 2, ROPE_FREQS)
    x1, x2 = xr[..., 0, :], xr[..., 1, :]
    cs, sn = cos[None, :, None], sin[None, :, None]
    out = jnp.stack([x1 * cs - x2 * sn, x1 * sn + x2 * cs], axis=-2)
    return out.reshape(b, n, h, HEAD_DIM)


def gqa_attend(q, k, v):
    b, nq = q.shape[:2]
    qg = q.reshape(b, nq, N_KV_HEADS, Q_GROUP, HEAD_DIM)
    s = jnp.einsum('bqkgd,btkd->bkgqt', qg, k) * (HEAD_DIM ** -0.5)
    p = jax.nn.softmax(s, axis=-1)
    o = jnp.einsum('bkgqt,btkd->bqkgd', p, v)
    return o.reshape(b, nq, N_HEADS * HEAD_DIM)


def attention_mixer(h_lat, h_ctx, w_qkv, q_gain, k_gain, w_o, need_ctx):
    bsz, n_lat, _ = h_lat.shape
    hq = N_HEADS * HEAD_DIM
    hkv = N_KV_HEADS * HEAD_DIM

    def heads_q(q):
        return _rms_f32(q.reshape(q.shape[0], q.shape[1], N_HEADS, HEAD_DIM), q_gain)

    def heads_kv(kv):
        k, v = kv[..., :hkv], kv[..., hkv:]
        k = _rms_f32(k.reshape(k.shape[0], k.shape[1], N_KV_HEADS, HEAD_DIM), k_gain)
        v = v.reshape(v.shape[0], v.shape[1], N_KV_HEADS, HEAD_DIM).astype(F32)
        return k, v

    qkv_l = h_lat @ w_qkv
    q_l = heads_q(qkv_l[..., :hq])
    k_l, v_l = heads_kv(qkv_l[..., hq:])
    k_c, v_c = heads_kv(h_ctx @ w_qkv[:, hq:])
    cos, sin = axial_rope_tables(n_lat)
    q_l = apply_axial_rope(q_l, cos, sin)
    k_l = apply_axial_rope(k_l, cos, sin)
    k_all = jnp.concatenate([k_c, k_l], axis=1)
    v_all = jnp.concatenate([v_c, v_l], axis=1)
    n_blk = n_lat // Q_BLOCK
    q_blk = q_l.reshape(bsz, n_blk, Q_BLOCK, N_HEADS, HEAD_DIM).transpose(1, 0, 2, 3, 4)
    o_blk = lax.map(lambda qb: gqa_attend(qb, k_all, v_all), q_blk)
    o_l = o_blk.transpose(1, 0, 2, 3).reshape(bsz, n_lat, hq).astype(h_lat.dtype)
    out_l = o_l @ w_o
    if not need_ctx:
        return out_l, None
    q_c = heads_q(h_ctx @ w_qkv[:, :hq])
    o_c = gqa_attend(q_c, k_c, v_c).astype(h_ctx.dtype)
    return out_l, o_c @ w_o


def conformer_mixer(h_lat, h_ctx, w_in, b_in, w_dw, b_dw, n_g, n_b, w_out, b_out, need_ctx):
    half = CONV_WIDTH // 2

    def conv_module(h):
        u = h @ w_in + b_in
        a, g = u[..., :D_MODEL], u[..., D_MODEL:]
        u = a * jax.nn.sigmoid(g)
        u = depthwise_conv(u, w_dw, b_dw, (half, half))
        u = jax.nn.silu(layer_norm(u, n_g, n_b))
        return u @ w_out + b_out

    out_l = conv_module(h_lat)
    if not need_ctx:
        return out_l, None
    return out_l, conv_module(h_ctx)


def _linear_combine(e1, e2):
    a1, b1 = e1
    a2, b2 = e2
    return a1 * a2, a2 * b1 + b2


def rglru_scan(u, gate_w, gate_b, lam, h0):
    bsz, n, _ = u.shape
    ub = u.reshape(bsz, n, N_LRU_BLOCKS, LRU_BLOCK)
    gates = jnp.einsum('blnd,gnde->gblne', ub, gate_w).reshape(2, bsz, n, D_RNN) + gate_b[:, None, None, :]
    gates = jax.nn.sigmoid(gates.astype(F32))
    r, i = gates[0], gates[1]
    log_a = -LRU_C * r * jax.nn.softplus(-lam.astype(F32))
    a = jnp.exp(log_a)
    mult = jnp.sqrt(-jnp.expm1(2.0 * log_a))
    if h0 is None:
        mult = mult.at[:, 0].set(1.0)
    b = mult * i * u.astype(F32)
    a_cum, b_cum = lax.associative_scan(_linear_combine, (a, b), axis=1)
    if h0 is None:
        return b_cum
    return a_cum * h0[:, None, :] + b_cum


def rglru_direction(x_c, x_l, conv_w, conv_b, gate_w, gate_b, lam):
    pad = (LRU_CONV_WIDTH - 1, 0)
    h_c = rglru_scan(depthwise_conv(x_c, conv_w, conv_b, pad), gate_w, gate_b, lam, None)
    h_l = rglru_scan(depthwise_conv(x_l, conv_w, conv_b, pad), gate_w, gate_b, lam, h_c[:, -1])
    return h_l, h_c


def recurrent_mixer(h_lat, h_ctx, w_in, conv_w, conv_b, gate_w, gate_b, lam, w_out, need_ctx):
    gx_l = h_lat @ w_in
    g_l = jax.nn.gelu(gx_l[..., :D_RNN].astype(F32))
    x_l = gx_l[..., D_RNN:]
    if need_ctx:
        gx_c = h_ctx @ w_in
        g_c = jax.nn.gelu(gx_c[..., :D_RNN].astype(F32))
        x_c = gx_c[..., D_RNN:]
    else:
        x_c = h_ctx @ w_in[:, D_RNN:]
    hf_l, hf_c = rglru_direction(x_c, x_l, conv_w[0], conv_b[0], gate_w[0], gate_b[0], lam[0])
    hb_l, hb_c = rglru_direction(x_c[:, ::-1], x_l[:, ::-1], conv_w[1], conv_b[1], gate_w[1], gate_b[1], lam[1])
    y_l = ((hf_l + hb_l[:, ::-1]) * g_l).astype(h_lat.dtype)
    out_l = y_l @ w_out
    if not need_ctx:
        return out_l, None
    y_c = ((hf_c + hb_c[:, ::-1]) * g_c).astype(h_ctx.dtype)
    return out_l, y_c @ w_out


def sq_relu_mlp(h, w1, w2):
    return jnp.square(jax.nn.relu(h @ w1)) @ w2


def setup_inputs(seed: int = 0) -> dict:
    key = jax.random.key(seed)
    ks = iter(jax.random.split(key, 40))

    def nrm(shape, s):
        return jax.random.normal(next(ks), shape, F32) * s

    def gain(shape):
        return 1.0 + nrm(shape, 0.02)

    hq = N_HEADS * HEAD_DIM
    hkv = N_KV_HEADS * HEAD_DIM
    inp = {}
    inp['x'] = nrm((BATCH, SEQ, D_MODEL), 1.0)
    inp['c'] = nrm((BATCH, D_MODEL), 1.0)
    inp['ctx'] = nrm((BATCH, CTX_LEN, D_MODEL), 1.0)
    inp['c_ctx'] = nrm((D_MODEL,), 1.0)
    inp['mod_w'] = nrm((DEPTH, D_MODEL, 6 * D_MODEL), 0.3 * D_MODEL ** -0.5)
    inp['mod_b'] = nrm((DEPTH, 6 * D_MODEL), 0.01)
    inp['norm_mix_g'] = gain((DEPTH, D_MODEL))
    inp['norm_mlp_g'] = gain((DEPTH, D_MODEL))
    inp['mlp_w1'] = nrm((DEPTH, D_MODEL, D_FF), D_MODEL ** -0.5)
    inp['mlp_w2'] = nrm((DEPTH, D_FF, D_MODEL), D_FF ** -0.5)
    inp['attn_w_qkv'] = nrm((N_A, D_MODEL, hq + 2 * hkv), D_MODEL ** -0.5)
    inp['attn_q_gain'] = gain((N_A, HEAD_DIM))
    inp['attn_k_gain'] = gain((N_A, HEAD_DIM))
    inp['attn_w_o'] = nrm((N_A, hq, D_MODEL), hq ** -0.5)
    inp['conv_w_in'] = nrm((N_B, D_MODEL, 2 * D_MODEL), D_MODEL ** -0.5)
    inp['conv_b_in'] = nrm((N_B, 2 * D_MODEL), 0.02)
    inp['conv_w_dw'] = nrm((N_B, CONV_WIDTH, D_MODEL), CONV_WIDTH ** -0.5)
    inp['conv_b_dw'] = nrm((N_B, D_MODEL), 0.02)
    inp['conv_norm_g'] = gain((N_B, D_MODEL))
    inp['conv_norm_b'] = nrm((N_B, D_MODEL), 0.02)
    inp['conv_w_out'] = nrm((N_B, D_MODEL, D_MODEL), D_MODEL ** -0.5)
    inp['conv_b_out'] = nrm((N_B, D_MODEL), 0.02)
    inp['lru_w_in'] = nrm((N_C, D_MODEL, 2 * D_RNN), D_MODEL ** -0.5)
    inp['lru_conv_w'] = nrm((N_C, 2, LRU_CONV_WIDTH, D_RNN), LRU_CONV_WIDTH ** -0.5)
    inp['lru_conv_b'] = nrm((N_C, 2, D_RNN), 0.02)
    inp['lru_gate_w'] = nrm((N_C, 2, 2, N_LRU_BLOCKS, LRU_BLOCK, LRU_BLOCK), LRU_BLOCK ** -0.5)
    inp['lru_gate_b'] = nrm((N_C, 2, 2, D_RNN), 0.02)
    a8 = jax.random.uniform(next(ks), (N_C, 2, D_RNN), F32, 0.9, 0.999)
    a0 = a8 ** (1.0 / LRU_C)
    inp['lru_lambda'] = jnp.log(a0) - jnp.log1p(-a0)
    inp['lru_w_out'] = nrm((N_C, D_RNN, D_MODEL), D_RNN ** -0.5)
    return inp


def reference(x, c, ctx, c_ctx, mod_w, mod_b, norm_mix_g, norm_mlp_g, mlp_w1, mlp_w2,
              attn_w_qkv, attn_q_gain, attn_k_gain, attn_w_o,
              conv_w_in, conv_b_in, conv_w_dw, conv_b_dw, conv_norm_g, conv_norm_b, conv_w_out, conv_b_out,
              lru_w_in, lru_conv_w, lru_conv_b, lru_gate_w, lru_gate_b, lru_lambda, lru_w_out):
    x_lat, x_ctx = x, ctx
    cond_ctx = c_ctx[None, :]
    for i in range(DEPTH):
        kind = i % N_MIXERS
        j = i // N_MIXERS
        need_ctx = i < DEPTH - 1
        sh1, sc1, g1, sh2, sc2, g2 = ada_mod(c, mod_w[i], mod_b[i])
        csh1, csc1, cg1, csh2, csc2, cg2 = ada_mod(cond_ctx, mod_w[i], mod_b[i])
        h_l = modulate(rms_norm(x_lat, norm_mix_g[i]), sh1, sc1)
        h_c = modulate(rms_norm(x_ctx, norm_mix_g[i]), csh1, csc1)
        if kind == 0:
            out_l, out_c = attention_mixer(h_l, h_c, attn_w_qkv[j], attn_q_gain[j], attn_k_gain[j],
                                           attn_w_o[j], need_ctx)
        elif kind == 1:
            out_l, out_c = conformer_mixer(h_l, h_c, conv_w_in[j], conv_b_in[j], conv_w_dw[j], conv_b_dw[j],
                                           conv_norm_g[j], conv_norm_b[j], conv_w_out[j], conv_b_out[j], need_ctx)
        else:
            out_l, out_c = recurrent_mixer(h_l, h_c, lru_w_in[j], lru_conv_w[j], lru_conv_b[j], lru_gate_w[j],
                                           lru_gate_b[j], lru_lambda[j], lru_w_out[j], need_ctx)
        x_lat = x_lat + g1 * out_l
        x_lat = x_lat + g2 * sq_relu_mlp(modulate(rms_norm(x_lat, norm_mlp_g[i]), sh2, sc2), mlp_w1[i], mlp_w2[i])
        if need_ctx:
            x_ctx = x_ctx + cg1 * out_c
            x_ctx = x_ctx + cg2 * sq_relu_mlp(modulate(rms_norm(x_ctx, norm_mlp_g[i]), csh2, csc2),
                                              mlp_w1[i], mlp_w2[i])
    return x_lat
```

```python
import numpy as np
from contextlib import ExitStack
import concourse.bass as bass
import concourse.mybir as mybir
from concourse.bass_utils import run_bass_kernel_spmd
import ml_dtypes

F32 = mybir.dt.float32
BF16 = mybir.dt.bfloat16
AF = mybir.ActivationFunctionType
ALU = mybir.AluOpType

SELF_WAIT = True

NT, NCTX, NLAT, D = 2304, 256, 2048, 1024
TCH = [(0, 256), (256, 512), (768, 512), (1280, 512), (1792, 512)]
EPS = 1e-6
DEPTH = 4


class Sem:
    def __init__(self, h, name):
        self.h = h
        self.name = name
        self.count = 0


class Buf:
    __slots__ = ("name", "w", "r", "dsem")

    def __init__(self, name, dsem=None):
        self.name = name
        self.w = None
        self.r = []
        self.dsem = dsem


class Prog:
    ENG = ("pe", "act", "dve", "pool", "sp")

    def __init__(self, nc, stack):
        self.nc = nc
        self.stack = stack
        self.ops = {e: [] for e in self.ENG}
        self.esem = {}
        for e in ("pe", "act", "dve"):
            self.esem[e] = self.new_sem("s_" + e)
        self.known = {e: {} for e in self.ENG}
        self.pending_noinc = {e: False for e in self.ENG}
        self.nbuf = 0

    def new_sem(self, name):
        h = self.stack.enter_context(self.nc.semaphore(name))
        return Sem(h, name)

    def buf(self, name=None, dma=False):
        self.nbuf += 1
        name = name or f"b{self.nbuf}"
        return Buf(name, self.new_sem("d_" + name) if dma else None)

    def sb(self, name, shape, dt):
        return self.stack.enter_context(self.nc.sbuf_tensor(name, list(shape), dt))

    def ps(self, name, shape, dt=F32):
        return self.stack.enter_context(self.nc.psum_tensor(name, list(shape), dt))

    def _wait(self, eng, tok):
        if tok is None:
            return
        sem, val = tok
        if eng == "pe" and sem is self.esem["pe"]:
            return
        if (not SELF_WAIT) and eng in self.esem and sem is self.esem[eng]:
            return
        k = self.known[eng]
        if k.get(sem, 0) >= val:
            return
        k[sem] = val
        h = sem.h
        self.ops[eng].append(lambda e, h=h, val=val: e.wait_ge(h, val))

    def _deps(self, eng, reads, writes):
        for b in reads:
            self._wait(eng, b.w)
        for b in writes:
            self._wait(eng, b.w)
            for t in b.r:
                self._wait(eng, t)

    def _commit(self, tok, reads, writes):
        for b in reads:
            b.r.append(tok)
            if len(b.r) > 16:
                d = {}
                for s, v in b.r:
                    if d.get(s, 0) < v:
                        d[s] = v
                b.r = list(d.items())
        for b in writes:
            b.w = tok
            b.r = []

    def op(self, eng, fn, reads=(), writes=(), inc=True):
        self._deps(eng, reads, writes)
        sem = self.esem[eng]
        if inc:
            sem.count += 1
            val = sem.count
            h = sem.h
            self.ops[eng].append(lambda e, fn=fn, h=h: fn(e).then_inc(h, 1))
            self.pending_noinc[eng] = False
        else:
            val = sem.count + 1
            self.ops[eng].append(lambda e, fn=fn: fn(e))
            self.pending_noinc[eng] = True
        tok = (sem, val)
        self._commit(tok, reads, writes)
        return tok

    def dma(self, q, out, in_, reads=(), writes=(), dsem=None):
        self._deps(q, reads, writes)
        sem = dsem if dsem is not None else writes[0].dsem
        sem.count += 16
        val = sem.count
        h = sem.h
        self.ops[q].append(lambda e, out=out, in_=in_, h=h: e.dma_start(out=out, in_=in_).then_inc(h, 16))
        tok = (sem, val)
        self._commit(tok, reads, writes)
        return tok

    def barrier(self, engs=("pe", "act", "dve", "sp")):
        for e in ("pe", "act", "dve"):
            assert not self.pending_noinc[e]
        toks = [(self.esem[e], self.esem[e].count) for e in ("pe", "act", "dve")]
        for e in engs:
            for t in toks:
                if t[1] > 0:
                    self._wait(e, t)

    def emit(self):
        for e in ("pe", "act", "dve"):
            assert not self.pending_noinc[e], f"engine {e} has trailing non-inc op"
        ops = self.ops
        with self.nc.Block() as block:
            @block.sync
            def _(eng):
                for f in ops["sp"]:
                    f(eng)

            @block.tensor
            def _(eng):
                for f in ops["pe"]:
                    f(eng)

            @block.scalar
            def _(eng):
                for f in ops["act"]:
                    f(eng)

            @block.vector
            def _(eng):
                for f in ops["dve"]:
                    f(eng)

            @block.gpsimd
            def _(eng):
                for f in ops["pool"]:
                    f(eng)


def _vec_layout():
    L = [("c", 8), ("cctx", 8)]
    for i in range(DEPTH):
        L += [(f"modb{i}", 48), (f"gmix{i}", 8), (f"gmlp{i}", 8)]
    for j in range(2):
        L += [(f"qg{j}", 1), (f"kg{j}", 1)]
    L += [("cbin", 16), ("cwdw", 248), ("cbdw", 8), ("cng", 8), ("cnb", 8), ("cbout", 8)]
    L += [("lcw", 64), ("lcb", 16), ("lgb", 32), ("llam", 16)]
    off = {}
    o = 0
    for n, k in L:
        off[n] = o
        o += k
    return off, o


VOFF, NV = _vec_layout()


def _pk(v):
    v = np.asarray(v, np.float32).reshape(-1, 128)
    return np.ascontiguousarray(v.T)


def _pack_vecs(inp, b):
    V = np.zeros((128, NV), np.float32)

    def put(name, arr):
        arr = np.asarray(arr, np.float32)
        V[:, VOFF[name]:VOFF[name] + arr.shape[1]] = arr

    put("c", _pk(inp["c"][b]))
    put("cctx", _pk(inp["c_ctx"]))
    for i in range(DEPTH):
        put(f"modb{i}", _pk(inp["mod_b"][i]))
        put(f"gmix{i}", _pk(inp["norm_mix_g"][i]))
        put(f"gmlp{i}", _pk(inp["norm_mlp_g"][i]))
    for j in range(2):
        put(f"qg{j}", np.tile(np.asarray(inp["attn_q_gain"][j], np.float32), 2)[:, None])
        put(f"kg{j}", np.tile(np.asarray(inp["attn_k_gain"][j], np.float32), 2)[:, None])
    put("cbin", _pk(inp["conv_b_in"][0]))
    wdw = np.asarray(inp["conv_w_dw"][0], np.float32).reshape(31, 8, 128).transpose(2, 0, 1).reshape(128, 248)
    put("cwdw", wdw)
    put("cbdw", _pk(inp["conv_b_dw"][0]))
    put("cng", _pk(inp["conv_norm_g"][0]))
    put("cnb", _pk(inp["conv_norm_b"][0]))
    put("cbout", _pk(inp["conv_b_out"][0]))
    lcw = np.asarray(inp["lru_conv_w"][0], np.float32).reshape(2, 4, 8, 128).transpose(3, 0, 1, 2).reshape(128, 64)
    put("lcw", lcw)
    put("lcb", np.asarray(inp["lru_conv_b"][0], np.float32).reshape(2, 8, 128).transpose(2, 0, 1).reshape(128, 16))
    put("lgb", np.asarray(inp["lru_gate_b"][0], np.float32).reshape(2, 2, 8, 128).transpose(3, 0, 1, 2).reshape(128, 32))
    put("llam", np.asarray(inp["lru_lambda"][0], np.float32).reshape(2, 8, 128).transpose(2, 0, 1).reshape(128, 16))
    return V


def _consts():
    p = np.arange(128)
    perm = (p[:, None] == (p[None, :] ^ 16)).astype(np.float32)
    onesblk = ((p[:, None] // 64) == (p[None, :] // 64)).astype(np.float32)
    ones = np.ones((128, 128), np.float32)
    ident = np.eye(128, dtype=np.float32)
    return np.concatenate([perm, onesblk, ones, ident], axis=1).astype(ml_dtypes.bfloat16)


def _rope_tables():
    t = np.arange(NLAT)
    row = (t // 64).astype(np.float64)
    col = (t % 64).astype(np.float64)
    inv = 10000.0 ** (-np.arange(16, dtype=np.float64) / 16.0)
    p = np.arange(128)
    d = p % 64
    a = d // 32
    half = (d // 16) % 2
    f = d % 16
    pos = np.where(a[:, None] == 0, row[None, :], col[None, :])
    ang = (pos.astype(np.float32) * inv.astype(np.float32)[f][:, None]).astype(np.float32)
    C = np.cos(ang).astype(np.float32)
    S = np.sin(ang).astype(np.float32) * np.where(half == 0, -1.0, 1.0)[:, None].astype(np.float32)
    return np.ascontiguousarray(np.concatenate([C, S], axis=1).astype(np.float32))


class K:
    pass


def build_program(layers=(0, 1, 2, 3), first=True, last=True):
    nc = bass.Bass("TRN2", target_bir_lowering=False)
    dr = {}

    def din(name, shape, dt=F32):
        dr[name] = nc.dram_tensor(name, list(shape), dt, kind="ExternalInput").ap()
        return dr[name]

    if first:
        din("xT", [D, NLAT])
        din("ctxT", [D, NCTX])
    else:
        din("xs_in", [128, 8 * NT])
    din("vecs", [128, NV])
    din("consts", [128, 512], BF16)
    din("rope", [128, 2 * NLAT])
    din("mod_w", [4, D, 6 * D])
    din("mlp_w1", [4, D, 4 * D])
    din("mlp_w2", [4, 4 * D, D])
    din("attn_w_qkv", [2, D, 1536])
    din("attn_w_o", [2, D, D])
    din("conv_w_in", [1, D, 2 * D])
    din("conv_w_out", [1, D, D])
    din("lru_w_in", [1, D, 2 * D])
    din("lru_gate_w", [1, 2, 2, 4, 256, 256])
    din("lru_w_out", [1, D, D])
    if last:
        outT = nc.dram_tensor("outT", [D, NLAT], F32, kind="ExternalOutput").ap()
    else:
        xs_out = nc.dram_tensor("xs_out", [128, 8 * NT], F32, kind="ExternalOutput").ap()
    xs = nc.dram_tensor("xs_scr", [128, 8 * NT], F32, kind="Internal").ap()
    import os as _os
    DBG = bool(_os.environ.get("DBG_DUMP"))
    if DBG:
        dbg_mods = nc.dram_tensor("dbg_mods", [128, 96], F32, kind="ExternalOutput").ap()
        dbg_ab = nc.dram_tensor("dbg_ab", [128, 64], F32, kind="ExternalOutput").ap()
        dbg_h1 = nc.dram_tensor("dbg_h1", [128, 8 * NT], BF16, kind="ExternalOutput").ap()
        dbg_h2 = nc.dram_tensor("dbg_h2", [128, 8 * NT], BF16, kind="ExternalOutput").ap()
        dbg_x1 = nc.dram_tensor("dbg_x1", [128, 8 * NT], F32, kind="ExternalOutput").ap()

    with ExitStack() as st:
        p = Prog(nc, st)
        XR = p.sb("XR", [128, 8 * NT], F32)
        HBt = p.sb("HB", [128, 8 * NT], BF16)
        AUX = p.sb("AUX", [128, 10368], BF16)
        SLOT = [p.sb(f"slot{i}", [128, 4096], BF16) for i in range(4)]
        VEC = p.sb("VEC", [128, NV], F32)
        CONST = p.sb("CONST", [128, 512], BF16)
        MODS = p.sb("MODS", [128, 96], F32)
        AB = p.sb("AB", [128, 64], F32)
        SC = p.sb("SC", [128, 16], BF16)
        MISC = p.sb("MISC", [128, 256], F32)
        T32 = [p.sb(f"t32_{i}", [128, 512], F32) for i in range(6)]
        TB = p.sb("TB", [128, 8 * 512], BF16)
        T32X = [p.sb(f"t32x_{i}", [128, 512], F32) for i in range(2)]
        t32xb = [p.buf(f"t32x_{i}") for i in range(2)]
        T32H = [p.sb(f"t32h_{i}", [128, 512], F32) for i in range(2)]
        t32hb = [p.buf(f"t32h_{i}") for i in range(2)]
        b_tb = p.buf("tb")
        b_car = p.buf("car")
        PT = [p.sb(f"pt{i}", [128, 512], BF16) for i in range(6)]
        PS = [p.ps(f"ps{i}", [128, 512], F32) for i in range(8)]
        psb = [p.buf(f"ps{i}") for i in range(8)]
        t32b = [p.buf(f"t32_{i}") for i in range(6)]
        ptb = [p.buf(f"pt{i}") for i in range(6)]
        slotb = [p.buf(f"slot{i}", dma=True) for i in range(4)]
        b_vec = p.buf("vec", dma=True)
        b_const = p.buf("const", dma=True)
        b_mods = p.buf("mods")
        b_ab = p.buf("ab")
        b_sc = p.buf("sc")
        b_misc = p.buf("misc")
        x_dsem = p.new_sem("d_x")
        o_buf = p.buf("out", dma=True)
        spill_buf = p.buf("spill", dma=True)

        dbg_outs = {}

        def dbg_dump(name, ap, shape, dt):
            if not DBG:
                return
            t = nc.dram_tensor("dd_" + name, list(shape), dt, kind="ExternalOutput").ap()
            p.barrier(("sp",))
            p.dma("sp", t, ap, writes=[o_buf])
            for e in ("pe", "act", "dve"):
                p._wait(e, o_buf.w)

        PERM = CONST[:, 0:128]
        ONESBLK = CONST[:, 128:256]
        ONES = CONST[:, 256:384]
        IDENT = CONST[:, 384:512]

        X3 = XR[:, :].rearrange("p (c t) -> p c t", c=8)
        XRb = XR[:, :].bitcast(BF16)
        H3 = HBt[:, :].rearrange("p (c t) -> p c t", c=8)
        HBf = HBt[:, :].bitcast(F32)
        AUXf = AUX[:, :].bitcast(F32)

        def grid(name):
            return [[p.buf(f"{name}{c}_{t}") for t in range(5)] for c in range(8)]

        st_ = K()
        st_.xb = grid("x")
        st_.hb = grid("h")
        st_.rr = {"t32": 0, "pt": 0, "ps": 0}

        def vcol(name, j=0):
            o = VOFF[name] + j
            return VEC[:, o:o + 1]

        def tmp32():
            i = st_.rr["t32"] % 6
            st_.rr["t32"] += 1
            return T32[i], t32b[i]

        def tmppt():
            i = st_.rr["pt"] % 6
            st_.rr["pt"] += 1
            return PT[i], ptb[i]

        def psum(group=None):
            group = group if group is not None else list(range(8))
            key = ("ps",) + tuple(group)
            k_ = st_.rr.get(key, 0)
            st_.rr[key] = k_ + 1
            i = group[k_ % len(group)]
            return PS[i], psb[i]

        def mm(out, lhsT, rhs, start, stop, reads, writes, inc):
            p.op("pe", lambda e: e.matmul(out, lhsT, rhs, start=start, stop=stop), reads, writes, inc=inc)

        def act(out, in_, func, reads, writes, bias=None, scale=None):
            kw = {}
            if bias is not None:
                kw["bias"] = bias
            if scale is not None:
                kw["scale"] = scale
            p.op("act", lambda e: e.activation(out=out, in_=in_, func=func, **kw), reads, writes)

        def tt(out, in0, in1, op, reads, writes):
            p.op("dve", lambda e: e.tensor_tensor(out=out, in0=in0, in1=in1, op=op), reads, writes)

        def ts(out, in0, s1, s2, op0, op1, reads, writes):
            if s2 is None:
                p.op("dve", lambda e: e.tensor_scalar(out=out, in0=in0, scalar1=s1, scalar2=None, op0=op0), reads, writes)
            else:
                p.op("dve", lambda e: e.tensor_scalar(out=out, in0=in0, scalar1=s1, scalar2=s2, op0=op0, op1=op1), reads, writes)

        def stt(out, in0, scalar, in1, op0, op1, reads, writes):
            p.op("dve", lambda e: e.scalar_tensor_tensor(out=out, in0=in0, scalar=scalar, in1=in1, op0=op0, op1=op1), reads, writes)

        def recip(out, in_, reads, writes):
            p.op("dve", lambda e: e.reciprocal(out=out, in_=in_), reads, writes)

        def vcopy(out, in_, reads, writes):
            p.op("dve", lambda e: e.tensor_copy(out=out, in_=in_), reads, writes)

        def vmemset(ap, val, writes):
            p.op("dve", lambda e: e.memset(ap, val), (), writes)

        wspecs = []

        def wv(ap2d):
            return ap2d.rearrange("(k p) n -> p k n", p=128)

        for li in layers:
            kind = li % 3
            j = li // 3
            for b in range(12):
                wspecs.append([(0, 8, 512, wv(dr["mod_w"][li, :, b * 512:(b + 1) * 512]))])
            if kind == 0:
                wq = dr["attn_w_qkv"][j]
                for b in range(2):
                    wspecs.append([(0, 8, 512, wv(wq[:, b * 512:(b + 1) * 512]))])
                sp_ = []
                for g in range(4):
                    for dup in range(2):
                        sp_.append(((g * 2 + dup) * 64, 8, 64, wv(wq[:, 1024 + g * 64:1024 + (g + 1) * 64]), 512))
                wspecs.append(sp_)
                wspecs.append([(0, 8, 256, wv(wq[:, 1280:1536]))])
                for b in range(2):
                    wspecs.append([(0, 8, 512, wv(dr["attn_w_o"][j][:, b * 512:(b + 1) * 512]))])
            elif kind == 1:
                wi = dr["conv_w_in"][0]
                for b in range(4):
                    wspecs.append([(0, 8, 256, wv(wi[:, b * 256:(b + 1) * 256]), 512),
                                   (256, 8, 256, wv(wi[:, 1024 + b * 256:1024 + (b + 1) * 256]), 512)])
                for b in range(2):
                    wspecs.append([(0, 8, 512, wv(dr["conv_w_out"][0][:, b * 512:(b + 1) * 512]))])
            else:
                wi = dr["lru_w_in"][0]
                for b in range(4):
                    wspecs.append([(0, 8, 512, wv(wi[:, b * 512:(b + 1) * 512]))])
                for d in range(2):
                    gw = dr["lru_gate_w"][0, d].rearrange("g n k e -> (g n k) e")
                    wspecs.append([(0, 16, 256, wv(gw))])
                for b in range(2):
                    wspecs.append([(0, 8, 512, wv(dr["lru_w_out"][0][:, b * 512:(b + 1) * 512]))])
            for hb in range(8):
                wspecs.append([(0, 8, 512, wv(dr["mlp_w1"][li, :, hb * 512:(hb + 1) * 512]))])
                wspecs.append([(0, 4, 1024, wv(dr["mlp_w2"][li, hb * 512:(hb + 1) * 512, :]))])

        ws = K()
        ws.issued = 0
        ws.consumed = 0

        def w_issue(jb):
            s = jb % 4
            for spec in wspecs[jb]:
                if len(spec) == 5:
                    off, kcn, ncol, src, rowlen = spec
                    dst = SLOT[s][:, 0:kcn * rowlen].rearrange("p (k n) -> p k n", k=kcn)[:, :, off:off + ncol]
                else:
                    off, kcn, ncol, src = spec
                    dst = SLOT[s][:, off:off + kcn * ncol].rearrange("p (k n) -> p k n", k=kcn)
                p.dma("pool", dst, src, writes=[slotb[s]])

        ws.released = set()
        ws.pinned = set()

        def w_release(i):
            ws.pinned.discard(i)
            ws.released.add(i)

        def w_next(kcn, ncol, pin=False):
            i = ws.consumed
            if i - 1 >= 0 and (i - 1) not in ws.pinned:
                ws.released.add(i - 1)
            while ws.issued < min(i + 4, len(wspecs)) and (ws.issued < 4 or (ws.issued - 4) in ws.released):
                w_issue(ws.issued)
                ws.issued += 1
            assert ws.issued > i, "weight block not issued (pinned slot deadlock)"
            if pin:
                ws.pinned.add(i)
            ws.consumed += 1
            s = i % 4
            return SLOT[s][:, 0:kcn * ncol].rearrange("p (k n) -> p k n", k=kcn), slotb[s]

        p.dma("sp", VEC[:, :], dr["vecs"], writes=[b_vec])
        p.dma("sp", CONST[:, :], dr["consts"], writes=[b_const])

        def load_x_from_input():
            allb = [b for row in st_.xb for b in row]
            if first:
                p.dma("sp", X3[:, :, 0:NCTX], dr["ctxT"].rearrange("(c p) t -> p c t", p=128), writes=allb, dsem=x_dsem)
                for c in range(8):
                    p.dma("sp", X3[:, c, NCTX:NT], dr["xT"][c * 128:(c + 1) * 128, :], writes=allb, dsem=x_dsem)
            else:
                for c in range(8):
                    p.dma("sp", X3[:, c, :], dr["xs_in"][:, c * NT:(c + 1) * NT], writes=allb, dsem=x_dsem)

        load_x_from_input()
        SC3 = SC[:, :].rearrange("p (k s) -> p k s", s=2)
        act(SC3[:, :, 0], VEC[:, VOFF["cctx"]:VOFF["cctx"] + 8], AF.Silu, [b_vec], [b_sc])
        act(SC3[:, :, 1], VEC[:, VOFF["c"]:VOFF["c"] + 8], AF.Silu, [b_vec], [b_sc])

        MODS3 = MODS[:, :].rearrange("p (j s) -> p j s", s=2)

        def modcol(grp, c, s):
            return MODS[:, (grp * 8 + c) * 2 + s:(grp * 8 + c) * 2 + s + 1]

        def mod_phase(li):
            ps_t, ps_b = psum()
            for b in range(12):
                wt, wb = w_next(8, 512)
                for jj in range(4):
                    jx = b * 4 + jj
                    for kc in range(8):
                        mm(ps_t[:, 2 * jx:2 * jx + 2], wt[:, kc, jj * 128:(jj + 1) * 128], SC3[:, kc, :],
                           kc == 0, kc == 7, [wb, b_sc], [ps_b], inc=(jj == 3 and kc == 7))
            ps3 = ps_t[:, 0:96].rearrange("p (j s) -> p j s", s=2)
            mb = VEC[:, VOFF[f"modb{li}"]:VOFF[f"modb{li}"] + 48]
            for s in range(2):
                tt(MODS3[:, :, s], ps3[:, :, s], mb, ALU.add, [ps_b, b_vec], [b_mods])
            gm = VEC[:, VOFF[f"gmix{li}"]:VOFF[f"gmix{li}"] + 8]
            gl = VEC[:, VOFF[f"gmlp{li}"]:VOFF[f"gmlp{li}"] + 8]
            for s in range(2):
                stt(AB[:, s * 8:s * 8 + 8], MODS3[:, 8:16, s], 1.0, gm, ALU.add, ALU.mult, [b_mods, b_vec], [b_ab])
                stt(AB[:, 16 + s * 8:16 + s * 8 + 8], MODS3[:, 32:40, s], 1.0, gl, ALU.add, ALU.mult, [b_mods, b_vec], [b_ab])

        def norm_phase(which, tis):
            TB3 = TB[:, :].rearrange("p (c t) -> p c t", c=8)
            for ti in tis:
                t0, n = TCH[ti]
                s = 0 if ti == 0 else 1
                for c in range(8):
                    act(TB3[:, c, 0:n], X3[:, c, t0:t0 + n], AF.Square, [st_.xb[c][ti]], [b_tb])
                ps_t, ps_b = psum()
                for c in range(8):
                    mm(ps_t[:, 0:n], ONES, TB3[:, c, 0:n], c == 0, c == 7, [b_tb, b_const], [ps_b], inc=(c == 7))
                sd, sdb = tmp32()
                act(sd[:, 0:n], ps_t[:, 0:n], AF.Sqrt, [ps_b, b_misc], [sdb], bias=MISC[:, 0:1], scale=1.0 / D)
                rs, rsb = T32H[0], t32hb[0]
                recip(rs[:, 0:n], sd[:, 0:n], [sdb], [rsb])
                for c in range(8):
                    t_, tb_ = tmp32()
                    a_ap = AB[:, which * 16 + s * 8 + c:which * 16 + s * 8 + c + 1]
                    stt(t_[:, 0:n], X3[:, c, t0:t0 + n], a_ap, rs[:, 0:n], ALU.mult, ALU.mult,
                        [st_.xb[c][ti], b_ab, rsb], [tb_])
                    act(H3[:, c, t0:t0 + n], t_[:, 0:n], AF.Identity, [tb_, b_mods], [st_.hb[c][ti]],
                        bias=modcol(0 if which == 0 else 3, c, s))

        def spill(li_index):
            allb = [b for row in st_.xb for b in row]
            if li_index == 0 and True:
                tok = None
            else:
                tok = p.dma("sp", xs, XR[:, :], reads=allb, writes=[spill_buf])
            p.barrier(("pe", "act", "dve", "sp"))
            if tok is not None:
                for e in ("pe", "act", "dve", "sp"):
                    p._wait(e, tok)
            return tok

        def reload(li_index):
            p.barrier(("pe", "act", "dve", "sp"))
            st_.xb = grid(f"x{li_index}_")
            allb = [b for row in st_.xb for b in row]
            if li_index == 0:
                load_x_from_input()
            else:
                for c in range(8):
                    p.dma("sp", X3[:, c, :], xs[:, c * NT:(c + 1) * NT], reads=[spill_buf], writes=allb, dsem=x_dsem)

        def linear(nblocks, ocs_per_block, kcn, wcols, lhs_fn, rhs_fn, rhs_bufs_fn, tis, evac, psgroup=None):
            for b in range(nblocks):
                wt, wb = w_next(kcn, wcols)
                for ocl in range(ocs_per_block):
                    for ti in tis:
                        t0, n = TCH[ti]
                        ps_t, ps_b = psum(psgroup)
                        for kc in range(kcn):
                            mm(ps_t[:, 0:n], lhs_fn(wt, ocl, kc), rhs_fn(kc, t0, n), kc == 0, kc == kcn - 1,
                               [wb] + rhs_bufs_fn(kc, ti), [ps_b], inc=(kc == kcn - 1))
                        evac(b, ocl, ti, ps_t[:, 0:n], ps_b)

        def resid_evac(grp, tis_all, bias_name=None):
            def ev(b, ocl, ti, ps_ap, ps_b):
                oc = b * 4 + ocl
                t0, n = TCH[ti]
                s = 0 if ti == 0 else 1
                src = ps_ap
                rd = [ps_b]
                if bias_name is not None:
                    t_, tb_ = tmp32()
                    act(t_[:, 0:n], ps_ap, AF.Identity, [ps_b, b_vec], [tb_], bias=vcol(bias_name, oc))
                    src = t_[:, 0:n]
                    rd = [tb_]
                stt(X3[:, oc, t0:t0 + n], src, modcol(grp, oc, s), X3[:, oc, t0:t0 + n], ALU.mult, ALU.add,
                    rd + [b_mods, st_.xb[oc][ti]], [st_.xb[oc][ti]])
            return ev

        def hb_rhs(kc, t0, n):
            return H3[:, kc, t0:t0 + n]

        def hb_bufs(kc, ti):
            return [st_.hb[kc][ti]]

        def mlp_phase(li, tis):
            HID = AUX[:, 0:4 * NT].rearrange("p (c t) -> p c t", c=4)
            hidb = [[p.buf() for _ in range(5)] for _ in range(4)]
            for hb_i in range(8):
                w1, w1b = w_next(8, 512)
                for ti in tis:
                    t0, n = TCH[ti]
                    for ocl in range(4):
                        ps_t, ps_b = psum()
                        for kc in range(8):
                            mm(ps_t[:, 0:n], w1[:, kc, ocl * 128:(ocl + 1) * 128], H3[:, kc, t0:t0 + n], kc == 0, kc == 7,
                               [w1b, st_.hb[kc][ti]], [ps_b], inc=(kc == 7))
                        t_, tb_ = tmp32()
                        act(t_[:, 0:n], ps_t[:, 0:n], AF.Relu, [ps_b], [tb_])
                        tt(HID[:, ocl, t0:t0 + n], t_[:, 0:n], t_[:, 0:n], ALU.mult, [tb_], [hidb[ocl][ti]])
                w2, w2b = w_next(4, 1024)
                for ti in tis:
                    t0, n = TCH[ti]
                    s = 0 if ti == 0 else 1
                    for oc in range(8):
                        ps_t, ps_b = psum()
                        for kc in range(4):
                            mm(ps_t[:, 0:n], w2[:, kc, oc * 128:(oc + 1) * 128], HID[:, kc, t0:t0 + n], kc == 0, kc == 3,
                               [w2b, hidb[kc][ti]], [ps_b], inc=(kc == 3))
                        stt(X3[:, oc, t0:t0 + n], ps_t[:, 0:n], modcol(5, oc, s), X3[:, oc, t0:t0 + n], ALU.mult, ALU.add,
                            [ps_b, b_mods, st_.xb[oc][ti]], [st_.xb[oc][ti]])

        def attention(li, j_att, need_ctx, li_index):
            QT = XRb[:, 0:18432].rearrange("p (c t) -> p c t", c=8)
            KT2 = XRb[:, 18432:27648].rearrange("p (c t) -> p c t", c=4)
            ROC = XR[:, 13824:15872]
            ROS = XR[:, 15872:17920]
            VA = AUX[:, 0:18 * 576].rearrange("p (k x) -> p k x", k=18)
            b_rope = p.buf(f"rope{li}", dma=True)
            qb = [[p.buf() for _ in range(5)] for _ in range(8)]
            kb = [[p.buf() for _ in range(5)] for _ in range(4)]
            vab = [p.buf() for _ in range(18)]
            b_va_init = p.buf()
            p.dma("sp", XR[:, 13824:17920], dr["rope"], writes=[b_rope])
            vmemset(AUX[:, 0:18 * 576], 1.0, [b_va_init] + vab)
            q_tis = [0, 1, 2, 3, 4] if need_ctx else [1, 2, 3, 4]

            PS_A = [0, 1, 2]
            PS_B = [3, 4]
            PS_C = [5, 6]
            pending = []

            def qk_item(ps_t, ps_b, n, ti, gain_ap, dst_ap, dst_buf):
                t0 = TCH[ti][0]
                lat = ti != 0
                state = {}

                def stage_b():
                    sq, sqb = tmppt()
                    act(sq[:, 0:n], ps_t[:, 0:n], AF.Square, [ps_b], [sqb])
                    ss_t, ss_b = psum(PS_B)
                    mm(ss_t[:, 0:n], ONESBLK, sq[:, 0:n], True, True, [sqb, b_const], [ss_b], inc=True)
                    sd, sdb = tmp32()
                    act(sd[:, 0:n], ss_t[:, 0:n], AF.Sqrt, [ss_b, b_misc], [sdb], bias=MISC[:, 0:1], scale=1.0 / 64)
                    rs, rsb = tmp32()
                    recip(rs[:, 0:n], sd[:, 0:n], [sdb], [rsb])
                    if not lat:
                        stt(dst_ap, ps_t[:, 0:n], gain_ap, rs[:, 0:n], ALU.mult, ALU.mult, [ps_b, rsb, b_vec], [dst_buf])
                    else:
                        qn, qnb = tmppt()
                        stt(qn[:, 0:n], ps_t[:, 0:n], gain_ap, rs[:, 0:n], ALU.mult, ALU.mult, [ps_b, rsb, b_vec], [qnb])
                        state["qn"] = (qn, qnb)

                def stage_c():
                    if not lat:
                        return
                    qn, qnb = state["qn"]
                    rot_t, rot_b = psum(PS_C)
                    mm(rot_t[:, 0:n], PERM, qn[:, 0:n], True, True, [qnb, b_const], [rot_b], inc=True)
                    t1, t1b = tmp32()
                    tt(t1[:, 0:n], qn[:, 0:n], ROC[:, t0 - NCTX:t0 - NCTX + n], ALU.mult, [qnb, b_rope], [t1b])
                    t2, t2b = tmp32()
                    tt(t2[:, 0:n], rot_t[:, 0:n], ROS[:, t0 - NCTX:t0 - NCTX + n], ALU.mult, [rot_b, b_rope], [t2b])
                    tt(dst_ap, t1[:, 0:n], t2[:, 0:n], ALU.add, [t1b, t2b], [dst_buf])
                return stage_b, stage_c

            def pipe_push(item):
                pending.append(item)
                if len(pending) >= 2:
                    pending[-2][0]()
                if len(pending) >= 3:
                    pending[-3][1]()

            def pipe_flush():
                if len(pending) >= 1:
                    pending[-1][0]()
                if len(pending) >= 2:
                    pending[-2][1]()
                if len(pending) >= 1:
                    pending[-1][1]()
                pending.clear()

            for b in range(2):
                wt, wb = w_next(8, 512)
                for ocl in range(4):
                    oc = b * 4 + ocl
                    for ti in q_tis:
                        t0, n = TCH[ti]
                        ps_t, ps_b = psum(PS_A)
                        for kc in range(8):
                            mm(ps_t[:, 0:n], wt[:, kc, ocl * 128:(ocl + 1) * 128], H3[:, kc, t0:t0 + n], kc == 0, kc == 7,
                               [wb, st_.hb[kc][ti]], [ps_b], inc=(kc == 7))
                        pipe_push(qk_item(ps_t, ps_b, n, ti, vcol(f"qg{j_att}"), QT[:, oc, t0:t0 + n], qb[oc][ti]))
            wt, wb = w_next(8, 512)
            for g in range(4):
                for ti in range(5):
                    t0, n = TCH[ti]
                    ps_t, ps_b = psum(PS_A)
                    for kc in range(8):
                        mm(ps_t[:, 0:n], wt[:, kc, g * 128:(g + 1) * 128], H3[:, kc, t0:t0 + n], kc == 0, kc == 7,
                           [wb, st_.hb[kc][ti]], [ps_b], inc=(kc == 7))
                    pipe_push(qk_item(ps_t, ps_b, n, ti, vcol(f"kg{j_att}"), KT2[:, g, t0:t0 + n], kb[g][ti]))
            pipe_flush()
            wt, wb = w_next(8, 256)
            for kt in range(18):
                ti = 0 if kt < 2 else 1 + (kt - 2) // 4
                ps_t, ps_b = psum(PS_A)
                for kc in range(8):
                    mm(ps_t[:, 0:256], H3[:, kc, kt * 128:(kt + 1) * 128], wt[:, kc, :], kc == 0, kc == 7,
                       [wb, st_.hb[kc][ti]], [ps_b], inc=(kc == 7))
                dst = VA[:, kt, 64:576].rearrange("p (g x) -> p g x", x=128)[:, :, 0:64]
                src = ps_t[:, 0:256].rearrange("p (g x) -> p g x", x=64)
                act(dst, src, AF.Identity, [ps_b], [vab[kt]])

            OBANK = [[0, 1], [2, 3]]
            STB = [4, 5, 6, 7]
            it = 0
            for jp in range(8):
                g = jp // 2
                for ti in q_tis:
                    t0, n = TCH[ti]
                    kts = list(range(18)) if ti != 0 else [0, 1]
                    ob = OBANK[it % 2]
                    it += 1
                    o_t = [PS[ob[0]], PS[ob[1]]]
                    o_b = [psb[ob[0]], psb[ob[1]]]

                    def s_stage(kt):
                        res = []
                        tik = 0 if kt < 2 else 1 + (kt - 2) // 4
                        for h in range(2):
                            s_t, s_b = psum(STB)
                            mm(s_t[:, 0:n], KT2[h * 64:(h + 1) * 64, g, kt * 128:(kt + 1) * 128],
                               QT[h * 64:(h + 1) * 64, jp, t0:t0 + n], True, True,
                               [kb[g][tik], qb[jp][ti]], [s_b], inc=True)
                            res.append((s_t, s_b))
                        return res

                    def pv_stage(kt, sres):
                        for h in range(2):
                            s_t, s_b = sres[h]
                            pt, ptb_ = tmppt()
                            act(pt[:, 0:n], s_t[:, 0:n], AF.Exp, [s_b], [ptb_], scale=0.125)
                            if h == 0:
                                lhs = VA[:, kt, 64 + 128 * g:192 + 128 * g]
                            else:
                                lhs = VA[:, kt, 128 * g:128 + 128 * g]
                            mm(o_t[h][:, 0:n], lhs, pt[:, 0:n], kt == kts[0], kt == kts[-1],
                               [ptb_, vab[kt]], [o_b[h]], inc=True)

                    prev = s_stage(kts[0])
                    for idx, kt in enumerate(kts):
                        nxt = s_stage(kts[idx + 1]) if idx + 1 < len(kts) else None
                        pv_stage(kt, prev)
                        prev = nxt
                    for h in range(2):
                        rc, rcb = tmp32()
                        recip(rc[:, 0:n], o_t[h][:, 0:n], [o_b[h]], [rcb])
                        lo, hi = (0, 64) if h == 0 else (64, 128)
                        dlo, dhi = (64, 128) if h == 0 else (0, 64)
                        tt(H3[lo:hi, jp, t0:t0 + n], o_t[h][lo:hi, 0:n], rc[dlo:dhi, 0:n], ALU.mult,
                           [o_b[h], rcb], [st_.hb[jp][ti]])
            dbg_dump("att_xr", XR[:, :], [128, 8 * NT], F32)
            dbg_dump("att_aux", AUX[:, :], [128, 10368], BF16)
            dbg_dump("att_hb", HBt[:, :], [128, 8 * NT], BF16)
            reload(li_index)
            linear(2, 4, 8, 512, lambda wt, ocl, kc: wt[:, kc, ocl * 128:(ocl + 1) * 128], hb_rhs, hb_bufs,
                   q_tis, resid_evac(2, q_tis))

        def conformer(li, need_ctx, li_index):
            UC = XRb[:, 0:8 * 286].rearrange("p (c t) -> p c t", c=8)
            UL = XRb[:, 2288:2288 + 8 * 2078].rearrange("p (c t) -> p c t", c=8)
            DG = [XRb[:, 18912 + i * 3968:18912 + (i + 1) * 3968].rearrange("p (k m) -> p k m", k=31) for i in range(2)]
            ub = [[p.buf() for _ in range(5)] for _ in range(8)]
            upad = p.buf()
            dgb = [p.buf(), p.buf()]
            vmemset(XRb[:, 0:18912], 0.0, [upad] + [b for row in ub for b in row])
            tis = [0, 1, 2, 3, 4]

            def useg(c, ti, k, n):
                if ti == 0:
                    return UC[:, c, k:k + n]
                o = TCH[ti][0] - NCTX
                return UL[:, c, o + k:o + k + n]

            for b in range(4):
                wt, wb = w_next(8, 512)
                for cl in range(2):
                    c = b * 2 + cl
                    for ti in tis:
                        t0, n = TCH[ti]
                        pa_t, pa_b = psum()
                        for kc in range(8):
                            mm(pa_t[:, 0:n], wt[:, kc, cl * 128:(cl + 1) * 128], H3[:, kc, t0:t0 + n], kc == 0, kc == 7,
                               [wb, st_.hb[kc][ti]], [pa_b], inc=(kc == 7))
                        pg_t, pg_b = psum()
                        for kc in range(8):
                            mm(pg_t[:, 0:n], wt[:, kc, 256 + cl * 128:256 + (cl + 1) * 128], H3[:, kc, t0:t0 + n], kc == 0, kc == 7,
                               [wb, st_.hb[kc][ti]], [pg_b], inc=(kc == 7))
                        sg, sgb = tmp32()
                        act(sg[:, 0:n], pg_t[:, 0:n], AF.Sigmoid, [pg_b, b_vec], [sgb], bias=vcol("cbin", 8 + c))
                        stt(useg(c, ti, 15, n), pa_t[:, 0:n], vcol("cbin", c), sg[:, 0:n], ALU.add, ALU.mult,
                            [pa_b, sgb, b_vec, upad], [ub[c][ti]])
            vb = [[p.buf() for _ in range(5)] for _ in range(8)]
            for c in range(8):
                par = c % 2
                for k in range(31):
                    ts(DG[par][:, k, :], IDENT, vcol("cwdw", k * 8 + c), None, ALU.mult, None, [b_const, b_vec], [dgb[par]])
                for ti in tis:
                    t0, n = TCH[ti]
                    ps_t, ps_b = psum()
                    nb = [ub[c][ti]]
                    if ti > 1:
                        nb.append(ub[c][ti - 1])
                    if 1 <= ti < 4:
                        nb.append(ub[c][ti + 1])
                    for k in range(31):
                        mm(ps_t[:, 0:n], DG[par][:, k, :], useg(c, ti, k, n), k == 0, k == 30,
                           [dgb[par], upad] + nb, [ps_b], inc=(k == 30))
                    act(H3[:, c, t0:t0 + n], ps_t[:, 0:n], AF.Identity, [ps_b, b_vec],
                        [vb[c][ti], st_.hb[c][ti]], bias=vcol("cbdw", c))
            TB3 = TB[:, :].rearrange("p (c t) -> p c t", c=8)
            yb = [[p.buf() for _ in range(5)] for _ in range(8)]
            for ti in tis:
                t0, n = TCH[ti]
                pm_t, pm_b = psum()
                for c in range(8):
                    mm(pm_t[:, 0:n], ONES, H3[:, c, t0:t0 + n], c == 0, c == 7, [vb[c][ti], b_const], [pm_b], inc=(c == 7))
                for c in range(8):
                    act(TB3[:, c, 0:n], H3[:, c, t0:t0 + n], AF.Square, [vb[c][ti]], [b_tb])
                pq_t, pq_b = psum()
                for c in range(8):
                    mm(pq_t[:, 0:n], ONES, TB3[:, c, 0:n], c == 0, c == 7, [b_tb, b_const], [pq_b], inc=(c == 7))
                mean, meanb = T32H[1], t32hb[1]
                act(mean[:, 0:n], pm_t[:, 0:n], AF.Identity, [pm_b], [meanb], scale=1.0 / D)
                m2, m2b = tmp32()
                tt(m2[:, 0:n], mean[:, 0:n], mean[:, 0:n], ALU.mult, [meanb], [m2b])
                var, varb = tmp32()
                stt(var[:, 0:n], pq_t[:, 0:n], 1.0 / D, m2[:, 0:n], ALU.mult, ALU.subtract, [pq_b, m2b], [varb])
                sd, sdb = tmp32()
                act(sd[:, 0:n], var[:, 0:n], AF.Sqrt, [varb, b_misc], [sdb], bias=MISC[:, 0:1], scale=1.0)
                rs, rsb = T32H[0], t32hb[0]
                recip(rs[:, 0:n], sd[:, 0:n], [sdb], [rsb])
                for c in range(8):
                    t_, tb_ = tmppt32()
                    tt(t_[:, 0:n], H3[:, c, t0:t0 + n], mean[:, 0:n], ALU.subtract, [vb[c][ti], meanb], [tb_])
                    tt(t_[:, 0:n], t_[:, 0:n], rs[:, 0:n], ALU.mult, [tb_, rsb], [tb_])
                    act(H3[:, c, t0:t0 + n], t_[:, 0:n], AF.Silu, [tb_, b_vec], [yb[c][ti], vb[c][ti]],
                        bias=vcol("cnb", c), scale=vcol("cng", c))
            st_.hb = yb
            reload(li_index)
            linear(2, 4, 8, 512, lambda wt, ocl, kc: wt[:, kc, ocl * 128:(ocl + 1) * 128], hb_rhs, hb_bufs,
                   tis, resid_evac(2, tis, bias_name="cbout"))

        st_.rr["tbx"] = 0

        def tmppt32():
            i = st_.rr["tbx"] % 2
            st_.rr["tbx"] += 1
            return T32X[i], t32xb[i]

        def rglru(li, need_ctx, li_index):
            G3 = XRb[:, 0:18432].rearrange("p (c t) -> p c t", c=8)
            XL3 = XRb[:, 18432:36864].rearrange("p (c t) -> p c t", c=8)
            gb = [[p.buf() for _ in range(5)] for _ in range(8)]
            xlb = [[p.buf() for _ in range(5)] for _ in range(8)]
            tis = [0, 1, 2, 3, 4]
            for b in range(4):
                wt, wb = w_next(8, 512)
                for ocl in range(4):
                    oc = b * 4 + ocl
                    for ti in tis:
                        t0, n = TCH[ti]
                        ps_t, ps_b = psum()
                        for kc in range(8):
                            mm(ps_t[:, 0:n], wt[:, kc, ocl * 128:(ocl + 1) * 128], H3[:, kc, t0:t0 + n], kc == 0, kc == 7,
                               [wb, st_.hb[kc][ti]], [ps_b], inc=(kc == 7))
                        if oc < 8:
                            act(G3[:, oc, t0:t0 + n], ps_t[:, 0:n], AF.Gelu_apprx_tanh, [ps_b], [gb[oc][ti]])
                        else:
                            vcopy(XL3[:, oc - 8, t0:t0 + n], ps_t[:, 0:n], [ps_b], [xlb[oc - 8][ti]])
            lam = VEC[:, VOFF["llam"]:VOFF["llam"] + 16]
            b_ca = p.buf()
            act(MISC[:, 64:80], lam, AF.Exp, [b_vec], [b_ca], scale=-1.0)
            act(MISC[:, 80:96], MISC[:, 64:80], AF.Ln, [b_ca, b_misc], [b_ca], bias=MISC[:, 1:2], scale=1.0)
            ts(MISC[:, 16:32], MISC[:, 80:96], -8.0, None, ALU.mult, None, [b_ca], [b_ca])
            ts(MISC[:, 32:48], MISC[:, 80:96], -16.0, None, ALU.mult, None, [b_ca], [b_ca])
            dbg_dump("lru_xr0", XR[:, :], [128, 8 * NT], F32)
            p.barrier(("pe", "act", "dve"))
            U32 = HBf[:, 0:4608].rearrange("p (c t) -> p c t", c=2)
            HS = HBf[:, 4608:9216].rearrange("p (c t) -> p c t", c=2)
            UBF = AUX[:, 0:4608].rearrange("p (c t) -> p c t", c=2)
            XT = [AUXf[:, 2304 + i * 512:2304 + (i + 1) * 512] for i in range(5)]
            xtb = [p.buf() for _ in range(5)]
            CAR = MISC[:, 128:256]
            car_i = [0]
            rrx = [0]

            def ltmp():
                i = rrx[0] % 11
                rrx[0] += 1
                if i < 5:
                    return XT[i], xtb[i]
                return T32[i - 5], t32b[i - 5]

            gslots = []
            gidx = []
            for d in range(2):
                gidx.append(ws.consumed)
                gslots.append(w_next(16, 256, pin=True))
            SEGS = [(0, NCTX), (NCTX, NLAT)]
            u32b = [p.buf(), p.buf()]
            ubfb = [p.buf(), p.buf()]
            hsb = [[p.buf() for _ in range(5)] for _ in range(2)]
            for nblk in range(4):
                c0 = nblk * 2
                for d in range(2):
                    gwt, gwb = gslots[d]
                    for cl in range(2):
                        c = c0 + cl
                        xall = xlb[c]
                        for (s0, sn) in SEGS:
                            ts(U32[:, cl, s0:s0 + sn], XL3[:, c, s0:s0 + sn], vcol("lcw", (d * 4 + 3) * 8 + c), vcol("lcb", d * 8 + c),
                               ALU.mult, ALU.add, xall + [b_vec], [u32b[cl]])
                            for k in range(3):
                                sh = 3 - k
                                if d == 0:
                                    o_ap = U32[:, cl, s0 + sh:s0 + sn]
                                    i_ap = XL3[:, c, s0:s0 + sn - sh]
                                else:
                                    o_ap = U32[:, cl, s0:s0 + sn - sh]
                                    i_ap = XL3[:, c, s0 + sh:s0 + sn]
                                stt(o_ap, i_ap, vcol("lcw", (d * 4 + k) * 8 + c), o_ap, ALU.mult, ALU.add,
                                    xall + [b_vec, u32b[cl]], [u32b[cl]])
                        act(UBF[:, cl, :], U32[:, cl, :], AF.Identity, [u32b[cl]], [ubfb[cl]])
                    for cl in range(2):
                        c = c0 + cl
                        order = [0, 1, 2, 3, 4] if d == 0 else [0, 4, 3, 2, 1]
                        prev_car = None
                        for oi, ti in enumerate(order):
                            t0, n = TCH[ti]
                            gps = []
                            for gi in range(2):
                                ps_t, ps_b = psum()
                                for kc in range(2):
                                    mm(ps_t[:, 0:n], gwt[:, (gi * 4 + nblk) * 2 + kc, cl * 128:(cl + 1) * 128], UBF[:, kc, t0:t0 + n],
                                       kc == 0, kc == 1, [gwb, ubfb[kc]], [ps_b], inc=(kc == 1))
                                gps.append((ps_t, ps_b))
                            r_, rb_ = ltmp()
                            act(r_[:, 0:n], gps[0][0][:, 0:n], AF.Sigmoid, [gps[0][1], b_vec], [rb_], bias=vcol("lgb", (d * 2 + 0) * 8 + c))
                            a_, ab_ = ltmp()
                            act(a_[:, 0:n], r_[:, 0:n], AF.Exp, [rb_, b_ca], [ab_], scale=MISC[:, 16 + d * 8 + c:17 + d * 8 + c])
                            m_, mb_ = ltmp()
                            act(m_[:, 0:n], r_[:, 0:n], AF.Exp, [rb_, b_ca], [mb_], scale=MISC[:, 32 + d * 8 + c:33 + d * 8 + c])
                            act(m_[:, 0:n], m_[:, 0:n], AF.Sqrt, [mb_, b_misc], [mb_], bias=MISC[:, 1:2], scale=-1.0)
                            i_, ib_ = ltmp()
                            act(i_[:, 0:n], gps[1][0][:, 0:n], AF.Sigmoid, [gps[1][1], b_vec], [ib_], bias=vcol("lgb", (d * 2 + 1) * 8 + c))
                            if ti == 0:
                                fc = 0 if d == 0 else NCTX - 1
                                vmemset(m_[:, fc:fc + 1], 1.0, [mb_])
                            tt(i_[:, 0:n], i_[:, 0:n], m_[:, 0:n], ALU.mult, [ib_, mb_], [ib_])
                            tt(i_[:, 0:n], i_[:, 0:n], U32[:, cl, t0:t0 + n], ALU.mult, [ib_, u32b[cl]], [ib_])
                            if d == 0:
                                init = 0.0 if oi == 0 else HS[:, cl, t0 - 1:t0]
                                rd = [ab_, ib_] + ([hsb[cl][ti - 1]] if oi > 0 else [])
                                p.op("dve", lambda e, o=HS[:, cl, t0:t0 + n], a=a_[:, 0:n], b=i_[:, 0:n], init=init:
                                     e.tensor_tensor_scan(out=o, data0=a, data1=b, initial=init, op0=ALU.mult, op1=ALU.add),
                                     rd, [hsb[cl][ti]])
                            else:
                                h_, hb_ = ltmp()
                                init = 0.0 if oi == 0 else prev_car
                                p.op("dve", lambda e, o=h_[:, 0:n][:, ::-1], a=a_[:, 0:n][:, ::-1], b=i_[:, 0:n][:, ::-1], init=init:
                                     e.tensor_tensor_scan(out=o, data0=a, data1=b, initial=init, op0=ALU.mult, op1=ALU.add),
                                     [ab_, ib_, b_car], [hb_])
                                ci = car_i[0] % 128
                                car_i[0] += 1
                                vcopy(CAR[:, ci:ci + 1], h_[:, 0:1], [hb_], [b_car])
                                prev_car = CAR[:, ci:ci + 1]
                                tt(h_[:, 0:n], h_[:, 0:n], HS[:, cl, t0:t0 + n], ALU.add, [hb_, hsb[cl][ti]], [hb_])
                                tt(G3[:, c, t0:t0 + n], h_[:, 0:n], G3[:, c, t0:t0 + n], ALU.mult, [hb_, gb[c][ti]], [gb[c][ti]])
            for gi_ in gidx:
                w_release(gi_)
            dbg_dump("lru_xr1", XR[:, :], [128, 8 * NT], F32)
            p.barrier(("pe", "act", "dve"))
            yb = [[p.buf() for _ in range(5)] for _ in range(8)]
            for c in range(8):
                for ti in tis:
                    t0, n = TCH[ti]
                    if (c + ti) % 2 == 0:
                        vcopy(H3[:, c, t0:t0 + n], G3[:, c, t0:t0 + n], [gb[c][ti]], [yb[c][ti]])
                    else:
                        act(H3[:, c, t0:t0 + n], G3[:, c, t0:t0 + n], AF.Identity, [gb[c][ti]], [yb[c][ti]])
            st_.hb = yb
            reload(li_index)
            linear(2, 4, 8, 512, lambda wt, ocl, kc: wt[:, kc, ocl * 128:(ocl + 1) * 128], hb_rhs, hb_bufs,
                   tis, resid_evac(2, tis))

        vmemset(MISC[:, 0:1], EPS, [b_misc])
        vmemset(MISC[:, 1:2], 1.0, [b_misc])
        for idx, li in enumerate(layers):
            kind = li % 3
            need_ctx = li < DEPTH - 1
            mod_phase(li)
            st_.hb = grid(f"h{li}_")
            norm_phase(0, [0, 1, 2, 3, 4])
            if DBG and idx == 0:
                p.dma("sp", dbg_mods, MODS[:, :], reads=[b_mods], writes=[o_buf])
                p.dma("sp", dbg_ab, AB[:, :], reads=[b_ab], writes=[o_buf])
                p.dma("sp", dbg_h1, HBt[:, :], reads=[b for row in st_.hb for b in row], writes=[o_buf])
            first_in_prog = (idx == 0)
            spill(0 if first_in_prog else 1)
            lidx = 0 if first_in_prog else 1
            import os as _os
            if _os.environ.get("DBG_SKIP_MIX"):
                for _ in range({0: 6, 1: 6, 2: 8}[kind]):
                    w_next(8, 512)
                reload(lidx)
            elif kind == 0:
                attention(li, li // 3, need_ctx, lidx)
            elif kind == 1:
                conformer(li, need_ctx, lidx)
            else:
                rglru(li, need_ctx, lidx)
            tis = [0, 1, 2, 3, 4] if need_ctx else [1, 2, 3, 4]
            p.barrier(("pe", "act", "dve"))
            st_.hb = grid(f"h2{li}_")
            if _os.environ.get("DBG_SKIP_MLP"):
                for _ in range(16):
                    w_next(8, 512)
            else:
                if DBG and idx == 0:
                    p.dma("sp", dbg_x1, XR[:, :], reads=[b for row in st_.xb for b in row], writes=[o_buf])
                norm_phase(1, tis)
                if DBG and idx == 0:
                    p.dma("sp", dbg_h2, HBt[:, :], reads=[b for row in st_.hb for b in row], writes=[o_buf])
                mlp_phase(li, tis)
        allb = [b for row in st_.xb for b in row]
        if last:
            for c in range(8):
                p.dma("sp", outT[c * 128:(c + 1) * 128, :], X3[:, c, NCTX:NT], reads=allb, writes=[o_buf])
        else:
            p.dma("sp", xs_out, XR[:, :], reads=allb, writes=[o_buf])
        p._wait("sp", o_buf.w)
        assert ws.consumed == len(wspecs), (ws.consumed, len(wspecs))
        p.emit()
    return nc


_WKEYS = ["mod_w", "mlp_w1", "mlp_w2", "attn_w_qkv", "attn_w_o", "conv_w_in", "conv_w_out",
          "lru_w_in", "lru_gate_w", "lru_w_out"]


def _common_maps(inputs):
    m = {k: np.ascontiguousarray(np.asarray(inputs[k], np.float32)) for k in _WKEYS}
    m["consts"] = _consts()
    m["rope"] = _rope_tables()
    return m


def kernel(**inputs):
    n = 8
    common = _common_maps(inputs)
    x = np.asarray(inputs["x"], np.float32)
    ctx = np.asarray(inputs["ctx"], np.float32)
    in_maps = []
    for b in range(n):
        m = dict(common)
        m["xT"] = np.ascontiguousarray(x[b].T)
        m["ctxT"] = np.ascontiguousarray(ctx[b].T)
        m["vecs"] = _pack_vecs(inputs, b)
        in_maps.append(m)
    nc = build_program((0, 1, 2, 3), True, True)
    res = run_bass_kernel_spmd(nc, in_maps, core_ids=list(range(n)))
    out = np.stack([np.ascontiguousarray(res.results[b]["outT"].T) for b in range(n)], axis=0)
    return out.astype(np.float32)
```

```python
import numpy as np
from contextlib import ExitStack
import concourse.bass as bass
import concourse.mybir as mybir
from concourse.bass_utils import run_bass_kernel_spmd
import ml_dtypes

F32 = mybir.dt.float32
BF16 = mybir.dt.bfloat16
AF = mybir.ActivationFunctionType
ALU = mybir.AluOpType

SELF_WAIT = True

NT, NCTX, NLAT, D = 2304, 256, 2048, 1024
TCH = [(0, 256), (256, 512), (768, 512), (1280, 512), (1792, 512)]
EPS = 1e-6
DEPTH = 4


class Sem:
    def __init__(self, h, name):
        self.h = h
        self.name = name
        self.count = 0


class Buf:
    __slots__ = ("name", "w", "r", "dsem")

    def __init__(self, name, dsem=None):
        self.name = name
        self.w = None
        self.r = []
        self.dsem = dsem


class Prog:
    ENG = ("pe", "act", "dve", "pool", "sp")

    def __init__(self, nc, stack):
        self.nc = nc
        self.stack = stack
        self.ops = {e: [] for e in self.ENG}
        self.esem = {}
        for e in ("pe", "act", "dve"):
            self.esem[e] = self.new_sem("s_" + e)
        self.known = {e: {} for e in self.ENG}
        self.pending_noinc = {e: False for e in self.ENG}
        self.nbuf = 0

    def new_sem(self, name):
        h = self.stack.enter_context(self.nc.semaphore(name))
        return Sem(h, name)

    def buf(self, name=None, dma=False):
        self.nbuf += 1
        name = name or f"b{self.nbuf}"
        return Buf(name, self.new_sem("d_" + name) if dma else None)

    def sb(self, name, shape, dt):
        return self.stack.enter_context(self.nc.sbuf_tensor(name, list(shape), dt))

    def ps(self, name, shape, dt=F32):
        return self.stack.enter_context(self.nc.psum_tensor(name, list(shape), dt))

    def _wait(self, eng, tok):
        if tok is None:
            return
        sem, val = tok
        if eng == "pe" and sem is self.esem["pe"]:
            return
        if (not SELF_WAIT) and eng in self.esem and sem is self.esem[eng]:
            return
        k = self.known[eng]
        if k.get(sem, 0) >= val:
            return
        k[sem] = val
        h = sem.h
        self.ops[eng].append(lambda e, h=h, val=val: e.wait_ge(h, val))

    def _deps(self, eng, reads, writes):
        for b in reads:
            self._wait(eng, b.w)
        for b in writes:
            self._wait(eng, b.w)
            for t in b.r:
                self._wait(eng, t)

    def _commit(self, tok, reads, writes):
        for b in reads:
            b.r.append(tok)
            if len(b.r) > 16:
                d = {}
                for s, v in b.r:
                    if d.get(s, 0) < v:
                        d[s] = v
                b.r = list(d.items())
        for b in writes:
            b.w = tok
            b.r = []

    def op(self, eng, fn, reads=(), writes=(), inc=True):
        self._deps(eng, reads, writes)
        sem = self.esem[eng]
        if inc:
            sem.count += 1
            val = sem.count
            h = sem.h
            self.ops[eng].append(lambda e, fn=fn, h=h: fn(e).then_inc(h, 1))
            self.pending_noinc[eng] = False
        else:
            val = sem.count + 1
            self.ops[eng].append(lambda e, fn=fn: fn(e))
            self.pending_noinc[eng] = True
        tok = (sem, val)
        self._commit(tok, reads, writes)
        return tok

    def dma(self, q, out, in_, reads=(), writes=(), dsem=None):
        self._deps(q, reads, writes)
        sem = dsem if dsem is not None else writes[0].dsem
        sem.count += 16
        val = sem.count
        h = sem.h
        self.ops[q].append(lambda e, out=out, in_=in_, h=h: e.dma_start(out=out, in_=in_).then_inc(h, 16))
        tok = (sem, val)
        self._commit(tok, reads, writes)
        return tok

    def barrier(self, engs=("pe", "act", "dve", "sp")):
        for e in ("pe", "act", "dve"):
            assert not self.pending_noinc[e]
        toks = [(self.esem[e], self.esem[e].count) for e in ("pe", "act", "dve")]
        for e in engs:
            for t in toks:
                if t[1] > 0:
                    self._wait(e, t)

    def emit(self):
        for e in ("pe", "act", "dve"):
            assert not self.pending_noinc[e], f"engine {e} has trailing non-inc op"
        ops = self.ops
        with self.nc.Block() as block:
            @block.sync
            def _(eng):
                for f in ops["sp"]:
                    f(eng)

            @block.tensor
            def _(eng):
                for f in ops["pe"]:
                    f(eng)

            @block.scalar
            def _(eng):
                for f in ops["act"]:
                    f(eng)

            @block.vector
            def _(eng):
                for f in ops["dve"]:
                    f(eng)

            @block.gpsimd
            def _(eng):
                for f in ops["pool"]:
                    f(eng)


def _vec_layout():
    L = [("c", 8), ("cctx", 8)]
    for i in range(DEPTH):
        L += [(f"modb{i}", 48), (f"gmix{i}", 8), (f"gmlp{i}", 8)]
    for j in range(2):
        L += [(f"qg{j}", 1), (f"kg{j}", 1)]
    L += [("cbin", 16), ("cwdw", 248), ("cbdw", 8), ("cng", 8), ("cnb", 8), ("cbout", 8)]
    L += [("lcw", 64), ("lcb", 16), ("lgb", 32), ("llam", 16)]
    off = {}
    o = 0
    for n, k in L:
        off[n] = o
        o += k
    return off, o


VOFF, NV = _vec_layout()


def _pk(v):
    v = np.asarray(v, np.float32).reshape(-1, 128)
    return np.ascontiguousarray(v.T)


def _pack_vecs(inp, b):
    V = np.zeros((128, NV), np.float32)

    def put(name, arr):
        arr = np.asarray(arr, np.float32)
        V[:, VOFF[name]:VOFF[name] + arr.shape[1]] = arr

    put("c", _pk(inp["c"][b]))
    put("cctx", _pk(inp["c_ctx"]))
    for i in range(DEPTH):
        put(f"modb{i}", _pk(inp["mod_b"][i]))
        put(f"gmix{i}", _pk(inp["norm_mix_g"][i]))
        put(f"gmlp{i}", _pk(inp["norm_mlp_g"][i]))
    for j in range(2):
        put(f"qg{j}", np.tile(np.asarray(inp["attn_q_gain"][j], np.float32), 2)[:, None])
        put(f"kg{j}", np.tile(np.asarray(inp["attn_k_gain"][j], np.float32), 2)[:, None])
    put("cbin", _pk(inp["conv_b_in"][0]))
    wdw = np.asarray(inp["conv_w_dw"][0], np.float32).reshape(31, 8, 128).transpose(2, 0, 1).reshape(128, 248)
    put("cwdw", wdw)
    put("cbdw", _pk(inp["conv_b_dw"][0]))
    put("cng", _pk(inp["conv_norm_g"][0]))
    put("cnb", _pk(inp["conv_norm_b"][0]))
    put("cbout", _pk(inp["conv_b_out"][0]))
    lcw = np.asarray(inp["lru_conv_w"][0], np.float32).reshape(2, 4, 8, 128).transpose(3, 0, 1, 2).reshape(128, 64)
    put("lcw", lcw)
    put("lcb", np.asarray(inp["lru_conv_b"][0], np.float32).reshape(2, 8, 128).transpose(2, 0, 1).reshape(128, 16))
    put("lgb", np.asarray(inp["lru_gate_b"][0], np.float32).reshape(2, 2, 8, 128).transpose(3, 0, 1, 2).reshape(128, 32))
    put("llam", np.asarray(inp["lru_lambda"][0], np.float32).reshape(2, 8, 128).transpose(2, 0, 1).reshape(128, 16))
    return V


def _consts():
    p = np.arange(128)
    perm = (p[:, None] == (p[None, :] ^ 16)).astype(np.float32)
    onesblk = ((p[:, None] // 64) == (p[None, :] // 64)).astype(np.float32)
    ones = np.ones((128, 128), np.float32)
    ident = np.eye(128, dtype=np.float32)
    return np.concatenate([perm, onesblk, ones, ident], axis=1).astype(ml_dtypes.bfloat16)


def _rope_tables():
    t = np.arange(NLAT)
    row = (t // 64).astype(np.float64)
    col = (t % 64).astype(np.float64)
    inv = 10000.0 ** (-np.arange(16, dtype=np.float64) / 16.0)
    p = np.arange(128)
    d = p % 64
    a = d // 32
    half = (d // 16) % 2
    f = d % 16
    pos = np.where(a[:, None] == 0, row[None, :], col[None, :])
    ang = (pos.astype(np.float32) * inv.astype(np.float32)[f][:, None]).astype(np.float32)
    C = np.cos(ang).astype(np.float32)
    S = np.sin(ang).astype(np.float32) * np.where(half == 0, -1.0, 1.0)[:, None].astype(np.float32)
    return np.ascontiguousarray(np.concatenate([C, S], axis=1).astype(np.float32))


class K:
    pass


def build_program(layers=(0, 1, 2, 3), first=True, last=True):
    nc = bass.Bass("TRN2", target_bir_lowering=False)
    dr = {}

    def din(name, shape, dt=F32):
        dr[name] = nc.dram_tensor(name, list(shape), dt, kind="ExternalInput").ap()
        return dr[name]

    if first:
        din("xT", [D, NLAT])
        din("ctxT", [D, NCTX])
    else:
        din("xs_in", [128, 8 * NT])
    din("vecs", [128, NV])
    din("consts", [128, 512], BF16)
    din("rope", [128, 2 * NLAT])
    din("mod_w", [4, D, 6 * D])
    din("mlp_w1", [4, D, 4 * D])
    din("mlp_w2", [4, 4 * D, D])
    din("attn_w_qkv", [2, D, 1536])
    din("attn_w_o", [2, D, D])
    din("conv_w_in", [1, D, 2 * D])
    din("conv_w_out", [1, D, D])
    din("lru_w_in", [1, D, 2 * D])
    din("lru_gate_w", [1, 2, 2, 4, 256, 256])
    din("lru_w_out", [1, D, D])
    if last:
        outT = nc.dram_tensor("outT", [D, NLAT], F32, kind="ExternalOutput").ap()
    else:
        xs_out = nc.dram_tensor("xs_out", [128, 8 * NT], F32, kind="ExternalOutput").ap()
    xs = nc.dram_tensor("xs_scr", [128, 8 * NT], F32, kind="Internal").ap()
    import os as _os
    DBG = bool(_os.environ.get("DBG_DUMP"))
    if DBG:
        dbg_mods = nc.dram_tensor("dbg_mods", [128, 96], F32, kind="ExternalOutput").ap()
        dbg_ab = nc.dram_tensor("dbg_ab", [128, 64], F32, kind="ExternalOutput").ap()
        dbg_h1 = nc.dram_tensor("dbg_h1", [128, 8 * NT], BF16, kind="ExternalOutput").ap()
        dbg_h2 = nc.dram_tensor("dbg_h2", [128, 8 * NT], BF16, kind="ExternalOutput").ap()
        dbg_x1 = nc.dram_tensor("dbg_x1", [128, 8 * NT], F32, kind="ExternalOutput").ap()

    with ExitStack() as st:
        p = Prog(nc, st)
        XR = p.sb("XR", [128, 8 * NT], F32)
        HBt = p.sb("HB", [128, 8 * NT], BF16)
        AUX = p.sb("AUX", [128, 10368], BF16)
        SLOT = [p.sb(f"slot{i}", [128, 4096], BF16) for i in range(4)]
        VEC = p.sb("VEC", [128, NV], F32)
        CONST = p.sb("CONST", [128, 512], BF16)
        MODS_ = [p.sb(f"MODS{i}", [128, 96], F32) for i in range(2)]
        AB_ = [p.sb(f"AB{i}", [128, 64], F32) for i in range(2)]
        SC = p.sb("SC", [128, 16], BF16)
        MISC = p.sb("MISC", [128, 256], F32)
        T32 = [p.sb(f"t32_{i}", [128, 512], F32) for i in range(6)]
        TB = p.sb("TB", [128, 8 * 512], BF16)
        T32X = [p.sb(f"t32x_{i}", [128, 512], F32) for i in range(2)]
        t32xb = [p.buf(f"t32x_{i}") for i in range(2)]
        T32H = [p.sb(f"t32h_{i}", [128, 512], F32) for i in range(2)]
        t32hb = [p.buf(f"t32h_{i}") for i in range(2)]
        b_tb = p.buf("tb")
        b_car = p.buf("car")
        PT = [p.sb(f"pt{i}", [128, 512], BF16) for i in range(6)]
        PS = [p.ps(f"ps{i}", [128, 512], F32) for i in range(8)]
        psb = [p.buf(f"ps{i}") for i in range(8)]
        t32b = [p.buf(f"t32_{i}") for i in range(6)]
        ptb = [p.buf(f"pt{i}") for i in range(6)]
        slotb = [p.buf(f"slot{i}", dma=True) for i in range(4)]
        b_vec = p.buf("vec", dma=True)
        b_const = p.buf("const", dma=True)
        b_mods_ = [p.buf("mods0"), p.buf("mods1")]
        b_ab_ = [p.buf("ab0"), p.buf("ab1")]
        b_sc = p.buf("sc")
        b_misc = p.buf("misc")
        x_dsem = p.new_sem("d_x")
        o_buf = p.buf("out", dma=True)
        spill_buf = p.buf("spill", dma=True)

        dbg_outs = {}

        def dbg_dump(name, ap, shape, dt):
            if not DBG:
                return
            t = nc.dram_tensor("dd_" + name, list(shape), dt, kind="ExternalOutput").ap()
            p.barrier(("sp",))
            p.dma("sp", t, ap, writes=[o_buf])
            for e in ("pe", "act", "dve"):
                p._wait(e, o_buf.w)

        PERM = CONST[:, 0:128]
        ONESBLK = CONST[:, 128:256]
        ONES = CONST[:, 256:384]
        IDENT = CONST[:, 384:512]

        X3 = XR[:, :].rearrange("p (c t) -> p c t", c=8)
        XRb = XR[:, :].bitcast(BF16)
        H3 = HBt[:, :].rearrange("p (c t) -> p c t", c=8)
        HBf = HBt[:, :].bitcast(F32)
        AUXf = AUX[:, :].bitcast(F32)

        def grid(name):
            return [[p.buf(f"{name}{c}_{t}") for t in range(5)] for c in range(8)]

        st_ = K()
        st_.xb = grid("x")
        st_.hb = grid("h")
        st_.rr = {"t32": 0, "pt": 0, "ps": 0}

        def vcol(name, j=0):
            o = VOFF[name] + j
            return VEC[:, o:o + 1]

        def tmp32():
            i = st_.rr["t32"] % 6
            st_.rr["t32"] += 1
            return T32[i], t32b[i]

        def tmppt():
            i = st_.rr["pt"] % 6
            st_.rr["pt"] += 1
            return PT[i], ptb[i]

        def psum(group=None):
            group = group if group is not None else list(range(7))
            key = ("ps",) + tuple(group)
            k_ = st_.rr.get(key, 0)
            st_.rr[key] = k_ + 1
            i = group[k_ % len(group)]
            return PS[i], psb[i]

        def mm(out, lhsT, rhs, start, stop, reads, writes, inc):
            p.op("pe", lambda e: e.matmul(out, lhsT, rhs, start=start, stop=stop), reads, writes, inc=inc)

        def act(out, in_, func, reads, writes, bias=None, scale=None):
            kw = {}
            if bias is not None:
                kw["bias"] = bias
            if scale is not None:
                kw["scale"] = scale
            p.op("act", lambda e: e.activation(out=out, in_=in_, func=func, **kw), reads, writes)

        def tt(out, in0, in1, op, reads, writes):
            p.op("dve", lambda e: e.tensor_tensor(out=out, in0=in0, in1=in1, op=op), reads, writes)

        def ts(out, in0, s1, s2, op0, op1, reads, writes):
            if s2 is None:
                p.op("dve", lambda e: e.tensor_scalar(out=out, in0=in0, scalar1=s1, scalar2=None, op0=op0), reads, writes)
            else:
                p.op("dve", lambda e: e.tensor_scalar(out=out, in0=in0, scalar1=s1, scalar2=s2, op0=op0, op1=op1), reads, writes)

        def stt(out, in0, scalar, in1, op0, op1, reads, writes):
            p.op("dve", lambda e: e.scalar_tensor_tensor(out=out, in0=in0, scalar=scalar, in1=in1, op0=op0, op1=op1), reads, writes)

        def recip(out, in_, reads, writes):
            p.op("dve", lambda e: e.reciprocal(out=out, in_=in_), reads, writes)

        def vcopy(out, in_, reads, writes):
            p.op("dve", lambda e: e.tensor_copy(out=out, in_=in_), reads, writes)

        def vmemset(ap, val, writes):
            p.op("dve", lambda e: e.memset(ap, val), (), writes)

        wspecs = []

        def wv(ap2d):
            return ap2d.rearrange("(k p) n -> p k n", p=128)

        NMOD = [2, 1, 2, 1, 2, 1, 2, 1]
        for lidx_, li in enumerate(layers):
            kind = li % 3
            j = li // 3
            if lidx_ == 0:
                for b in range(12):
                    wspecs.append([(0, 8, 512, wv(dr["mod_w"][li, :, b * 512:(b + 1) * 512]))])
            if kind == 0:
                wq = dr["attn_w_qkv"][j]
                for b in range(2):
                    wspecs.append([(0, 8, 512, wv(wq[:, b * 512:(b + 1) * 512]))])
                sp_ = []
                for g in range(4):
                    for dup in range(2):
                        sp_.append(((g * 2 + dup) * 64, 8, 64, wv(wq[:, 1024 + g * 64:1024 + (g + 1) * 64]), 512))
                wspecs.append(sp_)
                wspecs.append([(0, 8, 256, wv(wq[:, 1280:1536]))])
                for b in range(2):
                    wspecs.append([(0, 8, 512, wv(dr["attn_w_o"][j][:, b * 512:(b + 1) * 512]))])
            elif kind == 1:
                wi = dr["conv_w_in"][0]
                for b in range(4):
                    wspecs.append([(0, 8, 256, wv(wi[:, b * 256:(b + 1) * 256]), 512),
                                   (256, 8, 256, wv(wi[:, 1024 + b * 256:1024 + (b + 1) * 256]), 512)])
                for b in range(2):
                    wspecs.append([(0, 8, 512, wv(dr["conv_w_out"][0][:, b * 512:(b + 1) * 512]))])
            else:
                wi = dr["lru_w_in"][0]
                for b in range(4):
                    wspecs.append([(0, 8, 512, wv(wi[:, b * 512:(b + 1) * 512]))])
                for d in range(2):
                    gw = dr["lru_gate_w"][0, d].rearrange("g n k e -> (g n k) e")
                    wspecs.append([(0, 16, 256, wv(gw))])
                for b in range(2):
                    wspecs.append([(0, 8, 512, wv(dr["lru_w_out"][0][:, b * 512:(b + 1) * 512]))])
            mb_ = 0
            for hb in range(8):
                wspecs.append([(0, 8, 512, wv(dr["mlp_w1"][li, :, hb * 512:(hb + 1) * 512]))])
                wspecs.append([(0, 4, 1024, wv(dr["mlp_w2"][li, hb * 512:(hb + 1) * 512, :]))])
                if lidx_ + 1 < len(layers):
                    nl_ = layers[lidx_ + 1]
                    for _ in range(NMOD[hb]):
                        wspecs.append([(0, 8, 512, wv(dr["mod_w"][nl_, :, mb_ * 512:(mb_ + 1) * 512]))])
                        mb_ += 1

        ws = K()
        ws.issued = 0
        ws.consumed = 0

        def w_issue(jb):
            s = jb % 4
            for spec in wspecs[jb]:
                if len(spec) == 5:
                    off, kcn, ncol, src, rowlen = spec
                    dst = SLOT[s][:, 0:kcn * rowlen].rearrange("p (k n) -> p k n", k=kcn)[:, :, off:off + ncol]
                else:
                    off, kcn, ncol, src = spec
                    dst = SLOT[s][:, off:off + kcn * ncol].rearrange("p (k n) -> p k n", k=kcn)
                p.dma("pool", dst, src, writes=[slotb[s]])

        ws.released = set()
        ws.pinned = set()

        def w_release(i):
            ws.pinned.discard(i)
            ws.released.add(i)

        def w_next(kcn, ncol, pin=False):
            i = ws.consumed
            if i - 1 >= 0 and (i - 1) not in ws.pinned:
                ws.released.add(i - 1)
            while ws.issued < min(i + 4, len(wspecs)) and (ws.issued < 4 or (ws.issued - 4) in ws.released):
                w_issue(ws.issued)
                ws.issued += 1
            assert ws.issued > i, "weight block not issued (pinned slot deadlock)"
            if pin:
                ws.pinned.add(i)
            ws.consumed += 1
            s = i % 4
            return SLOT[s][:, 0:kcn * ncol].rearrange("p (k n) -> p k n", k=kcn), slotb[s]

        p.dma("sp", VEC[:, :], dr["vecs"], writes=[b_vec])
        p.dma("sp", CONST[:, :], dr["consts"], writes=[b_const])

        def load_x_from_input():
            allb = [b for row in st_.xb for b in row]
            if first:
                p.dma("sp", X3[:, :, 0:NCTX], dr["ctxT"].rearrange("(c p) t -> p c t", p=128), writes=allb, dsem=x_dsem)
                for c in range(8):
                    p.dma("sp", X3[:, c, NCTX:NT], dr["xT"][c * 128:(c + 1) * 128, :], writes=allb, dsem=x_dsem)
            else:
                for c in range(8):
                    p.dma("sp", X3[:, c, :], dr["xs_in"][:, c * NT:(c + 1) * NT], writes=allb, dsem=x_dsem)

        load_x_from_input()
        SC3 = SC[:, :].rearrange("p (k s) -> p k s", s=2)
        act(SC3[:, :, 0], VEC[:, VOFF["cctx"]:VOFF["cctx"] + 8], AF.Silu, [b_vec], [b_sc])
        act(SC3[:, :, 1], VEC[:, VOFF["c"]:VOFF["c"] + 8], AF.Silu, [b_vec], [b_sc])

        st_.par = 0

        def MODS():
            return MODS_[st_.par]

        def AB():
            return AB_[st_.par]

        def b_mods():
            return b_mods_[st_.par]

        def b_ab():
            return b_ab_[st_.par]

        def modcol(grp, c, s):
            return MODS()[:, (grp * 8 + c) * 2 + s:(grp * 8 + c) * 2 + s + 1]

        modst = K()
        modst.nb = 0

        def mod_block():
            b = modst.nb
            modst.nb += 1
            ps_t, ps_b = PS[7], psb[7]
            wt, wb = w_next(8, 512)
            for jj in range(4):
                jx = b * 4 + jj
                for kc in range(8):
                    mm(ps_t[:, 2 * jx:2 * jx + 2], wt[:, kc, jj * 128:(jj + 1) * 128], SC3[:, kc, :],
                       kc == 0, kc == 7, [wb, b_sc], [ps_b], inc=(jj == 3 and kc == 7))

        def mod_finish(li, par):
            assert modst.nb == 12
            modst.nb = 0
            ps_t, ps_b = PS[7], psb[7]
            M = MODS_[par]
            A = AB_[par]
            M3 = M[:, :].rearrange("p (j s) -> p j s", s=2)
            ps3 = ps_t[:, 0:96].rearrange("p (j s) -> p j s", s=2)
            mb = VEC[:, VOFF[f"modb{li}"]:VOFF[f"modb{li}"] + 48]
            for s in range(2):
                tt(M3[:, :, s], ps3[:, :, s], mb, ALU.add, [ps_b, b_vec], [b_mods_[par]])
            gm = VEC[:, VOFF[f"gmix{li}"]:VOFF[f"gmix{li}"] + 8]
            gl = VEC[:, VOFF[f"gmlp{li}"]:VOFF[f"gmlp{li}"] + 8]
            for s in range(2):
                stt(A[:, s * 8:s * 8 + 8], M3[:, 8:16, s], 1.0, gm, ALU.add, ALU.mult, [b_mods_[par], b_vec], [b_ab_[par]])
                stt(A[:, 16 + s * 8:16 + s * 8 + 8], M3[:, 32:40, s], 1.0, gl, ALU.add, ALU.mult, [b_mods_[par], b_vec], [b_ab_[par]])

        def norm_phase(which, tis):
            TB3 = TB[:, :].rearrange("p (c t) -> p c t", c=8)
            for ti in tis:
                t0, n = TCH[ti]
                s = 0 if ti == 0 else 1
                for c in range(8):
                    act(TB3[:, c, 0:n], X3[:, c, t0:t0 + n], AF.Square, [st_.xb[c][ti]], [b_tb])
                ps_t, ps_b = psum()
                for c in range(8):
                    mm(ps_t[:, 0:n], ONES, TB3[:, c, 0:n], c == 0, c == 7, [b_tb, b_const], [ps_b], inc=(c == 7))
                sd, sdb = tmp32()
                act(sd[:, 0:n], ps_t[:, 0:n], AF.Sqrt, [ps_b, b_misc], [sdb], bias=MISC[:, 0:1], scale=1.0 / D)
                rs, rsb = T32H[0], t32hb[0]
                recip(rs[:, 0:n], sd[:, 0:n], [sdb], [rsb])
                for c in range(8):
                    t_, tb_ = tmp32()
                    a_ap = AB()[:, which * 16 + s * 8 + c:which * 16 + s * 8 + c + 1]
                    stt(t_[:, 0:n], X3[:, c, t0:t0 + n], a_ap, rs[:, 0:n], ALU.mult, ALU.mult,
                        [st_.xb[c][ti], b_ab(), rsb], [tb_])
                    act(H3[:, c, t0:t0 + n], t_[:, 0:n], AF.Identity, [tb_, b_mods()], [st_.hb[c][ti]],
                        bias=modcol(0 if which == 0 else 3, c, s))

        def spill(li_index):
            allb = [b for row in st_.xb for b in row]
            if li_index == 0 and True:
                tok = None
            else:
                tok = p.dma("sp", xs, XR[:, :], reads=allb, writes=[spill_buf])
            p.barrier(("pe", "act", "dve", "sp"))
            if tok is not None:
                for e in ("pe", "act", "dve", "sp"):
                    p._wait(e, tok)
            return tok

        def reload(li_index):
            p.barrier(("pe", "act", "dve", "sp"))
            st_.xb = grid(f"x{li_index}_")
            allb = [b for row in st_.xb for b in row]
            if li_index == 0:
                load_x_from_input()
            else:
                for c in range(8):
                    p.dma("sp", X3[:, c, :], xs[:, c * NT:(c + 1) * NT], reads=[spill_buf], writes=allb, dsem=x_dsem)

        def linear(nblocks, ocs_per_block, kcn, wcols, lhs_fn, rhs_fn, rhs_bufs_fn, tis, evac, psgroup=None):
            for b in range(nblocks):
                wt, wb = w_next(kcn, wcols)
                for ocl in range(ocs_per_block):
                    for ti in tis:
                        t0, n = TCH[ti]
                        ps_t, ps_b = psum(psgroup)
                        for kc in range(kcn):
                            mm(ps_t[:, 0:n], lhs_fn(wt, ocl, kc), rhs_fn(kc, t0, n), kc == 0, kc == kcn - 1,
                               [wb] + rhs_bufs_fn(kc, ti), [ps_b], inc=(kc == kcn - 1))
                        evac(b, ocl, ti, ps_t[:, 0:n], ps_b)

        def resid_evac(grp, tis_all, bias_name=None):
            def ev(b, ocl, ti, ps_ap, ps_b):
                oc = b * 4 + ocl
                t0, n = TCH[ti]
                s = 0 if ti == 0 else 1
                src = ps_ap
                rd = [ps_b]
                if bias_name is not None:
                    t_, tb_ = tmp32()
                    act(t_[:, 0:n], ps_ap, AF.Identity, [ps_b, b_vec], [tb_], bias=vcol(bias_name, oc))
                    src = t_[:, 0:n]
                    rd = [tb_]
                stt(X3[:, oc, t0:t0 + n], src, modcol(grp, oc, s), X3[:, oc, t0:t0 + n], ALU.mult, ALU.add,
                    rd + [b_mods(), st_.xb[oc][ti]], [st_.xb[oc][ti]])
            return ev

        def hb_rhs(kc, t0, n):
            return H3[:, kc, t0:t0 + n]

        def hb_bufs(kc, ti):
            return [st_.hb[kc][ti]]

        def mlp_phase(li, tis, next_li=None, next_par=None):
            HID = AUX[:, 0:4 * NT].rearrange("p (c t) -> p c t", c=4)
            hidb = [[p.buf() for _ in range(5)] for _ in range(4)]
            for hb_i in range(8):
                w1, w1b = w_next(8, 512)
                for ti in tis:
                    t0, n = TCH[ti]
                    for ocl in range(4):
                        ps_t, ps_b = psum()
                        for kc in range(8):
                            mm(ps_t[:, 0:n], w1[:, kc, ocl * 128:(ocl + 1) * 128], H3[:, kc, t0:t0 + n], kc == 0, kc == 7,
                               [w1b, st_.hb[kc][ti]], [ps_b], inc=(kc == 7))
                        t_, tb_ = tmp32()
                        act(t_[:, 0:n], ps_t[:, 0:n], AF.Relu, [ps_b], [tb_])
                        tt(HID[:, ocl, t0:t0 + n], t_[:, 0:n], t_[:, 0:n], ALU.mult, [tb_], [hidb[ocl][ti]])
                w2, w2b = w_next(4, 1024)
                for ti in tis:
                    t0, n = TCH[ti]
                    s = 0 if ti == 0 else 1
                    for oc in range(8):
                        ps_t, ps_b = psum()
                        for kc in range(4):
                            mm(ps_t[:, 0:n], w2[:, kc, oc * 128:(oc + 1) * 128], HID[:, kc, t0:t0 + n], kc == 0, kc == 3,
                               [w2b, hidb[kc][ti]], [ps_b], inc=(kc == 3))
                        stt(X3[:, oc, t0:t0 + n], ps_t[:, 0:n], modcol(5, oc, s), X3[:, oc, t0:t0 + n], ALU.mult, ALU.add,
                            [ps_b, b_mods(), st_.xb[oc][ti]], [st_.xb[oc][ti]])
                if next_li is not None:
                    for _ in range(NMOD[hb_i]):
                        mod_block()
            if next_li is not None:
                mod_finish(next_li, next_par)

        def attention(li, j_att, need_ctx, li_index):
            QT = XRb[:, 0:18432].rearrange("p (c t) -> p c t", c=8)
            KT2 = XRb[:, 18432:27648].rearrange("p (c t) -> p c t", c=4)
            ROC = XR[:, 13824:15872]
            ROS = XR[:, 15872:17920]
            VA = AUX[:, 0:18 * 576].rearrange("p (k x) -> p k x", k=18)
            b_rope = p.buf(f"rope{li}", dma=True)
            qb = [[p.buf() for _ in range(5)] for _ in range(8)]
            kb = [[p.buf() for _ in range(5)] for _ in range(4)]
            vab = [p.buf() for _ in range(18)]
            b_va_init = p.buf()
            p.dma("sp", XR[:, 13824:17920], dr["rope"], writes=[b_rope])
            vmemset(AUX[:, 0:18 * 576], 1.0, [b_va_init] + vab)
            q_tis = [0, 1, 2, 3, 4] if need_ctx else [1, 2, 3, 4]

            PS_A = [0, 1, 2]
            PS_B = [3, 4]
            PS_C = [5, 6]
            pending = []

            def qk_item(ps_t, ps_b, n, ti, gain_ap, dst_ap, dst_buf):
                t0 = TCH[ti][0]
                lat = ti != 0
                state = {}

                def stage_b():
                    sq, sqb = tmppt()
                    act(sq[:, 0:n], ps_t[:, 0:n], AF.Square, [ps_b], [sqb])
                    ss_t, ss_b = psum(PS_B)
                    mm(ss_t[:, 0:n], ONESBLK, sq[:, 0:n], True, True, [sqb, b_const], [ss_b], inc=True)
                    sd, sdb = tmp32()
                    act(sd[:, 0:n], ss_t[:, 0:n], AF.Sqrt, [ss_b, b_misc], [sdb], bias=MISC[:, 0:1], scale=1.0 / 64)
                    rs, rsb = tmp32()
                    recip(rs[:, 0:n], sd[:, 0:n], [sdb], [rsb])
                    if not lat:
                        stt(dst_ap, ps_t[:, 0:n], gain_ap, rs[:, 0:n], ALU.mult, ALU.mult, [ps_b, rsb, b_vec], [dst_buf])
                    else:
                        qn, qnb = tmppt()
                        stt(qn[:, 0:n], ps_t[:, 0:n], gain_ap, rs[:, 0:n], ALU.mult, ALU.mult, [ps_b, rsb, b_vec], [qnb])
                        state["qn"] = (qn, qnb)

                def stage_c():
                    if not lat:
                        return
                    qn, qnb = state["qn"]
                    rot_t, rot_b = psum(PS_C)
                    mm(rot_t[:, 0:n], PERM, qn[:, 0:n], True, True, [qnb, b_const], [rot_b], inc=True)
                    t1, t1b = tmp32()
                    tt(t1[:, 0:n], qn[:, 0:n], ROC[:, t0 - NCTX:t0 - NCTX + n], ALU.mult, [qnb, b_rope], [t1b])
                    t2, t2b = tmp32()
                    tt(t2[:, 0:n], rot_t[:, 0:n], ROS[:, t0 - NCTX:t0 - NCTX + n], ALU.mult, [rot_b, b_rope], [t2b])
                    tt(dst_ap, t1[:, 0:n], t2[:, 0:n], ALU.add, [t1b, t2b], [dst_buf])
                return stage_b, stage_c

            def pipe_push(item):
                pending.append(item)
                if len(pending) >= 2:
                    pending[-2][0]()
                if len(pending) >= 3:
                    pending[-3][1]()

            def pipe_flush():
                if len(pending) >= 1:
                    pending[-1][0]()
                if len(pending) >= 2:
                    pending[-2][1]()
                if len(pending) >= 1:
                    pending[-1][1]()
                pending.clear()

            for b in range(2):
                wt, wb = w_next(8, 512)
                for ocl in range(4):
                    oc = b * 4 + ocl
                    for ti in q_tis:
                        t0, n = TCH[ti]
                        ps_t, ps_b = psum(PS_A)
                        for kc in range(8):
                            mm(ps_t[:, 0:n], wt[:, kc, ocl * 128:(ocl + 1) * 128], H3[:, kc, t0:t0 + n], kc == 0, kc == 7,
                               [wb, st_.hb[kc][ti]], [ps_b], inc=(kc == 7))
                        pipe_push(qk_item(ps_t, ps_b, n, ti, vcol(f"qg{j_att}"), QT[:, oc, t0:t0 + n], qb[oc][ti]))
            wt, wb = w_next(8, 512)
            for g in range(4):
                for ti in range(5):
                    t0, n = TCH[ti]
                    ps_t, ps_b = psum(PS_A)
                    for kc in range(8):
                        mm(ps_t[:, 0:n], wt[:, kc, g * 128:(g + 1) * 128], H3[:, kc, t0:t0 + n], kc == 0, kc == 7,
                           [wb, st_.hb[kc][ti]], [ps_b], inc=(kc == 7))
                    pipe_push(qk_item(ps_t, ps_b, n, ti, vcol(f"kg{j_att}"), KT2[:, g, t0:t0 + n], kb[g][ti]))
            pipe_flush()
            wt, wb = w_next(8, 256)
            for kt in range(18):
                ti = 0 if kt < 2 else 1 + (kt - 2) // 4
                ps_t, ps_b = psum(PS_A)
                for kc in range(8):
                    mm(ps_t[:, 0:256], H3[:, kc, kt * 128:(kt + 1) * 128], wt[:, kc, :], kc == 0, kc == 7,
                       [wb, st_.hb[kc][ti]], [ps_b], inc=(kc == 7))
                dst = VA[:, kt, 64:576].rearrange("p (g x) -> p g x", x=128)[:, :, 0:64]
                src = ps_t[:, 0:256].rearrange("p (g x) -> p g x", x=64)
                act(dst, src, AF.Identity, [ps_b], [vab[kt]])

            OBANK = [[0, 1], [2, 3]]
            STB = [4, 5, 6, 7]
            it = 0
            for jp in range(8):
                g = jp // 2
                for ti in q_tis:
                    t0, n = TCH[ti]
                    kts = list(range(18)) if ti != 0 else [0, 1]
                    ob = OBANK[it % 2]
                    it += 1
                    o_t = [PS[ob[0]], PS[ob[1]]]
                    o_b = [psb[ob[0]], psb[ob[1]]]

                    def s_stage(kt):
                        res = []
                        tik = 0 if kt < 2 else 1 + (kt - 2) // 4
                        for h in range(2):
                            s_t, s_b = psum(STB)
                            mm(s_t[:, 0:n], KT2[h * 64:(h + 1) * 64, g, kt * 128:(kt + 1) * 128],
                               QT[h * 64:(h + 1) * 64, jp, t0:t0 + n], True, True,
                               [kb[g][tik], qb[jp][ti]], [s_b], inc=True)
                            res.append((s_t, s_b))
                        return res

                    def pv_stage(kt, sres):
                        for h in range(2):
                            s_t, s_b = sres[h]
                            pt, ptb_ = tmppt()
                            act(pt[:, 0:n], s_t[:, 0:n], AF.Exp, [s_b], [ptb_], scale=0.125)
                            if h == 0:
                                lhs = VA[:, kt, 64 + 128 * g:192 + 128 * g]
                            else:
                                lhs = VA[:, kt, 128 * g:128 + 128 * g]
                            mm(o_t[h][:, 0:n], lhs, pt[:, 0:n], kt == kts[0], kt == kts[-1],
                               [ptb_, vab[kt]], [o_b[h]], inc=True)

                    prev = s_stage(kts[0])
                    for idx, kt in enumerate(kts):
                        nxt = s_stage(kts[idx + 1]) if idx + 1 < len(kts) else None
                        pv_stage(kt, prev)
                        prev = nxt
                    for h in range(2):
                        rc, rcb = tmp32()
                        recip(rc[:, 0:n], o_t[h][:, 0:n], [o_b[h]], [rcb])
                        lo, hi = (0, 64) if h == 0 else (64, 128)
                        dlo, dhi = (64, 128) if h == 0 else (0, 64)
                        tt(H3[lo:hi, jp, t0:t0 + n], o_t[h][lo:hi, 0:n], rc[dlo:dhi, 0:n], ALU.mult,
                           [o_b[h], rcb], [st_.hb[jp][ti]])
            dbg_dump("att_xr", XR[:, :], [128, 8 * NT], F32)
            dbg_dump("att_aux", AUX[:, :], [128, 10368], BF16)
            dbg_dump("att_hb", HBt[:, :], [128, 8 * NT], BF16)
            reload(li_index)
            linear(2, 4, 8, 512, lambda wt, ocl, kc: wt[:, kc, ocl * 128:(ocl + 1) * 128], hb_rhs, hb_bufs,
                   q_tis, resid_evac(2, q_tis))

        def conformer(li, need_ctx, li_index):
            UC = XRb[:, 0:8 * 286].rearrange("p (c t) -> p c t", c=8)
            UL = XRb[:, 2288:2288 + 8 * 2078].rearrange("p (c t) -> p c t", c=8)
            DG = [XRb[:, 18912 + i * 3968:18912 + (i + 1) * 3968].rearrange("p (k m) -> p k m", k=31) for i in range(2)]
            ub = [[p.buf() for _ in range(5)] for _ in range(8)]
            upad = p.buf()
            dgb = [p.buf(), p.buf()]
            vmemset(XRb[:, 0:18912], 0.0, [upad] + [b for row in ub for b in row])
            tis = [0, 1, 2, 3, 4]

            def useg(c, ti, k, n):
                if ti == 0:
                    return UC[:, c, k:k + n]
                o = TCH[ti][0] - NCTX
                return UL[:, c, o + k:o + k + n]

            for b in range(4):
                wt, wb = w_next(8, 512)
                for cl in range(2):
                    c = b * 2 + cl
                    for ti in tis:
                        t0, n = TCH[ti]
                        pa_t, pa_b = psum()
                        for kc in range(8):
                            mm(pa_t[:, 0:n], wt[:, kc, cl * 128:(cl + 1) * 128], H3[:, kc, t0:t0 + n], kc == 0, kc == 7,
                               [wb, st_.hb[kc][ti]], [pa_b], inc=(kc == 7))
                        pg_t, pg_b = psum()
                        for kc in range(8):
                            mm(pg_t[:, 0:n], wt[:, kc, 256 + cl * 128:256 + (cl + 1) * 128], H3[:, kc, t0:t0 + n], kc == 0, kc == 7,
                               [wb, st_.hb[kc][ti]], [pg_b], inc=(kc == 7))
                        sg, sgb = tmp32()
                        act(sg[:, 0:n], pg_t[:, 0:n], AF.Sigmoid, [pg_b, b_vec], [sgb], bias=vcol("cbin", 8 + c))
                        stt(useg(c, ti, 15, n), pa_t[:, 0:n], vcol("cbin", c), sg[:, 0:n], ALU.add, ALU.mult,
                            [pa_b, sgb, b_vec, upad], [ub[c][ti]])
            vb = [[p.buf() for _ in range(5)] for _ in range(8)]
            for c in range(8):
                par = c % 2
                for k in range(31):
                    ts(DG[par][:, k, :], IDENT, vcol("cwdw", k * 8 + c), None, ALU.mult, None, [b_const, b_vec], [dgb[par]])
                for ti in tis:
                    t0, n = TCH[ti]
                    ps_t, ps_b = psum()
                    nb = [ub[c][ti]]
                    if ti > 1:
                        nb.append(ub[c][ti - 1])
                    if 1 <= ti < 4:
                        nb.append(ub[c][ti + 1])
                    for k in range(31):
                        mm(ps_t[:, 0:n], DG[par][:, k, :], useg(c, ti, k, n), k == 0, k == 30,
                           [dgb[par], upad] + nb, [ps_b], inc=(k == 30))
                    act(H3[:, c, t0:t0 + n], ps_t[:, 0:n], AF.Identity, [ps_b, b_vec],
                        [vb[c][ti], st_.hb[c][ti]], bias=vcol("cbdw", c))
            TB3 = TB[:, :].rearrange("p (c t) -> p c t", c=8)
            yb = [[p.buf() for _ in range(5)] for _ in range(8)]
            for ti in tis:
                t0, n = TCH[ti]
                pm_t, pm_b = psum()
                for c in range(8):
                    mm(pm_t[:, 0:n], ONES, H3[:, c, t0:t0 + n], c == 0, c == 7, [vb[c][ti], b_const], [pm_b], inc=(c == 7))
                for c in range(8):
                    act(TB3[:, c, 0:n], H3[:, c, t0:t0 + n], AF.Square, [vb[c][ti]], [b_tb])
                pq_t, pq_b = psum()
                for c in range(8):
                    mm(pq_t[:, 0:n], ONES, TB3[:, c, 0:n], c == 0, c == 7, [b_tb, b_const], [pq_b], inc=(c == 7))
                mean, meanb = T32H[1], t32hb[1]
                act(mean[:, 0:n], pm_t[:, 0:n], AF.Identity, [pm_b], [meanb], scale=1.0 / D)
                m2, m2b = tmp32()
                tt(m2[:, 0:n], mean[:, 0:n], mean[:, 0:n], ALU.mult, [meanb], [m2b])
                var, varb = tmp32()
                stt(var[:, 0:n], pq_t[:, 0:n], 1.0 / D, m2[:, 0:n], ALU.mult, ALU.subtract, [pq_b, m2b], [varb])
                sd, sdb = tmp32()
                act(sd[:, 0:n], var[:, 0:n], AF.Sqrt, [varb, b_misc], [sdb], bias=MISC[:, 0:1], scale=1.0)
                rs, rsb = T32H[0], t32hb[0]
                recip(rs[:, 0:n], sd[:, 0:n], [sdb], [rsb])
                for c in range(8):
                    t_, tb_ = tmppt32()
                    tt(t_[:, 0:n], H3[:, c, t0:t0 + n], mean[:, 0:n], ALU.subtract, [vb[c][ti], meanb], [tb_])
                    tt(t_[:, 0:n], t_[:, 0:n], rs[:, 0:n], ALU.mult, [tb_, rsb], [tb_])
                    act(H3[:, c, t0:t0 + n], t_[:, 0:n], AF.Silu, [tb_, b_vec], [yb[c][ti], vb[c][ti]],
                        bias=vcol("cnb", c), scale=vcol("cng", c))
            st_.hb = yb
            reload(li_index)
            linear(2, 4, 8, 512, lambda wt, ocl, kc: wt[:, kc, ocl * 128:(ocl + 1) * 128], hb_rhs, hb_bufs,
                   tis, resid_evac(2, tis, bias_name="cbout"))

        st_.rr["tbx"] = 0

        def tmppt32():
            i = st_.rr["tbx"] % 2
            st_.rr["tbx"] += 1
            return T32X[i], t32xb[i]

        def rglru(li, need_ctx, li_index):
            G3 = XRb[:, 0:18432].rearrange("p (c t) -> p c t", c=8)
            XL3 = XRb[:, 18432:36864].rearrange("p (c t) -> p c t", c=8)
            gb = [[p.buf() for _ in range(5)] for _ in range(8)]
            xlb = [[p.buf() for _ in range(5)] for _ in range(8)]
            tis = [0, 1, 2, 3, 4]
            for b in range(4):
                wt, wb = w_next(8, 512)
                for ocl in range(4):
                    oc = b * 4 + ocl
                    for ti in tis:
                        t0, n = TCH[ti]
                        ps_t, ps_b = psum()
                        for kc in range(8):
                            mm(ps_t[:, 0:n], wt[:, kc, ocl * 128:(ocl + 1) * 128], H3[:, kc, t0:t0 + n], kc == 0, kc == 7,
                               [wb, st_.hb[kc][ti]], [ps_b], inc=(kc == 7))
                        if oc < 8:
                            act(G3[:, oc, t0:t0 + n], ps_t[:, 0:n], AF.Gelu_apprx_tanh, [ps_b], [gb[oc][ti]])
                        else:
                            vcopy(XL3[:, oc - 8, t0:t0 + n], ps_t[:, 0:n], [ps_b], [xlb[oc - 8][ti]])
            lam = VEC[:, VOFF["llam"]:VOFF["llam"] + 16]
            b_ca = p.buf()
            act(MISC[:, 64:80], lam, AF.Exp, [b_vec], [b_ca], scale=-1.0)
            act(MISC[:, 80:96], MISC[:, 64:80], AF.Ln, [b_ca, b_misc], [b_ca], bias=MISC[:, 1:2], scale=1.0)
            ts(MISC[:, 16:32], MISC[:, 80:96], -8.0, None, ALU.mult, None, [b_ca], [b_ca])
            ts(MISC[:, 32:48], MISC[:, 80:96], -16.0, None, ALU.mult, None, [b_ca], [b_ca])
            dbg_dump("lru_xr0", XR[:, :], [128, 8 * NT], F32)
            p.barrier(("pe", "act", "dve"))
            U32 = HBf[:, 0:4608].rearrange("p (c t) -> p c t", c=2)
            HS = HBf[:, 4608:9216].rearrange("p (c t) -> p c t", c=2)
            UBF = AUX[:, 0:4608].rearrange("p (c t) -> p c t", c=2)
            XT = [AUXf[:, 2304 + i * 512:2304 + (i + 1) * 512] for i in range(5)]
            xtb = [p.buf() for _ in range(5)]
            CAR = MISC[:, 128:256]
            car_i = [0]
            rrx = [0]

            def ltmp():
                i = rrx[0] % 11
                rrx[0] += 1
                if i < 5:
                    return XT[i], xtb[i]
                return T32[i - 5], t32b[i - 5]

            gslots = []
            gidx = []
            for d in range(2):
                gidx.append(ws.consumed)
                gslots.append(w_next(16, 256, pin=True))
            SEGS = [(0, NCTX), (NCTX, NLAT)]
            u32b = [p.buf(), p.buf()]
            ubfb = [p.buf(), p.buf()]
            hsb = [[p.buf() for _ in range(5)] for _ in range(2)]
            for nblk in range(4):
                c0 = nblk * 2
                for d in range(2):
                    gwt, gwb = gslots[d]
                    for cl in range(2):
                        c = c0 + cl
                        xall = xlb[c]
                        for (s0, sn) in SEGS:
                            ts(U32[:, cl, s0:s0 + sn], XL3[:, c, s0:s0 + sn], vcol("lcw", (d * 4 + 3) * 8 + c), vcol("lcb", d * 8 + c),
                               ALU.mult, ALU.add, xall + [b_vec], [u32b[cl]])
                            for k in range(3):
                                sh = 3 - k
                                if d == 0:
                                    o_ap = U32[:, cl, s0 + sh:s0 + sn]
                                    i_ap = XL3[:, c, s0:s0 + sn - sh]
                                else:
                                    o_ap = U32[:, cl, s0:s0 + sn - sh]
                                    i_ap = XL3[:, c, s0 + sh:s0 + sn]
                                stt(o_ap, i_ap, vcol("lcw", (d * 4 + k) * 8 + c), o_ap, ALU.mult, ALU.add,
                                    xall + [b_vec, u32b[cl]], [u32b[cl]])
                        act(UBF[:, cl, :], U32[:, cl, :], AF.Identity, [u32b[cl]], [ubfb[cl]])
                    for cl in range(2):
                        c = c0 + cl
                        order = [0, 1, 2, 3, 4] if d == 0 else [0, 4, 3, 2, 1]
                        prev_car = None
                        for oi, ti in enumerate(order):
                            t0, n = TCH[ti]
                            gps = []
                            for gi in range(2):
                                ps_t, ps_b = psum()
                                for kc in range(2):
                                    mm(ps_t[:, 0:n], gwt[:, (gi * 4 + nblk) * 2 + kc, cl * 128:(cl + 1) * 128], UBF[:, kc, t0:t0 + n],
                                       kc == 0, kc == 1, [gwb, ubfb[kc]], [ps_b], inc=(kc == 1))
                                gps.append((ps_t, ps_b))
                            r_, rb_ = ltmp()
                            act(r_[:, 0:n], gps[0][0][:, 0:n], AF.Sigmoid, [gps[0][1], b_vec], [rb_], bias=vcol("lgb", (d * 2 + 0) * 8 + c))
                            a_, ab_ = ltmp()
                            act(a_[:, 0:n], r_[:, 0:n], AF.Exp, [rb_, b_ca], [ab_], scale=MISC[:, 16 + d * 8 + c:17 + d * 8 + c])
                            m_, mb_ = ltmp()
                            act(m_[:, 0:n], r_[:, 0:n], AF.Exp, [rb_, b_ca], [mb_], scale=MISC[:, 32 + d * 8 + c:33 + d * 8 + c])
                            act(m_[:, 0:n], m_[:, 0:n], AF.Sqrt, [mb_, b_misc], [mb_], bias=MISC[:, 1:2], scale=-1.0)
                            i_, ib_ = ltmp()
                            act(i_[:, 0:n], gps[1][0][:, 0:n], AF.Sigmoid, [gps[1][1], b_vec], [ib_], bias=vcol("lgb", (d * 2 + 1) * 8 + c))
                            if ti == 0:
                                fc = 0 if d == 0 else NCTX - 1
                                vmemset(m_[:, fc:fc + 1], 1.0, [mb_])
                            tt(i_[:, 0:n], i_[:, 0:n], m_[:, 0:n], ALU.mult, [ib_, mb_], [ib_])
                            tt(i_[:, 0:n], i_[:, 0:n], U32[:, cl, t0:t0 + n], ALU.mult, [ib_, u32b[cl]], [ib_])
                            if d == 0:
                                init = 0.0 if oi == 0 else HS[:, cl, t0 - 1:t0]
                                rd = [ab_, ib_] + ([hsb[cl][ti - 1]] if oi > 0 else [])
                                p.op("dve", lambda e, o=HS[:, cl, t0:t0 + n], a=a_[:, 0:n], b=i_[:, 0:n], init=init:
                                     e.tensor_tensor_scan(out=o, data0=a, data1=b, initial=init, op0=ALU.mult, op1=ALU.add),
                                     rd, [hsb[cl][ti]])
                            else:
                                h_, hb_ = ltmp()
                                init = 0.0 if oi == 0 else prev_car
                                p.op("dve", lambda e, o=h_[:, 0:n][:, ::-1], a=a_[:, 0:n][:, ::-1], b=i_[:, 0:n][:, ::-1], init=init:
                                     e.tensor_tensor_scan(out=o, data0=a, data1=b, initial=init, op0=ALU.mult, op1=ALU.add),
                                     [ab_, ib_, b_car], [hb_])
                                ci = car_i[0] % 128
                                car_i[0] += 1
                                vcopy(CAR[:, ci:ci + 1], h_[:, 0:1], [hb_], [b_car])
                                prev_car = CAR[:, ci:ci + 1]
                                tt(h_[:, 0:n], h_[:, 0:n], HS[:, cl, t0:t0 + n], ALU.add, [hb_, hsb[cl][ti]], [hb_])
                                tt(G3[:, c, t0:t0 + n], h_[:, 0:n], G3[:, c, t0:t0 + n], ALU.mult, [hb_, gb[c][ti]], [gb[c][ti]])
            for gi_ in gidx:
                w_release(gi_)
            dbg_dump("lru_xr1", XR[:, :], [128, 8 * NT], F32)
            p.barrier(("pe", "act", "dve"))
            yb = [[p.buf() for _ in range(5)] for _ in range(8)]
            for c in range(8):
                for ti in tis:
                    t0, n = TCH[ti]
                    if (c + ti) % 2 == 0:
                        vcopy(H3[:, c, t0:t0 + n], G3[:, c, t0:t0 + n], [gb[c][ti]], [yb[c][ti]])
                    else:
                        act(H3[:, c, t0:t0 + n], G3[:, c, t0:t0 + n], AF.Identity, [gb[c][ti]], [yb[c][ti]])
            st_.hb = yb
            reload(li_index)
            linear(2, 4, 8, 512, lambda wt, ocl, kc: wt[:, kc, ocl * 128:(ocl + 1) * 128], hb_rhs, hb_bufs,
                   tis, resid_evac(2, tis))

        vmemset(MISC[:, 0:1], EPS, [b_misc])
        vmemset(MISC[:, 1:2], 1.0, [b_misc])
        for idx, li in enumerate(layers):
            kind = li % 3
            need_ctx = li < DEPTH - 1
            st_.par = idx % 2
            if idx == 0:
                for _ in range(12):
                    mod_block()
                mod_finish(li, 0)
            st_.hb = grid(f"h{li}_")
            norm_phase(0, [0, 1, 2, 3, 4])
            if DBG and idx == 0:
                p.dma("sp", dbg_mods, MODS()[:, :], reads=[b_mods()], writes=[o_buf])
                p.dma("sp", dbg_ab, AB()[:, :], reads=[b_ab()], writes=[o_buf])
                p.dma("sp", dbg_h1, HBt[:, :], reads=[b for row in st_.hb for b in row], writes=[o_buf])
            first_in_prog = (idx == 0)
            spill(0 if first_in_prog else 1)
            lidx = 0 if first_in_prog else 1
            import os as _os
            if _os.environ.get("DBG_SKIP_MIX"):
                for _ in range({0: 6, 1: 6, 2: 8}[kind]):
                    w_next(8, 512)
                reload(lidx)
            elif kind == 0:
                attention(li, li // 3, need_ctx, lidx)
            elif kind == 1:
                conformer(li, need_ctx, lidx)
            else:
                rglru(li, need_ctx, lidx)
            tis = [0, 1, 2, 3, 4] if need_ctx else [1, 2, 3, 4]
            p.barrier(("pe", "act", "dve"))
            st_.hb = grid(f"h2{li}_")
            if _os.environ.get("DBG_SKIP_MLP"):
                for _ in range(16 + (12 if idx + 1 < len(layers) else 0)):
                    w_next(8, 512)
            else:
                if DBG and idx == 0:
                    p.dma("sp", dbg_x1, XR[:, :], reads=[b for row in st_.xb for b in row], writes=[o_buf])
                norm_phase(1, tis)
                if DBG and idx == 0:
                    p.dma("sp", dbg_h2, HBt[:, :], reads=[b for row in st_.hb for b in row], writes=[o_buf])
                if idx + 1 < len(layers):
                    mlp_phase(li, tis, layers[idx + 1], (idx + 1) % 2)
                else:
                    mlp_phase(li, tis)
        allb = [b for row in st_.xb for b in row]
        if last:
            for c in range(8):
                p.dma("sp", outT[c * 128:(c + 1) * 128, :], X3[:, c, NCTX:NT], reads=allb, writes=[o_buf])
        else:
            p.dma("sp", xs_out, XR[:, :], reads=allb, writes=[o_buf])
        p._wait("sp", o_buf.w)
        assert ws.consumed == len(wspecs), (ws.consumed, len(wspecs))
        p.emit()
    return nc


_WKEYS = ["mod_w", "mlp_w1", "mlp_w2", "attn_w_qkv", "attn_w_o", "conv_w_in", "conv_w_out",
          "lru_w_in", "lru_gate_w", "lru_w_out"]


def _common_maps(inputs):
    m = {k: np.ascontiguousarray(np.asarray(inputs[k], np.float32)) for k in _WKEYS}
    m["consts"] = _consts()
    m["rope"] = _rope_tables()
    return m


def kernel(**inputs):
    n = 8
    common = _common_maps(inputs)
    x = np.asarray(inputs["x"], np.float32)
    ctx = np.asarray(inputs["ctx"], np.float32)
    in_maps = []
    for b in range(n):
        m = dict(common)
        m["xT"] = np.ascontiguousarray(x[b].T)
        m["ctxT"] = np.ascontiguousarray(ctx[b].T)
        m["vecs"] = _pack_vecs(inputs, b)
        in_maps.append(m)
    nc = build_program((0, 1, 2, 3), True, True)
    res = run_bass_kernel_spmd(nc, in_maps, core_ids=list(range(n)))
    out = np.stack([np.ascontiguousarray(res.results[b]["outT"].T) for b in range(n)], axis=0)
    return out.astype(np.float32)
```

```python
import numpy as np
from contextlib import ExitStack
import concourse.bass as bass
import concourse.mybir as mybir
from concourse.bass_utils import run_bass_kernel_spmd
import ml_dtypes

F32 = mybir.dt.float32
BF16 = mybir.dt.bfloat16
AF = mybir.ActivationFunctionType
ALU = mybir.AluOpType

SELF_WAIT = True

NT, NCTX, NLAT, D = 2304, 256, 2048, 1024
TCH = [(0, 256), (256, 512), (768, 512), (1280, 512), (1792, 512)]
EPS = 1e-6
DEPTH = 4


class Sem:
    def __init__(self, h, name):
        self.h = h
        self.name = name
        self.count = 0


class Buf:
    __slots__ = ("name", "w", "r", "dsem")

    def __init__(self, name, dsem=None):
        self.name = name
        self.w = None
        self.r = []
        self.dsem = dsem


class Prog:
    ENG = ("pe", "act", "dve", "pool", "sp")

    def __init__(self, nc, stack):
        self.nc = nc
        self.stack = stack
        self.ops = {e: [] for e in self.ENG}
        self.esem = {}
        for e in ("pe", "act", "dve"):
            self.esem[e] = self.new_sem("s_" + e)
        self.known = {e: {} for e in self.ENG}
        self.pending_noinc = {e: False for e in self.ENG}
        self.nbuf = 0

    def new_sem(self, name):
        h = self.stack.enter_context(self.nc.semaphore(name))
        return Sem(h, name)

    def buf(self, name=None, dma=False):
        self.nbuf += 1
        name = name or f"b{self.nbuf}"
        return Buf(name, self.new_sem("d_" + name) if dma else None)

    def sb(self, name, shape, dt):
        return self.stack.enter_context(self.nc.sbuf_tensor(name, list(shape), dt))

    def ps(self, name, shape, dt=F32):
        return self.stack.enter_context(self.nc.psum_tensor(name, list(shape), dt))

    def _wait(self, eng, tok):
        if tok is None:
            return
        sem, val = tok
        if eng == "pe" and sem is self.esem["pe"]:
            return
        if (not SELF_WAIT) and eng in self.esem and sem is self.esem[eng]:
            return
        k = self.known[eng]
        if k.get(sem, 0) >= val:
            return
        k[sem] = val
        h = sem.h
        self.ops[eng].append(lambda e, h=h, val=val: e.wait_ge(h, val))

    def _deps(self, eng, reads, writes):
        for b in reads:
            self._wait(eng, b.w)
        for b in writes:
            self._wait(eng, b.w)
            for t in b.r:
                self._wait(eng, t)

    def _commit(self, tok, reads, writes):
        for b in reads:
            b.r.append(tok)
            if len(b.r) > 16:
                d = {}
                for s, v in b.r:
                    if d.get(s, 0) < v:
                        d[s] = v
                b.r = list(d.items())
        for b in writes:
            b.w = tok
            b.r = []

    def op(self, eng, fn, reads=(), writes=(), inc=True):
        self._deps(eng, reads, writes)
        sem = self.esem[eng]
        if inc:
            sem.count += 1
            val = sem.count
            h = sem.h
            self.ops[eng].append(lambda e, fn=fn, h=h: fn(e).then_inc(h, 1))
            self.pending_noinc[eng] = False
        else:
            val = sem.count + 1
            self.ops[eng].append(lambda e, fn=fn: fn(e))
            self.pending_noinc[eng] = True
        tok = (sem, val)
        self._commit(tok, reads, writes)
        return tok

    def dma(self, q, out, in_, reads=(), writes=(), dsem=None):
        self._deps(q, reads, writes)
        sem = dsem if dsem is not None else writes[0].dsem
        sem.count += 16
        val = sem.count
        h = sem.h
        self.ops[q].append(lambda e, out=out, in_=in_, h=h: e.dma_start(out=out, in_=in_).then_inc(h, 16))
        tok = (sem, val)
        self._commit(tok, reads, writes)
        return tok

    def barrier(self, engs=("pe", "act", "dve", "sp")):
        for e in ("pe", "act", "dve"):
            assert not self.pending_noinc[e]
        toks = [(self.esem[e], self.esem[e].count) for e in ("pe", "act", "dve")]
        for e in engs:
            for t in toks:
                if t[1] > 0:
                    self._wait(e, t)

    def emit(self):
        for e in ("pe", "act", "dve"):
            assert not self.pending_noinc[e], f"engine {e} has trailing non-inc op"
        ops = self.ops
        with self.nc.Block() as block:
            @block.sync
            def _(eng):
                for f in ops["sp"]:
                    f(eng)

            @block.tensor
            def _(eng):
                for f in ops["pe"]:
                    f(eng)

            @block.scalar
            def _(eng):
                for f in ops["act"]:
                    f(eng)

            @block.vector
            def _(eng):
                for f in ops["dve"]:
                    f(eng)

            @block.gpsimd
            def _(eng):
                for f in ops["pool"]:
                    f(eng)


def _vec_layout():
    L = [("c", 8), ("cctx", 8)]
    for i in range(DEPTH):
        L += [(f"modb{i}", 48), (f"gmix{i}", 8), (f"gmlp{i}", 8)]
    for j in range(2):
        L += [(f"qg{j}", 1), (f"kg{j}", 1)]
    L += [("cbin", 16), ("cwdw", 248), ("cbdw", 8), ("cng", 8), ("cnb", 8), ("cbout", 8)]
    L += [("lcw", 64), ("lcb", 16), ("lgb", 32), ("llam", 16)]
    off = {}
    o = 0
    for n, k in L:
        off[n] = o
        o += k
    return off, o


VOFF, NV = _vec_layout()


def _pk(v):
    v = np.asarray(v, np.float32).reshape(-1, 128)
    return np.ascontiguousarray(v.T)


def _pack_vecs(inp, b):
    V = np.zeros((128, NV), np.float32)

    def put(name, arr):
        arr = np.asarray(arr, np.float32)
        V[:, VOFF[name]:VOFF[name] + arr.shape[1]] = arr

    put("c", _pk(inp["c"][b]))
    put("cctx", _pk(inp["c_ctx"]))
    for i in range(DEPTH):
        put(f"modb{i}", _pk(inp["mod_b"][i]))
        put(f"gmix{i}", _pk(inp["norm_mix_g"][i]))
        put(f"gmlp{i}", _pk(inp["norm_mlp_g"][i]))
    for j in range(2):
        put(f"qg{j}", np.tile(np.asarray(inp["attn_q_gain"][j], np.float32), 2)[:, None])
        put(f"kg{j}", np.tile(np.asarray(inp["attn_k_gain"][j], np.float32), 2)[:, None])
    put("cbin", _pk(inp["conv_b_in"][0]))
    wdw = np.asarray(inp["conv_w_dw"][0], np.float32).reshape(31, 8, 128).transpose(2, 0, 1).reshape(128, 248)
    put("cwdw", wdw)
    put("cbdw", _pk(inp["conv_b_dw"][0]))
    put("cng", _pk(inp["conv_norm_g"][0]))
    put("cnb", _pk(inp["conv_norm_b"][0]))
    put("cbout", _pk(inp["conv_b_out"][0]))
    lcw = np.asarray(inp["lru_conv_w"][0], np.float32).reshape(2, 4, 8, 128).transpose(3, 0, 1, 2).reshape(128, 64)
    put("lcw", lcw)
    put("lcb", np.asarray(inp["lru_conv_b"][0], np.float32).reshape(2, 8, 128).transpose(2, 0, 1).reshape(128, 16))
    put("lgb", np.asarray(inp["lru_gate_b"][0], np.float32).reshape(2, 2, 8, 128).transpose(3, 0, 1, 2).reshape(128, 32))
    put("llam", np.asarray(inp["lru_lambda"][0], np.float32).reshape(2, 8, 128).transpose(2, 0, 1).reshape(128, 16))
    return V


def _consts():
    p = np.arange(128)
    perm = (p[:, None] == (p[None, :] ^ 16)).astype(np.float32)
    onesblk = ((p[:, None] // 64) == (p[None, :] // 64)).astype(np.float32)
    ones = np.ones((128, 128), np.float32)
    ident = np.eye(128, dtype=np.float32)
    return np.concatenate([perm, onesblk, ones, ident], axis=1).astype(ml_dtypes.bfloat16)


def _rope_tables():
    t = np.arange(NLAT)
    row = (t // 64).astype(np.float64)
    col = (t % 64).astype(np.float64)
    inv = 10000.0 ** (-np.arange(16, dtype=np.float64) / 16.0)
    p = np.arange(128)
    d = p % 64
    a = d // 32
    half = (d // 16) % 2
    f = d % 16
    pos = np.where(a[:, None] == 0, row[None, :], col[None, :])
    ang = (pos.astype(np.float32) * inv.astype(np.float32)[f][:, None]).astype(np.float32)
    C = np.cos(ang).astype(np.float32)
    S = np.sin(ang).astype(np.float32) * np.where(half == 0, -1.0, 1.0)[:, None].astype(np.float32)
    return np.ascontiguousarray(np.concatenate([C, S], axis=1).astype(np.float32))


class K:
    pass


def build_program(layers=(0, 1, 2, 3), first=True, last=True):
    nc = bass.Bass("TRN2", target_bir_lowering=False)
    dr = {}

    def din(name, shape, dt=F32):
        dr[name] = nc.dram_tensor(name, list(shape), dt, kind="ExternalInput").ap()
        return dr[name]

    if first:
        din("xT", [D, NLAT])
        din("ctxT", [D, NCTX])
    else:
        din("xs_in", [128, 8 * NT])
    din("vecs", [128, NV])
    din("consts", [128, 512], BF16)
    din("rope", [128, 2 * NLAT])
    din("mod_w", [4, D, 6 * D])
    din("mlp_w1", [4, D, 4 * D])
    din("mlp_w2", [4, 4 * D, D])
    din("attn_w_qkv", [2, D, 1536])
    din("attn_w_o", [2, D, D])
    din("conv_w_in", [1, D, 2 * D])
    din("conv_w_out", [1, D, D])
    din("lru_w_in", [1, D, 2 * D])
    din("lru_gate_w", [1, 2, 2, 4, 256, 256])
    din("lru_w_out", [1, D, D])
    if last:
        outT = nc.dram_tensor("outT", [D, NLAT], F32, kind="ExternalOutput").ap()
    else:
        xs_out = nc.dram_tensor("xs_out", [128, 8 * NT], F32, kind="ExternalOutput").ap()
    xs = nc.dram_tensor("xs_scr", [128, 8 * NT], F32, kind="Internal").ap()
    import os as _os
    DBG = bool(_os.environ.get("DBG_DUMP"))
    if DBG:
        dbg_mods = nc.dram_tensor("dbg_mods", [128, 96], F32, kind="ExternalOutput").ap()
        dbg_ab = nc.dram_tensor("dbg_ab", [128, 64], F32, kind="ExternalOutput").ap()
        dbg_h1 = nc.dram_tensor("dbg_h1", [128, 8 * NT], BF16, kind="ExternalOutput").ap()
        dbg_h2 = nc.dram_tensor("dbg_h2", [128, 8 * NT], BF16, kind="ExternalOutput").ap()
        dbg_x1 = nc.dram_tensor("dbg_x1", [128, 8 * NT], F32, kind="ExternalOutput").ap()

    with ExitStack() as st:
        p = Prog(nc, st)
        XR = p.sb("XR", [128, 8 * NT], F32)
        HBt = p.sb("HB", [128, 8 * NT], BF16)
        AUX = p.sb("AUX", [128, 10368], BF16)
        SLOT = [p.sb(f"slot{i}", [128, 4096], BF16) for i in range(4)]
        VEC = p.sb("VEC", [128, NV], F32)
        CONST = p.sb("CONST", [128, 512], BF16)
        MODS_ = [p.sb(f"MODS{i}", [128, 96], F32) for i in range(2)]
        AB_ = [p.sb(f"AB{i}", [128, 64], F32) for i in range(2)]
        SC = p.sb("SC", [128, 16], BF16)
        MISC = p.sb("MISC", [128, 256], F32)
        T32 = [p.sb(f"t32_{i}", [128, 512], F32) for i in range(6)]
        TB = p.sb("TB", [128, 8 * 512], BF16)
        T32X = [p.sb(f"t32x_{i}", [128, 512], F32) for i in range(2)]
        t32xb = [p.buf(f"t32x_{i}") for i in range(2)]
        T32H = [p.sb(f"t32h_{i}", [128, 512], F32) for i in range(2)]
        t32hb = [p.buf(f"t32h_{i}") for i in range(2)]
        b_tb = p.buf("tb")
        b_car = p.buf("car")
        PT = [p.sb(f"pt{i}", [128, 512], BF16) for i in range(6)]
        PS = [p.ps(f"ps{i}", [128, 512], F32) for i in range(4)]
        PSD = [p.ps(f"psd{i}", [128, 1024], F32) for i in range(2)]
        PS = PS + [PSD[0][:, 0:512], PSD[0][:, 512:1024], PSD[1][:, 0:512], PSD[1][:, 512:1024]]
        psb = [p.buf(f"ps{i}") for i in range(8)]
        t32b = [p.buf(f"t32_{i}") for i in range(6)]
        ptb = [p.buf(f"pt{i}") for i in range(6)]
        slotb = [p.buf(f"slot{i}", dma=True) for i in range(4)]
        b_vec = p.buf("vec", dma=True)
        b_const = p.buf("const", dma=True)
        b_mods_ = [p.buf("mods0"), p.buf("mods1")]
        b_ab_ = [p.buf("ab0"), p.buf("ab1")]
        b_sc = p.buf("sc")
        b_misc = p.buf("misc")
        x_dsem = p.new_sem("d_x")
        o_buf = p.buf("out", dma=True)
        spill_buf = p.buf("spill", dma=True)

        dbg_outs = {}

        def dbg_dump(name, ap, shape, dt):
            if not DBG:
                return
            t = nc.dram_tensor("dd_" + name, list(shape), dt, kind="ExternalOutput").ap()
            p.barrier(("sp",))
            p.dma("sp", t, ap, writes=[o_buf])
            for e in ("pe", "act", "dve"):
                p._wait(e, o_buf.w)

        PERM = CONST[:, 0:128]
        ONESBLK = CONST[:, 128:256]
        ONES = CONST[:, 256:384]
        IDENT = CONST[:, 384:512]

        X3 = XR[:, :].rearrange("p (c t) -> p c t", c=8)
        XRb = XR[:, :].bitcast(BF16)
        H3 = HBt[:, :].rearrange("p (c t) -> p c t", c=8)
        HBf = HBt[:, :].bitcast(F32)
        AUXf = AUX[:, :].bitcast(F32)

        def grid(name):
            return [[p.buf(f"{name}{c}_{t}") for t in range(5)] for c in range(8)]

        st_ = K()
        st_.xb = grid("x")
        st_.hb = grid("h")
        st_.rr = {"t32": 0, "pt": 0, "ps": 0}

        def vcol(name, j=0):
            o = VOFF[name] + j
            return VEC[:, o:o + 1]

        def tmp32():
            i = st_.rr["t32"] % 6
            st_.rr["t32"] += 1
            return T32[i], t32b[i]

        def tmppt():
            i = st_.rr["pt"] % 6
            st_.rr["pt"] += 1
            return PT[i], ptb[i]

        def psum(group=None):
            group = group if group is not None else list(range(7))
            key = ("ps",) + tuple(group)
            k_ = st_.rr.get(key, 0)
            st_.rr[key] = k_ + 1
            i = group[k_ % len(group)]
            return PS[i], psb[i]

        def mm(out, lhsT, rhs, start, stop, reads, writes, inc):
            p.op("pe", lambda e: e.matmul(out, lhsT, rhs, start=start, stop=stop), reads, writes, inc=inc)

        def act(out, in_, func, reads, writes, bias=None, scale=None):
            kw = {}
            if bias is not None:
                kw["bias"] = bias
            if scale is not None:
                kw["scale"] = scale
            p.op("act", lambda e: e.activation(out=out, in_=in_, func=func, **kw), reads, writes)

        def tt(out, in0, in1, op, reads, writes):
            p.op("dve", lambda e: e.tensor_tensor(out=out, in0=in0, in1=in1, op=op), reads, writes)

        def ts(out, in0, s1, s2, op0, op1, reads, writes):
            if s2 is None:
                p.op("dve", lambda e: e.tensor_scalar(out=out, in0=in0, scalar1=s1, scalar2=None, op0=op0), reads, writes)
            else:
                p.op("dve", lambda e: e.tensor_scalar(out=out, in0=in0, scalar1=s1, scalar2=s2, op0=op0, op1=op1), reads, writes)

        def stt(out, in0, scalar, in1, op0, op1, reads, writes):
            p.op("dve", lambda e: e.scalar_tensor_tensor(out=out, in0=in0, scalar=scalar, in1=in1, op0=op0, op1=op1), reads, writes)

        def recip(out, in_, reads, writes):
            p.op("dve", lambda e: e.reciprocal(out=out, in_=in_), reads, writes)

        def vcopy(out, in_, reads, writes):
            p.op("dve", lambda e: e.tensor_copy(out=out, in_=in_), reads, writes)

        def vmemset(ap, val, writes):
            p.op("dve", lambda e: e.memset(ap, val), (), writes)

        wspecs = []

        def wv(ap2d):
            return ap2d.rearrange("(k p) n -> p k n", p=128)

        NMOD = [2, 1, 2, 1, 2, 1, 2, 1]
        for lidx_, li in enumerate(layers):
            kind = li % 3
            j = li // 3
            if lidx_ == 0:
                for b in range(12):
                    wspecs.append([(0, 8, 512, wv(dr["mod_w"][li, :, b * 512:(b + 1) * 512]))])
            if kind == 0:
                wq = dr["attn_w_qkv"][j]
                for b in range(2):
                    wspecs.append([(0, 8, 512, wv(wq[:, b * 512:(b + 1) * 512]))])
                sp_ = []
                for g in range(4):
                    for dup in range(2):
                        sp_.append(((g * 2 + dup) * 64, 8, 64, wv(wq[:, 1024 + g * 64:1024 + (g + 1) * 64]), 512))
                wspecs.append(sp_)
                wspecs.append([(0, 8, 256, wv(wq[:, 1280:1536]))])
                for b in range(2):
                    wspecs.append([(0, 8, 512, wv(dr["attn_w_o"][j][:, b * 512:(b + 1) * 512]))])
            elif kind == 1:
                wi = dr["conv_w_in"][0]
                for b in range(4):
                    wspecs.append([(0, 8, 256, wv(wi[:, b * 256:(b + 1) * 256]), 512),
                                   (256, 8, 256, wv(wi[:, 1024 + b * 256:1024 + (b + 1) * 256]), 512)])
                for b in range(2):
                    wspecs.append([(0, 8, 512, wv(dr["conv_w_out"][0][:, b * 512:(b + 1) * 512]))])
            else:
                wi = dr["lru_w_in"][0]
                for b in range(4):
                    wspecs.append([(0, 8, 512, wv(wi[:, b * 512:(b + 1) * 512]))])
                for d in range(2):
                    gw = dr["lru_gate_w"][0, d].rearrange("g n k e -> (g n k) e")
                    wspecs.append([(0, 16, 256, wv(gw))])
                for b in range(2):
                    wspecs.append([(0, 8, 512, wv(dr["lru_w_out"][0][:, b * 512:(b + 1) * 512]))])
            mb_ = 0
            for hb in range(8):
                wspecs.append([(0, 8, 512, wv(dr["mlp_w1"][li, :, hb * 512:(hb + 1) * 512]))])
                wspecs.append([(0, 4, 1024, wv(dr["mlp_w2"][li, hb * 512:(hb + 1) * 512, :]))])
                if lidx_ + 1 < len(layers):
                    nl_ = layers[lidx_ + 1]
                    for _ in range(NMOD[hb]):
                        wspecs.append([(0, 8, 512, wv(dr["mod_w"][nl_, :, mb_ * 512:(mb_ + 1) * 512]))])
                        mb_ += 1

        ws = K()
        ws.issued = 0
        ws.consumed = 0

        def w_issue(jb):
            s = jb % 4
            for spec in wspecs[jb]:
                if len(spec) == 5:
                    off, kcn, ncol, src, rowlen = spec
                    dst = SLOT[s][:, 0:kcn * rowlen].rearrange("p (k n) -> p k n", k=kcn)[:, :, off:off + ncol]
                else:
                    off, kcn, ncol, src = spec
                    dst = SLOT[s][:, off:off + kcn * ncol].rearrange("p (k n) -> p k n", k=kcn)
                p.dma("pool", dst, src, writes=[slotb[s]])

        ws.released = set()
        ws.pinned = set()

        def w_release(i):
            ws.pinned.discard(i)
            ws.released.add(i)

        def w_next(kcn, ncol, pin=False):
            i = ws.consumed
            if i - 1 >= 0 and (i - 1) not in ws.pinned:
                ws.released.add(i - 1)
            while ws.issued < min(i + 4, len(wspecs)) and (ws.issued < 4 or (ws.issued - 4) in ws.released):
                w_issue(ws.issued)
                ws.issued += 1
            assert ws.issued > i, "weight block not issued (pinned slot deadlock)"
            if pin:
                ws.pinned.add(i)
            ws.consumed += 1
            s = i % 4
            return SLOT[s][:, 0:kcn * ncol].rearrange("p (k n) -> p k n", k=kcn), slotb[s]

        p.dma("sp", VEC[:, :], dr["vecs"], writes=[b_vec])
        p.dma("sp", CONST[:, :], dr["consts"], writes=[b_const])

        def load_x_from_input():
            allb = [b for row in st_.xb for b in row]
            if first:
                p.dma("sp", X3[:, :, 0:NCTX], dr["ctxT"].rearrange("(c p) t -> p c t", p=128), writes=allb, dsem=x_dsem)
                for c in range(8):
                    p.dma("sp", X3[:, c, NCTX:NT], dr["xT"][c * 128:(c + 1) * 128, :], writes=allb, dsem=x_dsem)
            else:
                for c in range(8):
                    p.dma("sp", X3[:, c, :], dr["xs_in"][:, c * NT:(c + 1) * NT], writes=allb, dsem=x_dsem)

        load_x_from_input()
        SC3 = SC[:, :].rearrange("p (k s) -> p k s", s=2)
        act(SC3[:, :, 0], VEC[:, VOFF["cctx"]:VOFF["cctx"] + 8], AF.Silu, [b_vec], [b_sc])
        act(SC3[:, :, 1], VEC[:, VOFF["c"]:VOFF["c"] + 8], AF.Silu, [b_vec], [b_sc])

        st_.par = 0

        def MODS():
            return MODS_[st_.par]

        def AB():
            return AB_[st_.par]

        def b_mods():
            return b_mods_[st_.par]

        def b_ab():
            return b_ab_[st_.par]

        def modcol(grp, c, s):
            return MODS()[:, (grp * 8 + c) * 2 + s:(grp * 8 + c) * 2 + s + 1]

        modst = K()
        modst.nb = 0

        def mod_block():
            b = modst.nb
            modst.nb += 1
            ps_t, ps_b = PS[7], psb[7]
            wt, wb = w_next(8, 512)
            for jj in range(4):
                jx = b * 4 + jj
                for kc in range(8):
                    mm(ps_t[:, 2 * jx:2 * jx + 2], wt[:, kc, jj * 128:(jj + 1) * 128], SC3[:, kc, :],
                       kc == 0, kc == 7, [wb, b_sc], [ps_b], inc=(jj == 3 and kc == 7))

        def mod_finish(li, par):
            assert modst.nb == 12
            modst.nb = 0
            ps_t, ps_b = PS[7], psb[7]
            M = MODS_[par]
            A = AB_[par]
            M3 = M[:, :].rearrange("p (j s) -> p j s", s=2)
            ps3 = ps_t[:, 0:96].rearrange("p (j s) -> p j s", s=2)
            mb = VEC[:, VOFF[f"modb{li}"]:VOFF[f"modb{li}"] + 48]
            for s in range(2):
                tt(M3[:, :, s], ps3[:, :, s], mb, ALU.add, [ps_b, b_vec], [b_mods_[par]])
            gm = VEC[:, VOFF[f"gmix{li}"]:VOFF[f"gmix{li}"] + 8]
            gl = VEC[:, VOFF[f"gmlp{li}"]:VOFF[f"gmlp{li}"] + 8]
            for s in range(2):
                stt(A[:, s * 8:s * 8 + 8], M3[:, 8:16, s], 1.0, gm, ALU.add, ALU.mult, [b_mods_[par], b_vec], [b_ab_[par]])
                stt(A[:, 16 + s * 8:16 + s * 8 + 8], M3[:, 32:40, s], 1.0, gl, ALU.add, ALU.mult, [b_mods_[par], b_vec], [b_ab_[par]])

        def norm_phase(which, tis):
            TB3 = TB[:, :].rearrange("p (c t) -> p c t", c=8)
            for ti in tis:
                t0, n = TCH[ti]
                s = 0 if ti == 0 else 1
                for c in range(8):
                    act(TB3[:, c, 0:n], X3[:, c, t0:t0 + n], AF.Square, [st_.xb[c][ti]], [b_tb])
                ps_t, ps_b = psum()
                for c in range(8):
                    mm(ps_t[:, 0:n], ONES, TB3[:, c, 0:n], c == 0, c == 7, [b_tb, b_const], [ps_b], inc=(c == 7))
                sd, sdb = tmp32()
                act(sd[:, 0:n], ps_t[:, 0:n], AF.Sqrt, [ps_b, b_misc], [sdb], bias=MISC[:, 0:1], scale=1.0 / D)
                rs, rsb = T32H[0], t32hb[0]
                recip(rs[:, 0:n], sd[:, 0:n], [sdb], [rsb])
                for c in range(8):
                    t_, tb_ = tmp32()
                    a_ap = AB()[:, which * 16 + s * 8 + c:which * 16 + s * 8 + c + 1]
                    stt(t_[:, 0:n], X3[:, c, t0:t0 + n], a_ap, rs[:, 0:n], ALU.mult, ALU.mult,
                        [st_.xb[c][ti], b_ab(), rsb], [tb_])
                    act(H3[:, c, t0:t0 + n], t_[:, 0:n], AF.Identity, [tb_, b_mods()], [st_.hb[c][ti]],
                        bias=modcol(0 if which == 0 else 3, c, s))

        def spill(li_index):
            allb = [b for row in st_.xb for b in row]
            if li_index == 0 and True:
                tok = None
            else:
                tok = p.dma("sp", xs, XR[:, :], reads=allb, writes=[spill_buf])
            p.barrier(("pe", "act", "dve", "sp"))
            if tok is not None:
                for e in ("pe", "act", "dve", "sp"):
                    p._wait(e, tok)
            return tok

        def reload(li_index):
            p.barrier(("pe", "act", "dve", "sp"))
            st_.xb = grid(f"x{li_index}_")
            allb = [b for row in st_.xb for b in row]
            if li_index == 0:
                load_x_from_input()
            else:
                for c in range(8):
                    p.dma("sp", X3[:, c, :], xs[:, c * NT:(c + 1) * NT], reads=[spill_buf], writes=allb, dsem=x_dsem)

        def linear(nblocks, ocs_per_block, kcn, wcols, lhs_fn, rhs_fn, rhs_bufs_fn, tis, evac, psgroup=None):
            for b in range(nblocks):
                wt, wb = w_next(kcn, wcols)
                for ocl in range(ocs_per_block):
                    for ti in tis:
                        t0, n = TCH[ti]
                        ps_t, ps_b = psum(psgroup)
                        for kc in range(kcn):
                            mm(ps_t[:, 0:n], lhs_fn(wt, ocl, kc), rhs_fn(kc, t0, n), kc == 0, kc == kcn - 1,
                               [wb] + rhs_bufs_fn(kc, ti), [ps_b], inc=(kc == kcn - 1))
                        evac(b, ocl, ti, ps_t[:, 0:n], ps_b)

        def resid_evac(grp, tis_all, bias_name=None):
            def ev(b, ocl, ti, ps_ap, ps_b):
                oc = b * 4 + ocl
                t0, n = TCH[ti]
                s = 0 if ti == 0 else 1
                src = ps_ap
                rd = [ps_b]
                if bias_name is not None:
                    t_, tb_ = tmp32()
                    act(t_[:, 0:n], ps_ap, AF.Identity, [ps_b, b_vec], [tb_], bias=vcol(bias_name, oc))
                    src = t_[:, 0:n]
                    rd = [tb_]
                stt(X3[:, oc, t0:t0 + n], src, modcol(grp, oc, s), X3[:, oc, t0:t0 + n], ALU.mult, ALU.add,
                    rd + [b_mods(), st_.xb[oc][ti]], [st_.xb[oc][ti]])
            return ev

        def hb_rhs(kc, t0, n):
            return H3[:, kc, t0:t0 + n]

        def hb_bufs(kc, ti):
            return [st_.hb[kc][ti]]

        def mlp_phase(li, tis, next_li=None, next_par=None):
            HID = AUX[:, 0:4 * NT].rearrange("p (c t) -> p c t", c=4)
            hidb = [[p.buf() for _ in range(5)] for _ in range(4)]
            for hb_i in range(8):
                w1, w1b = w_next(8, 512)
                for ti in tis:
                    t0, n = TCH[ti]
                    for ocl in range(4):
                        ps_t, ps_b = psum()
                        for kc in range(8):
                            mm(ps_t[:, 0:n], w1[:, kc, ocl * 128:(ocl + 1) * 128], H3[:, kc, t0:t0 + n], kc == 0, kc == 7,
                               [w1b, st_.hb[kc][ti]], [ps_b], inc=(kc == 7))
                        t_, tb_ = tmp32()
                        act(t_[:, 0:n], ps_t[:, 0:n], AF.Relu, [ps_b], [tb_])
                        tt(HID[:, ocl, t0:t0 + n], t_[:, 0:n], t_[:, 0:n], ALU.mult, [tb_], [hidb[ocl][ti]])
                w2, w2b = w_next(4, 1024)
                for ti in tis:
                    t0, n = TCH[ti]
                    s = 0 if ti == 0 else 1
                    for oc in range(8):
                        ps_t, ps_b = psum()
                        for kc in range(4):
                            mm(ps_t[:, 0:n], w2[:, kc, oc * 128:(oc + 1) * 128], HID[:, kc, t0:t0 + n], kc == 0, kc == 3,
                               [w2b, hidb[kc][ti]], [ps_b], inc=(kc == 3))
                        stt(X3[:, oc, t0:t0 + n], ps_t[:, 0:n], modcol(5, oc, s), X3[:, oc, t0:t0 + n], ALU.mult, ALU.add,
                            [ps_b, b_mods(), st_.xb[oc][ti]], [st_.xb[oc][ti]])
                if next_li is not None:
                    for _ in range(NMOD[hb_i]):
                        mod_block()
            if next_li is not None:
                mod_finish(next_li, next_par)

        def attention(li, j_att, need_ctx, li_index):
            QT = XRb[:, 0:18432].rearrange("p (c t) -> p c t", c=8)
            KT2 = XRb[:, 18432:27648].rearrange("p (c t) -> p c t", c=4)
            ROC = XR[:, 13824:15872]
            ROS = XR[:, 15872:17920]
            VA = AUX[:, 0:18 * 576].rearrange("p (k x) -> p k x", k=18)
            b_rope = p.buf(f"rope{li}", dma=True)
            qb = [[p.buf() for _ in range(5)] for _ in range(8)]
            kb = [[p.buf() for _ in range(5)] for _ in range(4)]
            vab = [p.buf() for _ in range(18)]
            b_va_init = p.buf()
            p.dma("sp", XR[:, 13824:17920], dr["rope"], writes=[b_rope])
            vmemset(AUX[:, 0:18 * 576], 1.0, [b_va_init] + vab)
            q_tis = [0, 1, 2, 3, 4] if need_ctx else [1, 2, 3, 4]

            PS_A = [0, 1, 2]
            PS_B = [3, 4]
            PS_C = [5, 6]
            pending = []

            def qk_item(ps_t, ps_b, n, ti, gain_ap, dst_ap, dst_buf):
                t0 = TCH[ti][0]
                lat = ti != 0
                state = {}

                def stage_b():
                    sq, sqb = tmppt()
                    act(sq[:, 0:n], ps_t[:, 0:n], AF.Square, [ps_b], [sqb])
                    ss_t, ss_b = psum(PS_B)
                    mm(ss_t[:, 0:n], ONESBLK, sq[:, 0:n], True, True, [sqb, b_const], [ss_b], inc=True)
                    sd, sdb = tmp32()
                    act(sd[:, 0:n], ss_t[:, 0:n], AF.Sqrt, [ss_b, b_misc], [sdb], bias=MISC[:, 0:1], scale=1.0 / 64)
                    rs, rsb = tmp32()
                    recip(rs[:, 0:n], sd[:, 0:n], [sdb], [rsb])
                    if not lat:
                        stt(dst_ap, ps_t[:, 0:n], gain_ap, rs[:, 0:n], ALU.mult, ALU.mult, [ps_b, rsb, b_vec], [dst_buf])
                    else:
                        qn, qnb = tmppt()
                        stt(qn[:, 0:n], ps_t[:, 0:n], gain_ap, rs[:, 0:n], ALU.mult, ALU.mult, [ps_b, rsb, b_vec], [qnb])
                        state["qn"] = (qn, qnb)

                def stage_c():
                    if not lat:
                        return
                    qn, qnb = state["qn"]
                    rot_t, rot_b = psum(PS_C)
                    mm(rot_t[:, 0:n], PERM, qn[:, 0:n], True, True, [qnb, b_const], [rot_b], inc=True)
                    t1, t1b = tmp32()
                    tt(t1[:, 0:n], qn[:, 0:n], ROC[:, t0 - NCTX:t0 - NCTX + n], ALU.mult, [qnb, b_rope], [t1b])
                    t2, t2b = tmp32()
                    tt(t2[:, 0:n], rot_t[:, 0:n], ROS[:, t0 - NCTX:t0 - NCTX + n], ALU.mult, [rot_b, b_rope], [t2b])
                    tt(dst_ap, t1[:, 0:n], t2[:, 0:n], ALU.add, [t1b, t2b], [dst_buf])
                return stage_b, stage_c

            def pipe_push(item):
                pending.append(item)
                if len(pending) >= 2:
                    pending[-2][0]()
                if len(pending) >= 3:
                    pending[-3][1]()

            def pipe_flush():
                if len(pending) >= 1:
                    pending[-1][0]()
                if len(pending) >= 2:
                    pending[-2][1]()
                if len(pending) >= 1:
                    pending[-1][1]()
                pending.clear()

            for b in range(2):
                wt, wb = w_next(8, 512)
                for ocl in range(4):
                    oc = b * 4 + ocl
                    for ti in q_tis:
                        t0, n = TCH[ti]
                        ps_t, ps_b = psum(PS_A)
                        for kc in range(8):
                            mm(ps_t[:, 0:n], wt[:, kc, ocl * 128:(ocl + 1) * 128], H3[:, kc, t0:t0 + n], kc == 0, kc == 7,
                               [wb, st_.hb[kc][ti]], [ps_b], inc=(kc == 7))
                        pipe_push(qk_item(ps_t, ps_b, n, ti, vcol(f"qg{j_att}"), QT[:, oc, t0:t0 + n], qb[oc][ti]))
            wt, wb = w_next(8, 512)
            for g in range(4):
                for ti in range(5):
                    t0, n = TCH[ti]
                    ps_t, ps_b = psum(PS_A)
                    for kc in range(8):
                        mm(ps_t[:, 0:n], wt[:, kc, g * 128:(g + 1) * 128], H3[:, kc, t0:t0 + n], kc == 0, kc == 7,
                           [wb, st_.hb[kc][ti]], [ps_b], inc=(kc == 7))
                    pipe_push(qk_item(ps_t, ps_b, n, ti, vcol(f"kg{j_att}"), KT2[:, g, t0:t0 + n], kb[g][ti]))
            pipe_flush()
            wt, wb = w_next(8, 256)
            for kt in range(18):
                ti = 0 if kt < 2 else 1 + (kt - 2) // 4
                ps_t, ps_b = psum(PS_A)
                for kc in range(8):
                    mm(ps_t[:, 0:256], H3[:, kc, kt * 128:(kt + 1) * 128], wt[:, kc, :], kc == 0, kc == 7,
                       [wb, st_.hb[kc][ti]], [ps_b], inc=(kc == 7))
                dst = VA[:, kt, 64:576].rearrange("p (g x) -> p g x", x=128)[:, :, 0:64]
                src = ps_t[:, 0:256].rearrange("p (g x) -> p g x", x=64)
                act(dst, src, AF.Identity, [ps_b], [vab[kt]])

            OBANK = [[0, 1], [2, 3]]
            ptdb = [p.buf() for _ in range(4)]
            it = 0
            for jp in range(8):
                g = jp // 2
                for ti in q_tis:
                    t0, n = TCH[ti]
                    kts = list(range(18)) if ti != 0 else [0, 1]
                    ob = OBANK[it % 2]
                    it += 1
                    o_t = [PS[ob[0]], PS[ob[1]]]
                    o_b = [psb[ob[0]], psb[ob[1]]]

                    def s_stage(kt):
                        tik = 0 if kt < 2 else 1 + (kt - 2) // 4
                        di = st_.rr.get("psd", 0) % 2
                        st_.rr["psd"] = st_.rr.get("psd", 0) + 1
                        dt_ = PSD[di]
                        dbs = [psb[4 + 2 * di], psb[5 + 2 * di]]
                        for h in range(2):
                            mm(dt_[:, h * 512:h * 512 + n], KT2[h * 64:(h + 1) * 64, g, kt * 128:(kt + 1) * 128],
                               QT[h * 64:(h + 1) * 64, jp, t0:t0 + n], True, True,
                               [kb[g][tik], qb[jp][ti]], [dbs[h]], inc=(h == 1))
                        return dt_, dbs

                    def pv_stage(kt, sres):
                        dt_, dbs = sres
                        pi = st_.rr.get("ptd", 0) % 4
                        st_.rr["ptd"] = st_.rr.get("ptd", 0) + 1
                        ptd = TB[:, pi * 1024:(pi + 1) * 1024]
                        src = dt_[:, :].rearrange("p (h x) -> p h x", h=2)[:, :, 0:n]
                        dst = ptd.rearrange("p (h x) -> p h x", h=2)[:, :, 0:n]
                        act(dst, src, AF.Exp, dbs, [ptdb[pi]], scale=0.125)
                        for h in range(2):
                            if h == 0:
                                lhs = VA[:, kt, 64 + 128 * g:192 + 128 * g]
                            else:
                                lhs = VA[:, kt, 128 * g:128 + 128 * g]
                            mm(o_t[h][:, 0:n], lhs, ptd[:, h * 512:h * 512 + n], kt == kts[0], kt == kts[-1],
                               [ptdb[pi], vab[kt]], [o_b[h]], inc=True)

                    prev = s_stage(kts[0])
                    for idx, kt in enumerate(kts):
                        nxt = s_stage(kts[idx + 1]) if idx + 1 < len(kts) else None
                        pv_stage(kt, prev)
                        prev = nxt
                    for h in range(2):
                        rc, rcb = tmp32()
                        recip(rc[:, 0:n], o_t[h][:, 0:n], [o_b[h]], [rcb])
                        lo, hi = (0, 64) if h == 0 else (64, 128)
                        dlo, dhi = (64, 128) if h == 0 else (0, 64)
                        tt(H3[lo:hi, jp, t0:t0 + n], o_t[h][lo:hi, 0:n], rc[dlo:dhi, 0:n], ALU.mult,
                           [o_b[h], rcb], [st_.hb[jp][ti]])
            dbg_dump("att_xr", XR[:, :], [128, 8 * NT], F32)
            dbg_dump("att_aux", AUX[:, :], [128, 10368], BF16)
            dbg_dump("att_hb", HBt[:, :], [128, 8 * NT], BF16)
            reload(li_index)
            linear(2, 4, 8, 512, lambda wt, ocl, kc: wt[:, kc, ocl * 128:(ocl + 1) * 128], hb_rhs, hb_bufs,
                   q_tis, resid_evac(2, q_tis))

        def conformer(li, need_ctx, li_index):
            UC = XRb[:, 0:8 * 286].rearrange("p (c t) -> p c t", c=8)
            UL = XRb[:, 2288:2288 + 8 * 2078].rearrange("p (c t) -> p c t", c=8)
            DG = [XRb[:, 18912 + i * 3968:18912 + (i + 1) * 3968].rearrange("p (k m) -> p k m", k=31) for i in range(2)]
            ub = [[p.buf() for _ in range(5)] for _ in range(8)]
            upad = p.buf()
            dgb = [p.buf(), p.buf()]
            vmemset(XRb[:, 0:18912], 0.0, [upad] + [b for row in ub for b in row])
            tis = [0, 1, 2, 3, 4]

            def useg(c, ti, k, n):
                if ti == 0:
                    return UC[:, c, k:k + n]
                o = TCH[ti][0] - NCTX
                return UL[:, c, o + k:o + k + n]

            for b in range(4):
                wt, wb = w_next(8, 512)
                for cl in range(2):
                    c = b * 2 + cl
                    for ti in tis:
                        t0, n = TCH[ti]
                        pa_t, pa_b = psum()
                        for kc in range(8):
                            mm(pa_t[:, 0:n], wt[:, kc, cl * 128:(cl + 1) * 128], H3[:, kc, t0:t0 + n], kc == 0, kc == 7,
                               [wb, st_.hb[kc][ti]], [pa_b], inc=(kc == 7))
                        pg_t, pg_b = psum()
                        for kc in range(8):
                            mm(pg_t[:, 0:n], wt[:, kc, 256 + cl * 128:256 + (cl + 1) * 128], H3[:, kc, t0:t0 + n], kc == 0, kc == 7,
                               [wb, st_.hb[kc][ti]], [pg_b], inc=(kc == 7))
                        sg, sgb = tmp32()
                        act(sg[:, 0:n], pg_t[:, 0:n], AF.Sigmoid, [pg_b, b_vec], [sgb], bias=vcol("cbin", 8 + c))
                        stt(useg(c, ti, 15, n), pa_t[:, 0:n], vcol("cbin", c), sg[:, 0:n], ALU.add, ALU.mult,
                            [pa_b, sgb, b_vec, upad], [ub[c][ti]])
            vb = [[p.buf() for _ in range(5)] for _ in range(8)]
            for c in range(8):
                par = c % 2
                for k in range(31):
                    ts(DG[par][:, k, :], IDENT, vcol("cwdw", k * 8 + c), None, ALU.mult, None, [b_const, b_vec], [dgb[par]])
                for ti in tis:
                    t0, n = TCH[ti]
                    ps_t, ps_b = psum()
                    nb = [ub[c][ti]]
                    if ti > 1:
                        nb.append(ub[c][ti - 1])
                    if 1 <= ti < 4:
                        nb.append(ub[c][ti + 1])
                    for k in range(31):
                        mm(ps_t[:, 0:n], DG[par][:, k, :], useg(c, ti, k, n), k == 0, k == 30,
                           [dgb[par], upad] + nb, [ps_b], inc=(k == 30))
                    act(H3[:, c, t0:t0 + n], ps_t[:, 0:n], AF.Identity, [ps_b, b_vec],
                        [vb[c][ti], st_.hb[c][ti]], bias=vcol("cbdw", c))
            TB3 = TB[:, :].rearrange("p (c t) -> p c t", c=8)
            yb = [[p.buf() for _ in range(5)] for _ in range(8)]
            for ti in tis:
                t0, n = TCH[ti]
                pm_t, pm_b = psum()
                for c in range(8):
                    mm(pm_t[:, 0:n], ONES, H3[:, c, t0:t0 + n], c == 0, c == 7, [vb[c][ti], b_const], [pm_b], inc=(c == 7))
                for c in range(8):
                    act(TB3[:, c, 0:n], H3[:, c, t0:t0 + n], AF.Square, [vb[c][ti]], [b_tb])
                pq_t, pq_b = psum()
                for c in range(8):
                    mm(pq_t[:, 0:n], ONES, TB3[:, c, 0:n], c == 0, c == 7, [b_tb, b_const], [pq_b], inc=(c == 7))
                mean, meanb = T32H[1], t32hb[1]
                act(mean[:, 0:n], pm_t[:, 0:n], AF.Identity, [pm_b], [meanb], scale=1.0 / D)
                m2, m2b = tmp32()
                tt(m2[:, 0:n], mean[:, 0:n], mean[:, 0:n], ALU.mult, [meanb], [m2b])
                var, varb = tmp32()
                stt(var[:, 0:n], pq_t[:, 0:n], 1.0 / D, m2[:, 0:n], ALU.mult, ALU.subtract, [pq_b, m2b], [varb])
                sd, sdb = tmp32()
                act(sd[:, 0:n], var[:, 0:n], AF.Sqrt, [varb, b_misc], [sdb], bias=MISC[:, 0:1], scale=1.0)
                rs, rsb = T32H[0], t32hb[0]
                recip(rs[:, 0:n], sd[:, 0:n], [sdb], [rsb])
                for c in range(8):
                    t_, tb_ = tmppt32()
                    tt(t_[:, 0:n], H3[:, c, t0:t0 + n], mean[:, 0:n], ALU.subtract, [vb[c][ti], meanb], [tb_])
                    tt(t_[:, 0:n], t_[:, 0:n], rs[:, 0:n], ALU.mult, [tb_, rsb], [tb_])
                    act(H3[:, c, t0:t0 + n], t_[:, 0:n], AF.Silu, [tb_, b_vec], [yb[c][ti], vb[c][ti]],
                        bias=vcol("cnb", c), scale=vcol("cng", c))
            st_.hb = yb
            reload(li_index)
            linear(2, 4, 8, 512, lambda wt, ocl, kc: wt[:, kc, ocl * 128:(ocl + 1) * 128], hb_rhs, hb_bufs,
                   tis, resid_evac(2, tis, bias_name="cbout"))

        st_.rr["tbx"] = 0

        def tmppt32():
            i = st_.rr["tbx"] % 2
            st_.rr["tbx"] += 1
            return T32X[i], t32xb[i]

        def rglru(li, need_ctx, li_index):
            G3 = XRb[:, 0:18432].rearrange("p (c t) -> p c t", c=8)
            XL3 = XRb[:, 18432:36864].rearrange("p (c t) -> p c t", c=8)
            gb = [[p.buf() for _ in range(5)] for _ in range(8)]
            xlb = [[p.buf() for _ in range(5)] for _ in range(8)]
            tis = [0, 1, 2, 3, 4]
            for b in range(4):
                wt, wb = w_next(8, 512)
                for ocl in range(4):
                    oc = b * 4 + ocl
                    for ti in tis:
                        t0, n = TCH[ti]
                        ps_t, ps_b = psum()
                        for kc in range(8):
                            mm(ps_t[:, 0:n], wt[:, kc, ocl * 128:(ocl + 1) * 128], H3[:, kc, t0:t0 + n], kc == 0, kc == 7,
                               [wb, st_.hb[kc][ti]], [ps_b], inc=(kc == 7))
                        if oc < 8:
                            act(G3[:, oc, t0:t0 + n], ps_t[:, 0:n], AF.Gelu_apprx_tanh, [ps_b], [gb[oc][ti]])
                        else:
                            vcopy(XL3[:, oc - 8, t0:t0 + n], ps_t[:, 0:n], [ps_b], [xlb[oc - 8][ti]])
            lam = VEC[:, VOFF["llam"]:VOFF["llam"] + 16]
            b_ca = p.buf()
            act(MISC[:, 64:80], lam, AF.Exp, [b_vec], [b_ca], scale=-1.0)
            act(MISC[:, 80:96], MISC[:, 64:80], AF.Ln, [b_ca, b_misc], [b_ca], bias=MISC[:, 1:2], scale=1.0)
            ts(MISC[:, 16:32], MISC[:, 80:96], -8.0, None, ALU.mult, None, [b_ca], [b_ca])
            ts(MISC[:, 48:64], MISC[:, 80:96], -4.0, None, ALU.mult, None, [b_ca], [b_ca])
            ts(MISC[:, 96:128], VEC[:, VOFF["lgb"]:VOFF["lgb"] + 32], 0.5, None, ALU.mult, None, [b_vec, b_ca], [b_ca])
            dbg_dump("lru_xr0", XR[:, :], [128, 8 * NT], F32)
            p.barrier(("pe", "act", "dve"))
            U32 = HBf[:, 0:4608].rearrange("p (c t) -> p c t", c=2)
            HS = HBf[:, 4608:9216].rearrange("p (c t) -> p c t", c=2)
            UBF = AUX[:, 0:4608].rearrange("p (c t) -> p c t", c=2)
            XT = [AUXf[:, 2304 + i * 512:2304 + (i + 1) * 512] for i in range(5)]
            xtb = [p.buf() for _ in range(5)]
            CAR = MISC[:, 128:256]
            car_i = [0]
            rrx = [0]

            def ltmp():
                i = rrx[0] % 15
                rrx[0] += 1
                if i < 5:
                    return XT[i], xtb[i]
                if i < 11:
                    return T32[i - 5], t32b[i - 5]
                if i < 13:
                    return T32X[i - 11], t32xb[i - 11]
                return T32H[i - 13], t32hb[i - 13]

            gslots = []
            gidx = []
            for d in range(2):
                gidx.append(ws.consumed)
                gslots.append(w_next(16, 256, pin=True))
            SEGS = [(0, NCTX), (NCTX, NLAT)]
            u32b = [p.buf(), p.buf()]
            ubfb = [p.buf(), p.buf()]
            hsb = [[p.buf() for _ in range(5)] for _ in range(2)]
            for nblk in range(4):
                c0 = nblk * 2
                for d in range(2):
                    gwt, gwb = gslots[d]
                    for cl in range(2):
                        c = c0 + cl
                        xall = xlb[c]
                        for (s0, sn) in SEGS:
                            ts(U32[:, cl, s0:s0 + sn], XL3[:, c, s0:s0 + sn], vcol("lcw", (d * 4 + 3) * 8 + c), vcol("lcb", d * 8 + c),
                               ALU.mult, ALU.add, xall + [b_vec], [u32b[cl]])
                            for k in range(3):
                                sh = 3 - k
                                if d == 0:
                                    o_ap = U32[:, cl, s0 + sh:s0 + sn]
                                    i_ap = XL3[:, c, s0:s0 + sn - sh]
                                else:
                                    o_ap = U32[:, cl, s0:s0 + sn - sh]
                                    i_ap = XL3[:, c, s0 + sh:s0 + sn]
                                stt(o_ap, i_ap, vcol("lcw", (d * 4 + k) * 8 + c), o_ap, ALU.mult, ALU.add,
                                    xall + [b_vec, u32b[cl]], [u32b[cl]])
                        act(UBF[:, cl, :], U32[:, cl, :], AF.Identity, [u32b[cl]], [ubfb[cl]])
                    for cl in range(2):
                        c = c0 + cl
                        groups = [[0, 1, 2], [3, 4]] if d == 0 else [[0, 4, 3], [2, 1]]
                        prev_car = None
                        oi = -1
                        c4 = MISC[:, 48 + d * 8 + c:49 + d * 8 + c]
                        for grp in groups:
                            items = []
                            for ti in grp:
                                t0, n = TCH[ti]
                                gps = []
                                for gi in range(2):
                                    ps_t, ps_b = psum()
                                    for kc in range(2):
                                        mm(ps_t[:, 0:n], gwt[:, (gi * 4 + nblk) * 2 + kc, cl * 128:(cl + 1) * 128], UBF[:, kc, t0:t0 + n],
                                           kc == 0, kc == 1, [gwb, ubfb[kc]], [ps_b], inc=(kc == 1))
                                    gps.append((ps_t, ps_b))
                                r_, rb_ = ltmp()
                                act(r_[:, 0:n], gps[0][0][:, 0:n], AF.Tanh, [gps[0][1], b_ca], [rb_],
                                    bias=MISC[:, 96 + (d * 2 + 0) * 8 + c:97 + (d * 2 + 0) * 8 + c], scale=0.5)
                                a_, ab_ = ltmp()
                                act(a_[:, 0:n], r_[:, 0:n], AF.Exp, [rb_, b_ca], [ab_], bias=c4, scale=c4)
                                i_, ib_ = ltmp()
                                act(i_[:, 0:n], gps[1][0][:, 0:n], AF.Tanh, [gps[1][1], b_ca], [ib_],
                                    bias=MISC[:, 96 + (d * 2 + 1) * 8 + c:97 + (d * 2 + 1) * 8 + c], scale=0.5)
                                stt(i_[:, 0:n], i_[:, 0:n], 1.0, U32[:, cl, t0:t0 + n], ALU.add, ALU.mult, [ib_, u32b[cl]], [ib_])
                                items.append((ti, a_, ab_, i_, ib_))
                            for (ti, a_, ab_, i_, ib_) in items:
                                oi += 1
                                t0, n = TCH[ti]
                                m_, mb_ = ltmp()
                                act(m_[:, 0:n], a_[:, 0:n], AF.Square, [ab_], [mb_])
                                act(m_[:, 0:n], m_[:, 0:n], AF.Sqrt, [mb_, b_misc], [mb_], bias=MISC[:, 1:2], scale=-1.0)
                                if ti == 0:
                                    fc = 0 if d == 0 else NCTX - 1
                                    vmemset(m_[:, fc:fc + 1], 1.0, [mb_])
                                stt(i_[:, 0:n], i_[:, 0:n], 0.5, m_[:, 0:n], ALU.mult, ALU.mult, [ib_, mb_], [ib_])
                                if d == 0:
                                    init = 0.0 if oi == 0 else HS[:, cl, t0 - 1:t0]
                                    rd = [ab_, ib_] + ([hsb[cl][ti - 1]] if oi > 0 else [])
                                    p.op("dve", lambda e, o=HS[:, cl, t0:t0 + n], a=a_[:, 0:n], b=i_[:, 0:n], init=init:
                                         e.tensor_tensor_scan(out=o, data0=a, data1=b, initial=init, op0=ALU.mult, op1=ALU.add),
                                         rd, [hsb[cl][ti]])
                                else:
                                    h_, hb_ = ltmp()
                                    init = 0.0 if oi == 0 else prev_car
                                    p.op("dve", lambda e, o=h_[:, 0:n][:, ::-1], a=a_[:, 0:n][:, ::-1], b=i_[:, 0:n][:, ::-1], init=init:
                                         e.tensor_tensor_scan(out=o, data0=a, data1=b, initial=init, op0=ALU.mult, op1=ALU.add),
                                         [ab_, ib_, b_car], [hb_])
                                    ci = car_i[0] % 128
                                    car_i[0] += 1
                                    vcopy(CAR[:, ci:ci + 1], h_[:, 0:1], [hb_], [b_car])
                                    prev_car = CAR[:, ci:ci + 1]
                                    tt(h_[:, 0:n], h_[:, 0:n], HS[:, cl, t0:t0 + n], ALU.add, [hb_, hsb[cl][ti]], [hb_])
                                    tt(G3[:, c, t0:t0 + n], h_[:, 0:n], G3[:, c, t0:t0 + n], ALU.mult, [hb_, gb[c][ti]], [gb[c][ti]])
            for gi_ in gidx:
                w_release(gi_)
            dbg_dump("lru_xr1", XR[:, :], [128, 8 * NT], F32)
            p.barrier(("pe", "act", "dve"))
            yb = [[p.buf() for _ in range(5)] for _ in range(8)]
            for c in range(8):
                for ti in tis:
                    t0, n = TCH[ti]
                    if (c + ti) % 2 == 0:
                        vcopy(H3[:, c, t0:t0 + n], G3[:, c, t0:t0 + n], [gb[c][ti]], [yb[c][ti]])
                    else:
                        act(H3[:, c, t0:t0 + n], G3[:, c, t0:t0 + n], AF.Identity, [gb[c][ti]], [yb[c][ti]])
            st_.hb = yb
            reload(li_index)
            linear(2, 4, 8, 512, lambda wt, ocl, kc: wt[:, kc, ocl * 128:(ocl + 1) * 128], hb_rhs, hb_bufs,
                   tis, resid_evac(2, tis))

        vmemset(MISC[:, 0:1], EPS, [b_misc])
        vmemset(MISC[:, 1:2], 1.0, [b_misc])
        for idx, li in enumerate(layers):
            kind = li % 3
            need_ctx = li < DEPTH - 1
            st_.par = idx % 2
            if idx == 0:
                for _ in range(12):
                    mod_block()
                mod_finish(li, 0)
            st_.hb = grid(f"h{li}_")
            norm_phase(0, [0, 1, 2, 3, 4])
            if DBG and idx == 0:
                p.dma("sp", dbg_mods, MODS()[:, :], reads=[b_mods()], writes=[o_buf])
                p.dma("sp", dbg_ab, AB()[:, :], reads=[b_ab()], writes=[o_buf])
                p.dma("sp", dbg_h1, HBt[:, :], reads=[b for row in st_.hb for b in row], writes=[o_buf])
            first_in_prog = (idx == 0)
            spill(0 if first_in_prog else 1)
            lidx = 0 if first_in_prog else 1
            import os as _os
            if _os.environ.get("DBG_SKIP_MIX"):
                for _ in range({0: 6, 1: 6, 2: 8}[kind]):
                    w_next(8, 512)
                reload(lidx)
            elif kind == 0:
                attention(li, li // 3, need_ctx, lidx)
            elif kind == 1:
                conformer(li, need_ctx, lidx)
            else:
                rglru(li, need_ctx, lidx)
            tis = [0, 1, 2, 3, 4] if need_ctx else [1, 2, 3, 4]
            p.barrier(("pe", "act", "dve"))
            st_.hb = grid(f"h2{li}_")
            if _os.environ.get("DBG_SKIP_MLP"):
                for _ in range(16 + (12 if idx + 1 < len(layers) else 0)):
                    w_next(8, 512)
            else:
                if DBG and idx == 0:
                    p.dma("sp", dbg_x1, XR[:, :], reads=[b for row in st_.xb for b in row], writes=[o_buf])
                norm_phase(1, tis)
                if DBG and idx == 0:
                    p.dma("sp", dbg_h2, HBt[:, :], reads=[b for row in st_.hb for b in row], writes=[o_buf])
                if idx + 1 < len(layers):
                    mlp_phase(li, tis, layers[idx + 1], (idx + 1) % 2)
                else:
                    mlp_phase(li, tis)
        allb = [b for row in st_.xb for b in row]
        if last:
            for c in range(8):
                p.dma("sp", outT[c * 128:(c + 1) * 128, :], X3[:, c, NCTX:NT], reads=allb, writes=[o_buf])
        else:
            p.dma("sp", xs_out, XR[:, :], reads=allb, writes=[o_buf])
        p._wait("sp", o_buf.w)
        assert ws.consumed == len(wspecs), (ws.consumed, len(wspecs))
        p.emit()
    return nc


_WKEYS = ["mod_w", "mlp_w1", "mlp_w2", "attn_w_qkv", "attn_w_o", "conv_w_in", "conv_w_out",
          "lru_w_in", "lru_gate_w", "lru_w_out"]


def _common_maps(inputs):
    m = {k: np.ascontiguousarray(np.asarray(inputs[k], np.float32)) for k in _WKEYS}
    m["consts"] = _consts()
    m["rope"] = _rope_tables()
    return m


def kernel(**inputs):
    n = 8
    common = _common_maps(inputs)
    x = np.asarray(inputs["x"], np.float32)
    ctx = np.asarray(inputs["ctx"], np.float32)
    in_maps = []
    for b in range(n):
        m = dict(common)
        m["xT"] = np.ascontiguousarray(x[b].T)
        m["ctxT"] = np.ascontiguousarray(ctx[b].T)
        m["vecs"] = _pack_vecs(inputs, b)
        in_maps.append(m)
    nc = build_program((0, 1, 2, 3), True, True)
    res = run_bass_kernel_spmd(nc, in_maps, core_ids=list(range(n)))
    out = np.stack([np.ascontiguousarray(res.results[b]["outT"].T) for b in range(n)], axis=0)
    return out.astype(np.float32)
```

```python
import numpy as np
from contextlib import ExitStack
import concourse.bass as bass
import concourse.mybir as mybir
from concourse.bass_utils import run_bass_kernel_spmd
import ml_dtypes

F32 = mybir.dt.float32
BF16 = mybir.dt.bfloat16
AF = mybir.ActivationFunctionType
ALU = mybir.AluOpType

SELF_WAIT = True

NT, NCTX, NLAT, D = 2304, 256, 2048, 1024
TCH = [(0, 256), (256, 512), (768, 512), (1280, 512), (1792, 512)]
EPS = 1e-6
DEPTH = 4


class Sem:
    def __init__(self, h, name):
        self.h = h
        self.name = name
        self.count = 0


class Buf:
    __slots__ = ("name", "w", "r", "dsem")

    def __init__(self, name, dsem=None):
        self.name = name
        self.w = None
        self.r = []
        self.dsem = dsem


class Prog:
    ENG = ("pe", "act", "dve", "pool", "sp")

    def __init__(self, nc, stack):
        self.nc = nc
        self.stack = stack
        self.ops = {e: [] for e in self.ENG}
        self.esem = {}
        for e in ("pe", "act", "dve", "pool"):
            self.esem[e] = self.new_sem("s_" + e)
        self.known = {e: {} for e in self.ENG}
        self.pending_noinc = {e: False for e in self.ENG}
        self.nbuf = 0

    def new_sem(self, name):
        h = self.stack.enter_context(self.nc.semaphore(name))
        return Sem(h, name)

    def buf(self, name=None, dma=False):
        self.nbuf += 1
        name = name or f"b{self.nbuf}"
        return Buf(name, self.new_sem("d_" + name) if dma else None)

    def sb(self, name, shape, dt):
        return self.stack.enter_context(self.nc.sbuf_tensor(name, list(shape), dt))

    def ps(self, name, shape, dt=F32):
        return self.stack.enter_context(self.nc.psum_tensor(name, list(shape), dt))

    def _wait(self, eng, tok):
        if tok is None:
            return
        sem, val = tok
        if eng == "pe" and sem is self.esem["pe"]:
            return
        if (not SELF_WAIT) and eng in self.esem and sem is self.esem[eng]:
            return
        k = self.known[eng]
        if k.get(sem, 0) >= val:
            return
        k[sem] = val
        h = sem.h
        self.ops[eng].append(lambda e, h=h, val=val: e.wait_ge(h, val))

    def _deps(self, eng, reads, writes):
        for b in reads:
            self._wait(eng, b.w)
        for b in writes:
            self._wait(eng, b.w)
            for t in b.r:
                self._wait(eng, t)

    def _commit(self, tok, reads, writes):
        for b in reads:
            b.r.append(tok)
            if len(b.r) > 16:
                d = {}
                for s, v in b.r:
                    if d.get(s, 0) < v:
                        d[s] = v
                b.r = list(d.items())
        for b in writes:
            b.w = tok
            b.r = []

    def op(self, eng, fn, reads=(), writes=(), inc=True):
        self._deps(eng, reads, writes)
        sem = self.esem[eng]
        if inc:
            sem.count += 1
            val = sem.count
            h = sem.h
            self.ops[eng].append(lambda e, fn=fn, h=h: fn(e).then_inc(h, 1))
            self.pending_noinc[eng] = False
        else:
            val = sem.count + 1
            self.ops[eng].append(lambda e, fn=fn: fn(e))
            self.pending_noinc[eng] = True
        tok = (sem, val)
        self._commit(tok, reads, writes)
        return tok

    def dma(self, q, out, in_, reads=(), writes=(), dsem=None):
        self._deps(q, reads, writes)
        sem = dsem if dsem is not None else writes[0].dsem
        sem.count += 16
        val = sem.count
        h = sem.h
        self.ops[q].append(lambda e, out=out, in_=in_, h=h: e.dma_start(out=out, in_=in_).then_inc(h, 16))
        tok = (sem, val)
        self._commit(tok, reads, writes)
        return tok

    def barrier(self, engs=("pe", "act", "dve", "sp")):
        for e in ("pe", "act", "dve", "pool"):
            assert not self.pending_noinc[e]
        toks = [(self.esem[e], self.esem[e].count) for e in ("pe", "act", "dve", "pool")]
        if "pool" not in engs:
            engs = tuple(engs) + ("pool",)
        for e in engs:
            for t in toks:
                if t[1] > 0:
                    self._wait(e, t)

    def emit(self):
        for e in ("pe", "act", "dve"):
            assert not self.pending_noinc[e], f"engine {e} has trailing non-inc op"
        ops = self.ops
        with self.nc.Block() as block:
            @block.sync
            def _(eng):
                for f in ops["sp"]:
                    f(eng)

            @block.tensor
            def _(eng):
                for f in ops["pe"]:
                    f(eng)

            @block.scalar
            def _(eng):
                for f in ops["act"]:
                    f(eng)

            @block.vector
            def _(eng):
                for f in ops["dve"]:
                    f(eng)

            @block.gpsimd
            def _(eng):
                for f in ops["pool"]:
                    f(eng)


def _vec_layout():
    L = [("c", 8), ("cctx", 8)]
    for i in range(DEPTH):
        L += [(f"modb{i}", 48), (f"gmix{i}", 8), (f"gmlp{i}", 8)]
    for j in range(2):
        L += [(f"qg{j}", 1), (f"kg{j}", 1)]
    L += [("cbin", 16), ("cwdw", 248), ("cbdw", 8), ("cng", 8), ("cnb", 8), ("cbout", 8)]
    L += [("lcw", 64), ("lcb", 16), ("lgb", 32), ("llam", 16)]
    off = {}
    o = 0
    for n, k in L:
        off[n] = o
        o += k
    return off, o


VOFF, NV = _vec_layout()


def _pk(v):
    v = np.asarray(v, np.float32).reshape(-1, 128)
    return np.ascontiguousarray(v.T)


def _pack_vecs(inp, b):
    V = np.zeros((128, NV), np.float32)

    def put(name, arr):
        arr = np.asarray(arr, np.float32)
        V[:, VOFF[name]:VOFF[name] + arr.shape[1]] = arr

    put("c", _pk(inp["c"][b]))
    put("cctx", _pk(inp["c_ctx"]))
    for i in range(DEPTH):
        put(f"modb{i}", _pk(inp["mod_b"][i]))
        put(f"gmix{i}", _pk(inp["norm_mix_g"][i]))
        put(f"gmlp{i}", _pk(inp["norm_mlp_g"][i]))
    for j in range(2):
        put(f"qg{j}", np.tile(np.asarray(inp["attn_q_gain"][j], np.float32), 2)[:, None])
        put(f"kg{j}", np.tile(np.asarray(inp["attn_k_gain"][j], np.float32), 2)[:, None])
    put("cbin", _pk(inp["conv_b_in"][0]))
    wdw = np.asarray(inp["conv_w_dw"][0], np.float32).reshape(31, 8, 128).transpose(2, 0, 1).reshape(128, 248)
    put("cwdw", wdw)
    put("cbdw", _pk(inp["conv_b_dw"][0]))
    put("cng", _pk(inp["conv_norm_g"][0]))
    put("cnb", _pk(inp["conv_norm_b"][0]))
    put("cbout", _pk(inp["conv_b_out"][0]))
    lcw = np.asarray(inp["lru_conv_w"][0], np.float32).reshape(2, 4, 8, 128).transpose(3, 0, 1, 2).reshape(128, 64)
    put("lcw", lcw)
    put("lcb", np.asarray(inp["lru_conv_b"][0], np.float32).reshape(2, 8, 128).transpose(2, 0, 1).reshape(128, 16))
    put("lgb", np.asarray(inp["lru_gate_b"][0], np.float32).reshape(2, 2, 8, 128).transpose(3, 0, 1, 2).reshape(128, 32))
    put("llam", np.asarray(inp["lru_lambda"][0], np.float32).reshape(2, 8, 128).transpose(2, 0, 1).reshape(128, 16))
    return V


def _consts():
    p = np.arange(128)
    perm = (p[:, None] == (p[None, :] ^ 16)).astype(np.float32)
    onesblk = ((p[:, None] // 64) == (p[None, :] // 64)).astype(np.float32)
    ones = np.ones((128, 128), np.float32)
    ident = np.eye(128, dtype=np.float32)
    return np.concatenate([perm, onesblk, ones, ident], axis=1).astype(ml_dtypes.bfloat16)


def _rope_tables():
    t = np.arange(NLAT)
    row = (t // 64).astype(np.float64)
    col = (t % 64).astype(np.float64)
    inv = 10000.0 ** (-np.arange(16, dtype=np.float64) / 16.0)
    p = np.arange(128)
    d = p % 64
    a = d // 32
    half = (d // 16) % 2
    f = d % 16
    pos = np.where(a[:, None] == 0, row[None, :], col[None, :])
    ang = (pos.astype(np.float32) * inv.astype(np.float32)[f][:, None]).astype(np.float32)
    C = np.cos(ang).astype(np.float32)
    S = np.sin(ang).astype(np.float32) * np.where(half == 0, -1.0, 1.0)[:, None].astype(np.float32)
    return np.ascontiguousarray(np.concatenate([C, S], axis=1).astype(np.float32))


class K:
    pass


def build_program(layers=(0, 1, 2, 3), first=True, last=True):
    nc = bass.Bass("TRN2", target_bir_lowering=False)
    dr = {}

    def din(name, shape, dt=F32):
        dr[name] = nc.dram_tensor(name, list(shape), dt, kind="ExternalInput").ap()
        return dr[name]

    if first:
        din("xT", [D, NLAT])
        din("ctxT", [D, NCTX])
    else:
        din("xs_in", [128, 8 * NT])
    din("vecs", [128, NV])
    din("consts", [128, 512], BF16)
    din("rope", [128, 2 * NLAT])
    din("mod_w", [4, D, 6 * D])
    din("mlp_w1", [4, D, 4 * D])
    din("mlp_w2", [4, 4 * D, D])
    din("attn_w_qkv", [2, D, 1536])
    din("attn_w_o", [2, D, D])
    din("conv_w_in", [1, D, 2 * D])
    din("conv_w_out", [1, D, D])
    din("lru_w_in", [1, D, 2 * D])
    din("lru_gate_w", [1, 2, 2, 4, 256, 256])
    din("lru_w_out", [1, D, D])
    if last:
        outT = nc.dram_tensor("outT", [D, NLAT], F32, kind="ExternalOutput").ap()
    else:
        xs_out = nc.dram_tensor("xs_out", [128, 8 * NT], F32, kind="ExternalOutput").ap()
    xs = nc.dram_tensor("xs_scr", [128, 8 * NT], F32, kind="Internal").ap()
    import os as _os
    DBG = bool(_os.environ.get("DBG_DUMP"))
    if DBG:
        dbg_mods = nc.dram_tensor("dbg_mods", [128, 96], F32, kind="ExternalOutput").ap()
        dbg_ab = nc.dram_tensor("dbg_ab", [128, 64], F32, kind="ExternalOutput").ap()
        dbg_h1 = nc.dram_tensor("dbg_h1", [128, 8 * NT], BF16, kind="ExternalOutput").ap()
        dbg_h2 = nc.dram_tensor("dbg_h2", [128, 8 * NT], BF16, kind="ExternalOutput").ap()
        dbg_x1 = nc.dram_tensor("dbg_x1", [128, 8 * NT], F32, kind="ExternalOutput").ap()

    with ExitStack() as st:
        p = Prog(nc, st)
        XR = p.sb("XR", [128, 8 * NT], F32)
        HBt = p.sb("HB", [128, 8 * NT], BF16)
        AUX = p.sb("AUX", [128, 10368], BF16)
        SLOT = [p.sb(f"slot{i}", [128, 4096], BF16) for i in range(4)]
        VEC = p.sb("VEC", [128, NV], F32)
        CONST = p.sb("CONST", [128, 512], BF16)
        MODS_ = [p.sb(f"MODS{i}", [128, 96], F32) for i in range(2)]
        AB_ = [p.sb(f"AB{i}", [128, 64], F32) for i in range(2)]
        SC = p.sb("SC", [128, 16], BF16)
        MISC = p.sb("MISC", [128, 256], F32)
        T32 = [p.sb(f"t32_{i}", [128, 512], F32) for i in range(6)]
        TB = p.sb("TB", [128, 8 * 512], BF16)
        T32X = [p.sb(f"t32x_{i}", [128, 512], F32) for i in range(2)]
        t32xb = [p.buf(f"t32x_{i}") for i in range(2)]
        T32H = [p.sb(f"t32h_{i}", [128, 512], F32) for i in range(2)]
        t32hb = [p.buf(f"t32h_{i}") for i in range(2)]
        b_tb = p.buf("tb")
        b_car = p.buf("car")
        PT = [p.sb(f"pt{i}", [128, 512], BF16) for i in range(6)]
        PS = [p.ps(f"ps{i}", [128, 512], F32) for i in range(4)]
        PSD = [p.ps(f"psd{i}", [128, 1024], F32) for i in range(2)]
        PS = PS + [PSD[0][:, 0:512], PSD[0][:, 512:1024], PSD[1][:, 0:512], PSD[1][:, 512:1024]]
        psb = [p.buf(f"ps{i}") for i in range(8)]
        t32b = [p.buf(f"t32_{i}") for i in range(6)]
        ptb = [p.buf(f"pt{i}") for i in range(6)]
        slotb = [p.buf(f"slot{i}", dma=True) for i in range(4)]
        b_vec = p.buf("vec", dma=True)
        b_const = p.buf("const", dma=True)
        b_mods_ = [p.buf("mods0"), p.buf("mods1")]
        b_ab_ = [p.buf("ab0"), p.buf("ab1")]
        b_sc = p.buf("sc")
        b_misc = p.buf("misc")
        x_dsem = p.new_sem("d_x")
        o_buf = p.buf("out", dma=True)
        spill_buf = p.buf("spill", dma=True)

        dbg_outs = {}

        def dbg_dump(name, ap, shape, dt):
            if not DBG:
                return
            t = nc.dram_tensor("dd_" + name, list(shape), dt, kind="ExternalOutput").ap()
            p.barrier(("sp",))
            p.dma("sp", t, ap, writes=[o_buf])
            for e in ("pe", "act", "dve"):
                p._wait(e, o_buf.w)

        PERM = CONST[:, 0:128]
        ONESBLK = CONST[:, 128:256]
        ONES = CONST[:, 256:384]
        IDENT = CONST[:, 384:512]

        X3 = XR[:, :].rearrange("p (c t) -> p c t", c=8)
        XRb = XR[:, :].bitcast(BF16)
        H3 = HBt[:, :].rearrange("p (c t) -> p c t", c=8)
        HBf = HBt[:, :].bitcast(F32)
        AUXf = AUX[:, :].bitcast(F32)

        def grid(name):
            return [[p.buf(f"{name}{c}_{t}") for t in range(5)] for c in range(8)]

        st_ = K()
        st_.xb = grid("x")
        st_.hb = grid("h")
        st_.rr = {"t32": 0, "pt": 0, "ps": 0}

        def vcol(name, j=0):
            o = VOFF[name] + j
            return VEC[:, o:o + 1]

        def tmp32():
            i = st_.rr["t32"] % 6
            st_.rr["t32"] += 1
            return T32[i], t32b[i]

        def tmppt():
            i = st_.rr["pt"] % 6
            st_.rr["pt"] += 1
            return PT[i], ptb[i]

        def psum(group=None):
            group = group if group is not None else list(range(7))
            key = ("ps",) + tuple(group)
            k_ = st_.rr.get(key, 0)
            st_.rr[key] = k_ + 1
            i = group[k_ % len(group)]
            return PS[i], psb[i]

        def mm(out, lhsT, rhs, start, stop, reads, writes, inc):
            p.op("pe", lambda e: e.matmul(out, lhsT, rhs, start=start, stop=stop), reads, writes, inc=inc)

        def act(out, in_, func, reads, writes, bias=None, scale=None):
            kw = {}
            if bias is not None:
                kw["bias"] = bias
            if scale is not None:
                kw["scale"] = scale
            p.op("act", lambda e: e.activation(out=out, in_=in_, func=func, **kw), reads, writes)

        def tt(out, in0, in1, op, reads, writes, eng="dve"):
            p.op(eng, lambda e: e.tensor_tensor(out=out, in0=in0, in1=in1, op=op), reads, writes)

        def ts(out, in0, s1, s2, op0, op1, reads, writes, eng="dve"):
            if s2 is None:
                p.op(eng, lambda e: e.tensor_scalar(out=out, in0=in0, scalar1=s1, scalar2=None, op0=op0), reads, writes)
            else:
                p.op(eng, lambda e: e.tensor_scalar(out=out, in0=in0, scalar1=s1, scalar2=s2, op0=op0, op1=op1), reads, writes)

        def stt(out, in0, scalar, in1, op0, op1, reads, writes, eng="dve"):
            p.op(eng, lambda e: e.scalar_tensor_tensor(out=out, in0=in0, scalar=scalar, in1=in1, op0=op0, op1=op1), reads, writes)

        def recip(out, in_, reads, writes):
            p.op("dve", lambda e: e.reciprocal(out=out, in_=in_), reads, writes)

        def vcopy(out, in_, reads, writes):
            p.op("dve", lambda e: e.tensor_copy(out=out, in_=in_), reads, writes)

        def vmemset(ap, val, writes):
            p.op("dve", lambda e: e.memset(ap, val), (), writes)

        wspecs = []

        def wv(ap2d):
            return ap2d.rearrange("(k p) n -> p k n", p=128)

        NMOD = [2, 1, 2, 1, 2, 1, 2, 1]
        for lidx_, li in enumerate(layers):
            kind = li % 3
            j = li // 3
            if lidx_ == 0:
                for b in range(12):
                    wspecs.append([(0, 8, 512, wv(dr["mod_w"][li, :, b * 512:(b + 1) * 512]))])
            if kind == 0:
                wq = dr["attn_w_qkv"][j]
                for b in range(2):
                    wspecs.append([(0, 8, 512, wv(wq[:, b * 512:(b + 1) * 512]))])
                sp_ = []
                for g in range(4):
                    for dup in range(2):
                        sp_.append(((g * 2 + dup) * 64, 8, 64, wv(wq[:, 1024 + g * 64:1024 + (g + 1) * 64]), 512))
                wspecs.append(sp_)
                wspecs.append([(0, 8, 256, wv(wq[:, 1280:1536]))])
                for b in range(2):
                    wspecs.append([(0, 8, 512, wv(dr["attn_w_o"][j][:, b * 512:(b + 1) * 512]))])
            elif kind == 1:
                wi = dr["conv_w_in"][0]
                for b in range(4):
                    wspecs.append([(0, 8, 256, wv(wi[:, b * 256:(b + 1) * 256]), 512),
                                   (256, 8, 256, wv(wi[:, 1024 + b * 256:1024 + (b + 1) * 256]), 512)])
                for b in range(2):
                    wspecs.append([(0, 8, 512, wv(dr["conv_w_out"][0][:, b * 512:(b + 1) * 512]))])
            else:
                wi = dr["lru_w_in"][0]
                for b in range(4):
                    wspecs.append([(0, 8, 512, wv(wi[:, b * 512:(b + 1) * 512]))])
                for d in range(2):
                    gw = dr["lru_gate_w"][0, d].rearrange("g n k e -> (g n k) e")
                    wspecs.append([(0, 16, 256, wv(gw))])
                for b in range(2):
                    wspecs.append([(0, 8, 512, wv(dr["lru_w_out"][0][:, b * 512:(b + 1) * 512]))])
            mb_ = 0
            for hb in range(8):
                wspecs.append([(0, 8, 512, wv(dr["mlp_w1"][li, :, hb * 512:(hb + 1) * 512]))])
                wspecs.append([(0, 4, 1024, wv(dr["mlp_w2"][li, hb * 512:(hb + 1) * 512, :]))])
                if lidx_ + 1 < len(layers):
                    nl_ = layers[lidx_ + 1]
                    for _ in range(NMOD[hb]):
                        wspecs.append([(0, 8, 512, wv(dr["mod_w"][nl_, :, mb_ * 512:(mb_ + 1) * 512]))])
                        mb_ += 1

        ws = K()
        ws.issued = 0
        ws.consumed = 0

        def w_issue(jb):
            s = jb % 4
            for spec in wspecs[jb]:
                if len(spec) == 5:
                    off, kcn, ncol, src, rowlen = spec
                    dst = SLOT[s][:, 0:kcn * rowlen].rearrange("p (k n) -> p k n", k=kcn)[:, :, off:off + ncol]
                else:
                    off, kcn, ncol, src = spec
                    dst = SLOT[s][:, off:off + kcn * ncol].rearrange("p (k n) -> p k n", k=kcn)
                p.dma("pool", dst, src, writes=[slotb[s]])

        ws.released = set()
        ws.pinned = set()

        def w_release(i):
            ws.pinned.discard(i)
            ws.released.add(i)

        def w_next(kcn, ncol, pin=False):
            i = ws.consumed
            if i - 1 >= 0 and (i - 1) not in ws.pinned:
                ws.released.add(i - 1)
            while ws.issued < min(i + 4, len(wspecs)) and (ws.issued < 4 or (ws.issued - 4) in ws.released):
                w_issue(ws.issued)
                ws.issued += 1
            assert ws.issued > i, "weight block not issued (pinned slot deadlock)"
            if pin:
                ws.pinned.add(i)
            ws.consumed += 1
            s = i % 4
            return SLOT[s][:, 0:kcn * ncol].rearrange("p (k n) -> p k n", k=kcn), slotb[s]

        p.dma("sp", VEC[:, :], dr["vecs"], writes=[b_vec])
        p.dma("sp", CONST[:, :], dr["consts"], writes=[b_const])

        def load_x_from_input():
            allb = [b for row in st_.xb for b in row]
            if first:
                p.dma("sp", X3[:, :, 0:NCTX], dr["ctxT"].rearrange("(c p) t -> p c t", p=128), writes=allb, dsem=x_dsem)
                for c in range(8):
                    p.dma("sp", X3[:, c, NCTX:NT], dr["xT"][c * 128:(c + 1) * 128, :], writes=allb, dsem=x_dsem)
            else:
                for c in range(8):
                    p.dma("sp", X3[:, c, :], dr["xs_in"][:, c * NT:(c + 1) * NT], writes=allb, dsem=x_dsem)

        load_x_from_input()
        SC3 = SC[:, :].rearrange("p (k s) -> p k s", s=2)
        act(SC3[:, :, 0], VEC[:, VOFF["cctx"]:VOFF["cctx"] + 8], AF.Silu, [b_vec], [b_sc])
        act(SC3[:, :, 1], VEC[:, VOFF["c"]:VOFF["c"] + 8], AF.Silu, [b_vec], [b_sc])

        st_.par = 0

        def MODS():
            return MODS_[st_.par]

        def AB():
            return AB_[st_.par]

        def b_mods():
            return b_mods_[st_.par]

        def b_ab():
            return b_ab_[st_.par]

        def modcol(grp, c, s):
            return MODS()[:, (grp * 8 + c) * 2 + s:(grp * 8 + c) * 2 + s + 1]

        modst = K()
        modst.nb = 0

        def mod_block():
            b = modst.nb
            modst.nb += 1
            ps_t, ps_b = PS[7], psb[7]
            wt, wb = w_next(8, 512)
            for jj in range(4):
                jx = b * 4 + jj
                for kc in range(8):
                    mm(ps_t[:, 2 * jx:2 * jx + 2], wt[:, kc, jj * 128:(jj + 1) * 128], SC3[:, kc, :],
                       kc == 0, kc == 7, [wb, b_sc], [ps_b], inc=(jj == 3 and kc == 7))

        def mod_finish(li, par):
            assert modst.nb == 12
            modst.nb = 0
            ps_t, ps_b = PS[7], psb[7]
            M = MODS_[par]
            A = AB_[par]
            M3 = M[:, :].rearrange("p (j s) -> p j s", s=2)
            ps3 = ps_t[:, 0:96].rearrange("p (j s) -> p j s", s=2)
            mb = VEC[:, VOFF[f"modb{li}"]:VOFF[f"modb{li}"] + 48]
            for s in range(2):
                tt(M3[:, :, s], ps3[:, :, s], mb, ALU.add, [ps_b, b_vec], [b_mods_[par]])
            gm = VEC[:, VOFF[f"gmix{li}"]:VOFF[f"gmix{li}"] + 8]
            gl = VEC[:, VOFF[f"gmlp{li}"]:VOFF[f"gmlp{li}"] + 8]
            for s in range(2):
                stt(A[:, s * 8:s * 8 + 8], M3[:, 8:16, s], 1.0, gm, ALU.add, ALU.mult, [b_mods_[par], b_vec], [b_ab_[par]])
                stt(A[:, 16 + s * 8:16 + s * 8 + 8], M3[:, 32:40, s], 1.0, gl, ALU.add, ALU.mult, [b_mods_[par], b_vec], [b_ab_[par]])

        def norm_phase(which, tis):
            TB3 = TB[:, :].rearrange("p (c t) -> p c t", c=8)
            for ti in tis:
                t0, n = TCH[ti]
                s = 0 if ti == 0 else 1
                for c in range(8):
                    act(TB3[:, c, 0:n], X3[:, c, t0:t0 + n], AF.Square, [st_.xb[c][ti]], [b_tb])
                ps_t, ps_b = psum()
                for c in range(8):
                    mm(ps_t[:, 0:n], ONES, TB3[:, c, 0:n], c == 0, c == 7, [b_tb, b_const], [ps_b], inc=(c == 7))
                sd, sdb = tmp32()
                act(sd[:, 0:n], ps_t[:, 0:n], AF.Sqrt, [ps_b, b_misc], [sdb], bias=MISC[:, 0:1], scale=1.0 / D)
                rs, rsb = T32H[0], t32hb[0]
                recip(rs[:, 0:n], sd[:, 0:n], [sdb], [rsb])
                for c in range(8):
                    t_, tb_ = tmp32()
                    a_ap = AB()[:, which * 16 + s * 8 + c:which * 16 + s * 8 + c + 1]
                    stt(t_[:, 0:n], X3[:, c, t0:t0 + n], a_ap, rs[:, 0:n], ALU.mult, ALU.mult,
                        [st_.xb[c][ti], b_ab(), rsb], [tb_])
                    act(H3[:, c, t0:t0 + n], t_[:, 0:n], AF.Identity, [tb_, b_mods()], [st_.hb[c][ti]],
                        bias=modcol(0 if which == 0 else 3, c, s))

        def spill(li_index):
            allb = [b for row in st_.xb for b in row]
            if li_index == 0 and True:
                tok = None
            else:
                tok = p.dma("sp", xs, XR[:, :], reads=allb, writes=[spill_buf])
            p.barrier(("pe", "act", "dve", "sp"))
            if tok is not None:
                for e in ("pe", "act", "dve", "sp", "pool"):
                    p._wait(e, tok)
            return tok

        def reload(li_index):
            p.barrier(("pe", "act", "dve", "sp"))
            st_.xb = grid(f"x{li_index}_")
            allb = [b for row in st_.xb for b in row]
            if li_index == 0:
                load_x_from_input()
            else:
                for c in range(8):
                    p.dma("sp", X3[:, c, :], xs[:, c * NT:(c + 1) * NT], reads=[spill_buf], writes=allb, dsem=x_dsem)

        def linear(nblocks, ocs_per_block, kcn, wcols, lhs_fn, rhs_fn, rhs_bufs_fn, tis, evac, psgroup=None):
            for b in range(nblocks):
                wt, wb = w_next(kcn, wcols)
                for ocl in range(ocs_per_block):
                    for ti in tis:
                        t0, n = TCH[ti]
                        ps_t, ps_b = psum(psgroup)
                        for kc in range(kcn):
                            mm(ps_t[:, 0:n], lhs_fn(wt, ocl, kc), rhs_fn(kc, t0, n), kc == 0, kc == kcn - 1,
                               [wb] + rhs_bufs_fn(kc, ti), [ps_b], inc=(kc == kcn - 1))
                        evac(b, ocl, ti, ps_t[:, 0:n], ps_b)

        def resid_evac(grp, tis_all, bias_name=None):
            def ev(b, ocl, ti, ps_ap, ps_b):
                oc = b * 4 + ocl
                t0, n = TCH[ti]
                s = 0 if ti == 0 else 1
                src = ps_ap
                rd = [ps_b]
                if bias_name is not None:
                    t_, tb_ = tmp32()
                    act(t_[:, 0:n], ps_ap, AF.Identity, [ps_b, b_vec], [tb_], bias=vcol(bias_name, oc))
                    src = t_[:, 0:n]
                    rd = [tb_]
                stt(X3[:, oc, t0:t0 + n], src, modcol(grp, oc, s), X3[:, oc, t0:t0 + n], ALU.mult, ALU.add,
                    rd + [b_mods(), st_.xb[oc][ti]], [st_.xb[oc][ti]])
            return ev

        def hb_rhs(kc, t0, n):
            return H3[:, kc, t0:t0 + n]

        def hb_bufs(kc, ti):
            return [st_.hb[kc][ti]]

        def mlp_phase(li, tis, next_li=None, next_par=None):
            HID = AUX[:, 0:4 * NT].rearrange("p (c t) -> p c t", c=4)
            hidb = [[p.buf() for _ in range(5)] for _ in range(4)]
            for hb_i in range(8):
                w1, w1b = w_next(8, 512)
                for ti in tis:
                    t0, n = TCH[ti]
                    for ocl in range(4):
                        ps_t, ps_b = psum()
                        for kc in range(8):
                            mm(ps_t[:, 0:n], w1[:, kc, ocl * 128:(ocl + 1) * 128], H3[:, kc, t0:t0 + n], kc == 0, kc == 7,
                               [w1b, st_.hb[kc][ti]], [ps_b], inc=(kc == 7))
                        t_, tb_ = tmp32()
                        act(t_[:, 0:n], ps_t[:, 0:n], AF.Relu, [ps_b], [tb_])
                        tt(HID[:, ocl, t0:t0 + n], t_[:, 0:n], t_[:, 0:n], ALU.mult, [tb_], [hidb[ocl][ti]])
                w2, w2b = w_next(4, 1024)
                for ti in tis:
                    t0, n = TCH[ti]
                    s = 0 if ti == 0 else 1
                    for oc in range(8):
                        ps_t, ps_b = psum()
                        for kc in range(4):
                            mm(ps_t[:, 0:n], w2[:, kc, oc * 128:(oc + 1) * 128], HID[:, kc, t0:t0 + n], kc == 0, kc == 3,
                               [w2b, hidb[kc][ti]], [ps_b], inc=(kc == 3))
                        stt(X3[:, oc, t0:t0 + n], ps_t[:, 0:n], modcol(5, oc, s), X3[:, oc, t0:t0 + n], ALU.mult, ALU.add,
                            [ps_b, b_mods(), st_.xb[oc][ti]], [st_.xb[oc][ti]])
                if next_li is not None:
                    for _ in range(NMOD[hb_i]):
                        mod_block()
            if next_li is not None:
                mod_finish(next_li, next_par)

        def attention(li, j_att, need_ctx, li_index):
            QT = XRb[:, 0:18432].rearrange("p (c t) -> p c t", c=8)
            KT2 = XRb[:, 18432:27648].rearrange("p (c t) -> p c t", c=4)
            ROC = XR[:, 13824:15872]
            ROS = XR[:, 15872:17920]
            VA = AUX[:, 0:18 * 576].rearrange("p (k x) -> p k x", k=18)
            b_rope = p.buf(f"rope{li}", dma=True)
            qb = [[p.buf() for _ in range(5)] for _ in range(8)]
            kb = [[p.buf() for _ in range(5)] for _ in range(4)]
            vab = [p.buf() for _ in range(18)]
            b_va_init = p.buf()
            p.dma("sp", XR[:, 13824:17920], dr["rope"], writes=[b_rope])
            vmemset(AUX[:, 0:18 * 576], 1.0, [b_va_init] + vab)
            q_tis = [0, 1, 2, 3, 4] if need_ctx else [1, 2, 3, 4]

            PS_A = [0, 1, 2]
            PS_B = [3, 4]
            PS_C = [5, 6]
            pending = []

            def qk_item(ps_t, ps_b, n, ti, gain_ap, dst_ap, dst_buf):
                t0 = TCH[ti][0]
                lat = ti != 0
                state = {}

                def stage_b():
                    sq, sqb = tmppt()
                    act(sq[:, 0:n], ps_t[:, 0:n], AF.Square, [ps_b], [sqb])
                    ss_t, ss_b = psum(PS_B)
                    mm(ss_t[:, 0:n], ONESBLK, sq[:, 0:n], True, True, [sqb, b_const], [ss_b], inc=True)
                    sd, sdb = tmp32()
                    act(sd[:, 0:n], ss_t[:, 0:n], AF.Sqrt, [ss_b, b_misc], [sdb], bias=MISC[:, 0:1], scale=1.0 / 64)
                    rs, rsb = tmp32()
                    recip(rs[:, 0:n], sd[:, 0:n], [sdb], [rsb])
                    if not lat:
                        stt(dst_ap, ps_t[:, 0:n], gain_ap, rs[:, 0:n], ALU.mult, ALU.mult, [ps_b, rsb, b_vec], [dst_buf])
                    else:
                        qn, qnb = tmppt()
                        stt(qn[:, 0:n], ps_t[:, 0:n], gain_ap, rs[:, 0:n], ALU.mult, ALU.mult, [ps_b, rsb, b_vec], [qnb])
                        state["qn"] = (qn, qnb)

                def stage_c():
                    if not lat:
                        return
                    qn, qnb = state["qn"]
                    rot_t, rot_b = psum(PS_C)
                    mm(rot_t[:, 0:n], PERM, qn[:, 0:n], True, True, [qnb, b_const], [rot_b], inc=True)
                    t1, t1b = tmp32()
                    tt(t1[:, 0:n], qn[:, 0:n], ROC[:, t0 - NCTX:t0 - NCTX + n], ALU.mult, [qnb, b_rope], [t1b], eng="pool")
                    t2, t2b = tmp32()
                    tt(t2[:, 0:n], rot_t[:, 0:n], ROS[:, t0 - NCTX:t0 - NCTX + n], ALU.mult, [rot_b, b_rope], [t2b])
                    tt(dst_ap, t1[:, 0:n], t2[:, 0:n], ALU.add, [t1b, t2b], [dst_buf], eng="pool")
                return stage_b, stage_c

            def pipe_push(item):
                pending.append(item)
                if len(pending) >= 2:
                    pending[-2][0]()
                if len(pending) >= 3:
                    pending[-3][1]()

            def pipe_flush():
                if len(pending) >= 1:
                    pending[-1][0]()
                if len(pending) >= 2:
                    pending[-2][1]()
                if len(pending) >= 1:
                    pending[-1][1]()
                pending.clear()

            for b in range(2):
                wt, wb = w_next(8, 512)
                for ocl in range(4):
                    oc = b * 4 + ocl
                    for ti in q_tis:
                        t0, n = TCH[ti]
                        ps_t, ps_b = psum(PS_A)
                        for kc in range(8):
                            mm(ps_t[:, 0:n], wt[:, kc, ocl * 128:(ocl + 1) * 128], H3[:, kc, t0:t0 + n], kc == 0, kc == 7,
                               [wb, st_.hb[kc][ti]], [ps_b], inc=(kc == 7))
                        pipe_push(qk_item(ps_t, ps_b, n, ti, vcol(f"qg{j_att}"), QT[:, oc, t0:t0 + n], qb[oc][ti]))
            wt, wb = w_next(8, 512)
            for g in range(4):
                for ti in range(5):
                    t0, n = TCH[ti]
                    ps_t, ps_b = psum(PS_A)
                    for kc in range(8):
                        mm(ps_t[:, 0:n], wt[:, kc, g * 128:(g + 1) * 128], H3[:, kc, t0:t0 + n], kc == 0, kc == 7,
                           [wb, st_.hb[kc][ti]], [ps_b], inc=(kc == 7))
                    pipe_push(qk_item(ps_t, ps_b, n, ti, vcol(f"kg{j_att}"), KT2[:, g, t0:t0 + n], kb[g][ti]))
            pipe_flush()
            wt, wb = w_next(8, 256)
            for kt in range(18):
                ti = 0 if kt < 2 else 1 + (kt - 2) // 4
                ps_t, ps_b = psum(PS_A)
                for kc in range(8):
                    mm(ps_t[:, 0:256], H3[:, kc, kt * 128:(kt + 1) * 128], wt[:, kc, :], kc == 0, kc == 7,
                       [wb, st_.hb[kc][ti]], [ps_b], inc=(kc == 7))
                dst = VA[:, kt, 64:576].rearrange("p (g x) -> p g x", x=128)[:, :, 0:64]
                src = ps_t[:, 0:256].rearrange("p (g x) -> p g x", x=64)
                act(dst, src, AF.Identity, [ps_b], [vab[kt]])

            OBANK = [[0, 1], [2, 3]]
            ptdb = [p.buf() for _ in range(4)]
            it = 0
            for jp in range(8):
                g = jp // 2
                for ti in q_tis:
                    t0, n = TCH[ti]
                    kts = list(range(18)) if ti != 0 else [0, 1]
                    ob = OBANK[it % 2]
                    it += 1
                    o_t = [PS[ob[0]], PS[ob[1]]]
                    o_b = [psb[ob[0]], psb[ob[1]]]

                    def s_stage(kt):
                        tik = 0 if kt < 2 else 1 + (kt - 2) // 4
                        di = st_.rr.get("psd", 0) % 2
                        st_.rr["psd"] = st_.rr.get("psd", 0) + 1
                        dt_ = PSD[di]
                        dbs = [psb[4 + 2 * di], psb[5 + 2 * di]]
                        for h in range(2):
                            mm(dt_[:, h * 512:h * 512 + n], KT2[h * 64:(h + 1) * 64, g, kt * 128:(kt + 1) * 128],
                               QT[h * 64:(h + 1) * 64, jp, t0:t0 + n], True, True,
                               [kb[g][tik], qb[jp][ti]], [dbs[h]], inc=(h == 1))
                        return dt_, dbs

                    def pv_stage(kt, sres):
                        dt_, dbs = sres
                        pi = st_.rr.get("ptd", 0) % 4
                        st_.rr["ptd"] = st_.rr.get("ptd", 0) + 1
                        ptd = TB[:, pi * 1024:(pi + 1) * 1024]
                        src = dt_[:, :].rearrange("p (h x) -> p h x", h=2)[:, :, 0:n]
                        dst = ptd.rearrange("p (h x) -> p h x", h=2)[:, :, 0:n]
                        act(dst, src, AF.Exp, dbs, [ptdb[pi]], scale=0.125)
                        for h in range(2):
                            if h == 0:
                                lhs = VA[:, kt, 64 + 128 * g:192 + 128 * g]
                            else:
                                lhs = VA[:, kt, 128 * g:128 + 128 * g]
                            mm(o_t[h][:, 0:n], lhs, ptd[:, h * 512:h * 512 + n], kt == kts[0], kt == kts[-1],
                               [ptdb[pi], vab[kt]], [o_b[h]], inc=True)

                    prev = s_stage(kts[0])
                    for idx, kt in enumerate(kts):
                        nxt = s_stage(kts[idx + 1]) if idx + 1 < len(kts) else None
                        pv_stage(kt, prev)
                        prev = nxt
                    for h in range(2):
                        rc, rcb = tmp32()
                        recip(rc[:, 0:n], o_t[h][:, 0:n], [o_b[h]], [rcb])
                        lo, hi = (0, 64) if h == 0 else (64, 128)
                        dlo, dhi = (64, 128) if h == 0 else (0, 64)
                        tt(H3[lo:hi, jp, t0:t0 + n], o_t[h][lo:hi, 0:n], rc[dlo:dhi, 0:n], ALU.mult,
                           [o_b[h], rcb], [st_.hb[jp][ti]])
            dbg_dump("att_xr", XR[:, :], [128, 8 * NT], F32)
            dbg_dump("att_aux", AUX[:, :], [128, 10368], BF16)
            dbg_dump("att_hb", HBt[:, :], [128, 8 * NT], BF16)
            reload(li_index)
            linear(2, 4, 8, 512, lambda wt, ocl, kc: wt[:, kc, ocl * 128:(ocl + 1) * 128], hb_rhs, hb_bufs,
                   q_tis, resid_evac(2, q_tis))

        def conformer(li, need_ctx, li_index):
            UC = XRb[:, 0:8 * 286].rearrange("p (c t) -> p c t", c=8)
            UL = XRb[:, 2288:2288 + 8 * 2078].rearrange("p (c t) -> p c t", c=8)
            DG = [XRb[:, 18912 + i * 3968:18912 + (i + 1) * 3968].rearrange("p (k m) -> p k m", k=31) for i in range(2)]
            ub = [[p.buf() for _ in range(5)] for _ in range(8)]
            upad = p.buf()
            dgb = [p.buf(), p.buf()]
            vmemset(XRb[:, 0:18912], 0.0, [upad] + [b for row in ub for b in row])
            tis = [0, 1, 2, 3, 4]

            def useg(c, ti, k, n):
                if ti == 0:
                    return UC[:, c, k:k + n]
                o = TCH[ti][0] - NCTX
                return UL[:, c, o + k:o + k + n]

            for b in range(4):
                wt, wb = w_next(8, 512)
                for cl in range(2):
                    c = b * 2 + cl
                    for ti in tis:
                        t0, n = TCH[ti]
                        pa_t, pa_b = psum()
                        for kc in range(8):
                            mm(pa_t[:, 0:n], wt[:, kc, cl * 128:(cl + 1) * 128], H3[:, kc, t0:t0 + n], kc == 0, kc == 7,
                               [wb, st_.hb[kc][ti]], [pa_b], inc=(kc == 7))
                        pg_t, pg_b = psum()
                        for kc in range(8):
                            mm(pg_t[:, 0:n], wt[:, kc, 256 + cl * 128:256 + (cl + 1) * 128], H3[:, kc, t0:t0 + n], kc == 0, kc == 7,
                               [wb, st_.hb[kc][ti]], [pg_b], inc=(kc == 7))
                        sg, sgb = tmp32()
                        act(sg[:, 0:n], pg_t[:, 0:n], AF.Sigmoid, [pg_b, b_vec], [sgb], bias=vcol("cbin", 8 + c))
                        stt(useg(c, ti, 15, n), pa_t[:, 0:n], vcol("cbin", c), sg[:, 0:n], ALU.add, ALU.mult,
                            [pa_b, sgb, b_vec, upad], [ub[c][ti]])
            vb = [[p.buf() for _ in range(5)] for _ in range(8)]
            for c in range(8):
                par = c % 2
                for k in range(31):
                    ts(DG[par][:, k, :], IDENT, vcol("cwdw", k * 8 + c), None, ALU.mult, None, [b_const, b_vec], [dgb[par]])
                for ti in tis:
                    t0, n = TCH[ti]
                    ps_t, ps_b = psum()
                    nb = [ub[c][ti]]
                    if ti > 1:
                        nb.append(ub[c][ti - 1])
                    if 1 <= ti < 4:
                        nb.append(ub[c][ti + 1])
                    for k in range(31):
                        mm(ps_t[:, 0:n], DG[par][:, k, :], useg(c, ti, k, n), k == 0, k == 30,
                           [dgb[par], upad] + nb, [ps_b], inc=(k == 30))
                    act(H3[:, c, t0:t0 + n], ps_t[:, 0:n], AF.Identity, [ps_b, b_vec],
                        [vb[c][ti], st_.hb[c][ti]], bias=vcol("cbdw", c))
            TB3 = TB[:, :].rearrange("p (c t) -> p c t", c=8)
            yb = [[p.buf() for _ in range(5)] for _ in range(8)]
            for ti in tis:
                t0, n = TCH[ti]
                pm_t, pm_b = psum()
                for c in range(8):
                    mm(pm_t[:, 0:n], ONES, H3[:, c, t0:t0 + n], c == 0, c == 7, [vb[c][ti], b_const], [pm_b], inc=(c == 7))
                for c in range(8):
                    act(TB3[:, c, 0:n], H3[:, c, t0:t0 + n], AF.Square, [vb[c][ti]], [b_tb])
                pq_t, pq_b = psum()
                for c in range(8):
                    mm(pq_t[:, 0:n], ONES, TB3[:, c, 0:n], c == 0, c == 7, [b_tb, b_const], [pq_b], inc=(c == 7))
                mean, meanb = T32H[1], t32hb[1]
                act(mean[:, 0:n], pm_t[:, 0:n], AF.Identity, [pm_b], [meanb], scale=1.0 / D)
                m2, m2b = tmp32()
                tt(m2[:, 0:n], mean[:, 0:n], mean[:, 0:n], ALU.mult, [meanb], [m2b])
                var, varb = tmp32()
                stt(var[:, 0:n], pq_t[:, 0:n], 1.0 / D, m2[:, 0:n], ALU.mult, ALU.subtract, [pq_b, m2b], [varb])
                sd, sdb = tmp32()
                act(sd[:, 0:n], var[:, 0:n], AF.Sqrt, [varb, b_misc], [sdb], bias=MISC[:, 0:1], scale=1.0)
                rs, rsb = T32H[0], t32hb[0]
                recip(rs[:, 0:n], sd[:, 0:n], [sdb], [rsb])
                for c in range(8):
                    t_, tb_ = tmppt32()
                    tt(t_[:, 0:n], H3[:, c, t0:t0 + n], mean[:, 0:n], ALU.subtract, [vb[c][ti], meanb], [tb_])
                    tt(t_[:, 0:n], t_[:, 0:n], rs[:, 0:n], ALU.mult, [tb_, rsb], [tb_])
                    act(H3[:, c, t0:t0 + n], t_[:, 0:n], AF.Silu, [tb_, b_vec], [yb[c][ti], vb[c][ti]],
                        bias=vcol("cnb", c), scale=vcol("cng", c))
            st_.hb = yb
            reload(li_index)
            linear(2, 4, 8, 512, lambda wt, ocl, kc: wt[:, kc, ocl * 128:(ocl + 1) * 128], hb_rhs, hb_bufs,
                   tis, resid_evac(2, tis, bias_name="cbout"))

        st_.rr["tbx"] = 0

        def tmppt32():
            i = st_.rr["tbx"] % 2
            st_.rr["tbx"] += 1
            return T32X[i], t32xb[i]

        def rglru(li, need_ctx, li_index):
            G3 = XRb[:, 0:18432].rearrange("p (c t) -> p c t", c=8)
            XL3 = XRb[:, 18432:36864].rearrange("p (c t) -> p c t", c=8)
            gb = [[p.buf() for _ in range(5)] for _ in range(8)]
            xlb = [[p.buf() for _ in range(5)] for _ in range(8)]
            tis = [0, 1, 2, 3, 4]
            for b in range(4):
                wt, wb = w_next(8, 512)
                for ocl in range(4):
                    oc = b * 4 + ocl
                    for ti in tis:
                        t0, n = TCH[ti]
                        ps_t, ps_b = psum()
                        for kc in range(8):
                            mm(ps_t[:, 0:n], wt[:, kc, ocl * 128:(ocl + 1) * 128], H3[:, kc, t0:t0 + n], kc == 0, kc == 7,
                               [wb, st_.hb[kc][ti]], [ps_b], inc=(kc == 7))
                        if oc < 8:
                            act(G3[:, oc, t0:t0 + n], ps_t[:, 0:n], AF.Gelu_apprx_tanh, [ps_b], [gb[oc][ti]])
                        else:
                            vcopy(XL3[:, oc - 8, t0:t0 + n], ps_t[:, 0:n], [ps_b], [xlb[oc - 8][ti]])
            lam = VEC[:, VOFF["llam"]:VOFF["llam"] + 16]
            b_ca = p.buf()
            act(MISC[:, 64:80], lam, AF.Exp, [b_vec], [b_ca], scale=-1.0)
            act(MISC[:, 80:96], MISC[:, 64:80], AF.Ln, [b_ca, b_misc], [b_ca], bias=MISC[:, 1:2], scale=1.0)
            ts(MISC[:, 16:32], MISC[:, 80:96], -8.0, None, ALU.mult, None, [b_ca], [b_ca])
            ts(MISC[:, 48:64], MISC[:, 80:96], -4.0, None, ALU.mult, None, [b_ca], [b_ca])
            ts(MISC[:, 96:128], VEC[:, VOFF["lgb"]:VOFF["lgb"] + 32], 0.5, None, ALU.mult, None, [b_vec, b_ca], [b_ca])
            dbg_dump("lru_xr0", XR[:, :], [128, 8 * NT], F32)
            p.barrier(("pe", "act", "dve"))
            U32 = HBf[:, 0:4608].rearrange("p (c t) -> p c t", c=2)
            HS = HBf[:, 4608:9216].rearrange("p (c t) -> p c t", c=2)
            UBF = AUX[:, 0:4608].rearrange("p (c t) -> p c t", c=2)
            XT = [AUXf[:, 2304 + i * 512:2304 + (i + 1) * 512] for i in range(5)]
            xtb = [p.buf() for _ in range(5)]
            CAR = MISC[:, 128:256]
            car_i = [0]
            rrx = [0]

            def ltmp():
                i = rrx[0] % 15
                rrx[0] += 1
                if i < 5:
                    return XT[i], xtb[i]
                if i < 11:
                    return T32[i - 5], t32b[i - 5]
                if i < 13:
                    return T32X[i - 11], t32xb[i - 11]
                return T32H[i - 13], t32hb[i - 13]

            gslots = []
            gidx = []
            for d in range(2):
                gidx.append(ws.consumed)
                gslots.append(w_next(16, 256, pin=True))
            SEGS = [(0, NCTX), (NCTX, NLAT)]
            u32b = [p.buf(), p.buf()]
            ubfb = [p.buf(), p.buf()]
            hsb = [[p.buf() for _ in range(5)] for _ in range(2)]
            for nblk in range(4):
                c0 = nblk * 2
                for d in range(2):
                    gwt, gwb = gslots[d]
                    for cl in range(2):
                        c = c0 + cl
                        xall = xlb[c]
                        for (s0, sn) in SEGS:
                            ts(U32[:, cl, s0:s0 + sn], XL3[:, c, s0:s0 + sn], vcol("lcw", (d * 4 + 3) * 8 + c), vcol("lcb", d * 8 + c),
                               ALU.mult, ALU.add, xall + [b_vec], [u32b[cl]], eng="pool")
                            for k in range(3):
                                sh = 3 - k
                                if d == 0:
                                    o_ap = U32[:, cl, s0 + sh:s0 + sn]
                                    i_ap = XL3[:, c, s0:s0 + sn - sh]
                                else:
                                    o_ap = U32[:, cl, s0:s0 + sn - sh]
                                    i_ap = XL3[:, c, s0 + sh:s0 + sn]
                                stt(o_ap, i_ap, vcol("lcw", (d * 4 + k) * 8 + c), o_ap, ALU.mult, ALU.add,
                                    xall + [b_vec, u32b[cl]], [u32b[cl]])
                        act(UBF[:, cl, :], U32[:, cl, :], AF.Identity, [u32b[cl]], [ubfb[cl]])
                    for cl in range(2):
                        c = c0 + cl
                        groups = [[0, 1, 2], [3, 4]] if d == 0 else [[0, 4, 3], [2, 1]]
                        prev_car = None
                        oi = -1
                        c4 = MISC[:, 48 + d * 8 + c:49 + d * 8 + c]
                        for grp in groups:
                            items = []
                            for ti in grp:
                                t0, n = TCH[ti]
                                gps = []
                                for gi in range(2):
                                    ps_t, ps_b = psum()
                                    for kc in range(2):
                                        mm(ps_t[:, 0:n], gwt[:, (gi * 4 + nblk) * 2 + kc, cl * 128:(cl + 1) * 128], UBF[:, kc, t0:t0 + n],
                                           kc == 0, kc == 1, [gwb, ubfb[kc]], [ps_b], inc=(kc == 1))
                                    gps.append((ps_t, ps_b))
                                r_, rb_ = ltmp()
                                act(r_[:, 0:n], gps[0][0][:, 0:n], AF.Tanh, [gps[0][1], b_ca], [rb_],
                                    bias=MISC[:, 96 + (d * 2 + 0) * 8 + c:97 + (d * 2 + 0) * 8 + c], scale=0.5)
                                a_, ab_ = ltmp()
                                act(a_[:, 0:n], r_[:, 0:n], AF.Exp, [rb_, b_ca], [ab_], bias=c4, scale=c4)
                                i_, ib_ = ltmp()
                                act(i_[:, 0:n], gps[1][0][:, 0:n], AF.Tanh, [gps[1][1], b_ca], [ib_],
                                    bias=MISC[:, 96 + (d * 2 + 1) * 8 + c:97 + (d * 2 + 1) * 8 + c], scale=0.5)
                                stt(i_[:, 0:n], i_[:, 0:n], 1.0, U32[:, cl, t0:t0 + n], ALU.add, ALU.mult, [ib_, u32b[cl]], [ib_])
                                items.append((ti, a_, ab_, i_, ib_))
                            for (ti, a_, ab_, i_, ib_) in items:
                                oi += 1
                                t0, n = TCH[ti]
                                m_, mb_ = ltmp()
                                act(m_[:, 0:n], a_[:, 0:n], AF.Square, [ab_], [mb_])
                                act(m_[:, 0:n], m_[:, 0:n], AF.Sqrt, [mb_, b_misc], [mb_], bias=MISC[:, 1:2], scale=-1.0)
                                if ti == 0:
                                    fc = 0 if d == 0 else NCTX - 1
                                    vmemset(m_[:, fc:fc + 1], 1.0, [mb_])
                                stt(i_[:, 0:n], i_[:, 0:n], 0.5, m_[:, 0:n], ALU.mult, ALU.mult, [ib_, mb_], [ib_])
                                if d == 0:
                                    init = 0.0 if oi == 0 else HS[:, cl, t0 - 1:t0]
                                    rd = [ab_, ib_] + ([hsb[cl][ti - 1]] if oi > 0 else [])
                                    p.op("dve", lambda e, o=HS[:, cl, t0:t0 + n], a=a_[:, 0:n], b=i_[:, 0:n], init=init:
                                         e.tensor_tensor_scan(out=o, data0=a, data1=b, initial=init, op0=ALU.mult, op1=ALU.add),
                                         rd, [hsb[cl][ti]])
                                else:
                                    h_, hb_ = ltmp()
                                    init = 0.0 if oi == 0 else prev_car
                                    p.op("dve", lambda e, o=h_[:, 0:n][:, ::-1], a=a_[:, 0:n][:, ::-1], b=i_[:, 0:n][:, ::-1], init=init:
                                         e.tensor_tensor_scan(out=o, data0=a, data1=b, initial=init, op0=ALU.mult, op1=ALU.add),
                                         [ab_, ib_, b_car], [hb_])
                                    ci = car_i[0] % 128
                                    car_i[0] += 1
                                    vcopy(CAR[:, ci:ci + 1], h_[:, 0:1], [hb_], [b_car])
                                    prev_car = CAR[:, ci:ci + 1]
                                    tt(h_[:, 0:n], h_[:, 0:n], HS[:, cl, t0:t0 + n], ALU.add, [hb_, hsb[cl][ti]], [hb_], eng="pool")
                                    tt(G3[:, c, t0:t0 + n], h_[:, 0:n], G3[:, c, t0:t0 + n], ALU.mult, [hb_, gb[c][ti]], [gb[c][ti]], eng="pool")
            for gi_ in gidx:
                w_release(gi_)
            dbg_dump("lru_xr1", XR[:, :], [128, 8 * NT], F32)
            p.barrier(("pe", "act", "dve"))
            yb = [[p.buf() for _ in range(5)] for _ in range(8)]
            for c in range(8):
                for ti in tis:
                    t0, n = TCH[ti]
                    if (c + ti) % 2 == 0:
                        vcopy(H3[:, c, t0:t0 + n], G3[:, c, t0:t0 + n], [gb[c][ti]], [yb[c][ti]])
                    else:
                        act(H3[:, c, t0:t0 + n], G3[:, c, t0:t0 + n], AF.Identity, [gb[c][ti]], [yb[c][ti]])
            st_.hb = yb
            reload(li_index)
            linear(2, 4, 8, 512, lambda wt, ocl, kc: wt[:, kc, ocl * 128:(ocl + 1) * 128], hb_rhs, hb_bufs,
                   tis, resid_evac(2, tis))

        vmemset(MISC[:, 0:1], EPS, [b_misc])
        vmemset(MISC[:, 1:2], 1.0, [b_misc])
        for idx, li in enumerate(layers):
            kind = li % 3
            need_ctx = li < DEPTH - 1
            st_.par = idx % 2
            if idx == 0:
                for _ in range(12):
                    mod_block()
                mod_finish(li, 0)
            st_.hb = grid(f"h{li}_")
            norm_phase(0, [0, 1, 2, 3, 4])
            if DBG and idx == 0:
                p.dma("sp", dbg_mods, MODS()[:, :], reads=[b_mods()], writes=[o_buf])
                p.dma("sp", dbg_ab, AB()[:, :], reads=[b_ab()], writes=[o_buf])
                p.dma("sp", dbg_h1, HBt[:, :], reads=[b for row in st_.hb for b in row], writes=[o_buf])
            first_in_prog = (idx == 0)
            spill(0 if first_in_prog else 1)
            lidx = 0 if first_in_prog else 1
            import os as _os
            if _os.environ.get("DBG_SKIP_MIX"):
                for _ in range({0: 6, 1: 6, 2: 8}[kind]):
                    w_next(8, 512)
                reload(lidx)
            elif kind == 0:
                attention(li, li // 3, need_ctx, lidx)
            elif kind == 1:
                conformer(li, need_ctx, lidx)
            else:
                rglru(li, need_ctx, lidx)
            tis = [0, 1, 2, 3, 4] if need_ctx else [1, 2, 3, 4]
            p.barrier(("pe", "act", "dve"))
            st_.hb = grid(f"h2{li}_")
            if _os.environ.get("DBG_SKIP_MLP"):
                for _ in range(16 + (12 if idx + 1 < len(layers) else 0)):
                    w_next(8, 512)
            else:
                if DBG and idx == 0:
                    p.dma("sp", dbg_x1, XR[:, :], reads=[b for row in st_.xb for b in row], writes=[o_buf])
                norm_phase(1, tis)
                if DBG and idx == 0:
                    p.dma("sp", dbg_h2, HBt[:, :], reads=[b for row in st_.hb for b in row], writes=[o_buf])
                if idx + 1 < len(layers):
                    mlp_phase(li, tis, layers[idx + 1], (idx + 1) % 2)
                else:
                    mlp_phase(li, tis)
        allb = [b for row in st_.xb for b in row]
        if last:
            for c in range(8):
                p.dma("sp", outT[c * 128:(c + 1) * 128, :], X3[:, c, NCTX:NT], reads=allb, writes=[o_buf])
        else:
            p.dma("sp", xs_out, XR[:, :], reads=allb, writes=[o_buf])
        p._wait("sp", o_buf.w)
        assert ws.consumed == len(wspecs), (ws.consumed, len(wspecs))
        p.emit()
    return nc


_WKEYS = ["mod_w", "mlp_w1", "mlp_w2", "attn_w_qkv", "attn_w_o", "conv_w_in", "conv_w_out",
          "lru_w_in", "lru_gate_w", "lru_w_out"]


def _common_maps(inputs):
    m = {k: np.ascontiguousarray(np.asarray(inputs[k], np.float32)) for k in _WKEYS}
    m["consts"] = _consts()
    m["rope"] = _rope_tables()
    return m


def kernel(**inputs):
    n = 8
    common = _common_maps(inputs)
    x = np.asarray(inputs["x"], np.float32)
    ctx = np.asarray(inputs["ctx"], np.float32)
    in_maps = []
    for b in range(n):
        m = dict(common)
        m["xT"] = np.ascontiguousarray(x[b].T)
        m["ctxT"] = np.ascontiguousarray(ctx[b].T)
        m["vecs"] = _pack_vecs(inputs, b)
        in_maps.append(m)
    nc = build_program((0, 1, 2, 3), True, True)
    res = run_bass_kernel_spmd(nc, in_maps, core_ids=list(range(n)))
    out = np.stack([np.ascontiguousarray(res.results[b]["outT"].T) for b in range(n)], axis=0)
    return out.astype(np.float32)
```

```python
import numpy as np
from contextlib import ExitStack
import concourse.bass as bass
import concourse.mybir as mybir
from concourse.bass_utils import run_bass_kernel_spmd
import ml_dtypes

F32 = mybir.dt.float32
BF16 = mybir.dt.bfloat16
AF = mybir.ActivationFunctionType
ALU = mybir.AluOpType

SELF_WAIT = True

NT, NCTX, NLAT, D = 2304, 256, 2048, 1024
TCH = [(0, 256), (256, 512), (768, 512), (1280, 512), (1792, 512)]
EPS = 1e-6
DEPTH = 4


class Sem:
    def __init__(self, h, name):
        self.h = h
        self.name = name
        self.count = 0


class Buf:
    __slots__ = ("name", "w", "r", "dsem")

    def __init__(self, name, dsem=None):
        self.name = name
        self.w = None
        self.r = []
        self.dsem = dsem


class Prog:
    ENG = ("pe", "act", "dve", "pool", "sp")

    def __init__(self, nc, stack):
        self.nc = nc
        self.stack = stack
        self.ops = {e: [] for e in self.ENG}
        self.esem = {}
        for e in ("pe", "act", "dve", "pool"):
            self.esem[e] = self.new_sem("s_" + e)
        self.known = {e: {} for e in self.ENG}
        self.pending_noinc = {e: False for e in self.ENG}
        self.nbuf = 0

    def new_sem(self, name):
        h = self.stack.enter_context(self.nc.semaphore(name))
        return Sem(h, name)

    def buf(self, name=None, dma=False):
        self.nbuf += 1
        name = name or f"b{self.nbuf}"
        return Buf(name, self.new_sem("d_" + name) if dma else None)

    def sb(self, name, shape, dt):
        return self.stack.enter_context(self.nc.sbuf_tensor(name, list(shape), dt))

    def ps(self, name, shape, dt=F32):
        return self.stack.enter_context(self.nc.psum_tensor(name, list(shape), dt))

    def _wait(self, eng, tok):
        if tok is None:
            return
        sem, val = tok
        if eng == "pe" and sem is self.esem["pe"]:
            return
        if (not SELF_WAIT) and eng in self.esem and sem is self.esem[eng]:
            return
        k = self.known[eng]
        if k.get(sem, 0) >= val:
            return
        k[sem] = val
        h = sem.h
        self.ops[eng].append(lambda e, h=h, val=val: e.wait_ge(h, val))

    def _deps(self, eng, reads, writes):
        for b in reads:
            self._wait(eng, b.w)
        for b in writes:
            self._wait(eng, b.w)
            for t in b.r:
                self._wait(eng, t)

    def _commit(self, tok, reads, writes):
        for b in reads:
            b.r.append(tok)
            if len(b.r) > 16:
                d = {}
                for s, v in b.r:
                    if d.get(s, 0) < v:
                        d[s] = v
                b.r = list(d.items())
        for b in writes:
            b.w = tok
            b.r = []

    def op(self, eng, fn, reads=(), writes=(), inc=True):
        self._deps(eng, reads, writes)
        sem = self.esem[eng]
        if inc:
            sem.count += 1
            val = sem.count
            h = sem.h
            self.ops[eng].append(lambda e, fn=fn, h=h: fn(e).then_inc(h, 1))
            self.pending_noinc[eng] = False
        else:
            val = sem.count + 1
            self.ops[eng].append(lambda e, fn=fn: fn(e))
            self.pending_noinc[eng] = True
        tok = (sem, val)
        self._commit(tok, reads, writes)
        return tok

    def dma(self, q, out, in_, reads=(), writes=(), dsem=None):
        self._deps(q, reads, writes)
        sem = dsem if dsem is not None else writes[0].dsem
        sem.count += 16
        val = sem.count
        h = sem.h
        self.ops[q].append(lambda e, out=out, in_=in_, h=h: e.dma_start(out=out, in_=in_).then_inc(h, 16))
        tok = (sem, val)
        self._commit(tok, reads, writes)
        return tok

    def barrier(self, engs=("pe", "act", "dve", "sp")):
        for e in ("pe", "act", "dve", "pool"):
            assert not self.pending_noinc[e]
        toks = [(self.esem[e], self.esem[e].count) for e in ("pe", "act", "dve", "pool")]
        if "pool" not in engs:
            engs = tuple(engs) + ("pool",)
        for e in engs:
            for t in toks:
                if t[1] > 0:
                    self._wait(e, t)

    def emit(self):
        for e in ("pe", "act", "dve"):
            assert not self.pending_noinc[e], f"engine {e} has trailing non-inc op"
        ops = self.ops
        with self.nc.Block() as block:
            @block.sync
            def _(eng):
                for f in ops["sp"]:
                    f(eng)

            @block.tensor
            def _(eng):
                for f in ops["pe"]:
                    f(eng)

            @block.scalar
            def _(eng):
                for f in ops["act"]:
                    f(eng)

            @block.vector
            def _(eng):
                for f in ops["dve"]:
                    f(eng)

            @block.gpsimd
            def _(eng):
                for f in ops["pool"]:
                    f(eng)


def _vec_layout():
    L = [("c", 8), ("cctx", 8)]
    for i in range(DEPTH):
        L += [(f"modb{i}", 48), (f"gmix{i}", 8), (f"gmlp{i}", 8)]
    for j in range(2):
        L += [(f"qg{j}", 1), (f"kg{j}", 1)]
    L += [("cbin", 16), ("cwdw", 248), ("cbdw", 8), ("cng", 8), ("cnb", 8), ("cbout", 8)]
    L += [("lcw", 64), ("lcb", 16), ("lgb", 32), ("llam", 16)]
    off = {}
    o = 0
    for n, k in L:
        off[n] = o
        o += k
    return off, o


VOFF, NV = _vec_layout()


def _pk(v):
    v = np.asarray(v, np.float32).reshape(-1, 128)
    return np.ascontiguousarray(v.T)


def _pack_vecs(inp, b):
    V = np.zeros((128, NV), np.float32)

    def put(name, arr):
        arr = np.asarray(arr, np.float32)
        V[:, VOFF[name]:VOFF[name] + arr.shape[1]] = arr

    put("c", _pk(inp["c"][b]))
    put("cctx", _pk(inp["c_ctx"]))
    for i in range(DEPTH):
        put(f"modb{i}", _pk(inp["mod_b"][i]))
        put(f"gmix{i}", _pk(inp["norm_mix_g"][i]))
        put(f"gmlp{i}", _pk(inp["norm_mlp_g"][i]))
    for j in range(2):
        put(f"qg{j}", np.tile(np.asarray(inp["attn_q_gain"][j], np.float32), 2)[:, None])
        put(f"kg{j}", np.tile(np.asarray(inp["attn_k_gain"][j], np.float32), 2)[:, None])
    put("cbin", _pk(inp["conv_b_in"][0]))
    wdw = np.asarray(inp["conv_w_dw"][0], np.float32).reshape(31, 8, 128).transpose(2, 0, 1).reshape(128, 248)
    put("cwdw", wdw)
    put("cbdw", _pk(inp["conv_b_dw"][0]))
    put("cng", _pk(inp["conv_norm_g"][0]))
    put("cnb", _pk(inp["conv_norm_b"][0]))
    put("cbout", _pk(inp["conv_b_out"][0]))
    lcw = np.asarray(inp["lru_conv_w"][0], np.float32).reshape(2, 4, 8, 128).transpose(3, 0, 1, 2).reshape(128, 64)
    put("lcw", lcw)
    put("lcb", np.asarray(inp["lru_conv_b"][0], np.float32).reshape(2, 8, 128).transpose(2, 0, 1).reshape(128, 16))
    put("lgb", np.asarray(inp["lru_gate_b"][0], np.float32).reshape(2, 2, 8, 128).transpose(3, 0, 1, 2).reshape(128, 32))
    put("llam", np.asarray(inp["lru_lambda"][0], np.float32).reshape(2, 8, 128).transpose(2, 0, 1).reshape(128, 16))
    return V


def _consts():
    p = np.arange(128)
    perm = (p[:, None] == (p[None, :] ^ 16)).astype(np.float32)
    onesblk = ((p[:, None] // 64) == (p[None, :] // 64)).astype(np.float32)
    ones = np.ones((128, 128), np.float32)
    ident = np.eye(128, dtype=np.float32)
    return np.concatenate([perm, onesblk, ones, ident], axis=1).astype(ml_dtypes.bfloat16)


def _rope_tables():
    t = np.arange(NLAT)
    row = (t // 64).astype(np.float64)
    col = (t % 64).astype(np.float64)
    inv = 10000.0 ** (-np.arange(16, dtype=np.float64) / 16.0)
    p = np.arange(128)
    d = p % 64
    a = d // 32
    half = (d // 16) % 2
    f = d % 16
    pos = np.where(a[:, None] == 0, row[None, :], col[None, :])
    ang = (pos.astype(np.float32) * inv.astype(np.float32)[f][:, None]).astype(np.float32)
    C = np.cos(ang).astype(np.float32)
    S = np.sin(ang).astype(np.float32) * np.where(half == 0, -1.0, 1.0)[:, None].astype(np.float32)
    return np.ascontiguousarray(np.concatenate([C, S], axis=1).astype(np.float32))


class K:
    pass


def build_program(layers=(0, 1, 2, 3), first=True, last=True):
    nc = bass.Bass("TRN2", target_bir_lowering=False)
    dr = {}

    def din(name, shape, dt=F32):
        dr[name] = nc.dram_tensor(name, list(shape), dt, kind="ExternalInput").ap()
        return dr[name]

    if first:
        din("xT", [D, NLAT])
        din("ctxT", [D, NCTX])
    else:
        din("xs_in", [128, 8 * NT])
    din("vecs", [128, NV])
    din("consts", [128, 512], BF16)
    din("rope", [128, 2 * NLAT])
    din("mod_w", [4, D, 6 * D])
    din("mlp_w1", [4, D, 4 * D])
    din("mlp_w2", [4, 4 * D, D])
    din("attn_w_qkv", [2, D, 1536])
    din("attn_w_o", [2, D, D])
    din("conv_w_in", [1, D, 2 * D])
    din("conv_w_out", [1, D, D])
    din("lru_w_in", [1, D, 2 * D])
    din("lru_gate_w", [1, 2, 2, 4, 256, 256])
    din("lru_w_out", [1, D, D])
    if last:
        outT = nc.dram_tensor("outT", [D, NLAT], F32, kind="ExternalOutput").ap()
    else:
        xs_out = nc.dram_tensor("xs_out", [128, 8 * NT], F32, kind="ExternalOutput").ap()
    xs = nc.dram_tensor("xs_scr", [128, 8 * NT], F32, kind="Internal").ap()
    import os as _os
    DBG = bool(_os.environ.get("DBG_DUMP"))
    if DBG:
        dbg_mods = nc.dram_tensor("dbg_mods", [128, 96], F32, kind="ExternalOutput").ap()
        dbg_ab = nc.dram_tensor("dbg_ab", [128, 64], F32, kind="ExternalOutput").ap()
        dbg_h1 = nc.dram_tensor("dbg_h1", [128, 8 * NT], BF16, kind="ExternalOutput").ap()
        dbg_h2 = nc.dram_tensor("dbg_h2", [128, 8 * NT], BF16, kind="ExternalOutput").ap()
        dbg_x1 = nc.dram_tensor("dbg_x1", [128, 8 * NT], F32, kind="ExternalOutput").ap()

    with ExitStack() as st:
        p = Prog(nc, st)
        XR = p.sb("XR", [128, 8 * NT], F32)
        HBt = p.sb("HB", [128, 8 * NT], BF16)
        AUX = p.sb("AUX", [128, 10368], BF16)
        SLOT = [p.sb(f"slot{i}", [128, 4096], BF16) for i in range(4)]
        VEC = p.sb("VEC", [128, NV], F32)
        CONST = p.sb("CONST", [128, 512], BF16)
        MODS_ = [p.sb(f"MODS{i}", [128, 96], F32) for i in range(2)]
        AB_ = [p.sb(f"AB{i}", [128, 64], F32) for i in range(2)]
        SC = p.sb("SC", [128, 16], BF16)
        MISC = p.sb("MISC", [128, 256], F32)
        T32 = [p.sb(f"t32_{i}", [128, 512], F32) for i in range(6)]
        TB = p.sb("TB", [128, 8 * 512], BF16)
        T32X = [p.sb(f"t32x_{i}", [128, 512], F32) for i in range(2)]
        t32xb = [p.buf(f"t32x_{i}") for i in range(2)]
        T32H = [p.sb(f"t32h_{i}", [128, 512], F32) for i in range(2)]
        t32hb = [p.buf(f"t32h_{i}") for i in range(2)]
        b_tb = p.buf("tb")
        b_car = p.buf("car")
        PT = [p.sb(f"pt{i}", [128, 512], BF16) for i in range(6)]
        PS = [p.ps(f"ps{i}", [128, 512], F32) for i in range(4)]
        PSD = [p.ps(f"psd{i}", [128, 1024], F32) for i in range(2)]
        PS = PS + [PSD[0][:, 0:512], PSD[0][:, 512:1024], PSD[1][:, 0:512], PSD[1][:, 512:1024]]
        psb = [p.buf(f"ps{i}") for i in range(8)]
        t32b = [p.buf(f"t32_{i}") for i in range(6)]
        ptb = [p.buf(f"pt{i}") for i in range(6)]
        slotb = [p.buf(f"slot{i}", dma=True) for i in range(4)]
        b_vec = p.buf("vec", dma=True)
        b_const = p.buf("const", dma=True)
        b_mods_ = [p.buf("mods0"), p.buf("mods1")]
        b_ab_ = [p.buf("ab0"), p.buf("ab1")]
        b_sc = p.buf("sc")
        b_misc = p.buf("misc")
        x_dsem = p.new_sem("d_x")
        xc_dsem = [p.new_sem(f"d_xc{c}") for c in range(8)]
        o_buf = p.buf("out", dma=True)
        spill_buf = p.buf("spill", dma=True)

        dbg_outs = {}

        def dbg_dump(name, ap, shape, dt):
            if not DBG:
                return
            t = nc.dram_tensor("dd_" + name, list(shape), dt, kind="ExternalOutput").ap()
            p.barrier(("sp",))
            p.dma("sp", t, ap, writes=[o_buf])
            for e in ("pe", "act", "dve"):
                p._wait(e, o_buf.w)

        PERM = CONST[:, 0:128]
        ONESBLK = CONST[:, 128:256]
        ONES = CONST[:, 256:384]
        IDENT = CONST[:, 384:512]

        X3 = XR[:, :].rearrange("p (c t) -> p c t", c=8)
        XRb = XR[:, :].bitcast(BF16)
        H3 = HBt[:, :].rearrange("p (c t) -> p c t", c=8)
        HBf = HBt[:, :].bitcast(F32)
        AUXf = AUX[:, :].bitcast(F32)

        def grid(name):
            return [[p.buf(f"{name}{c}_{t}") for t in range(5)] for c in range(8)]

        st_ = K()
        st_.xb = grid("x")
        st_.hb = grid("h")
        st_.rr = {"t32": 0, "pt": 0, "ps": 0}

        def vcol(name, j=0):
            o = VOFF[name] + j
            return VEC[:, o:o + 1]

        def tmp32():
            i = st_.rr["t32"] % 6
            st_.rr["t32"] += 1
            return T32[i], t32b[i]

        def tmppt():
            i = st_.rr["pt"] % 6
            st_.rr["pt"] += 1
            return PT[i], ptb[i]

        def psum(group=None):
            group = group if group is not None else list(range(7))
            key = ("ps",) + tuple(group)
            k_ = st_.rr.get(key, 0)
            st_.rr[key] = k_ + 1
            i = group[k_ % len(group)]
            return PS[i], psb[i]

        def mm(out, lhsT, rhs, start, stop, reads, writes, inc):
            p.op("pe", lambda e: e.matmul(out, lhsT, rhs, start=start, stop=stop), reads, writes, inc=inc)

        def act(out, in_, func, reads, writes, bias=None, scale=None):
            kw = {}
            if bias is not None:
                kw["bias"] = bias
            if scale is not None:
                kw["scale"] = scale
            p.op("act", lambda e: e.activation(out=out, in_=in_, func=func, **kw), reads, writes)

        def tt(out, in0, in1, op, reads, writes, eng="dve"):
            p.op(eng, lambda e: e.tensor_tensor(out=out, in0=in0, in1=in1, op=op), reads, writes)

        def ts(out, in0, s1, s2, op0, op1, reads, writes, eng="dve"):
            if s2 is None:
                p.op(eng, lambda e: e.tensor_scalar(out=out, in0=in0, scalar1=s1, scalar2=None, op0=op0), reads, writes)
            else:
                p.op(eng, lambda e: e.tensor_scalar(out=out, in0=in0, scalar1=s1, scalar2=s2, op0=op0, op1=op1), reads, writes)

        def stt(out, in0, scalar, in1, op0, op1, reads, writes, eng="dve"):
            p.op(eng, lambda e: e.scalar_tensor_tensor(out=out, in0=in0, scalar=scalar, in1=in1, op0=op0, op1=op1), reads, writes)

        def recip(out, in_, reads, writes):
            p.op("dve", lambda e: e.reciprocal(out=out, in_=in_), reads, writes)

        def vcopy(out, in_, reads, writes):
            p.op("dve", lambda e: e.tensor_copy(out=out, in_=in_), reads, writes)

        def vmemset(ap, val, writes):
            p.op("dve", lambda e: e.memset(ap, val), (), writes)

        wspecs = []

        def wv(ap2d):
            return ap2d.rearrange("(k p) n -> p k n", p=128)

        NMOD = [2, 1, 2, 1, 2, 1, 2, 1]
        for lidx_, li in enumerate(layers):
            kind = li % 3
            j = li // 3
            if lidx_ == 0:
                for b in range(12):
                    wspecs.append([(0, 8, 512, wv(dr["mod_w"][li, :, b * 512:(b + 1) * 512]))])
            if kind == 0:
                wq = dr["attn_w_qkv"][j]
                for b in range(2):
                    wspecs.append([(0, 8, 512, wv(wq[:, b * 512:(b + 1) * 512]))])
                sp_ = []
                for g in range(4):
                    for dup in range(2):
                        sp_.append(((g * 2 + dup) * 64, 8, 64, wv(wq[:, 1024 + g * 64:1024 + (g + 1) * 64]), 512))
                wspecs.append(sp_)
                wspecs.append([(0, 8, 256, wv(wq[:, 1280:1536]))])
                for b in range(2):
                    wspecs.append([(0, 8, 512, wv(dr["attn_w_o"][j][:, b * 512:(b + 1) * 512]))])
            elif kind == 1:
                wi = dr["conv_w_in"][0]
                for b in range(4):
                    wspecs.append([(0, 8, 256, wv(wi[:, b * 256:(b + 1) * 256]), 512),
                                   (256, 8, 256, wv(wi[:, 1024 + b * 256:1024 + (b + 1) * 256]), 512)])
                for b in range(2):
                    wspecs.append([(0, 8, 512, wv(dr["conv_w_out"][0][:, b * 512:(b + 1) * 512]))])
            else:
                wi = dr["lru_w_in"][0]
                for b in range(4):
                    wspecs.append([(0, 8, 512, wv(wi[:, b * 512:(b + 1) * 512]))])
                for d in range(2):
                    gw = dr["lru_gate_w"][0, d].rearrange("g n k e -> (g n k) e")
                    wspecs.append([(0, 16, 256, wv(gw))])
                for b in range(2):
                    wspecs.append([(0, 8, 512, wv(dr["lru_w_out"][0][:, b * 512:(b + 1) * 512]))])
            mb_ = 0
            for hb in range(8):
                wspecs.append([(0, 8, 512, wv(dr["mlp_w1"][li, :, hb * 512:(hb + 1) * 512]))])
                wspecs.append([(0, 4, 1024, wv(dr["mlp_w2"][li, hb * 512:(hb + 1) * 512, :]))])
                if lidx_ + 1 < len(layers):
                    nl_ = layers[lidx_ + 1]
                    for _ in range(NMOD[hb]):
                        wspecs.append([(0, 8, 512, wv(dr["mod_w"][nl_, :, mb_ * 512:(mb_ + 1) * 512]))])
                        mb_ += 1

        ws = K()
        ws.issued = 0
        ws.consumed = 0

        def w_issue(jb):
            s = jb % 4
            for spec in wspecs[jb]:
                if len(spec) == 5:
                    off, kcn, ncol, src, rowlen = spec
                    dst = SLOT[s][:, 0:kcn * rowlen].rearrange("p (k n) -> p k n", k=kcn)[:, :, off:off + ncol]
                else:
                    off, kcn, ncol, src = spec
                    dst = SLOT[s][:, off:off + kcn * ncol].rearrange("p (k n) -> p k n", k=kcn)
                p.dma("pool", dst, src, writes=[slotb[s]])

        ws.released = set()
        ws.pinned = set()

        def w_release(i):
            ws.pinned.discard(i)
            ws.released.add(i)

        def w_next(kcn, ncol, pin=False):
            i = ws.consumed
            if i - 1 >= 0 and (i - 1) not in ws.pinned:
                ws.released.add(i - 1)
            while ws.issued < min(i + 4, len(wspecs)) and (ws.issued < 4 or (ws.issued - 4) in ws.released):
                w_issue(ws.issued)
                ws.issued += 1
            assert ws.issued > i, "weight block not issued (pinned slot deadlock)"
            if pin:
                ws.pinned.add(i)
            ws.consumed += 1
            s = i % 4
            return SLOT[s][:, 0:kcn * ncol].rearrange("p (k n) -> p k n", k=kcn), slotb[s]

        p.dma("sp", VEC[:, :], dr["vecs"], writes=[b_vec])
        p.dma("sp", CONST[:, :], dr["consts"], writes=[b_const])

        def load_x_from_input():
            if first:
                p.dma("sp", X3[:, :, 0:NCTX], dr["ctxT"].rearrange("(c p) t -> p c t", p=128),
                      writes=[st_.xb[c][0] for c in range(8)], dsem=x_dsem)
                for c in range(8):
                    p.dma("sp", X3[:, c, NCTX:NT], dr["xT"][c * 128:(c + 1) * 128, :], writes=st_.xb[c][1:5], dsem=xc_dsem[c])
            else:
                for c in range(8):
                    p.dma("sp", X3[:, c, :], dr["xs_in"][:, c * NT:(c + 1) * NT], writes=st_.xb[c], dsem=xc_dsem[c])

        load_x_from_input()
        SC3 = SC[:, :].rearrange("p (k s) -> p k s", s=2)
        act(SC3[:, :, 0], VEC[:, VOFF["cctx"]:VOFF["cctx"] + 8], AF.Silu, [b_vec], [b_sc])
        act(SC3[:, :, 1], VEC[:, VOFF["c"]:VOFF["c"] + 8], AF.Silu, [b_vec], [b_sc])

        st_.par = 0

        def MODS():
            return MODS_[st_.par]

        def AB():
            return AB_[st_.par]

        def b_mods():
            return b_mods_[st_.par]

        def b_ab():
            return b_ab_[st_.par]

        def modcol(grp, c, s):
            return MODS()[:, (grp * 8 + c) * 2 + s:(grp * 8 + c) * 2 + s + 1]

        modst = K()
        modst.nb = 0

        def mod_block():
            b = modst.nb
            modst.nb += 1
            ps_t, ps_b = PS[7], psb[7]
            wt, wb = w_next(8, 512)
            for jj in range(4):
                jx = b * 4 + jj
                for kc in range(8):
                    mm(ps_t[:, 2 * jx:2 * jx + 2], wt[:, kc, jj * 128:(jj + 1) * 128], SC3[:, kc, :],
                       kc == 0, kc == 7, [wb, b_sc], [ps_b], inc=(jj == 3 and kc == 7))

        def mod_finish(li, par):
            assert modst.nb == 12
            modst.nb = 0
            ps_t, ps_b = PS[7], psb[7]
            M = MODS_[par]
            A = AB_[par]
            M3 = M[:, :].rearrange("p (j s) -> p j s", s=2)
            ps3 = ps_t[:, 0:96].rearrange("p (j s) -> p j s", s=2)
            mb = VEC[:, VOFF[f"modb{li}"]:VOFF[f"modb{li}"] + 48]
            for s in range(2):
                tt(M3[:, :, s], ps3[:, :, s], mb, ALU.add, [ps_b, b_vec], [b_mods_[par]])
            gm = VEC[:, VOFF[f"gmix{li}"]:VOFF[f"gmix{li}"] + 8]
            gl = VEC[:, VOFF[f"gmlp{li}"]:VOFF[f"gmlp{li}"] + 8]
            for s in range(2):
                stt(A[:, s * 8:s * 8 + 8], M3[:, 8:16, s], 1.0, gm, ALU.add, ALU.mult, [b_mods_[par], b_vec], [b_ab_[par]])
                stt(A[:, 16 + s * 8:16 + s * 8 + 8], M3[:, 32:40, s], 1.0, gl, ALU.add, ALU.mult, [b_mods_[par], b_vec], [b_ab_[par]])

        def norm_phase(which, tis):
            TB3 = TB[:, :].rearrange("p (c t) -> p c t", c=8)
            for ti in tis:
                t0, n = TCH[ti]
                s = 0 if ti == 0 else 1
                for c in range(8):
                    act(TB3[:, c, 0:n], X3[:, c, t0:t0 + n], AF.Square, [st_.xb[c][ti]], [b_tb])
                ps_t, ps_b = psum()
                for c in range(8):
                    mm(ps_t[:, 0:n], ONES, TB3[:, c, 0:n], c == 0, c == 7, [b_tb, b_const], [ps_b], inc=(c == 7))
                sd, sdb = tmp32()
                act(sd[:, 0:n], ps_t[:, 0:n], AF.Sqrt, [ps_b, b_misc], [sdb], bias=MISC[:, 0:1], scale=1.0 / D)
                rs, rsb = T32H[0], t32hb[0]
                recip(rs[:, 0:n], sd[:, 0:n], [sdb], [rsb])
                for c in range(8):
                    t_, tb_ = tmp32()
                    a_ap = AB()[:, which * 16 + s * 8 + c:which * 16 + s * 8 + c + 1]
                    stt(t_[:, 0:n], X3[:, c, t0:t0 + n], a_ap, rs[:, 0:n], ALU.mult, ALU.mult,
                        [st_.xb[c][ti], b_ab(), rsb], [tb_])
                    act(H3[:, c, t0:t0 + n], t_[:, 0:n], AF.Identity, [tb_, b_mods()], [st_.hb[c][ti]],
                        bias=modcol(0 if which == 0 else 3, c, s))

        def spill_issue(li_index):
            if li_index == 0:
                return None
            tok = None
            for c in range(8):
                tok = p.dma("sp", xs[:, c * NT:(c + 1) * NT], X3[:, c, :], reads=st_.xb[c], writes=[spill_buf])
            return tok

        def spill(tok):
            p.barrier(("pe", "act", "dve", "sp"))
            if tok is not None:
                for e in ("pe", "act", "dve", "sp", "pool"):
                    p._wait(e, tok)
            return tok

        def reload(li_index):
            p.barrier(("pe", "act", "dve", "sp"))
            st_.xb = grid(f"x{li_index}_")
            allb = [b for row in st_.xb for b in row]
            if li_index == 0:
                load_x_from_input()
            else:
                for c in range(8):
                    p.dma("sp", X3[:, c, :], xs[:, c * NT:(c + 1) * NT], reads=[spill_buf], writes=st_.xb[c], dsem=xc_dsem[c])

        def linear(nblocks, ocs_per_block, kcn, wcols, lhs_fn, rhs_fn, rhs_bufs_fn, tis, evac, psgroup=None):
            for b in range(nblocks):
                wt, wb = w_next(kcn, wcols)
                for ocl in range(ocs_per_block):
                    for ti in tis:
                        t0, n = TCH[ti]
                        ps_t, ps_b = psum(psgroup)
                        for kc in range(kcn):
                            mm(ps_t[:, 0:n], lhs_fn(wt, ocl, kc), rhs_fn(kc, t0, n), kc == 0, kc == kcn - 1,
                               [wb] + rhs_bufs_fn(kc, ti), [ps_b], inc=(kc == kcn - 1))
                        evac(b, ocl, ti, ps_t[:, 0:n], ps_b)

        def resid_evac(grp, tis_all, bias_name=None):
            def ev(b, ocl, ti, ps_ap, ps_b):
                oc = b * 4 + ocl
                t0, n = TCH[ti]
                s = 0 if ti == 0 else 1
                src = ps_ap
                rd = [ps_b]
                if bias_name is not None:
                    t_, tb_ = tmp32()
                    act(t_[:, 0:n], ps_ap, AF.Identity, [ps_b, b_vec], [tb_], bias=vcol(bias_name, oc))
                    src = t_[:, 0:n]
                    rd = [tb_]
                stt(X3[:, oc, t0:t0 + n], src, modcol(grp, oc, s), X3[:, oc, t0:t0 + n], ALU.mult, ALU.add,
                    rd + [b_mods(), st_.xb[oc][ti]], [st_.xb[oc][ti]])
            return ev

        def hb_rhs(kc, t0, n):
            return H3[:, kc, t0:t0 + n]

        def hb_bufs(kc, ti):
            return [st_.hb[kc][ti]]

        def mlp_phase(li, tis, next_li=None, next_par=None):
            HID = AUX[:, 0:4 * NT].rearrange("p (c t) -> p c t", c=4)
            hidb = [[p.buf() for _ in range(5)] for _ in range(4)]
            for hb_i in range(8):
                w1, w1b = w_next(8, 512)
                for ti in tis:
                    t0, n = TCH[ti]
                    for ocl in range(4):
                        ps_t, ps_b = psum()
                        for kc in range(8):
                            mm(ps_t[:, 0:n], w1[:, kc, ocl * 128:(ocl + 1) * 128], H3[:, kc, t0:t0 + n], kc == 0, kc == 7,
                               [w1b, st_.hb[kc][ti]], [ps_b], inc=(kc == 7))
                        t_, tb_ = tmp32()
                        act(t_[:, 0:n], ps_t[:, 0:n], AF.Relu, [ps_b], [tb_])
                        tt(HID[:, ocl, t0:t0 + n], t_[:, 0:n], t_[:, 0:n], ALU.mult, [tb_], [hidb[ocl][ti]])
                w2, w2b = w_next(4, 1024)
                for ti in tis:
                    t0, n = TCH[ti]
                    s = 0 if ti == 0 else 1
                    for oc in range(8):
                        ps_t, ps_b = psum()
                        for kc in range(4):
                            mm(ps_t[:, 0:n], w2[:, kc, oc * 128:(oc + 1) * 128], HID[:, kc, t0:t0 + n], kc == 0, kc == 3,
                               [w2b, hidb[kc][ti]], [ps_b], inc=(kc == 3))
                        stt(X3[:, oc, t0:t0 + n], ps_t[:, 0:n], modcol(5, oc, s), X3[:, oc, t0:t0 + n], ALU.mult, ALU.add,
                            [ps_b, b_mods(), st_.xb[oc][ti]], [st_.xb[oc][ti]])
                if next_li is not None:
                    for _ in range(NMOD[hb_i]):
                        mod_block()
            if next_li is not None:
                mod_finish(next_li, next_par)

        def attention(li, j_att, need_ctx, li_index):
            QT = XRb[:, 0:18432].rearrange("p (c t) -> p c t", c=8)
            KT2 = XRb[:, 18432:27648].rearrange("p (c t) -> p c t", c=4)
            ROC = XR[:, 13824:15872]
            ROS = XR[:, 15872:17920]
            VA = AUX[:, 0:18 * 576].rearrange("p (k x) -> p k x", k=18)
            b_rope = p.buf(f"rope{li}", dma=True)
            qb = [[p.buf() for _ in range(5)] for _ in range(8)]
            kb = [[p.buf() for _ in range(5)] for _ in range(4)]
            vab = [p.buf() for _ in range(18)]
            b_va_init = p.buf()
            p.dma("sp", XR[:, 13824:17920], dr["rope"], writes=[b_rope])
            vmemset(AUX[:, 0:18 * 576], 1.0, [b_va_init] + vab)
            q_tis = [0, 1, 2, 3, 4] if need_ctx else [1, 2, 3, 4]

            PS_A = [0, 1, 2]
            PS_B = [3, 4]
            PS_C = [5, 6]
            pending = []

            def qk_item(ps_t, ps_b, n, ti, gain_ap, dst_ap, dst_buf):
                t0 = TCH[ti][0]
                lat = ti != 0
                state = {}

                def stage_b():
                    sq, sqb = tmppt()
                    act(sq[:, 0:n], ps_t[:, 0:n], AF.Square, [ps_b], [sqb])
                    ss_t, ss_b = psum(PS_B)
                    mm(ss_t[:, 0:n], ONESBLK, sq[:, 0:n], True, True, [sqb, b_const], [ss_b], inc=True)
                    sd, sdb = tmp32()
                    act(sd[:, 0:n], ss_t[:, 0:n], AF.Sqrt, [ss_b, b_misc], [sdb], bias=MISC[:, 0:1], scale=1.0 / 64)
                    rs, rsb = tmp32()
                    recip(rs[:, 0:n], sd[:, 0:n], [sdb], [rsb])
                    if not lat:
                        stt(dst_ap, ps_t[:, 0:n], gain_ap, rs[:, 0:n], ALU.mult, ALU.mult, [ps_b, rsb, b_vec], [dst_buf])
                    else:
                        qn, qnb = tmppt()
                        stt(qn[:, 0:n], ps_t[:, 0:n], gain_ap, rs[:, 0:n], ALU.mult, ALU.mult, [ps_b, rsb, b_vec], [qnb])
                        state["qn"] = (qn, qnb)

                def stage_c():
                    if not lat:
                        return
                    qn, qnb = state["qn"]
                    rot_t, rot_b = psum(PS_C)
                    mm(rot_t[:, 0:n], PERM, qn[:, 0:n], True, True, [qnb, b_const], [rot_b], inc=True)
                    t1, t1b = tmp32()
                    tt(t1[:, 0:n], qn[:, 0:n], ROC[:, t0 - NCTX:t0 - NCTX + n], ALU.mult, [qnb, b_rope], [t1b], eng="pool")
                    t2, t2b = tmp32()
                    tt(t2[:, 0:n], rot_t[:, 0:n], ROS[:, t0 - NCTX:t0 - NCTX + n], ALU.mult, [rot_b, b_rope], [t2b])
                    tt(dst_ap, t1[:, 0:n], t2[:, 0:n], ALU.add, [t1b, t2b], [dst_buf], eng="pool")
                return stage_b, stage_c

            def pipe_push(item):
                pending.append(item)
                if len(pending) >= 2:
                    pending[-2][0]()
                if len(pending) >= 3:
                    pending[-3][1]()

            def pipe_flush():
                if len(pending) >= 1:
                    pending[-1][0]()
                if len(pending) >= 2:
                    pending[-2][1]()
                if len(pending) >= 1:
                    pending[-1][1]()
                pending.clear()

            for b in range(2):
                wt, wb = w_next(8, 512)
                for ocl in range(4):
                    oc = b * 4 + ocl
                    for ti in q_tis:
                        t0, n = TCH[ti]
                        ps_t, ps_b = psum(PS_A)
                        for kc in range(8):
                            mm(ps_t[:, 0:n], wt[:, kc, ocl * 128:(ocl + 1) * 128], H3[:, kc, t0:t0 + n], kc == 0, kc == 7,
                               [wb, st_.hb[kc][ti]], [ps_b], inc=(kc == 7))
                        pipe_push(qk_item(ps_t, ps_b, n, ti, vcol(f"qg{j_att}"), QT[:, oc, t0:t0 + n], qb[oc][ti]))
            wt, wb = w_next(8, 512)
            for g in range(4):
                for ti in range(5):
                    t0, n = TCH[ti]
                    ps_t, ps_b = psum(PS_A)
                    for kc in range(8):
                        mm(ps_t[:, 0:n], wt[:, kc, g * 128:(g + 1) * 128], H3[:, kc, t0:t0 + n], kc == 0, kc == 7,
                           [wb, st_.hb[kc][ti]], [ps_b], inc=(kc == 7))
                    pipe_push(qk_item(ps_t, ps_b, n, ti, vcol(f"kg{j_att}"), KT2[:, g, t0:t0 + n], kb[g][ti]))
            pipe_flush()
            wt, wb = w_next(8, 256)
            for kt in range(18):
                ti = 0 if kt < 2 else 1 + (kt - 2) // 4
                ps_t, ps_b = psum(PS_A)
                for kc in range(8):
                    mm(ps_t[:, 0:256], H3[:, kc, kt * 128:(kt + 1) * 128], wt[:, kc, :], kc == 0, kc == 7,
                       [wb, st_.hb[kc][ti]], [ps_b], inc=(kc == 7))
                dst = VA[:, kt, 64:576].rearrange("p (g x) -> p g x", x=128)[:, :, 0:64]
                src = ps_t[:, 0:256].rearrange("p (g x) -> p g x", x=64)
                act(dst, src, AF.Identity, [ps_b], [vab[kt]])

            OBANK = [[0, 1], [2, 3]]
            ptdb = [p.buf() for _ in range(4)]
            it = 0
            for jp in range(8):
                g = jp // 2
                for ti in q_tis:
                    t0, n = TCH[ti]
                    kts = list(range(18)) if ti != 0 else [0, 1]
                    ob = OBANK[it % 2]
                    it += 1
                    o_t = [PS[ob[0]], PS[ob[1]]]
                    o_b = [psb[ob[0]], psb[ob[1]]]

                    def s_stage(kt):
                        tik = 0 if kt < 2 else 1 + (kt - 2) // 4
                        di = st_.rr.get("psd", 0) % 2
                        st_.rr["psd"] = st_.rr.get("psd", 0) + 1
                        dt_ = PSD[di]
                        dbs = [psb[4 + 2 * di], psb[5 + 2 * di]]
                        for h in range(2):
                            mm(dt_[:, h * 512:h * 512 + n], KT2[h * 64:(h + 1) * 64, g, kt * 128:(kt + 1) * 128],
                               QT[h * 64:(h + 1) * 64, jp, t0:t0 + n], True, True,
                               [kb[g][tik], qb[jp][ti]], [dbs[h]], inc=(h == 1))
                        return dt_, dbs

                    def pv_stage(kt, sres):
                        dt_, dbs = sres
                        pi = st_.rr.get("ptd", 0) % 4
                        st_.rr["ptd"] = st_.rr.get("ptd", 0) + 1
                        ptd = TB[:, pi * 1024:(pi + 1) * 1024]
                        src = dt_[:, :].rearrange("p (h x) -> p h x", h=2)[:, :, 0:n]
                        dst = ptd.rearrange("p (h x) -> p h x", h=2)[:, :, 0:n]
                        act(dst, src, AF.Exp, dbs, [ptdb[pi]], scale=0.125)
                        for h in range(2):
                            if h == 0:
                                lhs = VA[:, kt, 64 + 128 * g:192 + 128 * g]
                            else:
                                lhs = VA[:, kt, 128 * g:128 + 128 * g]
                            mm(o_t[h][:, 0:n], lhs, ptd[:, h * 512:h * 512 + n], kt == kts[0], kt == kts[-1],
                               [ptdb[pi], vab[kt]], [o_b[h]], inc=True)

                    prev = s_stage(kts[0])
                    for idx, kt in enumerate(kts):
                        nxt = s_stage(kts[idx + 1]) if idx + 1 < len(kts) else None
                        pv_stage(kt, prev)
                        prev = nxt
                    for h in range(2):
                        rc, rcb = tmp32()
                        recip(rc[:, 0:n], o_t[h][:, 0:n], [o_b[h]], [rcb])
                        lo, hi = (0, 64) if h == 0 else (64, 128)
                        dlo, dhi = (64, 128) if h == 0 else (0, 64)
                        tt(H3[lo:hi, jp, t0:t0 + n], o_t[h][lo:hi, 0:n], rc[dlo:dhi, 0:n], ALU.mult,
                           [o_b[h], rcb], [st_.hb[jp][ti]])
            dbg_dump("att_xr", XR[:, :], [128, 8 * NT], F32)
            dbg_dump("att_aux", AUX[:, :], [128, 10368], BF16)
            dbg_dump("att_hb", HBt[:, :], [128, 8 * NT], BF16)
            reload(li_index)
            linear(2, 4, 8, 512, lambda wt, ocl, kc: wt[:, kc, ocl * 128:(ocl + 1) * 128], hb_rhs, hb_bufs,
                   q_tis, resid_evac(2, q_tis))

        def conformer(li, need_ctx, li_index):
            UC = XRb[:, 0:8 * 286].rearrange("p (c t) -> p c t", c=8)
            UL = XRb[:, 2288:2288 + 8 * 2078].rearrange("p (c t) -> p c t", c=8)
            DG = [XRb[:, 18912 + i * 3968:18912 + (i + 1) * 3968].rearrange("p (k m) -> p k m", k=31) for i in range(2)]
            ub = [[p.buf() for _ in range(5)] for _ in range(8)]
            upad = p.buf()
            dgb = [p.buf(), p.buf()]
            vmemset(XRb[:, 0:18912], 0.0, [upad] + [b for row in ub for b in row])
            tis = [0, 1, 2, 3, 4]

            def useg(c, ti, k, n):
                if ti == 0:
                    return UC[:, c, k:k + n]
                o = TCH[ti][0] - NCTX
                return UL[:, c, o + k:o + k + n]

            for b in range(4):
                wt, wb = w_next(8, 512)
                for cl in range(2):
                    c = b * 2 + cl
                    for ti in tis:
                        t0, n = TCH[ti]
                        pa_t, pa_b = psum()
                        for kc in range(8):
                            mm(pa_t[:, 0:n], wt[:, kc, cl * 128:(cl + 1) * 128], H3[:, kc, t0:t0 + n], kc == 0, kc == 7,
                               [wb, st_.hb[kc][ti]], [pa_b], inc=(kc == 7))
                        pg_t, pg_b = psum()
                        for kc in range(8):
                            mm(pg_t[:, 0:n], wt[:, kc, 256 + cl * 128:256 + (cl + 1) * 128], H3[:, kc, t0:t0 + n], kc == 0, kc == 7,
                               [wb, st_.hb[kc][ti]], [pg_b], inc=(kc == 7))
                        sg, sgb = tmp32()
                        act(sg[:, 0:n], pg_t[:, 0:n], AF.Sigmoid, [pg_b, b_vec], [sgb], bias=vcol("cbin", 8 + c))
                        stt(useg(c, ti, 15, n), pa_t[:, 0:n], vcol("cbin", c), sg[:, 0:n], ALU.add, ALU.mult,
                            [pa_b, sgb, b_vec, upad], [ub[c][ti]])
            vb = [[p.buf() for _ in range(5)] for _ in range(8)]
            for c in range(8):
                par = c % 2
                for k in range(31):
                    ts(DG[par][:, k, :], IDENT, vcol("cwdw", k * 8 + c), None, ALU.mult, None, [b_const, b_vec], [dgb[par]])
                for ti in tis:
                    t0, n = TCH[ti]
                    ps_t, ps_b = psum()
                    nb = [ub[c][ti]]
                    if ti > 1:
                        nb.append(ub[c][ti - 1])
                    if 1 <= ti < 4:
                        nb.append(ub[c][ti + 1])
                    for k in range(31):
                        mm(ps_t[:, 0:n], DG[par][:, k, :], useg(c, ti, k, n), k == 0, k == 30,
                           [dgb[par], upad] + nb, [ps_b], inc=(k == 30))
                    act(H3[:, c, t0:t0 + n], ps_t[:, 0:n], AF.Identity, [ps_b, b_vec],
                        [vb[c][ti], st_.hb[c][ti]], bias=vcol("cbdw", c))
            TB3 = TB[:, :].rearrange("p (c t) -> p c t", c=8)
            yb = [[p.buf() for _ in range(5)] for _ in range(8)]
            for ti in tis:
                t0, n = TCH[ti]
                pm_t, pm_b = psum()
                for c in range(8):
                    mm(pm_t[:, 0:n], ONES, H3[:, c, t0:t0 + n], c == 0, c == 7, [vb[c][ti], b_const], [pm_b], inc=(c == 7))
                for c in range(8):
                    act(TB3[:, c, 0:n], H3[:, c, t0:t0 + n], AF.Square, [vb[c][ti]], [b_tb])
                pq_t, pq_b = psum()
                for c in range(8):
                    mm(pq_t[:, 0:n], ONES, TB3[:, c, 0:n], c == 0, c == 7, [b_tb, b_const], [pq_b], inc=(c == 7))
                mean, meanb = T32H[1], t32hb[1]
                act(mean[:, 0:n], pm_t[:, 0:n], AF.Identity, [pm_b], [meanb], scale=1.0 / D)
                m2, m2b = tmp32()
                tt(m2[:, 0:n], mean[:, 0:n], mean[:, 0:n], ALU.mult, [meanb], [m2b])
                var, varb = tmp32()
                stt(var[:, 0:n], pq_t[:, 0:n], 1.0 / D, m2[:, 0:n], ALU.mult, ALU.subtract, [pq_b, m2b], [varb])
                sd, sdb = tmp32()
                act(sd[:, 0:n], var[:, 0:n], AF.Sqrt, [varb, b_misc], [sdb], bias=MISC[:, 0:1], scale=1.0)
                rs, rsb = T32H[0], t32hb[0]
                recip(rs[:, 0:n], sd[:, 0:n], [sdb], [rsb])
                for c in range(8):
                    t_, tb_ = tmppt32()
                    tt(t_[:, 0:n], H3[:, c, t0:t0 + n], mean[:, 0:n], ALU.subtract, [vb[c][ti], meanb], [tb_])
                    tt(t_[:, 0:n], t_[:, 0:n], rs[:, 0:n], ALU.mult, [tb_, rsb], [tb_])
                    act(H3[:, c, t0:t0 + n], t_[:, 0:n], AF.Silu, [tb_, b_vec], [yb[c][ti], vb[c][ti]],
                        bias=vcol("cnb", c), scale=vcol("cng", c))
            st_.hb = yb
            reload(li_index)
            linear(2, 4, 8, 512, lambda wt, ocl, kc: wt[:, kc, ocl * 128:(ocl + 1) * 128], hb_rhs, hb_bufs,
                   tis, resid_evac(2, tis, bias_name="cbout"))

        st_.rr["tbx"] = 0

        def tmppt32():
            i = st_.rr["tbx"] % 2
            st_.rr["tbx"] += 1
            return T32X[i], t32xb[i]

        def rglru(li, need_ctx, li_index):
            G3 = XRb[:, 0:18432].rearrange("p (c t) -> p c t", c=8)
            XL3 = XRb[:, 18432:36864].rearrange("p (c t) -> p c t", c=8)
            gb = [[p.buf() for _ in range(5)] for _ in range(8)]
            xlb = [[p.buf() for _ in range(5)] for _ in range(8)]
            tis = [0, 1, 2, 3, 4]
            for b in range(4):
                wt, wb = w_next(8, 512)
                for ocl in range(4):
                    oc = b * 4 + ocl
                    for ti in tis:
                        t0, n = TCH[ti]
                        ps_t, ps_b = psum()
                        for kc in range(8):
                            mm(ps_t[:, 0:n], wt[:, kc, ocl * 128:(ocl + 1) * 128], H3[:, kc, t0:t0 + n], kc == 0, kc == 7,
                               [wb, st_.hb[kc][ti]], [ps_b], inc=(kc == 7))
                        if oc < 8:
                            act(G3[:, oc, t0:t0 + n], ps_t[:, 0:n], AF.Gelu_apprx_tanh, [ps_b], [gb[oc][ti]])
                        else:
                            vcopy(XL3[:, oc - 8, t0:t0 + n], ps_t[:, 0:n], [ps_b], [xlb[oc - 8][ti]])
            lam = VEC[:, VOFF["llam"]:VOFF["llam"] + 16]
            b_ca = p.buf()
            act(MISC[:, 64:80], lam, AF.Exp, [b_vec], [b_ca], scale=-1.0)
            act(MISC[:, 80:96], MISC[:, 64:80], AF.Ln, [b_ca, b_misc], [b_ca], bias=MISC[:, 1:2], scale=1.0)
            ts(MISC[:, 16:32], MISC[:, 80:96], -8.0, None, ALU.mult, None, [b_ca], [b_ca])
            ts(MISC[:, 48:64], MISC[:, 80:96], -4.0, None, ALU.mult, None, [b_ca], [b_ca])
            ts(MISC[:, 96:128], VEC[:, VOFF["lgb"]:VOFF["lgb"] + 32], 0.5, None, ALU.mult, None, [b_vec, b_ca], [b_ca])
            dbg_dump("lru_xr0", XR[:, :], [128, 8 * NT], F32)
            p.barrier(("pe", "act", "dve"))
            U32 = HBf[:, 0:4608].rearrange("p (c t) -> p c t", c=2)
            HS = HBf[:, 4608:9216].rearrange("p (c t) -> p c t", c=2)
            UBF = AUX[:, 0:4608].rearrange("p (c t) -> p c t", c=2)
            XT = [AUXf[:, 2304 + i * 512:2304 + (i + 1) * 512] for i in range(5)]
            xtb = [p.buf() for _ in range(5)]
            CAR = MISC[:, 128:256]
            car_i = [0]
            rrx = [0]

            def ltmp():
                i = rrx[0] % 15
                rrx[0] += 1
                if i < 5:
                    return XT[i], xtb[i]
                if i < 11:
                    return T32[i - 5], t32b[i - 5]
                if i < 13:
                    return T32X[i - 11], t32xb[i - 11]
                return T32H[i - 13], t32hb[i - 13]

            gslots = []
            gidx = []
            for d in range(2):
                gidx.append(ws.consumed)
                gslots.append(w_next(16, 256, pin=True))
            SEGS = [(0, NCTX), (NCTX, NLAT)]
            u32b = [p.buf(), p.buf()]
            ubfb = [p.buf(), p.buf()]
            hsb = [[p.buf() for _ in range(5)] for _ in range(2)]
            for nblk in range(4):
                c0 = nblk * 2
                for d in range(2):
                    gwt, gwb = gslots[d]
                    for cl in range(2):
                        c = c0 + cl
                        xall = xlb[c]
                        for (s0, sn) in SEGS:
                            ts(U32[:, cl, s0:s0 + sn], XL3[:, c, s0:s0 + sn], vcol("lcw", (d * 4 + 3) * 8 + c), vcol("lcb", d * 8 + c),
                               ALU.mult, ALU.add, xall + [b_vec], [u32b[cl]], eng="pool")
                            for k in range(3):
                                sh = 3 - k
                                if d == 0:
                                    o_ap = U32[:, cl, s0 + sh:s0 + sn]
                                    i_ap = XL3[:, c, s0:s0 + sn - sh]
                                else:
                                    o_ap = U32[:, cl, s0:s0 + sn - sh]
                                    i_ap = XL3[:, c, s0 + sh:s0 + sn]
                                stt(o_ap, i_ap, vcol("lcw", (d * 4 + k) * 8 + c), o_ap, ALU.mult, ALU.add,
                                    xall + [b_vec, u32b[cl]], [u32b[cl]])
                        act(UBF[:, cl, :], U32[:, cl, :], AF.Identity, [u32b[cl]], [ubfb[cl]])
                    for cl in range(2):
                        c = c0 + cl
                        groups = [[0, 1, 2], [3, 4]] if d == 0 else [[0, 4, 3], [2, 1]]
                        prev_car = None
                        oi = -1
                        c4 = MISC[:, 48 + d * 8 + c:49 + d * 8 + c]
                        for grp in groups:
                            items = []
                            for ti in grp:
                                t0, n = TCH[ti]
                                gps = []
                                for gi in range(2):
                                    ps_t, ps_b = psum()
                                    for kc in range(2):
                                        mm(ps_t[:, 0:n], gwt[:, (gi * 4 + nblk) * 2 + kc, cl * 128:(cl + 1) * 128], UBF[:, kc, t0:t0 + n],
                                           kc == 0, kc == 1, [gwb, ubfb[kc]], [ps_b], inc=(kc == 1))
                                    gps.append((ps_t, ps_b))
                                r_, rb_ = ltmp()
                                act(r_[:, 0:n], gps[0][0][:, 0:n], AF.Tanh, [gps[0][1], b_ca], [rb_],
                                    bias=MISC[:, 96 + (d * 2 + 0) * 8 + c:97 + (d * 2 + 0) * 8 + c], scale=0.5)
                                a_, ab_ = ltmp()
                                act(a_[:, 0:n], r_[:, 0:n], AF.Exp, [rb_, b_ca], [ab_], bias=c4, scale=c4)
                                i_, ib_ = ltmp()
                                act(i_[:, 0:n], gps[1][0][:, 0:n], AF.Tanh, [gps[1][1], b_ca], [ib_],
                                    bias=MISC[:, 96 + (d * 2 + 1) * 8 + c:97 + (d * 2 + 1) * 8 + c], scale=0.5)
                                stt(i_[:, 0:n], i_[:, 0:n], 1.0, U32[:, cl, t0:t0 + n], ALU.add, ALU.mult, [ib_, u32b[cl]], [ib_])
                                items.append((ti, a_, ab_, i_, ib_))
                            for (ti, a_, ab_, i_, ib_) in items:
                                oi += 1
                                t0, n = TCH[ti]
                                m_, mb_ = ltmp()
                                act(m_[:, 0:n], a_[:, 0:n], AF.Square, [ab_], [mb_])
                                act(m_[:, 0:n], m_[:, 0:n], AF.Sqrt, [mb_, b_misc], [mb_], bias=MISC[:, 1:2], scale=-1.0)
                                if ti == 0:
                                    fc = 0 if d == 0 else NCTX - 1
                                    vmemset(m_[:, fc:fc + 1], 1.0, [mb_])
                                stt(i_[:, 0:n], i_[:, 0:n], 0.5, m_[:, 0:n], ALU.mult, ALU.mult, [ib_, mb_], [ib_])
                                if d == 0:
                                    init = 0.0 if oi == 0 else HS[:, cl, t0 - 1:t0]
                                    rd = [ab_, ib_] + ([hsb[cl][ti - 1]] if oi > 0 else [])
                                    p.op("dve", lambda e, o=HS[:, cl, t0:t0 + n], a=a_[:, 0:n], b=i_[:, 0:n], init=init:
                                         e.tensor_tensor_scan(out=o, data0=a, data1=b, initial=init, op0=ALU.mult, op1=ALU.add),
                                         rd, [hsb[cl][ti]])
                                else:
                                    h_, hb_ = ltmp()
                                    init = 0.0 if oi == 0 else prev_car
                                    p.op("dve", lambda e, o=h_[:, 0:n][:, ::-1], a=a_[:, 0:n][:, ::-1], b=i_[:, 0:n][:, ::-1], init=init:
                                         e.tensor_tensor_scan(out=o, data0=a, data1=b, initial=init, op0=ALU.mult, op1=ALU.add),
                                         [ab_, ib_, b_car], [hb_])
                                    ci = car_i[0] % 128
                                    car_i[0] += 1
                                    vcopy(CAR[:, ci:ci + 1], h_[:, 0:1], [hb_], [b_car])
                                    prev_car = CAR[:, ci:ci + 1]
                                    tt(h_[:, 0:n], h_[:, 0:n], HS[:, cl, t0:t0 + n], ALU.add, [hb_, hsb[cl][ti]], [hb_], eng="pool")
                                    tt(G3[:, c, t0:t0 + n], h_[:, 0:n], G3[:, c, t0:t0 + n], ALU.mult, [hb_, gb[c][ti]], [gb[c][ti]], eng="pool")
            for gi_ in gidx:
                w_release(gi_)
            dbg_dump("lru_xr1", XR[:, :], [128, 8 * NT], F32)
            p.barrier(("pe", "act", "dve"))
            yb = [[p.buf() for _ in range(5)] for _ in range(8)]
            for c in range(8):
                for ti in tis:
                    t0, n = TCH[ti]
                    if (c + ti) % 2 == 0:
                        vcopy(H3[:, c, t0:t0 + n], G3[:, c, t0:t0 + n], [gb[c][ti]], [yb[c][ti]])
                    else:
                        act(H3[:, c, t0:t0 + n], G3[:, c, t0:t0 + n], AF.Identity, [gb[c][ti]], [yb[c][ti]])
            st_.hb = yb
            reload(li_index)
            linear(2, 4, 8, 512, lambda wt, ocl, kc: wt[:, kc, ocl * 128:(ocl + 1) * 128], hb_rhs, hb_bufs,
                   tis, resid_evac(2, tis))

        vmemset(MISC[:, 0:1], EPS, [b_misc])
        vmemset(MISC[:, 1:2], 1.0, [b_misc])
        for idx, li in enumerate(layers):
            kind = li % 3
            need_ctx = li < DEPTH - 1
            st_.par = idx % 2
            if idx == 0:
                for _ in range(12):
                    mod_block()
                mod_finish(li, 0)
            st_.hb = grid(f"h{li}_")
            sp_tok = spill_issue(0 if idx == 0 else 1)
            norm_phase(0, [0, 1, 2, 3, 4])
            if DBG and idx == 0:
                p.dma("sp", dbg_mods, MODS()[:, :], reads=[b_mods()], writes=[o_buf])
                p.dma("sp", dbg_ab, AB()[:, :], reads=[b_ab()], writes=[o_buf])
                p.dma("sp", dbg_h1, HBt[:, :], reads=[b for row in st_.hb for b in row], writes=[o_buf])
            first_in_prog = (idx == 0)
            spill(sp_tok)
            lidx = 0 if first_in_prog else 1
            import os as _os
            if _os.environ.get("DBG_SKIP_MIX"):
                for _ in range({0: 6, 1: 6, 2: 8}[kind]):
                    w_next(8, 512)
                reload(lidx)
            elif kind == 0:
                attention(li, li // 3, need_ctx, lidx)
            elif kind == 1:
                conformer(li, need_ctx, lidx)
            else:
                rglru(li, need_ctx, lidx)
            tis = [0, 1, 2, 3, 4] if need_ctx else [1, 2, 3, 4]
            p.barrier(("pe", "act", "dve"))
            st_.hb = grid(f"h2{li}_")
            if _os.environ.get("DBG_SKIP_MLP"):
                for _ in range(16 + (12 if idx + 1 < len(layers) else 0)):
                    w_next(8, 512)
            else:
                if DBG and idx == 0:
                    p.dma("sp", dbg_x1, XR[:, :], reads=[b for row in st_.xb for b in row], writes=[o_buf])
                norm_phase(1, tis)
                if DBG and idx == 0:
                    p.dma("sp", dbg_h2, HBt[:, :], reads=[b for row in st_.hb for b in row], writes=[o_buf])
                if idx + 1 < len(layers):
                    mlp_phase(li, tis, layers[idx + 1], (idx + 1) % 2)
                else:
                    mlp_phase(li, tis)
        allb = [b for row in st_.xb for b in row]
        if last:
            for c in range(8):
                p.dma("sp", outT[c * 128:(c + 1) * 128, :], X3[:, c, NCTX:NT], reads=allb, writes=[o_buf])
        else:
            p.dma("sp", xs_out, XR[:, :], reads=allb, writes=[o_buf])
        p._wait("sp", o_buf.w)
        assert ws.consumed == len(wspecs), (ws.consumed, len(wspecs))
        p.emit()
    return nc


_WKEYS = ["mod_w", "mlp_w1", "mlp_w2", "attn_w_qkv", "attn_w_o", "conv_w_in", "conv_w_out",
          "lru_w_in", "lru_gate_w", "lru_w_out"]


def _common_maps(inputs):
    m = {k: np.ascontiguousarray(np.asarray(inputs[k], np.float32)) for k in _WKEYS}
    m["consts"] = _consts()
    m["rope"] = _rope_tables()
    return m


def kernel(**inputs):
    n = 8
    common = _common_maps(inputs)
    x = np.asarray(inputs["x"], np.float32)
    ctx = np.asarray(inputs["ctx"], np.float32)
    in_maps = []
    for b in range(n):
        m = dict(common)
        m["xT"] = np.ascontiguousarray(x[b].T)
        m["ctxT"] = np.ascontiguousarray(ctx[b].T)
        m["vecs"] = _pack_vecs(inputs, b)
        in_maps.append(m)
    nc = build_program((0, 1, 2, 3), True, True)
    res = run_bass_kernel_spmd(nc, in_maps, core_ids=list(range(n)))
    out = np.stack([np.ascontiguousarray(res.results[b]["outT"].T) for b in range(n)], axis=0)
    return out.astype(np.float32)
```

```python
import numpy as np
from contextlib import ExitStack
import concourse.bass as bass
import concourse.mybir as mybir
from concourse.bass_utils import run_bass_kernel_spmd
import ml_dtypes

F32 = mybir.dt.float32
BF16 = mybir.dt.bfloat16
AF = mybir.ActivationFunctionType
ALU = mybir.AluOpType

SELF_WAIT = True

NT, NCTX, NLAT, D = 2304, 256, 2048, 1024
TCH = [(0, 256), (256, 512), (768, 512), (1280, 512), (1792, 512)]
EPS = 1e-6
DEPTH = 4


class Sem:
    def __init__(self, h, name):
        self.h = h
        self.name = name
        self.count = 0


class Buf:
    __slots__ = ("name", "w", "r", "dsem")

    def __init__(self, name, dsem=None):
        self.name = name
        self.w = None
        self.r = []
        self.dsem = dsem


class Prog:
    ENG = ("pe", "act", "dve", "pool", "sp")

    def __init__(self, nc, stack):
        self.nc = nc
        self.stack = stack
        self.ops = {e: [] for e in self.ENG}
        self.esem = {}
        for e in ("pe", "act", "dve", "pool"):
            self.esem[e] = self.new_sem("s_" + e)
        self.known = {e: {} for e in self.ENG}
        self.pending_noinc = {e: False for e in self.ENG}
        self.nbuf = 0

    def new_sem(self, name):
        h = self.stack.enter_context(self.nc.semaphore(name))
        return Sem(h, name)

    def buf(self, name=None, dma=False):
        self.nbuf += 1
        name = name or f"b{self.nbuf}"
        return Buf(name, self.new_sem("d_" + name) if dma else None)

    def sb(self, name, shape, dt):
        return self.stack.enter_context(self.nc.sbuf_tensor(name, list(shape), dt))

    def ps(self, name, shape, dt=F32):
        return self.stack.enter_context(self.nc.psum_tensor(name, list(shape), dt))

    def _wait(self, eng, tok):
        if tok is None:
            return
        sem, val = tok
        if eng == "pe" and sem is self.esem["pe"]:
            return
        if (not SELF_WAIT) and eng in self.esem and sem is self.esem[eng]:
            return
        k = self.known[eng]
        if k.get(sem, 0) >= val:
            return
        k[sem] = val
        h = sem.h
        self.ops[eng].append(lambda e, h=h, val=val: e.wait_ge(h, val))

    def _deps(self, eng, reads, writes):
        for b in reads:
            self._wait(eng, b.w)
        for b in writes:
            self._wait(eng, b.w)
            for t in b.r:
                self._wait(eng, t)

    def _commit(self, tok, reads, writes):
        for b in reads:
            b.r.append(tok)
            if len(b.r) > 16:
                d = {}
                for s, v in b.r:
                    if d.get(s, 0) < v:
                        d[s] = v
                b.r = list(d.items())
        for b in writes:
            b.w = tok
            b.r = []

    def op(self, eng, fn, reads=(), writes=(), inc=True):
        self._deps(eng, reads, writes)
        sem = self.esem[eng]
        if inc:
            sem.count += 1
            val = sem.count
            h = sem.h
            self.ops[eng].append(lambda e, fn=fn, h=h: fn(e).then_inc(h, 1))
            self.pending_noinc[eng] = False
        else:
            val = sem.count + 1
            self.ops[eng].append(lambda e, fn=fn: fn(e))
            self.pending_noinc[eng] = True
        tok = (sem, val)
        self._commit(tok, reads, writes)
        return tok

    def dma(self, q, out, in_, reads=(), writes=(), dsem=None):
        self._deps(q, reads, writes)
        sem = dsem if dsem is not None else writes[0].dsem
        sem.count += 16
        val = sem.count
        h = sem.h
        self.ops[q].append(lambda e, out=out, in_=in_, h=h: e.dma_start(out=out, in_=in_).then_inc(h, 16))
        tok = (sem, val)
        self._commit(tok, reads, writes)
        return tok

    def barrier(self, engs=("pe", "act", "dve", "sp")):
        for e in ("pe", "act", "dve", "pool"):
            assert not self.pending_noinc[e]
        toks = [(self.esem[e], self.esem[e].count) for e in ("pe", "act", "dve", "pool")]
        if "pool" not in engs:
            engs = tuple(engs) + ("pool",)
        for e in engs:
            for t in toks:
                if t[1] > 0:
                    self._wait(e, t)

    def emit(self):
        for e in ("pe", "act", "dve"):
            assert not self.pending_noinc[e], f"engine {e} has trailing non-inc op"
        ops = self.ops
        with self.nc.Block() as block:
            @block.sync
            def _(eng):
                for f in ops["sp"]:
                    f(eng)

            @block.tensor
            def _(eng):
                for f in ops["pe"]:
                    f(eng)

            @block.scalar
            def _(eng):
                for f in ops["act"]:
                    f(eng)

            @block.vector
            def _(eng):
                for f in ops["dve"]:
                    f(eng)

            @block.gpsimd
            def _(eng):
                for f in ops["pool"]:
                    f(eng)


def _vec_layout():
    L = [("c", 8), ("cctx", 8)]
    for i in range(DEPTH):
        L += [(f"modb{i}", 48), (f"gmix{i}", 8), (f"gmlp{i}", 8)]
    for j in range(2):
        L += [(f"qg{j}", 1), (f"kg{j}", 1)]
    L += [("cbin", 16), ("cwdw", 248), ("cbdw", 8), ("cng", 8), ("cnb", 8), ("cbout", 8)]
    L += [("lcw", 64), ("lcb", 16), ("lgb", 32), ("llam", 16)]
    off = {}
    o = 0
    for n, k in L:
        off[n] = o
        o += k
    return off, o


VOFF, NV = _vec_layout()


def _pk(v):
    v = np.asarray(v, np.float32).reshape(-1, 128)
    return np.ascontiguousarray(v.T)


def _pack_vecs(inp, b):
    V = np.zeros((128, NV), np.float32)

    def put(name, arr):
        arr = np.asarray(arr, np.float32)
        V[:, VOFF[name]:VOFF[name] + arr.shape[1]] = arr

    put("c", _pk(inp["c"][b]))
    put("cctx", _pk(inp["c_ctx"]))
    for i in range(DEPTH):
        put(f"modb{i}", _pk(inp["mod_b"][i]))
        put(f"gmix{i}", _pk(inp["norm_mix_g"][i]))
        put(f"gmlp{i}", _pk(inp["norm_mlp_g"][i]))
    for j in range(2):
        put(f"qg{j}", np.tile(np.asarray(inp["attn_q_gain"][j], np.float32), 2)[:, None])
        put(f"kg{j}", np.tile(np.asarray(inp["attn_k_gain"][j], np.float32), 2)[:, None])
    put("cbin", _pk(inp["conv_b_in"][0]))
    wdw = np.asarray(inp["conv_w_dw"][0], np.float32).reshape(31, 8, 128).transpose(2, 0, 1).reshape(128, 248)
    put("cwdw", wdw)
    put("cbdw", _pk(inp["conv_b_dw"][0]))
    put("cng", _pk(inp["conv_norm_g"][0]))
    put("cnb", _pk(inp["conv_norm_b"][0]))
    put("cbout", _pk(inp["conv_b_out"][0]))
    lcw = np.asarray(inp["lru_conv_w"][0], np.float32).reshape(2, 4, 8, 128).transpose(3, 0, 1, 2).reshape(128, 64)
    put("lcw", lcw)
    put("lcb", np.asarray(inp["lru_conv_b"][0], np.float32).reshape(2, 8, 128).transpose(2, 0, 1).reshape(128, 16))
    put("lgb", np.asarray(inp["lru_gate_b"][0], np.float32).reshape(2, 2, 8, 128).transpose(3, 0, 1, 2).reshape(128, 32))
    put("llam", np.asarray(inp["lru_lambda"][0], np.float32).reshape(2, 8, 128).transpose(2, 0, 1).reshape(128, 16))
    return V


def _consts():
    p = np.arange(128)
    perm = (p[:, None] == (p[None, :] ^ 16)).astype(np.float32)
    onesblk = ((p[:, None] // 64) == (p[None, :] // 64)).astype(np.float32)
    ones = np.ones((128, 128), np.float32)
    ident = np.eye(128, dtype=np.float32)
    return np.concatenate([perm, onesblk, ones, ident], axis=1).astype(ml_dtypes.bfloat16)


def _rope_tables():
    t = np.arange(NLAT)
    row = (t // 64).astype(np.float64)
    col = (t % 64).astype(np.float64)
    inv = 10000.0 ** (-np.arange(16, dtype=np.float64) / 16.0)
    p = np.arange(128)
    d = p % 64
    a = d // 32
    half = (d // 16) % 2
    f = d % 16
    pos = np.where(a[:, None] == 0, row[None, :], col[None, :])
    ang = (pos.astype(np.float32) * inv.astype(np.float32)[f][:, None]).astype(np.float32)
    C = np.cos(ang).astype(np.float32)
    S = np.sin(ang).astype(np.float32) * np.where(half == 0, -1.0, 1.0)[:, None].astype(np.float32)
    return np.ascontiguousarray(np.concatenate([C, S], axis=1).astype(np.float32))


class K:
    pass


def build_program(layers=(0, 1, 2, 3), first=True, last=True):
    nc = bass.Bass("TRN2", target_bir_lowering=False)
    dr = {}

    def din(name, shape, dt=F32):
        dr[name] = nc.dram_tensor(name, list(shape), dt, kind="ExternalInput").ap()
        return dr[name]

    if first:
        din("xT", [D, NLAT])
        din("ctxT", [D, NCTX])
    else:
        din("xs_in", [128, 8 * NT])
    din("vecs", [128, NV])
    din("consts", [128, 512], BF16)
    din("rope", [128, 2 * NLAT])
    din("mod_w", [4, D, 6 * D])
    din("mlp_w1", [4, D, 4 * D])
    din("mlp_w2", [4, 4 * D, D])
    din("attn_w_qkv", [2, D, 1536])
    din("attn_w_o", [2, D, D])
    din("conv_w_in", [1, D, 2 * D])
    din("conv_w_out", [1, D, D])
    din("lru_w_in", [1, D, 2 * D])
    din("lru_gate_w", [1, 2, 2, 4, 256, 256])
    din("lru_w_out", [1, D, D])
    if last:
        outT = nc.dram_tensor("outT", [D, NLAT], F32, kind="ExternalOutput").ap()
    else:
        xs_out = nc.dram_tensor("xs_out", [128, 8 * NT], F32, kind="ExternalOutput").ap()
    xs = nc.dram_tensor("xs_scr", [128, 8 * NT], F32, kind="Internal").ap()
    import os as _os
    DBG = bool(_os.environ.get("DBG_DUMP"))
    if DBG:
        dbg_mods = nc.dram_tensor("dbg_mods", [128, 96], F32, kind="ExternalOutput").ap()
        dbg_ab = nc.dram_tensor("dbg_ab", [128, 64], F32, kind="ExternalOutput").ap()
        dbg_h1 = nc.dram_tensor("dbg_h1", [128, 8 * NT], BF16, kind="ExternalOutput").ap()
        dbg_h2 = nc.dram_tensor("dbg_h2", [128, 8 * NT], BF16, kind="ExternalOutput").ap()
        dbg_x1 = nc.dram_tensor("dbg_x1", [128, 8 * NT], F32, kind="ExternalOutput").ap()

    with ExitStack() as st:
        p = Prog(nc, st)
        XR = p.sb("XR", [128, 8 * NT], F32)
        HBt = p.sb("HB", [128, 8 * NT], BF16)
        AUX = p.sb("AUX", [128, 10368], BF16)
        SLOT = [p.sb(f"slot{i}", [128, 4096], BF16) for i in range(4)]
        VEC = p.sb("VEC", [128, NV], F32)
        CONST = p.sb("CONST", [128, 512], BF16)
        MODS_ = [p.sb(f"MODS{i}", [128, 96], F32) for i in range(2)]
        AB_ = [p.sb(f"AB{i}", [128, 64], F32) for i in range(2)]
        SC = p.sb("SC", [128, 16], BF16)
        MISC = p.sb("MISC", [128, 256], F32)
        T32 = [p.sb(f"t32_{i}", [128, 512], F32) for i in range(6)]
        TB = p.sb("TB", [128, 8 * 512], BF16)
        T32X = [p.sb(f"t32x_{i}", [128, 512], F32) for i in range(2)]
        t32xb = [p.buf(f"t32x_{i}") for i in range(2)]
        T32H = [p.sb(f"t32h_{i}", [128, 512], F32) for i in range(2)]
        t32hb = [p.buf(f"t32h_{i}") for i in range(2)]
        b_tb = p.buf("tb")
        b_car = p.buf("car")
        PT = [p.sb(f"pt{i}", [128, 512], BF16) for i in range(6)]
        PS = [p.ps(f"ps{i}", [128, 512], F32) for i in range(4)]
        PSD = [p.ps(f"psd{i}", [128, 1024], F32) for i in range(2)]
        PS = PS + [PSD[0][:, 0:512], PSD[0][:, 512:1024], PSD[1][:, 0:512], PSD[1][:, 512:1024]]
        psb = [p.buf(f"ps{i}") for i in range(8)]
        t32b = [p.buf(f"t32_{i}") for i in range(6)]
        ptb = [p.buf(f"pt{i}") for i in range(6)]
        slotb = [p.buf(f"slot{i}", dma=True) for i in range(4)]
        b_vec = p.buf("vec", dma=True)
        b_const = p.buf("const", dma=True)
        b_mods_ = [p.buf("mods0"), p.buf("mods1")]
        b_ab_ = [p.buf("ab0"), p.buf("ab1")]
        b_sc = p.buf("sc")
        b_misc = p.buf("misc")
        x_dsem = p.new_sem("d_x")
        xc_dsem = [p.new_sem(f"d_xc{c}") for c in range(8)]
        o_buf = p.buf("out", dma=True)
        spill_buf = p.buf("spill", dma=True)

        dbg_outs = {}

        def dbg_dump(name, ap, shape, dt):
            if not DBG:
                return
            t = nc.dram_tensor("dd_" + name, list(shape), dt, kind="ExternalOutput").ap()
            p.barrier(("sp",))
            p.dma("sp", t, ap, writes=[o_buf])
            for e in ("pe", "act", "dve"):
                p._wait(e, o_buf.w)

        PERM = CONST[:, 0:128]
        ONESBLK = CONST[:, 128:256]
        ONES = CONST[:, 256:384]
        IDENT = CONST[:, 384:512]

        X3 = XR[:, :].rearrange("p (c t) -> p c t", c=8)
        XRb = XR[:, :].bitcast(BF16)
        H3 = HBt[:, :].rearrange("p (c t) -> p c t", c=8)
        HBf = HBt[:, :].bitcast(F32)
        AUXf = AUX[:, :].bitcast(F32)

        def grid(name):
            return [[p.buf(f"{name}{c}_{t}") for t in range(5)] for c in range(8)]

        st_ = K()
        st_.xb = grid("x")
        st_.hb = grid("h")
        st_.rr = {"t32": 0, "pt": 0, "ps": 0}

        def vcol(name, j=0):
            o = VOFF[name] + j
            return VEC[:, o:o + 1]

        def tmp32():
            i = st_.rr["t32"] % 6
            st_.rr["t32"] += 1
            return T32[i], t32b[i]

        def tmppt():
            i = st_.rr["pt"] % 6
            st_.rr["pt"] += 1
            return PT[i], ptb[i]

        def psum(group=None):
            group = group if group is not None else list(range(7))
            key = ("ps",) + tuple(group)
            k_ = st_.rr.get(key, 0)
            st_.rr[key] = k_ + 1
            i = group[k_ % len(group)]
            return PS[i], psb[i]

        def mm(out, lhsT, rhs, start, stop, reads, writes, inc):
            p.op("pe", lambda e: e.matmul(out, lhsT, rhs, start=start, stop=stop), reads, writes, inc=inc)

        def act(out, in_, func, reads, writes, bias=None, scale=None):
            kw = {}
            if bias is not None:
                kw["bias"] = bias
            if scale is not None:
                kw["scale"] = scale
            p.op("act", lambda e: e.activation(out=out, in_=in_, func=func, **kw), reads, writes)

        def tt(out, in0, in1, op, reads, writes, eng="dve"):
            p.op(eng, lambda e: e.tensor_tensor(out=out, in0=in0, in1=in1, op=op), reads, writes)

        def ts(out, in0, s1, s2, op0, op1, reads, writes, eng="dve"):
            if s2 is None:
                p.op(eng, lambda e: e.tensor_scalar(out=out, in0=in0, scalar1=s1, scalar2=None, op0=op0), reads, writes)
            else:
                p.op(eng, lambda e: e.tensor_scalar(out=out, in0=in0, scalar1=s1, scalar2=s2, op0=op0, op1=op1), reads, writes)

        def stt(out, in0, scalar, in1, op0, op1, reads, writes, eng="dve"):
            p.op(eng, lambda e: e.scalar_tensor_tensor(out=out, in0=in0, scalar=scalar, in1=in1, op0=op0, op1=op1), reads, writes)

        def recip(out, in_, reads, writes):
            p.op("dve", lambda e: e.reciprocal(out=out, in_=in_), reads, writes)

        def vcopy(out, in_, reads, writes):
            p.op("dve", lambda e: e.tensor_copy(out=out, in_=in_), reads, writes)

        def vmemset(ap, val, writes):
            p.op("dve", lambda e: e.memset(ap, val), (), writes)

        wspecs = []

        def wv(ap2d):
            return ap2d.rearrange("(k p) n -> p k n", p=128)

        NMOD = [2, 1, 2, 1, 2, 1, 2, 1]
        for lidx_, li in enumerate(layers):
            kind = li % 3
            j = li // 3
            if lidx_ == 0:
                for b in range(12):
                    wspecs.append([(0, 8, 512, wv(dr["mod_w"][li, :, b * 512:(b + 1) * 512]))])
            if kind == 0:
                wq = dr["attn_w_qkv"][j]
                for b in range(2):
                    wspecs.append([(0, 8, 512, wv(wq[:, b * 512:(b + 1) * 512]))])
                sp_ = []
                for g in range(4):
                    for dup in range(2):
                        sp_.append(((g * 2 + dup) * 64, 8, 64, wv(wq[:, 1024 + g * 64:1024 + (g + 1) * 64]), 512))
                wspecs.append(sp_)
                wspecs.append([(0, 8, 256, wv(wq[:, 1280:1536]))])
                for b in range(2):
                    wspecs.append([(0, 8, 512, wv(dr["attn_w_o"][j][:, b * 512:(b + 1) * 512]))])
            elif kind == 1:
                wi = dr["conv_w_in"][0]
                for b in range(4):
                    wspecs.append([(0, 8, 256, wv(wi[:, b * 256:(b + 1) * 256]), 512),
                                   (256, 8, 256, wv(wi[:, 1024 + b * 256:1024 + (b + 1) * 256]), 512)])
                for b in range(2):
                    wspecs.append([(0, 8, 512, wv(dr["conv_w_out"][0][:, b * 512:(b + 1) * 512]))])
            else:
                wi = dr["lru_w_in"][0]
                for b in range(4):
                    wspecs.append([(0, 8, 512, wv(wi[:, b * 512:(b + 1) * 512]))])
                for d in range(2):
                    gw = dr["lru_gate_w"][0, d].rearrange("g n k e -> (g n k) e")
                    wspecs.append([(0, 16, 256, wv(gw))])
                for b in range(2):
                    wspecs.append([(0, 8, 512, wv(dr["lru_w_out"][0][:, b * 512:(b + 1) * 512]))])
            mb_ = 0
            for hb in range(8):
                wspecs.append([(0, 8, 512, wv(dr["mlp_w1"][li, :, hb * 512:(hb + 1) * 512]))])
                wspecs.append([(0, 4, 1024, wv(dr["mlp_w2"][li, hb * 512:(hb + 1) * 512, :]))])
                if lidx_ + 1 < len(layers):
                    nl_ = layers[lidx_ + 1]
                    for _ in range(NMOD[hb]):
                        wspecs.append([(0, 8, 512, wv(dr["mod_w"][nl_, :, mb_ * 512:(mb_ + 1) * 512]))])
                        mb_ += 1

        ws = K()
        ws.issued = 0
        ws.consumed = 0

        def w_issue(jb):
            s = jb % 4
            for spec in wspecs[jb]:
                if len(spec) == 5:
                    off, kcn, ncol, src, rowlen = spec
                    dst = SLOT[s][:, 0:kcn * rowlen].rearrange("p (k n) -> p k n", k=kcn)[:, :, off:off + ncol]
                else:
                    off, kcn, ncol, src = spec
                    dst = SLOT[s][:, off:off + kcn * ncol].rearrange("p (k n) -> p k n", k=kcn)
                p.dma("pool", dst, src, writes=[slotb[s]])

        ws.released = set()
        ws.pinned = set()

        def w_release(i):
            ws.pinned.discard(i)
            ws.released.add(i)

        def w_next(kcn, ncol, pin=False):
            i = ws.consumed
            if i - 1 >= 0 and (i - 1) not in ws.pinned:
                ws.released.add(i - 1)
            while ws.issued < min(i + 4, len(wspecs)) and (ws.issued < 4 or (ws.issued - 4) in ws.released):
                w_issue(ws.issued)
                ws.issued += 1
            assert ws.issued > i, "weight block not issued (pinned slot deadlock)"
            if pin:
                ws.pinned.add(i)
            ws.consumed += 1
            s = i % 4
            return SLOT[s][:, 0:kcn * ncol].rearrange("p (k n) -> p k n", k=kcn), slotb[s]

        p.dma("sp", VEC[:, :], dr["vecs"], writes=[b_vec])
        p.dma("sp", CONST[:, :], dr["consts"], writes=[b_const])

        def load_x_from_input():
            if first:
                p.dma("sp", X3[:, :, 0:NCTX], dr["ctxT"].rearrange("(c p) t -> p c t", p=128),
                      writes=[st_.xb[c][0] for c in range(8)], dsem=x_dsem)
                for c in range(8):
                    p.dma("sp", X3[:, c, NCTX:NT], dr["xT"][c * 128:(c + 1) * 128, :], writes=st_.xb[c][1:5], dsem=xc_dsem[c])
            else:
                for c in range(8):
                    p.dma("sp", X3[:, c, :], dr["xs_in"][:, c * NT:(c + 1) * NT], writes=st_.xb[c], dsem=xc_dsem[c])

        load_x_from_input()
        SC3 = SC[:, :].rearrange("p (k s) -> p k s", s=2)
        act(SC3[:, :, 0], VEC[:, VOFF["cctx"]:VOFF["cctx"] + 8], AF.Silu, [b_vec], [b_sc])
        act(SC3[:, :, 1], VEC[:, VOFF["c"]:VOFF["c"] + 8], AF.Silu, [b_vec], [b_sc])

        st_.par = 0

        def MODS():
            return MODS_[st_.par]

        def AB():
            return AB_[st_.par]

        def b_mods():
            return b_mods_[st_.par]

        def b_ab():
            return b_ab_[st_.par]

        def modcol(grp, c, s):
            return MODS()[:, (grp * 8 + c) * 2 + s:(grp * 8 + c) * 2 + s + 1]

        modst = K()
        modst.nb = 0

        def mod_block():
            b = modst.nb
            modst.nb += 1
            ps_t, ps_b = PS[7], psb[7]
            wt, wb = w_next(8, 512)
            for jj in range(4):
                jx = b * 4 + jj
                for kc in range(8):
                    mm(ps_t[:, 2 * jx:2 * jx + 2], wt[:, kc, jj * 128:(jj + 1) * 128], SC3[:, kc, :],
                       kc == 0, kc == 7, [wb, b_sc], [ps_b], inc=(jj == 3 and kc == 7))

        def mod_finish(li, par):
            assert modst.nb == 12
            modst.nb = 0
            ps_t, ps_b = PS[7], psb[7]
            M = MODS_[par]
            A = AB_[par]
            M3 = M[:, :].rearrange("p (j s) -> p j s", s=2)
            ps3 = ps_t[:, 0:96].rearrange("p (j s) -> p j s", s=2)
            mb = VEC[:, VOFF[f"modb{li}"]:VOFF[f"modb{li}"] + 48]
            for s in range(2):
                tt(M3[:, :, s], ps3[:, :, s], mb, ALU.add, [ps_b, b_vec], [b_mods_[par]])
            gm = VEC[:, VOFF[f"gmix{li}"]:VOFF[f"gmix{li}"] + 8]
            gl = VEC[:, VOFF[f"gmlp{li}"]:VOFF[f"gmlp{li}"] + 8]
            for s in range(2):
                stt(A[:, s * 8:s * 8 + 8], M3[:, 8:16, s], 1.0, gm, ALU.add, ALU.mult, [b_mods_[par], b_vec], [b_ab_[par]])
                stt(A[:, 16 + s * 8:16 + s * 8 + 8], M3[:, 32:40, s], 1.0, gl, ALU.add, ALU.mult, [b_mods_[par], b_vec], [b_ab_[par]])

        def norm_phase(which, tis):
            TB3 = TB[:, :].rearrange("p (c t) -> p c t", c=8)

            def stage_a(ti, k):
                t0, n = TCH[ti]
                for c in range(8):
                    act(TB3[:, c, 0:n], X3[:, c, t0:t0 + n], AF.Square, [st_.xb[c][ti]], [b_tb])
                ps_t, ps_b = psum()
                for c in range(8):
                    mm(ps_t[:, 0:n], ONES, TB3[:, c, 0:n], c == 0, c == 7, [b_tb, b_const], [ps_b], inc=(c == 7))
                sd, sdb = tmp32()
                act(sd[:, 0:n], ps_t[:, 0:n], AF.Sqrt, [ps_b, b_misc], [sdb], bias=MISC[:, 0:1], scale=1.0 / D)
                rs, rsb = T32H[k % 2], t32hb[k % 2]
                recip(rs[:, 0:n], sd[:, 0:n], [sdb], [rsb])
                return rs, rsb

            def stage_b(ti, rs, rsb):
                t0, n = TCH[ti]
                s = 0 if ti == 0 else 1
                for c in range(8):
                    t_, tb_ = tmp32()
                    a_ap = AB()[:, which * 16 + s * 8 + c:which * 16 + s * 8 + c + 1]
                    stt(t_[:, 0:n], X3[:, c, t0:t0 + n], a_ap, rs[:, 0:n], ALU.mult, ALU.mult,
                        [st_.xb[c][ti], b_ab(), rsb], [tb_])
                    act(H3[:, c, t0:t0 + n], t_[:, 0:n], AF.Identity, [tb_, b_mods()], [st_.hb[c][ti]],
                        bias=modcol(0 if which == 0 else 3, c, s))

            prev = None
            for k, ti in enumerate(tis):
                cur = (ti,) + stage_a(ti, k)
                if prev is not None:
                    stage_b(*prev)
                prev = cur
            stage_b(*prev)

        def spill_issue(li_index):
            if li_index == 0:
                return None
            tok = None
            for c in range(8):
                tok = p.dma("sp", xs[:, c * NT:(c + 1) * NT], X3[:, c, :], reads=st_.xb[c], writes=[spill_buf])
            return tok

        def spill(tok):
            p.barrier(("pe", "act", "dve", "sp"))
            if tok is not None:
                for e in ("pe", "act", "dve", "sp", "pool"):
                    p._wait(e, tok)
            return tok

        def reload(li_index):
            p.barrier(("pe", "act", "dve", "sp"))
            st_.xb = grid(f"x{li_index}_")
            allb = [b for row in st_.xb for b in row]
            if li_index == 0:
                load_x_from_input()
            else:
                for c in range(8):
                    p.dma("sp", X3[:, c, :], xs[:, c * NT:(c + 1) * NT], reads=[spill_buf], writes=st_.xb[c], dsem=xc_dsem[c])

        def linear(nblocks, ocs_per_block, kcn, wcols, lhs_fn, rhs_fn, rhs_bufs_fn, tis, evac, psgroup=None):
            for b in range(nblocks):
                wt, wb = w_next(kcn, wcols)
                for ocl in range(ocs_per_block):
                    for ti in tis:
                        t0, n = TCH[ti]
                        ps_t, ps_b = psum(psgroup)
                        for kc in range(kcn):
                            mm(ps_t[:, 0:n], lhs_fn(wt, ocl, kc), rhs_fn(kc, t0, n), kc == 0, kc == kcn - 1,
                               [wb] + rhs_bufs_fn(kc, ti), [ps_b], inc=(kc == kcn - 1))
                        evac(b, ocl, ti, ps_t[:, 0:n], ps_b)

        def resid_evac(grp, tis_all, bias_name=None):
            def ev(b, ocl, ti, ps_ap, ps_b):
                oc = b * 4 + ocl
                t0, n = TCH[ti]
                s = 0 if ti == 0 else 1
                src = ps_ap
                rd = [ps_b]
                if bias_name is not None:
                    t_, tb_ = tmp32()
                    act(t_[:, 0:n], ps_ap, AF.Identity, [ps_b, b_vec], [tb_], bias=vcol(bias_name, oc))
                    src = t_[:, 0:n]
                    rd = [tb_]
                stt(X3[:, oc, t0:t0 + n], src, modcol(grp, oc, s), X3[:, oc, t0:t0 + n], ALU.mult, ALU.add,
                    rd + [b_mods(), st_.xb[oc][ti]], [st_.xb[oc][ti]])
            return ev

        def hb_rhs(kc, t0, n):
            return H3[:, kc, t0:t0 + n]

        def hb_bufs(kc, ti):
            return [st_.hb[kc][ti]]

        def mlp_phase(li, tis, next_li=None, next_par=None):
            HID = AUX[:, 0:4 * NT].rearrange("p (c t) -> p c t", c=4)
            hidb = [[p.buf() for _ in range(5)] for _ in range(4)]
            for hb_i in range(8):
                w1, w1b = w_next(8, 512)
                for ti in tis:
                    t0, n = TCH[ti]
                    for ocl in range(4):
                        ps_t, ps_b = psum()
                        for kc in range(8):
                            mm(ps_t[:, 0:n], w1[:, kc, ocl * 128:(ocl + 1) * 128], H3[:, kc, t0:t0 + n], kc == 0, kc == 7,
                               [w1b, st_.hb[kc][ti]], [ps_b], inc=(kc == 7))
                        t_, tb_ = tmp32()
                        act(t_[:, 0:n], ps_t[:, 0:n], AF.Relu, [ps_b], [tb_])
                        tt(HID[:, ocl, t0:t0 + n], t_[:, 0:n], t_[:, 0:n], ALU.mult, [tb_], [hidb[ocl][ti]])
                w2, w2b = w_next(4, 1024)
                for ti in tis:
                    t0, n = TCH[ti]
                    s = 0 if ti == 0 else 1
                    for oc in range(8):
                        ps_t, ps_b = psum()
                        for kc in range(4):
                            mm(ps_t[:, 0:n], w2[:, kc, oc * 128:(oc + 1) * 128], HID[:, kc, t0:t0 + n], kc == 0, kc == 3,
                               [w2b, hidb[kc][ti]], [ps_b], inc=(kc == 3))
                        stt(X3[:, oc, t0:t0 + n], ps_t[:, 0:n], modcol(5, oc, s), X3[:, oc, t0:t0 + n], ALU.mult, ALU.add,
                            [ps_b, b_mods(), st_.xb[oc][ti]], [st_.xb[oc][ti]])
                if next_li is not None:
                    for _ in range(NMOD[hb_i]):
                        mod_block()
            if next_li is not None:
                mod_finish(next_li, next_par)

        def attention(li, j_att, need_ctx, li_index):
            QT = XRb[:, 0:18432].rearrange("p (c t) -> p c t", c=8)
            KT2 = XRb[:, 18432:27648].rearrange("p (c t) -> p c t", c=4)
            ROC = XR[:, 13824:15872]
            ROS = XR[:, 15872:17920]
            VA = AUX[:, 0:18 * 576].rearrange("p (k x) -> p k x", k=18)
            b_rope = p.buf(f"rope{li}", dma=True)
            qb = [[p.buf() for _ in range(5)] for _ in range(8)]
            kb = [[p.buf() for _ in range(5)] for _ in range(4)]
            vab = [p.buf() for _ in range(18)]
            b_va_init = p.buf()
            p.dma("sp", XR[:, 13824:17920], dr["rope"], writes=[b_rope])
            vmemset(AUX[:, 0:18 * 576], 1.0, [b_va_init] + vab)
            q_tis = [0, 1, 2, 3, 4] if need_ctx else [1, 2, 3, 4]

            PS_A = [0, 1, 2]
            PS_B = [3, 4]
            PS_C = [5, 6]
            pending = []

            def qk_item(ps_t, ps_b, n, ti, gain_ap, dst_ap, dst_buf):
                t0 = TCH[ti][0]
                lat = ti != 0
                state = {}

                def stage_b():
                    sq, sqb = tmppt()
                    act(sq[:, 0:n], ps_t[:, 0:n], AF.Square, [ps_b], [sqb])
                    ss_t, ss_b = psum(PS_B)
                    mm(ss_t[:, 0:n], ONESBLK, sq[:, 0:n], True, True, [sqb, b_const], [ss_b], inc=True)
                    sd, sdb = tmp32()
                    act(sd[:, 0:n], ss_t[:, 0:n], AF.Sqrt, [ss_b, b_misc], [sdb], bias=MISC[:, 0:1], scale=1.0 / 64)
                    rs, rsb = tmp32()
                    recip(rs[:, 0:n], sd[:, 0:n], [sdb], [rsb])
                    if not lat:
                        stt(dst_ap, ps_t[:, 0:n], gain_ap, rs[:, 0:n], ALU.mult, ALU.mult, [ps_b, rsb, b_vec], [dst_buf])
                    else:
                        qn, qnb = tmppt()
                        stt(qn[:, 0:n], ps_t[:, 0:n], gain_ap, rs[:, 0:n], ALU.mult, ALU.mult, [ps_b, rsb, b_vec], [qnb])
                        state["qn"] = (qn, qnb)

                def stage_c():
                    if not lat:
                        return
                    qn, qnb = state["qn"]
                    rot_t, rot_b = psum(PS_C)
                    mm(rot_t[:, 0:n], PERM, qn[:, 0:n], True, True, [qnb, b_const], [rot_b], inc=True)
                    t1, t1b = tmp32()
                    tt(t1[:, 0:n], qn[:, 0:n], ROC[:, t0 - NCTX:t0 - NCTX + n], ALU.mult, [qnb, b_rope], [t1b], eng="pool")
                    t2, t2b = tmp32()
                    tt(t2[:, 0:n], rot_t[:, 0:n], ROS[:, t0 - NCTX:t0 - NCTX + n], ALU.mult, [rot_b, b_rope], [t2b])
                    tt(dst_ap, t1[:, 0:n], t2[:, 0:n], ALU.add, [t1b, t2b], [dst_buf], eng="pool")
                return stage_b, stage_c

            def pipe_push(item):
                pending.append(item)
                if len(pending) >= 2:
                    pending[-2][0]()
                if len(pending) >= 3:
                    pending[-3][1]()

            def pipe_flush():
                if len(pending) >= 1:
                    pending[-1][0]()
                if len(pending) >= 2:
                    pending[-2][1]()
                if len(pending) >= 1:
                    pending[-1][1]()
                pending.clear()

            for b in range(2):
                wt, wb = w_next(8, 512)
                for ocl in range(4):
                    oc = b * 4 + ocl
                    for ti in q_tis:
                        t0, n = TCH[ti]
                        ps_t, ps_b = psum(PS_A)
                        for kc in range(8):
                            mm(ps_t[:, 0:n], wt[:, kc, ocl * 128:(ocl + 1) * 128], H3[:, kc, t0:t0 + n], kc == 0, kc == 7,
                               [wb, st_.hb[kc][ti]], [ps_b], inc=(kc == 7))
                        pipe_push(qk_item(ps_t, ps_b, n, ti, vcol(f"qg{j_att}"), QT[:, oc, t0:t0 + n], qb[oc][ti]))
            wt, wb = w_next(8, 512)
            for g in range(4):
                for ti in range(5):
                    t0, n = TCH[ti]
                    ps_t, ps_b = psum(PS_A)
                    for kc in range(8):
                        mm(ps_t[:, 0:n], wt[:, kc, g * 128:(g + 1) * 128], H3[:, kc, t0:t0 + n], kc == 0, kc == 7,
                           [wb, st_.hb[kc][ti]], [ps_b], inc=(kc == 7))
                    pipe_push(qk_item(ps_t, ps_b, n, ti, vcol(f"kg{j_att}"), KT2[:, g, t0:t0 + n], kb[g][ti]))
            pipe_flush()
            wt, wb = w_next(8, 256)
            for kt in range(18):
                ti = 0 if kt < 2 else 1 + (kt - 2) // 4
                ps_t, ps_b = psum(PS_A)
                for kc in range(8):
                    mm(ps_t[:, 0:256], H3[:, kc, kt * 128:(kt + 1) * 128], wt[:, kc, :], kc == 0, kc == 7,
                       [wb, st_.hb[kc][ti]], [ps_b], inc=(kc == 7))
                dst = VA[:, kt, 64:576].rearrange("p (g x) -> p g x", x=128)[:, :, 0:64]
                src = ps_t[:, 0:256].rearrange("p (g x) -> p g x", x=64)
                act(dst, src, AF.Identity, [ps_b], [vab[kt]])

            OBANK = [[0, 1], [2, 3]]
            ptdb = [p.buf() for _ in range(4)]
            it = 0
            for jp in range(8):
                g = jp // 2
                for ti in q_tis:
                    t0, n = TCH[ti]
                    kts = list(range(18)) if ti != 0 else [0, 1]
                    ob = OBANK[it % 2]
                    it += 1
                    o_t = [PS[ob[0]], PS[ob[1]]]
                    o_b = [psb[ob[0]], psb[ob[1]]]

                    def s_stage(kt):
                        tik = 0 if kt < 2 else 1 + (kt - 2) // 4
                        di = st_.rr.get("psd", 0) % 2
                        st_.rr["psd"] = st_.rr.get("psd", 0) + 1
                        dt_ = PSD[di]
                        dbs = [psb[4 + 2 * di], psb[5 + 2 * di]]
                        for h in range(2):
                            mm(dt_[:, h * 512:h * 512 + n], KT2[h * 64:(h + 1) * 64, g, kt * 128:(kt + 1) * 128],
                               QT[h * 64:(h + 1) * 64, jp, t0:t0 + n], True, True,
                               [kb[g][tik], qb[jp][ti]], [dbs[h]], inc=(h == 1))
                        return dt_, dbs

                    def pv_stage(kt, sres):
                        dt_, dbs = sres
                        pi = st_.rr.get("ptd", 0) % 4
                        st_.rr["ptd"] = st_.rr.get("ptd", 0) + 1
                        ptd = TB[:, pi * 1024:(pi + 1) * 1024]
                        src = dt_[:, :].rearrange("p (h x) -> p h x", h=2)[:, :, 0:n]
                        dst = ptd.rearrange("p (h x) -> p h x", h=2)[:, :, 0:n]
                        act(dst, src, AF.Exp, dbs, [ptdb[pi]], scale=0.125)
                        for h in range(2):
                            if h == 0:
                                lhs = VA[:, kt, 64 + 128 * g:192 + 128 * g]
                            else:
                                lhs = VA[:, kt, 128 * g:128 + 128 * g]
                            mm(o_t[h][:, 0:n], lhs, ptd[:, h * 512:h * 512 + n], kt == kts[0], kt == kts[-1],
                               [ptdb[pi], vab[kt]], [o_b[h]], inc=True)

                    prev = s_stage(kts[0])
                    for idx, kt in enumerate(kts):
                        nxt = s_stage(kts[idx + 1]) if idx + 1 < len(kts) else None
                        pv_stage(kt, prev)
                        prev = nxt
                    for h in range(2):
                        rc, rcb = tmp32()
                        recip(rc[:, 0:n], o_t[h][:, 0:n], [o_b[h]], [rcb])
                        lo, hi = (0, 64) if h == 0 else (64, 128)
                        dlo, dhi = (64, 128) if h == 0 else (0, 64)
                        tt(H3[lo:hi, jp, t0:t0 + n], o_t[h][lo:hi, 0:n], rc[dlo:dhi, 0:n], ALU.mult,
                           [o_b[h], rcb], [st_.hb[jp][ti]])
            dbg_dump("att_xr", XR[:, :], [128, 8 * NT], F32)
            dbg_dump("att_aux", AUX[:, :], [128, 10368], BF16)
            dbg_dump("att_hb", HBt[:, :], [128, 8 * NT], BF16)
            reload(li_index)
            linear(2, 4, 8, 512, lambda wt, ocl, kc: wt[:, kc, ocl * 128:(ocl + 1) * 128], hb_rhs, hb_bufs,
                   q_tis, resid_evac(2, q_tis))

        def conformer(li, need_ctx, li_index):
            UC = XRb[:, 0:8 * 286].rearrange("p (c t) -> p c t", c=8)
            UL = XRb[:, 2288:2288 + 8 * 2078].rearrange("p (c t) -> p c t", c=8)
            DG = [XRb[:, 18912 + i * 3968:18912 + (i + 1) * 3968].rearrange("p (k m) -> p k m", k=31) for i in range(2)]
            ub = [[p.buf() for _ in range(5)] for _ in range(8)]
            upad = p.buf()
            dgb = [p.buf(), p.buf()]
            vmemset(XRb[:, 0:18912], 0.0, [upad] + [b for row in ub for b in row])
            tis = [0, 1, 2, 3, 4]

            def useg(c, ti, k, n):
                if ti == 0:
                    return UC[:, c, k:k + n]
                o = TCH[ti][0] - NCTX
                return UL[:, c, o + k:o + k + n]

            for b in range(4):
                wt, wb = w_next(8, 512)
                for cl in range(2):
                    c = b * 2 + cl
                    for ti in tis:
                        t0, n = TCH[ti]
                        pa_t, pa_b = psum()
                        for kc in range(8):
                            mm(pa_t[:, 0:n], wt[:, kc, cl * 128:(cl + 1) * 128], H3[:, kc, t0:t0 + n], kc == 0, kc == 7,
                               [wb, st_.hb[kc][ti]], [pa_b], inc=(kc == 7))
                        pg_t, pg_b = psum()
                        for kc in range(8):
                            mm(pg_t[:, 0:n], wt[:, kc, 256 + cl * 128:256 + (cl + 1) * 128], H3[:, kc, t0:t0 + n], kc == 0, kc == 7,
                               [wb, st_.hb[kc][ti]], [pg_b], inc=(kc == 7))
                        sg, sgb = tmp32()
                        act(sg[:, 0:n], pg_t[:, 0:n], AF.Sigmoid, [pg_b, b_vec], [sgb], bias=vcol("cbin", 8 + c))
                        stt(useg(c, ti, 15, n), pa_t[:, 0:n], vcol("cbin", c), sg[:, 0:n], ALU.add, ALU.mult,
                            [pa_b, sgb, b_vec, upad], [ub[c][ti]])
            vb = [[p.buf() for _ in range(5)] for _ in range(8)]
            for c in range(8):
                par = c % 2
                for k in range(31):
                    ts(DG[par][:, k, :], IDENT, vcol("cwdw", k * 8 + c), None, ALU.mult, None, [b_const, b_vec], [dgb[par]])
                for ti in tis:
                    t0, n = TCH[ti]
                    ps_t, ps_b = psum()
                    nb = [ub[c][ti]]
                    if ti > 1:
                        nb.append(ub[c][ti - 1])
                    if 1 <= ti < 4:
                        nb.append(ub[c][ti + 1])
                    for k in range(31):
                        mm(ps_t[:, 0:n], DG[par][:, k, :], useg(c, ti, k, n), k == 0, k == 30,
                           [dgb[par], upad] + nb, [ps_b], inc=(k == 30))
                    act(H3[:, c, t0:t0 + n], ps_t[:, 0:n], AF.Identity, [ps_b, b_vec],
                        [vb[c][ti], st_.hb[c][ti]], bias=vcol("cbdw", c))
            TB3 = TB[:, :].rearrange("p (c t) -> p c t", c=8)
            yb = [[p.buf() for _ in range(5)] for _ in range(8)]
            def ln_a(ti, k):
                t0, n = TCH[ti]
                pm_t, pm_b = psum()
                for c in range(8):
                    mm(pm_t[:, 0:n], ONES, H3[:, c, t0:t0 + n], c == 0, c == 7, [vb[c][ti], b_const], [pm_b], inc=(c == 7))
                for c in range(8):
                    act(TB3[:, c, 0:n], H3[:, c, t0:t0 + n], AF.Square, [vb[c][ti]], [b_tb])
                pq_t, pq_b = psum()
                for c in range(8):
                    mm(pq_t[:, 0:n], ONES, TB3[:, c, 0:n], c == 0, c == 7, [b_tb, b_const], [pq_b], inc=(c == 7))
                mean, meanb = (T32H[1], t32hb[1]) if k % 2 == 0 else (T32X[1], t32xb[1])
                act(mean[:, 0:n], pm_t[:, 0:n], AF.Identity, [pm_b], [meanb], scale=1.0 / D)
                m2, m2b = tmp32()
                tt(m2[:, 0:n], mean[:, 0:n], mean[:, 0:n], ALU.mult, [meanb], [m2b])
                var, varb = tmp32()
                stt(var[:, 0:n], pq_t[:, 0:n], 1.0 / D, m2[:, 0:n], ALU.mult, ALU.subtract, [pq_b, m2b], [varb])
                sd, sdb = tmp32()
                act(sd[:, 0:n], var[:, 0:n], AF.Sqrt, [varb, b_misc], [sdb], bias=MISC[:, 0:1], scale=1.0)
                rs, rsb = (T32H[0], t32hb[0]) if k % 2 == 0 else (T32X[0], t32xb[0])
                recip(rs[:, 0:n], sd[:, 0:n], [sdb], [rsb])
                return mean, meanb, rs, rsb

            def ln_b(ti, mean, meanb, rs, rsb):
                t0, n = TCH[ti]
                for c in range(8):
                    t_, tb_ = tmp32()
                    tt(t_[:, 0:n], H3[:, c, t0:t0 + n], mean[:, 0:n], ALU.subtract, [vb[c][ti], meanb], [tb_])
                    tt(t_[:, 0:n], t_[:, 0:n], rs[:, 0:n], ALU.mult, [tb_, rsb], [tb_])
                    act(H3[:, c, t0:t0 + n], t_[:, 0:n], AF.Silu, [tb_, b_vec], [yb[c][ti], vb[c][ti]],
                        bias=vcol("cnb", c), scale=vcol("cng", c))

            prev = None
            for k, ti in enumerate(tis):
                cur = (ti,) + ln_a(ti, k)
                if prev is not None:
                    ln_b(*prev)
                prev = cur
            ln_b(*prev)
            st_.hb = yb
            reload(li_index)
            linear(2, 4, 8, 512, lambda wt, ocl, kc: wt[:, kc, ocl * 128:(ocl + 1) * 128], hb_rhs, hb_bufs,
                   tis, resid_evac(2, tis, bias_name="cbout"))

        st_.rr["tbx"] = 0

        def tmppt32():
            i = st_.rr["tbx"] % 2
            st_.rr["tbx"] += 1
            return T32X[i], t32xb[i]

        def rglru(li, need_ctx, li_index):
            G3 = XRb[:, 0:18432].rearrange("p (c t) -> p c t", c=8)
            XL3 = XRb[:, 18432:36864].rearrange("p (c t) -> p c t", c=8)
            gb = [[p.buf() for _ in range(5)] for _ in range(8)]
            xlb = [[p.buf() for _ in range(5)] for _ in range(8)]
            tis = [0, 1, 2, 3, 4]
            for b in range(4):
                wt, wb = w_next(8, 512)
                for ocl in range(4):
                    oc = b * 4 + ocl
                    for ti in tis:
                        t0, n = TCH[ti]
                        ps_t, ps_b = psum()
                        for kc in range(8):
                            mm(ps_t[:, 0:n], wt[:, kc, ocl * 128:(ocl + 1) * 128], H3[:, kc, t0:t0 + n], kc == 0, kc == 7,
                               [wb, st_.hb[kc][ti]], [ps_b], inc=(kc == 7))
                        if oc < 8:
                            act(G3[:, oc, t0:t0 + n], ps_t[:, 0:n], AF.Gelu_apprx_tanh, [ps_b], [gb[oc][ti]])
                        else:
                            vcopy(XL3[:, oc - 8, t0:t0 + n], ps_t[:, 0:n], [ps_b], [xlb[oc - 8][ti]])
            lam = VEC[:, VOFF["llam"]:VOFF["llam"] + 16]
            b_ca = p.buf()
            act(MISC[:, 64:80], lam, AF.Exp, [b_vec], [b_ca], scale=-1.0)
            act(MISC[:, 80:96], MISC[:, 64:80], AF.Ln, [b_ca, b_misc], [b_ca], bias=MISC[:, 1:2], scale=1.0)
            ts(MISC[:, 16:32], MISC[:, 80:96], -8.0, None, ALU.mult, None, [b_ca], [b_ca])
            ts(MISC[:, 48:64], MISC[:, 80:96], -4.0, None, ALU.mult, None, [b_ca], [b_ca])
            ts(MISC[:, 96:128], VEC[:, VOFF["lgb"]:VOFF["lgb"] + 32], 0.5, None, ALU.mult, None, [b_vec, b_ca], [b_ca])
            dbg_dump("lru_xr0", XR[:, :], [128, 8 * NT], F32)
            p.barrier(("pe", "act", "dve"))
            U32 = HBf[:, 0:4608].rearrange("p (c t) -> p c t", c=2)
            HS = HBf[:, 4608:9216].rearrange("p (c t) -> p c t", c=2)
            UBF = AUX[:, 0:4608].rearrange("p (c t) -> p c t", c=2)
            XT = [AUXf[:, 2304 + i * 512:2304 + (i + 1) * 512] for i in range(5)]
            xtb = [p.buf() for _ in range(5)]
            CAR = MISC[:, 128:256]
            car_i = [0]
            rrx = [0]

            def ltmp():
                i = rrx[0] % 15
                rrx[0] += 1
                if i < 5:
                    return XT[i], xtb[i]
                if i < 11:
                    return T32[i - 5], t32b[i - 5]
                if i < 13:
                    return T32X[i - 11], t32xb[i - 11]
                return T32H[i - 13], t32hb[i - 13]

            gslots = []
            gidx = []
            for d in range(2):
                gidx.append(ws.consumed)
                gslots.append(w_next(16, 256, pin=True))
            SEGS = [(0, NCTX), (NCTX, NLAT)]
            u32b = [p.buf(), p.buf()]
            ubfb = [p.buf(), p.buf()]
            hsb = [[p.buf() for _ in range(5)] for _ in range(2)]
            for nblk in range(4):
                c0 = nblk * 2
                for d in range(2):
                    gwt, gwb = gslots[d]
                    for cl in range(2):
                        c = c0 + cl
                        xall = xlb[c]
                        for (s0, sn) in SEGS:
                            ts(U32[:, cl, s0:s0 + sn], XL3[:, c, s0:s0 + sn], vcol("lcw", (d * 4 + 3) * 8 + c), vcol("lcb", d * 8 + c),
                               ALU.mult, ALU.add, xall + [b_vec], [u32b[cl]], eng="pool")
                            for k in range(3):
                                sh = 3 - k
                                if d == 0:
                                    o_ap = U32[:, cl, s0 + sh:s0 + sn]
                                    i_ap = XL3[:, c, s0:s0 + sn - sh]
                                else:
                                    o_ap = U32[:, cl, s0:s0 + sn - sh]
                                    i_ap = XL3[:, c, s0 + sh:s0 + sn]
                                stt(o_ap, i_ap, vcol("lcw", (d * 4 + k) * 8 + c), o_ap, ALU.mult, ALU.add,
                                    xall + [b_vec, u32b[cl]], [u32b[cl]])
                        act(UBF[:, cl, :], U32[:, cl, :], AF.Identity, [u32b[cl]], [ubfb[cl]])
                    for cl in range(2):
                        c = c0 + cl
                        groups = [[0, 1, 2], [3, 4]] if d == 0 else [[0, 4, 3], [2, 1]]
                        prev_car = None
                        oi = -1
                        c4 = MISC[:, 48 + d * 8 + c:49 + d * 8 + c]
                        for grp in groups:
                            items = []
                            for ti in grp:
                                t0, n = TCH[ti]
                                gps = []
                                for gi in range(2):
                                    ps_t, ps_b = psum()
                                    for kc in range(2):
                                        mm(ps_t[:, 0:n], gwt[:, (gi * 4 + nblk) * 2 + kc, cl * 128:(cl + 1) * 128], UBF[:, kc, t0:t0 + n],
                                           kc == 0, kc == 1, [gwb, ubfb[kc]], [ps_b], inc=(kc == 1))
                                    gps.append((ps_t, ps_b))
                                r_, rb_ = ltmp()
                                act(r_[:, 0:n], gps[0][0][:, 0:n], AF.Tanh, [gps[0][1], b_ca], [rb_],
                                    bias=MISC[:, 96 + (d * 2 + 0) * 8 + c:97 + (d * 2 + 0) * 8 + c], scale=0.5)
                                a_, ab_ = ltmp()
                                act(a_[:, 0:n], r_[:, 0:n], AF.Exp, [rb_, b_ca], [ab_], bias=c4, scale=c4)
                                i_, ib_ = ltmp()
                                act(i_[:, 0:n], gps[1][0][:, 0:n], AF.Tanh, [gps[1][1], b_ca], [ib_],
                                    bias=MISC[:, 96 + (d * 2 + 1) * 8 + c:97 + (d * 2 + 1) * 8 + c], scale=0.5)
                                stt(i_[:, 0:n], i_[:, 0:n], 1.0, U32[:, cl, t0:t0 + n], ALU.add, ALU.mult, [ib_, u32b[cl]], [ib_])
                                items.append((ti, a_, ab_, i_, ib_))
                            for (ti, a_, ab_, i_, ib_) in items:
                                oi += 1
                                t0, n = TCH[ti]
                                m_, mb_ = ltmp()
                                act(m_[:, 0:n], a_[:, 0:n], AF.Square, [ab_], [mb_])
                                act(m_[:, 0:n], m_[:, 0:n], AF.Sqrt, [mb_, b_misc], [mb_], bias=MISC[:, 1:2], scale=-1.0)
                                if ti == 0:
                                    fc = 0 if d == 0 else NCTX - 1
                                    vmemset(m_[:, fc:fc + 1], 1.0, [mb_])
                                stt(i_[:, 0:n], i_[:, 0:n], 0.5, m_[:, 0:n], ALU.mult, ALU.mult, [ib_, mb_], [ib_])
                                if d == 0:
                                    init = 0.0 if oi == 0 else HS[:, cl, t0 - 1:t0]
                                    rd = [ab_, ib_] + ([hsb[cl][ti - 1]] if oi > 0 else [])
                                    p.op("dve", lambda e, o=HS[:, cl, t0:t0 + n], a=a_[:, 0:n], b=i_[:, 0:n], init=init:
                                         e.tensor_tensor_scan(out=o, data0=a, data1=b, initial=init, op0=ALU.mult, op1=ALU.add),
                                         rd, [hsb[cl][ti]])
                                else:
                                    h_, hb_ = ltmp()
                                    init = 0.0 if oi == 0 else prev_car
                                    p.op("dve", lambda e, o=h_[:, 0:n][:, ::-1], a=a_[:, 0:n][:, ::-1], b=i_[:, 0:n][:, ::-1], init=init:
                                         e.tensor_tensor_scan(out=o, data0=a, data1=b, initial=init, op0=ALU.mult, op1=ALU.add),
                                         [ab_, ib_, b_car], [hb_])
                                    ci = car_i[0] % 128
                                    car_i[0] += 1
                                    vcopy(CAR[:, ci:ci + 1], h_[:, 0:1], [hb_], [b_car])
                                    prev_car = CAR[:, ci:ci + 1]
                                    tt(h_[:, 0:n], h_[:, 0:n], HS[:, cl, t0:t0 + n], ALU.add, [hb_, hsb[cl][ti]], [hb_], eng="pool")
                                    tt(G3[:, c, t0:t0 + n], h_[:, 0:n], G3[:, c, t0:t0 + n], ALU.mult, [hb_, gb[c][ti]], [gb[c][ti]], eng="pool")
            for gi_ in gidx:
                w_release(gi_)
            dbg_dump("lru_xr1", XR[:, :], [128, 8 * NT], F32)
            p.barrier(("pe", "act", "dve"))
            yb = [[p.buf() for _ in range(5)] for _ in range(8)]
            for c in range(8):
                for ti in tis:
                    t0, n = TCH[ti]
                    if (c + ti) % 2 == 0:
                        vcopy(H3[:, c, t0:t0 + n], G3[:, c, t0:t0 + n], [gb[c][ti]], [yb[c][ti]])
                    else:
                        act(H3[:, c, t0:t0 + n], G3[:, c, t0:t0 + n], AF.Identity, [gb[c][ti]], [yb[c][ti]])
            st_.hb = yb
            reload(li_index)
            linear(2, 4, 8, 512, lambda wt, ocl, kc: wt[:, kc, ocl * 128:(ocl + 1) * 128], hb_rhs, hb_bufs,
                   tis, resid_evac(2, tis))

        vmemset(MISC[:, 0:1], EPS, [b_misc])
        vmemset(MISC[:, 1:2], 1.0, [b_misc])
        for idx, li in enumerate(layers):
            kind = li % 3
            need_ctx = li < DEPTH - 1
            st_.par = idx % 2
            if idx == 0:
                for _ in range(12):
                    mod_block()
                mod_finish(li, 0)
            st_.hb = grid(f"h{li}_")
            sp_tok = spill_issue(0 if idx == 0 else 1)
            norm_phase(0, [0, 1, 2, 3, 4])
            if DBG and idx == 0:
                p.dma("sp", dbg_mods, MODS()[:, :], reads=[b_mods()], writes=[o_buf])
                p.dma("sp", dbg_ab, AB()[:, :], reads=[b_ab()], writes=[o_buf])
                p.dma("sp", dbg_h1, HBt[:, :], reads=[b for row in st_.hb for b in row], writes=[o_buf])
            first_in_prog = (idx == 0)
            spill(sp_tok)
            lidx = 0 if first_in_prog else 1
            import os as _os
            if _os.environ.get("DBG_SKIP_MIX"):
                for _ in range({0: 6, 1: 6, 2: 8}[kind]):
                    w_next(8, 512)
                reload(lidx)
            elif kind == 0:
                attention(li, li // 3, need_ctx, lidx)
            elif kind == 1:
                conformer(li, need_ctx, lidx)
            else:
                rglru(li, need_ctx, lidx)
            tis = [0, 1, 2, 3, 4] if need_ctx else [1, 2, 3, 4]
            p.barrier(("pe", "act", "dve"))
            st_.hb = grid(f"h2{li}_")
            if _os.environ.get("DBG_SKIP_MLP"):
                for _ in range(16 + (12 if idx + 1 < len(layers) else 0)):
                    w_next(8, 512)
            else:
                if DBG and idx == 0:
                    p.dma("sp", dbg_x1, XR[:, :], reads=[b for row in st_.xb for b in row], writes=[o_buf])
                norm_phase(1, tis)
                if DBG and idx == 0:
                    p.dma("sp", dbg_h2, HBt[:, :], reads=[b for row in st_.hb for b in row], writes=[o_buf])
                if idx + 1 < len(layers):
                    mlp_phase(li, tis, layers[idx + 1], (idx + 1) % 2)
                else:
                    mlp_phase(li, tis)
        allb = [b for row in st_.xb for b in row]
        if last:
            for c in range(8):
                p.dma("sp", outT[c * 128:(c + 1) * 128, :], X3[:, c, NCTX:NT], reads=allb, writes=[o_buf])
        else:
            p.dma("sp", xs_out, XR[:, :], reads=allb, writes=[o_buf])
        p._wait("sp", o_buf.w)
        assert ws.consumed == len(wspecs), (ws.consumed, len(wspecs))
        p.emit()
    return nc


_WKEYS = ["mod_w", "mlp_w1", "mlp_w2", "attn_w_qkv", "attn_w_o", "conv_w_in", "conv_w_out",
          "lru_w_in", "lru_gate_w", "lru_w_out"]


def _common_maps(inputs):
    m = {k: np.ascontiguousarray(np.asarray(inputs[k], np.float32)) for k in _WKEYS}
    m["consts"] = _consts()
    m["rope"] = _rope_tables()
    return m


def kernel(**inputs):
    n = 8
    common = _common_maps(inputs)
    x = np.asarray(inputs["x"], np.float32)
    ctx = np.asarray(inputs["ctx"], np.float32)
    in_maps = []
    for b in range(n):
        m = dict(common)
        m["xT"] = np.ascontiguousarray(x[b].T)
        m["ctxT"] = np.ascontiguousarray(ctx[b].T)
        m["vecs"] = _pack_vecs(inputs, b)
        in_maps.append(m)
    nc = build_program((0, 1, 2, 3), True, True)
    res = run_bass_kernel_spmd(nc, in_maps, core_ids=list(range(n)))
    out = np.stack([np.ascontiguousarray(res.results[b]["outT"].T) for b in range(n)], axis=0)
    return out.astype(np.float32)
```

```python
import numpy as np
from contextlib import ExitStack
import concourse.bass as bass
import concourse.mybir as mybir
from concourse.bass_utils import run_bass_kernel_spmd
import ml_dtypes

F32 = mybir.dt.float32
BF16 = mybir.dt.bfloat16
AF = mybir.ActivationFunctionType
ALU = mybir.AluOpType

SELF_WAIT = True

NT, NCTX, NLAT, D = 2304, 256, 2048, 1024
TCH = [(0, 256), (256, 512), (768, 512), (1280, 512), (1792, 512)]
EPS = 1e-6
DEPTH = 4


class Sem:
    def __init__(self, h, name):
        self.h = h
        self.name = name
        self.count = 0


class Buf:
    __slots__ = ("name", "w", "r", "dsem")

    def __init__(self, name, dsem=None):
        self.name = name
        self.w = None
        self.r = []
        self.dsem = dsem


class Prog:
    ENG = ("pe", "act", "dve", "pool", "sp")

    def __init__(self, nc, stack):
        self.nc = nc
        self.stack = stack
        self.ops = {e: [] for e in self.ENG}
        self.esem = {}
        for e in ("pe", "act", "dve", "pool"):
            self.esem[e] = self.new_sem("s_" + e)
        self.known = {e: {} for e in self.ENG}
        self.pending_noinc = {e: False for e in self.ENG}
        self.nbuf = 0

    def new_sem(self, name):
        h = self.stack.enter_context(self.nc.semaphore(name))
        return Sem(h, name)

    def buf(self, name=None, dma=False):
        self.nbuf += 1
        name = name or f"b{self.nbuf}"
        return Buf(name, self.new_sem("d_" + name) if dma else None)

    def sb(self, name, shape, dt):
        return self.stack.enter_context(self.nc.sbuf_tensor(name, list(shape), dt))

    def ps(self, name, shape, dt=F32):
        return self.stack.enter_context(self.nc.psum_tensor(name, list(shape), dt))

    def _wait(self, eng, tok):
        if tok is None:
            return
        sem, val = tok
        if eng == "pe" and sem is self.esem["pe"]:
            return
        if (not SELF_WAIT) and eng in self.esem and sem is self.esem[eng]:
            return
        k = self.known[eng]
        if k.get(sem, 0) >= val:
            return
        k[sem] = val
        h = sem.h
        self.ops[eng].append(lambda e, h=h, val=val: e.wait_ge(h, val))

    def _deps(self, eng, reads, writes):
        for b in reads:
            self._wait(eng, b.w)
        for b in writes:
            self._wait(eng, b.w)
            for t in b.r:
                self._wait(eng, t)

    def _commit(self, tok, reads, writes):
        for b in reads:
            b.r.append(tok)
            if len(b.r) > 16:
                d = {}
                for s, v in b.r:
                    if d.get(s, 0) < v:
                        d[s] = v
                b.r = list(d.items())
        for b in writes:
            b.w = tok
            b.r = []

    def op(self, eng, fn, reads=(), writes=(), inc=True):
        self._deps(eng, reads, writes)
        sem = self.esem[eng]
        if inc:
            sem.count += 1
            val = sem.count
            h = sem.h
            self.ops[eng].append(lambda e, fn=fn, h=h: fn(e).then_inc(h, 1))
            self.pending_noinc[eng] = False
        else:
            val = sem.count + 1
            self.ops[eng].append(lambda e, fn=fn: fn(e))
            self.pending_noinc[eng] = True
        tok = (sem, val)
        self._commit(tok, reads, writes)
        return tok

    def dma(self, q, out, in_, reads=(), writes=(), dsem=None):
        self._deps(q, reads, writes)
        sem = dsem if dsem is not None else writes[0].dsem
        sem.count += 16
        val = sem.count
        h = sem.h
        self.ops[q].append(lambda e, out=out, in_=in_, h=h: e.dma_start(out=out, in_=in_).then_inc(h, 16))
        tok = (sem, val)
        self._commit(tok, reads, writes)
        return tok

    def barrier(self, engs=("pe", "act", "dve", "sp")):
        for e in ("pe", "act", "dve", "pool"):
            assert not self.pending_noinc[e]
        toks = [(self.esem[e], self.esem[e].count) for e in ("pe", "act", "dve", "pool")]
        if "pool" not in engs:
            engs = tuple(engs) + ("pool",)
        for e in engs:
            for t in toks:
                if t[1] > 0:
                    self._wait(e, t)

    def emit(self):
        for e in ("pe", "act", "dve"):
            assert not self.pending_noinc[e], f"engine {e} has trailing non-inc op"
        ops = self.ops
        with self.nc.Block() as block:
            @block.sync
            def _(eng):
                for f in ops["sp"]:
                    f(eng)

            @block.tensor
            def _(eng):
                for f in ops["pe"]:
                    f(eng)

            @block.scalar
            def _(eng):
                for f in ops["act"]:
                    f(eng)

            @block.vector
            def _(eng):
                for f in ops["dve"]:
                    f(eng)

            @block.gpsimd
            def _(eng):
                for f in ops["pool"]:
                    f(eng)


def _vec_layout():
    L = [("c", 8), ("cctx", 8)]
    for i in range(DEPTH):
        L += [(f"modb{i}", 48), (f"gmix{i}", 8), (f"gmlp{i}", 8)]
    for j in range(2):
        L += [(f"qg{j}", 1), (f"kg{j}", 1)]
    L += [("cbin", 16), ("cwdw", 248), ("cbdw", 8), ("cng", 8), ("cnb", 8), ("cbout", 8)]
    L += [("lcw", 64), ("lcb", 16), ("lgb", 32), ("llam", 16)]
    off = {}
    o = 0
    for n, k in L:
        off[n] = o
        o += k
    return off, o


VOFF, NV = _vec_layout()


def _pk(v):
    v = np.asarray(v, np.float32).reshape(-1, 128)
    return np.ascontiguousarray(v.T)


def _pack_vecs(inp, b):
    V = np.zeros((128, NV), np.float32)

    def put(name, arr):
        arr = np.asarray(arr, np.float32)
        V[:, VOFF[name]:VOFF[name] + arr.shape[1]] = arr

    put("c", _pk(inp["c"][b]))
    put("cctx", _pk(inp["c_ctx"]))
    for i in range(DEPTH):
        put(f"modb{i}", _pk(inp["mod_b"][i]))
        put(f"gmix{i}", _pk(inp["norm_mix_g"][i]))
        put(f"gmlp{i}", _pk(inp["norm_mlp_g"][i]))
    for j in range(2):
        put(f"qg{j}", np.tile(np.asarray(inp["attn_q_gain"][j], np.float32), 2)[:, None])
        put(f"kg{j}", np.tile(np.asarray(inp["attn_k_gain"][j], np.float32), 2)[:, None])
    put("cbin", _pk(inp["conv_b_in"][0]))
    wdw = np.asarray(inp["conv_w_dw"][0], np.float32).reshape(31, 8, 128).transpose(2, 0, 1).reshape(128, 248)
    put("cwdw", wdw)
    put("cbdw", _pk(inp["conv_b_dw"][0]))
    put("cng", _pk(inp["conv_norm_g"][0]))
    put("cnb", _pk(inp["conv_norm_b"][0]))
    put("cbout", _pk(inp["conv_b_out"][0]))
    lcw = np.asarray(inp["lru_conv_w"][0], np.float32).reshape(2, 4, 8, 128).transpose(3, 0, 1, 2).reshape(128, 64)
    put("lcw", lcw)
    put("lcb", np.asarray(inp["lru_conv_b"][0], np.float32).reshape(2, 8, 128).transpose(2, 0, 1).reshape(128, 16))
    put("lgb", np.asarray(inp["lru_gate_b"][0], np.float32).reshape(2, 2, 8, 128).transpose(3, 0, 1, 2).reshape(128, 32))
    put("llam", np.asarray(inp["lru_lambda"][0], np.float32).reshape(2, 8, 128).transpose(2, 0, 1).reshape(128, 16))
    return V


def _consts():
    p = np.arange(128)
    perm = (p[:, None] == (p[None, :] ^ 16)).astype(np.float32)
    onesblk = ((p[:, None] // 64) == (p[None, :] // 64)).astype(np.float32)
    ones = np.ones((128, 128), np.float32)
    ident = np.eye(128, dtype=np.float32)
    return np.concatenate([perm, onesblk, ones, ident], axis=1).astype(ml_dtypes.bfloat16)


def _rope_tables():
    t = np.arange(NLAT)
    row = (t // 64).astype(np.float64)
    col = (t % 64).astype(np.float64)
    inv = 10000.0 ** (-np.arange(16, dtype=np.float64) / 16.0)
    p = np.arange(128)
    d = p % 64
    a = d // 32
    half = (d // 16) % 2
    f = d % 16
    pos = np.where(a[:, None] == 0, row[None, :], col[None, :])
    ang = (pos.astype(np.float32) * inv.astype(np.float32)[f][:, None]).astype(np.float32)
    C = np.cos(ang).astype(np.float32)
    S = np.sin(ang).astype(np.float32) * np.where(half == 0, -1.0, 1.0)[:, None].astype(np.float32)
    return np.ascontiguousarray(np.concatenate([C, S], axis=1).astype(np.float32))


class K:
    pass


def build_program(layers=(0, 1, 2, 3), first=True, last=True):
    nc = bass.Bass("TRN2", target_bir_lowering=False)
    dr = {}

    def din(name, shape, dt=F32):
        dr[name] = nc.dram_tensor(name, list(shape), dt, kind="ExternalInput").ap()
        return dr[name]

    if first:
        din("xT", [D, NLAT])
        din("ctxT", [D, NCTX])
    else:
        din("xs_in", [128, 8 * NT])
    din("vecs", [128, NV])
    din("consts", [128, 512], BF16)
    din("rope", [128, 2 * NLAT])
    din("mod_w", [4, D, 6 * D])
    din("mlp_w1", [4, D, 4 * D])
    din("mlp_w2", [4, 4 * D, D])
    din("attn_w_qkv", [2, D, 1536])
    din("attn_w_o", [2, D, D])
    din("conv_w_in", [1, D, 2 * D])
    din("conv_w_out", [1, D, D])
    din("lru_w_in", [1, D, 2 * D])
    din("lru_gate_w", [1, 2, 2, 4, 256, 256])
    din("lru_w_out", [1, D, D])
    if last:
        outT = nc.dram_tensor("outT", [D, NLAT], F32, kind="ExternalOutput").ap()
    else:
        xs_out = nc.dram_tensor("xs_out", [128, 8 * NT], F32, kind="ExternalOutput").ap()
    xs = nc.dram_tensor("xs_scr", [128, 8 * NT], F32, kind="Internal").ap()
    import os as _os
    DBG = bool(_os.environ.get("DBG_DUMP"))
    if DBG:
        dbg_mods = nc.dram_tensor("dbg_mods", [128, 96], F32, kind="ExternalOutput").ap()
        dbg_ab = nc.dram_tensor("dbg_ab", [128, 64], F32, kind="ExternalOutput").ap()
        dbg_h1 = nc.dram_tensor("dbg_h1", [128, 8 * NT], BF16, kind="ExternalOutput").ap()
        dbg_h2 = nc.dram_tensor("dbg_h2", [128, 8 * NT], BF16, kind="ExternalOutput").ap()
        dbg_x1 = nc.dram_tensor("dbg_x1", [128, 8 * NT], F32, kind="ExternalOutput").ap()

    with ExitStack() as st:
        p = Prog(nc, st)
        XR = p.sb("XR", [128, 8 * NT], F32)
        HBt = p.sb("HB", [128, 8 * NT], BF16)
        AUX = p.sb("AUX", [128, 10368], BF16)
        SLOT = [p.sb(f"slot{i}", [128, 4096], BF16) for i in range(4)]
        VEC = p.sb("VEC", [128, NV], F32)
        CONST = p.sb("CONST", [128, 512], BF16)
        MODS_ = [p.sb(f"MODS{i}", [128, 96], F32) for i in range(2)]
        AB_ = [p.sb(f"AB{i}", [128, 64], F32) for i in range(2)]
        SC = p.sb("SC", [128, 16], BF16)
        MISC = p.sb("MISC", [128, 256], F32)
        T32 = [p.sb(f"t32_{i}", [128, 512], F32) for i in range(6)]
        TB = p.sb("TB", [128, 8 * 512], BF16)
        T32X = [p.sb(f"t32x_{i}", [128, 512], F32) for i in range(2)]
        t32xb = [p.buf(f"t32x_{i}") for i in range(2)]
        T32H = [p.sb(f"t32h_{i}", [128, 512], F32) for i in range(2)]
        t32hb = [p.buf(f"t32h_{i}") for i in range(2)]
        b_tb = p.buf("tb")
        b_car = p.buf("car")
        PT = [p.sb(f"pt{i}", [128, 512], BF16) for i in range(6)]
        PS = [p.ps(f"ps{i}", [128, 512], F32) for i in range(4)]
        PSD = [p.ps(f"psd{i}", [128, 1024], F32) for i in range(2)]
        PS = PS + [PSD[0][:, 0:512], PSD[0][:, 512:1024], PSD[1][:, 0:512], PSD[1][:, 512:1024]]
        psb = [p.buf(f"ps{i}") for i in range(8)]
        t32b = [p.buf(f"t32_{i}") for i in range(6)]
        ptb = [p.buf(f"pt{i}") for i in range(6)]
        slotb = [p.buf(f"slot{i}", dma=True) for i in range(4)]
        b_vec = p.buf("vec", dma=True)
        b_const = p.buf("const", dma=True)
        b_mods_ = [p.buf("mods0"), p.buf("mods1")]
        b_ab_ = [p.buf("ab0"), p.buf("ab1")]
        b_sc = p.buf("sc")
        b_misc = p.buf("misc")
        x_dsem = p.new_sem("d_x")
        xc_dsem = [p.new_sem(f"d_xc{c}") for c in range(8)]
        o_buf = p.buf("out", dma=True)
        spill_buf = p.buf("spill", dma=True)

        dbg_outs = {}

        def dbg_dump(name, ap, shape, dt):
            if not DBG:
                return
            t = nc.dram_tensor("dd_" + name, list(shape), dt, kind="ExternalOutput").ap()
            p.barrier(("sp",))
            p.dma("sp", t, ap, writes=[o_buf])
            for e in ("pe", "act", "dve"):
                p._wait(e, o_buf.w)

        PERM = CONST[:, 0:128]
        ONESBLK = CONST[:, 128:256]
        ONES = CONST[:, 256:384]
        IDENT = CONST[:, 384:512]

        X3 = XR[:, :].rearrange("p (c t) -> p c t", c=8)
        XRb = XR[:, :].bitcast(BF16)
        H3 = HBt[:, :].rearrange("p (c t) -> p c t", c=8)
        HBf = HBt[:, :].bitcast(F32)
        AUXf = AUX[:, :].bitcast(F32)

        def grid(name):
            return [[p.buf(f"{name}{c}_{t}") for t in range(5)] for c in range(8)]

        st_ = K()
        st_.xb = grid("x")
        st_.hb = grid("h")
        st_.rr = {"t32": 0, "pt": 0, "ps": 0}

        def vcol(name, j=0):
            o = VOFF[name] + j
            return VEC[:, o:o + 1]

        def tmp32():
            i = st_.rr["t32"] % 6
            st_.rr["t32"] += 1
            return T32[i], t32b[i]

        def tmppt():
            i = st_.rr["pt"] % 6
            st_.rr["pt"] += 1
            return PT[i], ptb[i]

        def psum(group=None):
            group = group if group is not None else list(range(7))
            key = ("ps",) + tuple(group)
            k_ = st_.rr.get(key, 0)
            st_.rr[key] = k_ + 1
            i = group[k_ % len(group)]
            return PS[i], psb[i]

        def mm(out, lhsT, rhs, start, stop, reads, writes, inc):
            p.op("pe", lambda e: e.matmul(out, lhsT, rhs, start=start, stop=stop), reads, writes, inc=inc)

        def act(out, in_, func, reads, writes, bias=None, scale=None):
            kw = {}
            if bias is not None:
                kw["bias"] = bias
            if scale is not None:
                kw["scale"] = scale
            p.op("act", lambda e: e.activation(out=out, in_=in_, func=func, **kw), reads, writes)

        def tt(out, in0, in1, op, reads, writes, eng="dve"):
            p.op(eng, lambda e: e.tensor_tensor(out=out, in0=in0, in1=in1, op=op), reads, writes)

        def ts(out, in0, s1, s2, op0, op1, reads, writes, eng="dve"):
            if s2 is None:
                p.op(eng, lambda e: e.tensor_scalar(out=out, in0=in0, scalar1=s1, scalar2=None, op0=op0), reads, writes)
            else:
                p.op(eng, lambda e: e.tensor_scalar(out=out, in0=in0, scalar1=s1, scalar2=s2, op0=op0, op1=op1), reads, writes)

        def stt(out, in0, scalar, in1, op0, op1, reads, writes, eng="dve"):
            p.op(eng, lambda e: e.scalar_tensor_tensor(out=out, in0=in0, scalar=scalar, in1=in1, op0=op0, op1=op1), reads, writes)

        def recip(out, in_, reads, writes):
            p.op("dve", lambda e: e.reciprocal(out=out, in_=in_), reads, writes)

        def vcopy(out, in_, reads, writes):
            p.op("dve", lambda e: e.tensor_copy(out=out, in_=in_), reads, writes)

        def vmemset(ap, val, writes):
            p.op("dve", lambda e: e.memset(ap, val), (), writes)

        wspecs = []

        def wv(ap2d):
            return ap2d.rearrange("(k p) n -> p k n", p=128)

        NMOD = [2, 1, 2, 1, 2, 1, 2, 1]
        for lidx_, li in enumerate(layers):
            kind = li % 3
            j = li // 3
            if lidx_ == 0:
                for b in range(12):
                    wspecs.append([(0, 8, 512, wv(dr["mod_w"][li, :, b * 512:(b + 1) * 512]))])
            if kind == 0:
                wq = dr["attn_w_qkv"][j]
                for b in range(2):
                    wspecs.append([(0, 8, 512, wv(wq[:, b * 512:(b + 1) * 512]))])
                sp_ = []
                for g in range(4):
                    for dup in range(2):
                        sp_.append(((g * 2 + dup) * 64, 8, 64, wv(wq[:, 1024 + g * 64:1024 + (g + 1) * 64]), 512))
                wspecs.append(sp_)
                wspecs.append([(0, 8, 256, wv(wq[:, 1280:1536]))])
                for b in range(2):
                    wspecs.append([(0, 8, 512, wv(dr["attn_w_o"][j][:, b * 512:(b + 1) * 512]))])
            elif kind == 1:
                wi = dr["conv_w_in"][0]
                for b in range(4):
                    wspecs.append([(0, 8, 256, wv(wi[:, b * 256:(b + 1) * 256]), 512),
                                   (256, 8, 256, wv(wi[:, 1024 + b * 256:1024 + (b + 1) * 256]), 512)])
                for b in range(2):
                    wspecs.append([(0, 8, 512, wv(dr["conv_w_out"][0][:, b * 512:(b + 1) * 512]))])
            else:
                wi = dr["lru_w_in"][0]
                for b in range(4):
                    wspecs.append([(0, 8, 512, wv(wi[:, b * 512:(b + 1) * 512]))])
                for d in range(2):
                    gw = dr["lru_gate_w"][0, d].rearrange("g n k e -> (g n k) e")
                    wspecs.append([(0, 16, 256, wv(gw))])
                for b in range(2):
                    wspecs.append([(0, 8, 512, wv(dr["lru_w_out"][0][:, b * 512:(b + 1) * 512]))])
            mb_ = 0
            for hb in range(8):
                wspecs.append([(0, 8, 512, wv(dr["mlp_w1"][li, :, hb * 512:(hb + 1) * 512]))])
                wspecs.append([(0, 4, 1024, wv(dr["mlp_w2"][li, hb * 512:(hb + 1) * 512, :]))])
                if lidx_ + 1 < len(layers):
                    nl_ = layers[lidx_ + 1]
                    for _ in range(NMOD[hb]):
                        wspecs.append([(0, 8, 512, wv(dr["mod_w"][nl_, :, mb_ * 512:(mb_ + 1) * 512]))])
                        mb_ += 1

        ws = K()
        ws.issued = 0
        ws.consumed = 0

        def w_issue(jb):
            s = jb % 4
            for spec in wspecs[jb]:
                if len(spec) == 5:
                    off, kcn, ncol, src, rowlen = spec
                    dst = SLOT[s][:, 0:kcn * rowlen].rearrange("p (k n) -> p k n", k=kcn)[:, :, off:off + ncol]
                else:
                    off, kcn, ncol, src = spec
                    dst = SLOT[s][:, off:off + kcn * ncol].rearrange("p (k n) -> p k n", k=kcn)
                p.dma("pool", dst, src, writes=[slotb[s]])

        ws.released = set()
        ws.pinned = set()

        def w_release(i):
            ws.pinned.discard(i)
            ws.released.add(i)

        def w_next(kcn, ncol, pin=False):
            i = ws.consumed
            if i - 1 >= 0 and (i - 1) not in ws.pinned:
                ws.released.add(i - 1)
            while ws.issued < min(i + 4, len(wspecs)) and (ws.issued < 4 or (ws.issued - 4) in ws.released):
                w_issue(ws.issued)
                ws.issued += 1
            assert ws.issued > i, "weight block not issued (pinned slot deadlock)"
            if pin:
                ws.pinned.add(i)
            ws.consumed += 1
            s = i % 4
            return SLOT[s][:, 0:kcn * ncol].rearrange("p (k n) -> p k n", k=kcn), slotb[s]

        p.dma("sp", VEC[:, :], dr["vecs"], writes=[b_vec])
        p.dma("sp", CONST[:, :], dr["consts"], writes=[b_const])

        def load_x_from_input():
            if first:
                p.dma("sp", X3[:, :, 0:NCTX], dr["ctxT"].rearrange("(c p) t -> p c t", p=128),
                      writes=[st_.xb[c][0] for c in range(8)], dsem=x_dsem)
                for c in range(8):
                    p.dma("sp", X3[:, c, NCTX:NT], dr["xT"][c * 128:(c + 1) * 128, :], writes=st_.xb[c][1:5], dsem=xc_dsem[c])
            else:
                for c in range(8):
                    p.dma("sp", X3[:, c, :], dr["xs_in"][:, c * NT:(c + 1) * NT], writes=st_.xb[c], dsem=xc_dsem[c])

        load_x_from_input()
        SC3 = SC[:, :].rearrange("p (k s) -> p k s", s=2)
        act(SC3[:, :, 0], VEC[:, VOFF["cctx"]:VOFF["cctx"] + 8], AF.Silu, [b_vec], [b_sc])
        act(SC3[:, :, 1], VEC[:, VOFF["c"]:VOFF["c"] + 8], AF.Silu, [b_vec], [b_sc])

        st_.par = 0

        def MODS():
            return MODS_[st_.par]

        def AB():
            return AB_[st_.par]

        def b_mods():
            return b_mods_[st_.par]

        def b_ab():
            return b_ab_[st_.par]

        def modcol(grp, c, s):
            return MODS()[:, (grp * 8 + c) * 2 + s:(grp * 8 + c) * 2 + s + 1]

        modst = K()
        modst.nb = 0

        def mod_block():
            b = modst.nb
            modst.nb += 1
            ps_t, ps_b = PS[7], psb[7]
            wt, wb = w_next(8, 512)
            for jj in range(4):
                jx = b * 4 + jj
                for kc in range(8):
                    mm(ps_t[:, 2 * jx:2 * jx + 2], wt[:, kc, jj * 128:(jj + 1) * 128], SC3[:, kc, :],
                       kc == 0, kc == 7, [wb, b_sc], [ps_b], inc=(jj == 3 and kc == 7))

        def mod_finish(li, par):
            assert modst.nb == 12
            modst.nb = 0
            ps_t, ps_b = PS[7], psb[7]
            M = MODS_[par]
            A = AB_[par]
            M3 = M[:, :].rearrange("p (j s) -> p j s", s=2)
            ps3 = ps_t[:, 0:96].rearrange("p (j s) -> p j s", s=2)
            mb = VEC[:, VOFF[f"modb{li}"]:VOFF[f"modb{li}"] + 48]
            for s in range(2):
                tt(M3[:, :, s], ps3[:, :, s], mb, ALU.add, [ps_b, b_vec], [b_mods_[par]])
            gm = VEC[:, VOFF[f"gmix{li}"]:VOFF[f"gmix{li}"] + 8]
            gl = VEC[:, VOFF[f"gmlp{li}"]:VOFF[f"gmlp{li}"] + 8]
            for s in range(2):
                stt(A[:, s * 8:s * 8 + 8], M3[:, 8:16, s], 1.0, gm, ALU.add, ALU.mult, [b_mods_[par], b_vec], [b_ab_[par]])
                stt(A[:, 16 + s * 8:16 + s * 8 + 8], M3[:, 32:40, s], 1.0, gl, ALU.add, ALU.mult, [b_mods_[par], b_vec], [b_ab_[par]])

        def norm_phase(which, tis):
            TB3 = TB[:, :].rearrange("p (c t) -> p c t", c=8)

            def stage_a1(ti, k):
                t0, n = TCH[ti]
                for c in range(8):
                    act(TB3[:, c, 0:n], X3[:, c, t0:t0 + n], AF.Square, [st_.xb[c][ti]], [b_tb])
                ps_t, ps_b = psum()
                for c in range(8):
                    mm(ps_t[:, 0:n], ONES, TB3[:, c, 0:n], c == 0, c == 7, [b_tb, b_const], [ps_b], inc=(c == 7))
                return ps_t, ps_b

            def stage_a2(ti, k, ps_t, ps_b):
                t0, n = TCH[ti]
                sd, sdb = tmp32()
                act(sd[:, 0:n], ps_t[:, 0:n], AF.Sqrt, [ps_b, b_misc], [sdb], bias=MISC[:, 0:1], scale=1.0 / D)
                rs, rsb = T32H[k % 2], t32hb[k % 2]
                recip(rs[:, 0:n], sd[:, 0:n], [sdb], [rsb])
                return rs, rsb

            def stage_b(ti, rs, rsb):
                t0, n = TCH[ti]
                s = 0 if ti == 0 else 1
                for c in range(8):
                    t_, tb_ = tmp32()
                    a_ap = AB()[:, which * 16 + s * 8 + c:which * 16 + s * 8 + c + 1]
                    stt(t_[:, 0:n], X3[:, c, t0:t0 + n], a_ap, rs[:, 0:n], ALU.mult, ALU.mult,
                        [st_.xb[c][ti], b_ab(), rsb], [tb_])
                    act(H3[:, c, t0:t0 + n], t_[:, 0:n], AF.Identity, [tb_, b_mods()], [st_.hb[c][ti]],
                        bias=modcol(0 if which == 0 else 3, c, s))

            prev = None
            for k, ti in enumerate(tis):
                pst = stage_a1(ti, k)
                if prev is not None:
                    stage_b(*prev)
                prev = (ti,) + stage_a2(ti, k, *pst)
            stage_b(*prev)

        def spill_issue(li_index):
            if li_index == 0:
                return None
            tok = None
            for c in range(8):
                tok = p.dma("sp", xs[:, c * NT:(c + 1) * NT], X3[:, c, :], reads=st_.xb[c], writes=[spill_buf])
            return tok

        def spill(tok):
            p.barrier(("pe", "act", "dve", "sp"))
            if tok is not None:
                for e in ("pe", "act", "dve", "sp", "pool"):
                    p._wait(e, tok)
            return tok

        def reload(li_index):
            p.barrier(("pe", "act", "dve", "sp"))
            st_.xb = grid(f"x{li_index}_")
            allb = [b for row in st_.xb for b in row]
            if li_index == 0:
                load_x_from_input()
            else:
                for c in range(8):
                    p.dma("sp", X3[:, c, :], xs[:, c * NT:(c + 1) * NT], reads=[spill_buf], writes=st_.xb[c], dsem=xc_dsem[c])

        def linear(nblocks, ocs_per_block, kcn, wcols, lhs_fn, rhs_fn, rhs_bufs_fn, tis, evac, psgroup=None):
            for b in range(nblocks):
                wt, wb = w_next(kcn, wcols)
                for ocl in range(ocs_per_block):
                    for ti in tis:
                        t0, n = TCH[ti]
                        ps_t, ps_b = psum(psgroup)
                        for kc in range(kcn):
                            mm(ps_t[:, 0:n], lhs_fn(wt, ocl, kc), rhs_fn(kc, t0, n), kc == 0, kc == kcn - 1,
                               [wb] + rhs_bufs_fn(kc, ti), [ps_b], inc=(kc == kcn - 1))
                        evac(b, ocl, ti, ps_t[:, 0:n], ps_b)

        def resid_evac(grp, tis_all, bias_name=None):
            def ev(b, ocl, ti, ps_ap, ps_b):
                oc = b * 4 + ocl
                t0, n = TCH[ti]
                s = 0 if ti == 0 else 1
                src = ps_ap
                rd = [ps_b]
                if bias_name is not None:
                    t_, tb_ = tmp32()
                    act(t_[:, 0:n], ps_ap, AF.Identity, [ps_b, b_vec], [tb_], bias=vcol(bias_name, oc))
                    src = t_[:, 0:n]
                    rd = [tb_]
                stt(X3[:, oc, t0:t0 + n], src, modcol(grp, oc, s), X3[:, oc, t0:t0 + n], ALU.mult, ALU.add,
                    rd + [b_mods(), st_.xb[oc][ti]], [st_.xb[oc][ti]])
            return ev

        def hb_rhs(kc, t0, n):
            return H3[:, kc, t0:t0 + n]

        def hb_bufs(kc, ti):
            return [st_.hb[kc][ti]]

        def mlp_phase(li, tis, next_li=None, next_par=None):
            HID = AUX[:, 0:4 * NT].rearrange("p (c t) -> p c t", c=4)
            hidb = [[p.buf() for _ in range(5)] for _ in range(4)]
            for hb_i in range(8):
                w1, w1b = w_next(8, 512)
                for ti in tis:
                    t0, n = TCH[ti]
                    for ocl in range(4):
                        ps_t, ps_b = psum()
                        for kc in range(8):
                            mm(ps_t[:, 0:n], w1[:, kc, ocl * 128:(ocl + 1) * 128], H3[:, kc, t0:t0 + n], kc == 0, kc == 7,
                               [w1b, st_.hb[kc][ti]], [ps_b], inc=(kc == 7))
                        t_, tb_ = tmp32()
                        act(t_[:, 0:n], ps_t[:, 0:n], AF.Relu, [ps_b], [tb_])
                        tt(HID[:, ocl, t0:t0 + n], t_[:, 0:n], t_[:, 0:n], ALU.mult, [tb_], [hidb[ocl][ti]])
                w2, w2b = w_next(4, 1024)
                for ti in tis:
                    t0, n = TCH[ti]
                    s = 0 if ti == 0 else 1
                    for oc in range(8):
                        ps_t, ps_b = psum()
                        for kc in range(4):
                            mm(ps_t[:, 0:n], w2[:, kc, oc * 128:(oc + 1) * 128], HID[:, kc, t0:t0 + n], kc == 0, kc == 3,
                               [w2b, hidb[kc][ti]], [ps_b], inc=(kc == 3))
                        stt(X3[:, oc, t0:t0 + n], ps_t[:, 0:n], modcol(5, oc, s), X3[:, oc, t0:t0 + n], ALU.mult, ALU.add,
                            [ps_b, b_mods(), st_.xb[oc][ti]], [st_.xb[oc][ti]])
                if next_li is not None:
                    for _ in range(NMOD[hb_i]):
                        mod_block()
            if next_li is not None:
                mod_finish(next_li, next_par)

        def attention(li, j_att, need_ctx, li_index):
            QT = XRb[:, 0:18432].rearrange("p (c t) -> p c t", c=8)
            KT2 = XRb[:, 18432:27648].rearrange("p (c t) -> p c t", c=4)
            ROC = XR[:, 13824:15872]
            ROS = XR[:, 15872:17920]
            VA = AUX[:, 0:18 * 576].rearrange("p (k x) -> p k x", k=18)
            b_rope = p.buf(f"rope{li}", dma=True)
            qb = [[p.buf() for _ in range(5)] for _ in range(8)]
            kb = [[p.buf() for _ in range(5)] for _ in range(4)]
            vab = [p.buf() for _ in range(18)]
            b_va_init = p.buf()
            p.dma("sp", XR[:, 13824:17920], dr["rope"], writes=[b_rope])
            vmemset(AUX[:, 0:18 * 576], 1.0, [b_va_init] + vab)
            q_tis = [0, 1, 2, 3, 4] if need_ctx else [1, 2, 3, 4]

            PS_A = [0, 1, 2, 3]
            PS_B = [4, 5]
            PS_C = [6, 7]
            pending = []

            def qk_item(ps_t, ps_b, n, ti, gain_ap, dst_ap, dst_buf):
                t0 = TCH[ti][0]
                lat = ti != 0
                state = {}

                def stage_b1():
                    sq, sqb = tmppt()
                    act(sq[:, 0:n], ps_t[:, 0:n], AF.Square, [ps_b], [sqb])
                    ss_t, ss_b = psum(PS_B)
                    mm(ss_t[:, 0:n], ONESBLK, sq[:, 0:n], True, True, [sqb, b_const], [ss_b], inc=True)
                    sd, sdb = tmp32()
                    act(sd[:, 0:n], ss_t[:, 0:n], AF.Sqrt, [ss_b, b_misc], [sdb], bias=MISC[:, 0:1], scale=1.0 / 64)
                    state["sd"] = (sd, sdb)

                def stage_b2():
                    sd, sdb = state["sd"]
                    rs, rsb = tmp32()
                    recip(rs[:, 0:n], sd[:, 0:n], [sdb], [rsb])
                    if not lat:
                        stt(dst_ap, ps_t[:, 0:n], gain_ap, rs[:, 0:n], ALU.mult, ALU.mult, [ps_b, rsb, b_vec], [dst_buf])
                    else:
                        qn, qnb = tmppt()
                        stt(qn[:, 0:n], ps_t[:, 0:n], gain_ap, rs[:, 0:n], ALU.mult, ALU.mult, [ps_b, rsb, b_vec], [qnb])
                        state["qn"] = (qn, qnb)

                def stage_c():
                    if not lat:
                        return
                    qn, qnb = state["qn"]
                    rot_t, rot_b = psum(PS_C)
                    mm(rot_t[:, 0:n], PERM, qn[:, 0:n], True, True, [qnb, b_const], [rot_b], inc=True)
                    t1, t1b = tmp32()
                    tt(t1[:, 0:n], qn[:, 0:n], ROC[:, t0 - NCTX:t0 - NCTX + n], ALU.mult, [qnb, b_rope], [t1b], eng="pool")
                    t2, t2b = tmp32()
                    tt(t2[:, 0:n], rot_t[:, 0:n], ROS[:, t0 - NCTX:t0 - NCTX + n], ALU.mult, [rot_b, b_rope], [t2b])
                    tt(dst_ap, t1[:, 0:n], t2[:, 0:n], ALU.add, [t1b, t2b], [dst_buf], eng="pool")
                return stage_b1, stage_b2, stage_c

            def pipe_push(item):
                pending.append(item)
                for back, stg in ((2, 0), (3, 1), (4, 2)):
                    if len(pending) >= back:
                        pending[-back][stg]()

            def pipe_flush():
                for extra in range(1, 4):
                    for back, stg in ((2, 0), (3, 1), (4, 2)):
                        idx_ = len(pending) + extra - back
                        if 0 <= idx_ < len(pending):
                            pending[idx_][stg]()
                pending.clear()

            for b in range(2):
                wt, wb = w_next(8, 512)
                for ocl in range(4):
                    oc = b * 4 + ocl
                    for ti in q_tis:
                        t0, n = TCH[ti]
                        ps_t, ps_b = psum(PS_A)
                        for kc in range(8):
                            mm(ps_t[:, 0:n], wt[:, kc, ocl * 128:(ocl + 1) * 128], H3[:, kc, t0:t0 + n], kc == 0, kc == 7,
                               [wb, st_.hb[kc][ti]], [ps_b], inc=(kc == 7))
                        pipe_push(qk_item(ps_t, ps_b, n, ti, vcol(f"qg{j_att}"), QT[:, oc, t0:t0 + n], qb[oc][ti]))
            wt, wb = w_next(8, 512)
            for g in range(4):
                for ti in range(5):
                    t0, n = TCH[ti]
                    ps_t, ps_b = psum(PS_A)
                    for kc in range(8):
                        mm(ps_t[:, 0:n], wt[:, kc, g * 128:(g + 1) * 128], H3[:, kc, t0:t0 + n], kc == 0, kc == 7,
                           [wb, st_.hb[kc][ti]], [ps_b], inc=(kc == 7))
                    pipe_push(qk_item(ps_t, ps_b, n, ti, vcol(f"kg{j_att}"), KT2[:, g, t0:t0 + n], kb[g][ti]))
            pipe_flush()
            wt, wb = w_next(8, 256)
            for kt in range(18):
                ti = 0 if kt < 2 else 1 + (kt - 2) // 4
                ps_t, ps_b = psum(PS_A)
                for kc in range(8):
                    mm(ps_t[:, 0:256], H3[:, kc, kt * 128:(kt + 1) * 128], wt[:, kc, :], kc == 0, kc == 7,
                       [wb, st_.hb[kc][ti]], [ps_b], inc=(kc == 7))
                dst = VA[:, kt, 64:576].rearrange("p (g x) -> p g x", x=128)[:, :, 0:64]
                src = ps_t[:, 0:256].rearrange("p (g x) -> p g x", x=64)
                act(dst, src, AF.Identity, [ps_b], [vab[kt]])

            OBANK = [[0, 1], [2, 3]]
            ptdb = [p.buf() for _ in range(4)]
            it = 0
            for jp in range(8):
                g = jp // 2
                for ti in q_tis:
                    t0, n = TCH[ti]
                    kts = list(range(18)) if ti != 0 else [0, 1]
                    ob = OBANK[it % 2]
                    it += 1
                    o_t = [PS[ob[0]], PS[ob[1]]]
                    o_b = [psb[ob[0]], psb[ob[1]]]

                    def s_stage(kt):
                        tik = 0 if kt < 2 else 1 + (kt - 2) // 4
                        di = st_.rr.get("psd", 0) % 2
                        st_.rr["psd"] = st_.rr.get("psd", 0) + 1
                        dt_ = PSD[di]
                        dbs = [psb[4 + 2 * di], psb[5 + 2 * di]]
                        for h in range(2):
                            mm(dt_[:, h * 512:h * 512 + n], KT2[h * 64:(h + 1) * 64, g, kt * 128:(kt + 1) * 128],
                               QT[h * 64:(h + 1) * 64, jp, t0:t0 + n], True, True,
                               [kb[g][tik], qb[jp][ti]], [dbs[h]], inc=(h == 1))
                        return dt_, dbs

                    def pv_stage(kt, sres):
                        dt_, dbs = sres
                        pi = st_.rr.get("ptd", 0) % 4
                        st_.rr["ptd"] = st_.rr.get("ptd", 0) + 1
                        ptd = TB[:, pi * 1024:(pi + 1) * 1024]
                        src = dt_[:, :].rearrange("p (h x) -> p h x", h=2)[:, :, 0:n]
                        dst = ptd.rearrange("p (h x) -> p h x", h=2)[:, :, 0:n]
                        act(dst, src, AF.Exp, dbs, [ptdb[pi]], scale=0.125)
                        for h in range(2):
                            if h == 0:
                                lhs = VA[:, kt, 64 + 128 * g:192 + 128 * g]
                            else:
                                lhs = VA[:, kt, 128 * g:128 + 128 * g]
                            mm(o_t[h][:, 0:n], lhs, ptd[:, h * 512:h * 512 + n], kt == kts[0], kt == kts[-1],
                               [ptdb[pi], vab[kt]], [o_b[h]], inc=True)

                    prev = s_stage(kts[0])
                    for idx, kt in enumerate(kts):
                        nxt = s_stage(kts[idx + 1]) if idx + 1 < len(kts) else None
                        pv_stage(kt, prev)
                        prev = nxt
                    for h in range(2):
                        rc, rcb = tmp32()
                        recip(rc[:, 0:n], o_t[h][:, 0:n], [o_b[h]], [rcb])
                        lo, hi = (0, 64) if h == 0 else (64, 128)
                        dlo, dhi = (64, 128) if h == 0 else (0, 64)
                        tt(H3[lo:hi, jp, t0:t0 + n], o_t[h][lo:hi, 0:n], rc[dlo:dhi, 0:n], ALU.mult,
                           [o_b[h], rcb], [st_.hb[jp][ti]])
            dbg_dump("att_xr", XR[:, :], [128, 8 * NT], F32)
            dbg_dump("att_aux", AUX[:, :], [128, 10368], BF16)
            dbg_dump("att_hb", HBt[:, :], [128, 8 * NT], BF16)
            reload(li_index)
            linear(2, 4, 8, 512, lambda wt, ocl, kc: wt[:, kc, ocl * 128:(ocl + 1) * 128], hb_rhs, hb_bufs,
                   q_tis, resid_evac(2, q_tis))

        def conformer(li, need_ctx, li_index):
            UC = XRb[:, 0:8 * 286].rearrange("p (c t) -> p c t", c=8)
            UL = XRb[:, 2288:2288 + 8 * 2078].rearrange("p (c t) -> p c t", c=8)
            DG = [XRb[:, 18912 + i * 3968:18912 + (i + 1) * 3968].rearrange("p (k m) -> p k m", k=31) for i in range(2)]
            ub = [[p.buf() for _ in range(5)] for _ in range(8)]
            upad = p.buf()
            dgb = [p.buf(), p.buf()]
            vmemset(XRb[:, 0:18912], 0.0, [upad] + [b for row in ub for b in row])
            tis = [0, 1, 2, 3, 4]

            def useg(c, ti, k, n):
                if ti == 0:
                    return UC[:, c, k:k + n]
                o = TCH[ti][0] - NCTX
                return UL[:, c, o + k:o + k + n]

            for b in range(4):
                wt, wb = w_next(8, 512)
                for cl in range(2):
                    c = b * 2 + cl
                    for ti in tis:
                        t0, n = TCH[ti]
                        pa_t, pa_b = psum()
                        for kc in range(8):
                            mm(pa_t[:, 0:n], wt[:, kc, cl * 128:(cl + 1) * 128], H3[:, kc, t0:t0 + n], kc == 0, kc == 7,
                               [wb, st_.hb[kc][ti]], [pa_b], inc=(kc == 7))
                        pg_t, pg_b = psum()
                        for kc in range(8):
                            mm(pg_t[:, 0:n], wt[:, kc, 256 + cl * 128:256 + (cl + 1) * 128], H3[:, kc, t0:t0 + n], kc == 0, kc == 7,
                               [wb, st_.hb[kc][ti]], [pg_b], inc=(kc == 7))
                        sg, sgb = tmp32()
                        act(sg[:, 0:n], pg_t[:, 0:n], AF.Sigmoid, [pg_b, b_vec], [sgb], bias=vcol("cbin", 8 + c))
                        stt(useg(c, ti, 15, n), pa_t[:, 0:n], vcol("cbin", c), sg[:, 0:n], ALU.add, ALU.mult,
                            [pa_b, sgb, b_vec, upad], [ub[c][ti]])
            vb = [[p.buf() for _ in range(5)] for _ in range(8)]
            for c in range(8):
                par = c % 2
                for k in range(31):
                    ts(DG[par][:, k, :], IDENT, vcol("cwdw", k * 8 + c), None, ALU.mult, None, [b_const, b_vec], [dgb[par]])
                for ti in tis:
                    t0, n = TCH[ti]
                    ps_t, ps_b = psum()
                    nb = [ub[c][ti]]
                    if ti > 1:
                        nb.append(ub[c][ti - 1])
                    if 1 <= ti < 4:
                        nb.append(ub[c][ti + 1])
                    for k in range(31):
                        mm(ps_t[:, 0:n], DG[par][:, k, :], useg(c, ti, k, n), k == 0, k == 30,
                           [dgb[par], upad] + nb, [ps_b], inc=(k == 30))
                    act(H3[:, c, t0:t0 + n], ps_t[:, 0:n], AF.Identity, [ps_b, b_vec],
                        [vb[c][ti], st_.hb[c][ti]], bias=vcol("cbdw", c))
            TB3 = TB[:, :].rearrange("p (c t) -> p c t", c=8)
            yb = [[p.buf() for _ in range(5)] for _ in range(8)]
            def ln_a1(ti, k):
                t0, n = TCH[ti]
                pm_t, pm_b = psum()
                for c in range(8):
                    mm(pm_t[:, 0:n], ONES, H3[:, c, t0:t0 + n], c == 0, c == 7, [vb[c][ti], b_const], [pm_b], inc=(c == 7))
                for c in range(8):
                    act(TB3[:, c, 0:n], H3[:, c, t0:t0 + n], AF.Square, [vb[c][ti]], [b_tb])
                pq_t, pq_b = psum()
                for c in range(8):
                    mm(pq_t[:, 0:n], ONES, TB3[:, c, 0:n], c == 0, c == 7, [b_tb, b_const], [pq_b], inc=(c == 7))
                mean, meanb = (T32H[1], t32hb[1]) if k % 2 == 0 else (T32X[1], t32xb[1])
                act(mean[:, 0:n], pm_t[:, 0:n], AF.Identity, [pm_b], [meanb], scale=1.0 / D)
                return mean, meanb, pq_t, pq_b

            def ln_a2(ti, k, mean, meanb, pq_t, pq_b):
                t0, n = TCH[ti]
                m2, m2b = tmp32()
                tt(m2[:, 0:n], mean[:, 0:n], mean[:, 0:n], ALU.mult, [meanb], [m2b])
                var, varb = tmp32()
                stt(var[:, 0:n], pq_t[:, 0:n], 1.0 / D, m2[:, 0:n], ALU.mult, ALU.subtract, [pq_b, m2b], [varb])
                sd, sdb = tmp32()
                act(sd[:, 0:n], var[:, 0:n], AF.Sqrt, [varb, b_misc], [sdb], bias=MISC[:, 0:1], scale=1.0)
                rs, rsb = (T32H[0], t32hb[0]) if k % 2 == 0 else (T32X[0], t32xb[0])
                recip(rs[:, 0:n], sd[:, 0:n], [sdb], [rsb])
                return mean, meanb, rs, rsb

            def ln_b(ti, mean, meanb, rs, rsb):
                t0, n = TCH[ti]
                for c in range(8):
                    t_, tb_ = tmp32()
                    tt(t_[:, 0:n], H3[:, c, t0:t0 + n], mean[:, 0:n], ALU.subtract, [vb[c][ti], meanb], [tb_])
                    tt(t_[:, 0:n], t_[:, 0:n], rs[:, 0:n], ALU.mult, [tb_, rsb], [tb_])
                    act(H3[:, c, t0:t0 + n], t_[:, 0:n], AF.Silu, [tb_, b_vec], [yb[c][ti], vb[c][ti]],
                        bias=vcol("cnb", c), scale=vcol("cng", c))

            prev = None
            for k, ti in enumerate(tis):
                a1 = ln_a1(ti, k)
                if prev is not None:
                    ln_b(*prev)
                prev = (ti,) + ln_a2(ti, k, *a1)
            ln_b(*prev)
            st_.hb = yb
            reload(li_index)
            linear(2, 4, 8, 512, lambda wt, ocl, kc: wt[:, kc, ocl * 128:(ocl + 1) * 128], hb_rhs, hb_bufs,
                   tis, resid_evac(2, tis, bias_name="cbout"))

        st_.rr["tbx"] = 0

        def tmppt32():
            i = st_.rr["tbx"] % 2
            st_.rr["tbx"] += 1
            return T32X[i], t32xb[i]

        def rglru(li, need_ctx, li_index):
            G3 = XRb[:, 0:18432].rearrange("p (c t) -> p c t", c=8)
            XL3 = XRb[:, 18432:36864].rearrange("p (c t) -> p c t", c=8)
            gb = [[p.buf() for _ in range(5)] for _ in range(8)]
            xlb = [[p.buf() for _ in range(5)] for _ in range(8)]
            tis = [0, 1, 2, 3, 4]
            for b in range(4):
                wt, wb = w_next(8, 512)
                for ocl in range(4):
                    oc = b * 4 + ocl
                    for ti in tis:
                        t0, n = TCH[ti]
                        ps_t, ps_b = psum()
                        for kc in range(8):
                            mm(ps_t[:, 0:n], wt[:, kc, ocl * 128:(ocl + 1) * 128], H3[:, kc, t0:t0 + n], kc == 0, kc == 7,
                               [wb, st_.hb[kc][ti]], [ps_b], inc=(kc == 7))
                        if oc < 8:
                            act(G3[:, oc, t0:t0 + n], ps_t[:, 0:n], AF.Gelu_apprx_tanh, [ps_b], [gb[oc][ti]])
                        else:
                            vcopy(XL3[:, oc - 8, t0:t0 + n], ps_t[:, 0:n], [ps_b], [xlb[oc - 8][ti]])
            lam = VEC[:, VOFF["llam"]:VOFF["llam"] + 16]
            b_ca = p.buf()
            act(MISC[:, 64:80], lam, AF.Exp, [b_vec], [b_ca], scale=-1.0)
            act(MISC[:, 80:96], MISC[:, 64:80], AF.Ln, [b_ca, b_misc], [b_ca], bias=MISC[:, 1:2], scale=1.0)
            ts(MISC[:, 16:32], MISC[:, 80:96], -8.0, None, ALU.mult, None, [b_ca], [b_ca])
            ts(MISC[:, 48:64], MISC[:, 80:96], -4.0, None, ALU.mult, None, [b_ca], [b_ca])
            ts(MISC[:, 96:128], VEC[:, VOFF["lgb"]:VOFF["lgb"] + 32], 0.5, None, ALU.mult, None, [b_vec, b_ca], [b_ca])
            dbg_dump("lru_xr0", XR[:, :], [128, 8 * NT], F32)
            p.barrier(("pe", "act", "dve"))
            U32 = HBf[:, 0:4608].rearrange("p (c t) -> p c t", c=2)
            HS = HBf[:, 4608:9216].rearrange("p (c t) -> p c t", c=2)
            UBF = AUX[:, 0:4608].rearrange("p (c t) -> p c t", c=2)
            XT = [AUXf[:, 2304 + i * 512:2304 + (i + 1) * 512] for i in range(5)]
            xtb = [p.buf() for _ in range(5)]
            CAR = MISC[:, 128:256]
            car_i = [0]
            rrx = [0]

            def ltmp():
                i = rrx[0] % 15
                rrx[0] += 1
                if i < 5:
                    return XT[i], xtb[i]
                if i < 11:
                    return T32[i - 5], t32b[i - 5]
                if i < 13:
                    return T32X[i - 11], t32xb[i - 11]
                return T32H[i - 13], t32hb[i - 13]

            gslots = []
            gidx = []
            for d in range(2):
                gidx.append(ws.consumed)
                gslots.append(w_next(16, 256, pin=True))
            SEGS = [(0, NCTX), (NCTX, NLAT)]
            u32b = [p.buf(), p.buf()]
            ubfb = [p.buf(), p.buf()]
            hsb = [[p.buf() for _ in range(5)] for _ in range(2)]
            for nblk in range(4):
                c0 = nblk * 2
                for d in range(2):
                    gwt, gwb = gslots[d]
                    for cl in range(2):
                        c = c0 + cl
                        xall = xlb[c]
                        for (s0, sn) in SEGS:
                            ts(U32[:, cl, s0:s0 + sn], XL3[:, c, s0:s0 + sn], vcol("lcw", (d * 4 + 3) * 8 + c), vcol("lcb", d * 8 + c),
                               ALU.mult, ALU.add, xall + [b_vec], [u32b[cl]], eng="pool")
                            for k in range(3):
                                sh = 3 - k
                                if d == 0:
                                    o_ap = U32[:, cl, s0 + sh:s0 + sn]
                                    i_ap = XL3[:, c, s0:s0 + sn - sh]
                                else:
                                    o_ap = U32[:, cl, s0:s0 + sn - sh]
                                    i_ap = XL3[:, c, s0 + sh:s0 + sn]
                                stt(o_ap, i_ap, vcol("lcw", (d * 4 + k) * 8 + c), o_ap, ALU.mult, ALU.add,
                                    xall + [b_vec, u32b[cl]], [u32b[cl]])
                        act(UBF[:, cl, :], U32[:, cl, :], AF.Identity, [u32b[cl]], [ubfb[cl]])
                    for cl in range(2):
                        c = c0 + cl
                        groups = [[0, 1, 2], [3, 4]] if d == 0 else [[0, 4, 3], [2, 1]]
                        prev_car = None
                        oi = -1
                        c4 = MISC[:, 48 + d * 8 + c:49 + d * 8 + c]
                        for grp in groups:
                            items = []
                            for ti in grp:
                                t0, n = TCH[ti]
                                gps = []
                                for gi in range(2):
                                    ps_t, ps_b = psum()
                                    for kc in range(2):
                                        mm(ps_t[:, 0:n], gwt[:, (gi * 4 + nblk) * 2 + kc, cl * 128:(cl + 1) * 128], UBF[:, kc, t0:t0 + n],
                                           kc == 0, kc == 1, [gwb, ubfb[kc]], [ps_b], inc=(kc == 1))
                                    gps.append((ps_t, ps_b))
                                r_, rb_ = ltmp()
                                act(r_[:, 0:n], gps[0][0][:, 0:n], AF.Tanh, [gps[0][1], b_ca], [rb_],
                                    bias=MISC[:, 96 + (d * 2 + 0) * 8 + c:97 + (d * 2 + 0) * 8 + c], scale=0.5)
                                a_, ab_ = ltmp()
                                act(a_[:, 0:n], r_[:, 0:n], AF.Exp, [rb_, b_ca], [ab_], bias=c4, scale=c4)
                                i_, ib_ = ltmp()
                                act(i_[:, 0:n], gps[1][0][:, 0:n], AF.Tanh, [gps[1][1], b_ca], [ib_],
                                    bias=MISC[:, 96 + (d * 2 + 1) * 8 + c:97 + (d * 2 + 1) * 8 + c], scale=0.5)
                                stt(i_[:, 0:n], i_[:, 0:n], 1.0, U32[:, cl, t0:t0 + n], ALU.add, ALU.mult, [ib_, u32b[cl]], [ib_])
                                items.append((ti, a_, ab_, i_, ib_))
                            for (ti, a_, ab_, i_, ib_) in items:
                                oi += 1
                                t0, n = TCH[ti]
                                m_, mb_ = ltmp()
                                act(m_[:, 0:n], a_[:, 0:n], AF.Square, [ab_], [mb_])
                                act(m_[:, 0:n], m_[:, 0:n], AF.Sqrt, [mb_, b_misc], [mb_], bias=MISC[:, 1:2], scale=-1.0)
                                if ti == 0:
                                    fc = 0 if d == 0 else NCTX - 1
                                    vmemset(m_[:, fc:fc + 1], 1.0, [mb_])
                                stt(i_[:, 0:n], i_[:, 0:n], 0.5, m_[:, 0:n], ALU.mult, ALU.mult, [ib_, mb_], [ib_])
                                if d == 0:
                                    init = 0.0 if oi == 0 else HS[:, cl, t0 - 1:t0]
                                    rd = [ab_, ib_] + ([hsb[cl][ti - 1]] if oi > 0 else [])
                                    p.op("dve", lambda e, o=HS[:, cl, t0:t0 + n], a=a_[:, 0:n], b=i_[:, 0:n], init=init:
                                         e.tensor_tensor_scan(out=o, data0=a, data1=b, initial=init, op0=ALU.mult, op1=ALU.add),
                                         rd, [hsb[cl][ti]])
                                else:
                                    h_, hb_ = ltmp()
                                    init = 0.0 if oi == 0 else prev_car
                                    p.op("dve", lambda e, o=h_[:, 0:n][:, ::-1], a=a_[:, 0:n][:, ::-1], b=i_[:, 0:n][:, ::-1], init=init:
                                         e.tensor_tensor_scan(out=o, data0=a, data1=b, initial=init, op0=ALU.mult, op1=ALU.add),
                                         [ab_, ib_, b_car], [hb_])
                                    ci = car_i[0] % 128
                                    car_i[0] += 1
                                    vcopy(CAR[:, ci:ci + 1], h_[:, 0:1], [hb_], [b_car])
                                    prev_car = CAR[:, ci:ci + 1]
                                    tt(h_[:, 0:n], h_[:, 0:n], HS[:, cl, t0:t0 + n], ALU.add, [hb_, hsb[cl][ti]], [hb_], eng="pool")
                                    tt(G3[:, c, t0:t0 + n], h_[:, 0:n], G3[:, c, t0:t0 + n], ALU.mult, [hb_, gb[c][ti]], [gb[c][ti]], eng="pool")
            for gi_ in gidx:
                w_release(gi_)
            dbg_dump("lru_xr1", XR[:, :], [128, 8 * NT], F32)
            p.barrier(("pe", "act", "dve"))
            yb = [[p.buf() for _ in range(5)] for _ in range(8)]
            for c in range(8):
                for ti in tis:
                    t0, n = TCH[ti]
                    if (c + ti) % 2 == 0:
                        vcopy(H3[:, c, t0:t0 + n], G3[:, c, t0:t0 + n], [gb[c][ti]], [yb[c][ti]])
                    else:
                        act(H3[:, c, t0:t0 + n], G3[:, c, t0:t0 + n], AF.Identity, [gb[c][ti]], [yb[c][ti]])
            st_.hb = yb
            reload(li_index)
            linear(2, 4, 8, 512, lambda wt, ocl, kc: wt[:, kc, ocl * 128:(ocl + 1) * 128], hb_rhs, hb_bufs,
                   tis, resid_evac(2, tis))

        vmemset(MISC[:, 0:1], EPS, [b_misc])
        vmemset(MISC[:, 1:2], 1.0, [b_misc])
        for idx, li in enumerate(layers):
            kind = li % 3
            need_ctx = li < DEPTH - 1
            st_.par = idx % 2
            if idx == 0:
                for _ in range(12):
                    mod_block()
                mod_finish(li, 0)
            st_.hb = grid(f"h{li}_")
            sp_tok = spill_issue(0 if idx == 0 else 1)
            norm_phase(0, [0, 1, 2, 3, 4])
            if DBG and idx == 0:
                p.dma("sp", dbg_mods, MODS()[:, :], reads=[b_mods()], writes=[o_buf])
                p.dma("sp", dbg_ab, AB()[:, :], reads=[b_ab()], writes=[o_buf])
                p.dma("sp", dbg_h1, HBt[:, :], reads=[b for row in st_.hb for b in row], writes=[o_buf])
            first_in_prog = (idx == 0)
            spill(sp_tok)
            lidx = 0 if first_in_prog else 1
            import os as _os
            if _os.environ.get("DBG_SKIP_MIX"):
                for _ in range({0: 6, 1: 6, 2: 8}[kind]):
                    w_next(8, 512)
                reload(lidx)
            elif kind == 0:
                attention(li, li // 3, need_ctx, lidx)
            elif kind == 1:
                conformer(li, need_ctx, lidx)
            else:
                rglru(li, need_ctx, lidx)
            tis = [0, 1, 2, 3, 4] if need_ctx else [1, 2, 3, 4]
            p.barrier(("pe", "act", "dve"))
            st_.hb = grid(f"h2{li}_")
            if _os.environ.get("DBG_SKIP_MLP"):
                for _ in range(16 + (12 if idx + 1 < len(layers) else 0)):
                    w_next(8, 512)
            else:
                if DBG and idx == 0:
                    p.dma("sp", dbg_x1, XR[:, :], reads=[b for row in st_.xb for b in row], writes=[o_buf])
                norm_phase(1, tis)
                if DBG and idx == 0:
                    p.dma("sp", dbg_h2, HBt[:, :], reads=[b for row in st_.hb for b in row], writes=[o_buf])
                if idx + 1 < len(layers):
                    mlp_phase(li, tis, layers[idx + 1], (idx + 1) % 2)
                else:
                    mlp_phase(li, tis)
        allb = [b for row in st_.xb for b in row]
        if last:
            for c in range(8):
                p.dma("sp", outT[c * 128:(c + 1) * 128, :], X3[:, c, NCTX:NT], reads=allb, writes=[o_buf])
        else:
            p.dma("sp", xs_out, XR[:, :], reads=allb, writes=[o_buf])
        p._wait("sp", o_buf.w)
        assert ws.consumed == len(wspecs), (ws.consumed, len(wspecs))
        p.emit()
    return nc


_WKEYS = ["mod_w", "mlp_w1", "mlp_w2", "attn_w_qkv", "attn_w_o", "conv_w_in", "conv_w_out",
          "lru_w_in", "lru_gate_w", "lru_w_out"]


def _common_maps(inputs):
    m = {k: np.ascontiguousarray(np.asarray(inputs[k], np.float32)) for k in _WKEYS}
    m["consts"] = _consts()
    m["rope"] = _rope_tables()
    return m


def kernel(**inputs):
    n = 8
    common = _common_maps(inputs)
    x = np.asarray(inputs["x"], np.float32)
    ctx = np.asarray(inputs["ctx"], np.float32)
    in_maps = []
    for b in range(n):
        m = dict(common)
        m["xT"] = np.ascontiguousarray(x[b].T)
        m["ctxT"] = np.ascontiguousarray(ctx[b].T)
        m["vecs"] = _pack_vecs(inputs, b)
        in_maps.append(m)
    nc = build_program((0, 1, 2, 3), True, True)
    res = run_bass_kernel_spmd(nc, in_maps, core_ids=list(range(n)))
    out = np.stack([np.ascontiguousarray(res.results[b]["outT"].T) for b in range(n)], axis=0)
    return out.astype(np.float32)
```

```python
import numpy as np
from contextlib import ExitStack
import concourse.bass as bass
import concourse.mybir as mybir
from concourse.bass_utils import run_bass_kernel_spmd
import ml_dtypes

F32 = mybir.dt.float32
BF16 = mybir.dt.bfloat16
AF = mybir.ActivationFunctionType
ALU = mybir.AluOpType

SELF_WAIT = True

NT, NCTX, NLAT, D = 2304, 256, 2048, 1024
TCH = [(0, 256), (256, 512), (768, 512), (1280, 512), (1792, 512)]
EPS = 1e-6
DEPTH = 4


class Sem:
    def __init__(self, h, name):
        self.h = h
        self.name = name
        self.count = 0


class Buf:
    __slots__ = ("name", "w", "r", "dsem")

    def __init__(self, name, dsem=None):
        self.name = name
        self.w = None
        self.r = []
        self.dsem = dsem


class Prog:
    ENG = ("pe", "act", "dve", "pool", "sp")

    def __init__(self, nc, stack):
        self.nc = nc
        self.stack = stack
        self.ops = {e: [] for e in self.ENG}
        self.esem = {}
        for e in ("pe", "act", "dve", "pool"):
            self.esem[e] = self.new_sem("s_" + e)
        self.known = {e: {} for e in self.ENG}
        self.pending_noinc = {e: False for e in self.ENG}
        self.nbuf = 0

    def new_sem(self, name):
        h = self.stack.enter_context(self.nc.semaphore(name))
        return Sem(h, name)

    def buf(self, name=None, dma=False):
        self.nbuf += 1
        name = name or f"b{self.nbuf}"
        return Buf(name, self.new_sem("d_" + name) if dma else None)

    def sb(self, name, shape, dt):
        return self.stack.enter_context(self.nc.sbuf_tensor(name, list(shape), dt))

    def ps(self, name, shape, dt=F32):
        return self.stack.enter_context(self.nc.psum_tensor(name, list(shape), dt))

    def _wait(self, eng, tok):
        if tok is None:
            return
        sem, val = tok
        if eng == "pe" and sem is self.esem["pe"]:
            return
        if (not SELF_WAIT) and eng in self.esem and sem is self.esem[eng]:
            return
        k = self.known[eng]
        if k.get(sem, 0) >= val:
            return
        k[sem] = val
        h = sem.h
        self.ops[eng].append(lambda e, h=h, val=val: e.wait_ge(h, val))

    def _deps(self, eng, reads, writes):
        for b in reads:
            self._wait(eng, b.w)
        for b in writes:
            self._wait(eng, b.w)
            for t in b.r:
                self._wait(eng, t)

    def _commit(self, tok, reads, writes):
        for b in reads:
            b.r.append(tok)
            if len(b.r) > 16:
                d = {}
                for s, v in b.r:
                    if d.get(s, 0) < v:
                        d[s] = v
                b.r = list(d.items())
        for b in writes:
            b.w = tok
            b.r = []

    def op(self, eng, fn, reads=(), writes=(), inc=True):
        self._deps(eng, reads, writes)
        sem = self.esem[eng]
        if inc:
            sem.count += 1
            val = sem.count
            h = sem.h
            self.ops[eng].append(lambda e, fn=fn, h=h: fn(e).then_inc(h, 1))
            self.pending_noinc[eng] = False
        else:
            val = sem.count + 1
            self.ops[eng].append(lambda e, fn=fn: fn(e))
            self.pending_noinc[eng] = True
        tok = (sem, val)
        self._commit(tok, reads, writes)
        return tok

    def dma(self, q, out, in_, reads=(), writes=(), dsem=None):
        self._deps(q, reads, writes)
        sem = dsem if dsem is not None else writes[0].dsem
        sem.count += 16
        val = sem.count
        h = sem.h
        self.ops[q].append(lambda e, out=out, in_=in_, h=h: e.dma_start(out=out, in_=in_).then_inc(h, 16))
        tok = (sem, val)
        self._commit(tok, reads, writes)
        return tok

    def barrier(self, engs=("pe", "act", "dve", "sp")):
        for e in ("pe", "act", "dve", "pool"):
            assert not self.pending_noinc[e]
        toks = [(self.esem[e], self.esem[e].count) for e in ("pe", "act", "dve", "pool")]
        if "pool" not in engs:
            engs = tuple(engs) + ("pool",)
        for e in engs:
            for t in toks:
                if t[1] > 0:
                    self._wait(e, t)

    def emit(self):
        for e in ("pe", "act", "dve"):
            assert not self.pending_noinc[e], f"engine {e} has trailing non-inc op"
        ops = self.ops
        with self.nc.Block() as block:
            @block.sync
            def _(eng):
                for f in ops["sp"]:
                    f(eng)

            @block.tensor
            def _(eng):
                for f in ops["pe"]:
                    f(eng)

            @block.scalar
            def _(eng):
                for f in ops["act"]:
                    f(eng)

            @block.vector
            def _(eng):
                for f in ops["dve"]:
                    f(eng)

            @block.gpsimd
            def _(eng):
                for f in ops["pool"]:
                    f(eng)


def _vec_layout():
    L = [("c", 8), ("cctx", 8)]
    for i in range(DEPTH):
        L += [(f"modb{i}", 48), (f"gmix{i}", 8), (f"gmlp{i}", 8)]
    for j in range(2):
        L += [(f"qg{j}", 1), (f"kg{j}", 1)]
    L += [("cbin", 16), ("cwdw", 248), ("cbdw", 8), ("cng", 8), ("cnb", 8), ("cbout", 8)]
    L += [("lcw", 64), ("lcb", 16), ("lgb", 32), ("llam", 16)]
    off = {}
    o = 0
    for n, k in L:
        off[n] = o
        o += k
    return off, o


VOFF, NV = _vec_layout()


def _pk(v):
    v = np.asarray(v, np.float32).reshape(-1, 128)
    return np.ascontiguousarray(v.T)


def _pack_vecs(inp, b):
    V = np.zeros((128, NV), np.float32)

    def put(name, arr):
        arr = np.asarray(arr, np.float32)
        V[:, VOFF[name]:VOFF[name] + arr.shape[1]] = arr

    put("c", _pk(inp["c"][b]))
    put("cctx", _pk(inp["c_ctx"]))
    for i in range(DEPTH):
        put(f"modb{i}", _pk(inp["mod_b"][i]))
        put(f"gmix{i}", _pk(inp["norm_mix_g"][i]))
        put(f"gmlp{i}", _pk(inp["norm_mlp_g"][i]))
    for j in range(2):
        put(f"qg{j}", np.tile(np.asarray(inp["attn_q_gain"][j], np.float32), 2)[:, None])
        put(f"kg{j}", np.tile(np.asarray(inp["attn_k_gain"][j], np.float32), 2)[:, None])
    put("cbin", _pk(inp["conv_b_in"][0]))
    wdw = np.asarray(inp["conv_w_dw"][0], np.float32).reshape(31, 8, 128).transpose(2, 0, 1).reshape(128, 248)
    put("cwdw", wdw)
    put("cbdw", _pk(inp["conv_b_dw"][0]))
    put("cng", _pk(inp["conv_norm_g"][0]))
    put("cnb", _pk(inp["conv_norm_b"][0]))
    put("cbout", _pk(inp["conv_b_out"][0]))
    lcw = np.asarray(inp["lru_conv_w"][0], np.float32).reshape(2, 4, 8, 128).transpose(3, 0, 1, 2).reshape(128, 64)
    put("lcw", lcw)
    put("lcb", np.asarray(inp["lru_conv_b"][0], np.float32).reshape(2, 8, 128).transpose(2, 0, 1).reshape(128, 16))
    put("lgb", np.asarray(inp["lru_gate_b"][0], np.float32).reshape(2, 2, 8, 128).transpose(3, 0, 1, 2).reshape(128, 32))
    put("llam", np.asarray(inp["lru_lambda"][0], np.float32).reshape(2, 8, 128).transpose(2, 0, 1).reshape(128, 16))
    return V


def _consts():
    p = np.arange(128)
    perm = (p[:, None] == (p[None, :] ^ 16)).astype(np.float32)
    onesblk = ((p[:, None] // 64) == (p[None, :] // 64)).astype(np.float32)
    ones = np.ones((128, 128), np.float32)
    ident = np.eye(128, dtype=np.float32)
    return np.concatenate([perm, onesblk, ones, ident], axis=1).astype(ml_dtypes.bfloat16)


def _rope_tables():
    t = np.arange(NLAT)
    row = (t // 64).astype(np.float64)
    col = (t % 64).astype(np.float64)
    inv = 10000.0 ** (-np.arange(16, dtype=np.float64) / 16.0)
    p = np.arange(128)
    d = p % 64
    a = d // 32
    half = (d // 16) % 2
    f = d % 16
    pos = np.where(a[:, None] == 0, row[None, :], col[None, :])
    ang = (pos.astype(np.float32) * inv.astype(np.float32)[f][:, None]).astype(np.float32)
    C = np.cos(ang).astype(np.float32)
    S = np.sin(ang).astype(np.float32) * np.where(half == 0, -1.0, 1.0)[:, None].astype(np.float32)
    return np.ascontiguousarray(np.concatenate([C, S], axis=1).astype(np.float32))


class K:
    pass


def build_program(layers=(0, 1, 2, 3), first=True, last=True):
    nc = bass.Bass("TRN2", target_bir_lowering=False)
    dr = {}

    def din(name, shape, dt=F32):
        dr[name] = nc.dram_tensor(name, list(shape), dt, kind="ExternalInput").ap()
        return dr[name]

    if first:
        din("xT", [D, NLAT])
        din("ctxT", [D, NCTX])
    else:
        din("xs_in", [128, 8 * NT])
    din("vecs", [128, NV])
    din("consts", [128, 512], BF16)
    din("rope", [128, 2 * NLAT])
    din("mod_w", [4, D, 6 * D])
    din("mlp_w1", [4, D, 4 * D])
    din("mlp_w2", [4, 4 * D, D])
    din("attn_w_qkv", [2, D, 1536])
    din("attn_w_o", [2, D, D])
    din("conv_w_in", [1, D, 2 * D])
    din("conv_w_out", [1, D, D])
    din("lru_w_in", [1, D, 2 * D])
    din("lru_gate_w", [1, 2, 2, 4, 256, 256])
    din("lru_w_out", [1, D, D])
    if last:
        outT = nc.dram_tensor("outT", [D, NLAT], F32, kind="ExternalOutput").ap()
    else:
        xs_out = nc.dram_tensor("xs_out", [128, 8 * NT], F32, kind="ExternalOutput").ap()
    xs = nc.dram_tensor("xs_scr", [128, 8 * NT], F32, kind="Internal").ap()
    import os as _os
    DBG = bool(_os.environ.get("DBG_DUMP"))
    if DBG:
        dbg_mods = nc.dram_tensor("dbg_mods", [128, 96], F32, kind="ExternalOutput").ap()
        dbg_ab = nc.dram_tensor("dbg_ab", [128, 64], F32, kind="ExternalOutput").ap()
        dbg_h1 = nc.dram_tensor("dbg_h1", [128, 8 * NT], BF16, kind="ExternalOutput").ap()
        dbg_h2 = nc.dram_tensor("dbg_h2", [128, 8 * NT], BF16, kind="ExternalOutput").ap()
        dbg_x1 = nc.dram_tensor("dbg_x1", [128, 8 * NT], F32, kind="ExternalOutput").ap()

    with ExitStack() as st:
        p = Prog(nc, st)
        XR = p.sb("XR", [128, 8 * NT], F32)
        HBt = p.sb("HB", [128, 8 * NT], BF16)
        AUX = p.sb("AUX", [128, 10368], BF16)
        SLOT = [p.sb(f"slot{i}", [128, 4096], BF16) for i in range(4)]
        VEC = p.sb("VEC", [128, NV], F32)
        CONST = p.sb("CONST", [128, 512], BF16)
        MODS_ = [p.sb(f"MODS{i}", [128, 96], F32) for i in range(2)]
        AB_ = [p.sb(f"AB{i}", [128, 64], F32) for i in range(2)]
        SC = p.sb("SC", [128, 16], BF16)
        MISC = p.sb("MISC", [128, 256], F32)
        T32 = [p.sb(f"t32_{i}", [128, 512], F32) for i in range(6)]
        TB = p.sb("TB", [128, 8 * 512], BF16)
        T32X = [p.sb(f"t32x_{i}", [128, 512], F32) for i in range(2)]
        t32xb = [p.buf(f"t32x_{i}") for i in range(2)]
        T32H = [p.sb(f"t32h_{i}", [128, 512], F32) for i in range(2)]
        t32hb = [p.buf(f"t32h_{i}") for i in range(2)]
        b_tb = p.buf("tb")
        b_car = p.buf("car")
        PT = [p.sb(f"pt{i}", [128, 512], BF16) for i in range(6)]
        PS = [p.ps(f"ps{i}", [128, 512], F32) for i in range(4)]
        PSD = [p.ps(f"psd{i}", [128, 1024], F32) for i in range(2)]
        PS = PS + [PSD[0][:, 0:512], PSD[0][:, 512:1024], PSD[1][:, 0:512], PSD[1][:, 512:1024]]
        psb = [p.buf(f"ps{i}") for i in range(8)]
        t32b = [p.buf(f"t32_{i}") for i in range(6)]
        ptb = [p.buf(f"pt{i}") for i in range(6)]
        slotb = [p.buf(f"slot{i}", dma=True) for i in range(4)]
        b_vec = p.buf("vec", dma=True)
        b_const = p.buf("const", dma=True)
        b_mods_ = [p.buf("mods0"), p.buf("mods1")]
        b_ab_ = [p.buf("ab0"), p.buf("ab1")]
        b_sc = p.buf("sc")
        b_misc = p.buf("misc")
        x_dsem = p.new_sem("d_x")
        xc_dsem = [p.new_sem(f"d_xc{c}") for c in range(8)]
        o_buf = p.buf("out", dma=True)
        spill_buf = p.buf("spill", dma=True)

        dbg_outs = {}

        def dbg_dump(name, ap, shape, dt):
            if not DBG:
                return
            t = nc.dram_tensor("dd_" + name, list(shape), dt, kind="ExternalOutput").ap()
            p.barrier(("sp",))
            p.dma("sp", t, ap, writes=[o_buf])
            for e in ("pe", "act", "dve"):
                p._wait(e, o_buf.w)

        PERM = CONST[:, 0:128]
        ONESBLK = CONST[:, 128:256]
        ONES = CONST[:, 256:384]
        IDENT = CONST[:, 384:512]

        X3 = XR[:, :].rearrange("p (c t) -> p c t", c=8)
        XRb = XR[:, :].bitcast(BF16)
        H3 = HBt[:, :].rearrange("p (c t) -> p c t", c=8)
        HBf = HBt[:, :].bitcast(F32)
        AUXf = AUX[:, :].bitcast(F32)

        def grid(name):
            return [[p.buf(f"{name}{c}_{t}") for t in range(5)] for c in range(8)]

        st_ = K()
        st_.xb = grid("x")
        st_.hb = grid("h")
        st_.rr = {"t32": 0, "pt": 0, "ps": 0}

        def vcol(name, j=0):
            o = VOFF[name] + j
            return VEC[:, o:o + 1]

        def tmp32():
            i = st_.rr["t32"] % 6
            st_.rr["t32"] += 1
            return T32[i], t32b[i]

        def tmppt():
            i = st_.rr["pt"] % 6
            st_.rr["pt"] += 1
            return PT[i], ptb[i]

        def psum(group=None):
            group = group if group is not None else list(range(7))
            key = ("ps",) + tuple(group)
            k_ = st_.rr.get(key, 0)
            st_.rr[key] = k_ + 1
            i = group[k_ % len(group)]
            return PS[i], psb[i]

        def mm(out, lhsT, rhs, start, stop, reads, writes, inc):
            p.op("pe", lambda e: e.matmul(out, lhsT, rhs, start=start, stop=stop), reads, writes, inc=inc)

        def act(out, in_, func, reads, writes, bias=None, scale=None):
            kw = {}
            if bias is not None:
                kw["bias"] = bias
            if scale is not None:
                kw["scale"] = scale
            p.op("act", lambda e: e.activation(out=out, in_=in_, func=func, **kw), reads, writes)

        def tt(out, in0, in1, op, reads, writes, eng="dve"):
            p.op(eng, lambda e: e.tensor_tensor(out=out, in0=in0, in1=in1, op=op), reads, writes)

        def ts(out, in0, s1, s2, op0, op1, reads, writes, eng="dve"):
            if s2 is None:
                p.op(eng, lambda e: e.tensor_scalar(out=out, in0=in0, scalar1=s1, scalar2=None, op0=op0), reads, writes)
            else:
                p.op(eng, lambda e: e.tensor_scalar(out=out, in0=in0, scalar1=s1, scalar2=s2, op0=op0, op1=op1), reads, writes)

        def stt(out, in0, scalar, in1, op0, op1, reads, writes, eng="dve"):
            p.op(eng, lambda e: e.scalar_tensor_tensor(out=out, in0=in0, scalar=scalar, in1=in1, op0=op0, op1=op1), reads, writes)

        def recip(out, in_, reads, writes):
            p.op("dve", lambda e: e.reciprocal(out=out, in_=in_), reads, writes)

        def vcopy(out, in_, reads, writes):
            p.op("dve", lambda e: e.tensor_copy(out=out, in_=in_), reads, writes)

        def vmemset(ap, val, writes):
            p.op("dve", lambda e: e.memset(ap, val), (), writes)

        wspecs = []

        def wv(ap2d):
            return ap2d.rearrange("(k p) n -> p k n", p=128)

        NMOD = [2, 1, 2, 1, 2, 1, 2, 1]
        for lidx_, li in enumerate(layers):
            kind = li % 3
            j = li // 3
            if lidx_ == 0:
                for b in range(12):
                    wspecs.append([(0, 8, 512, wv(dr["mod_w"][li, :, b * 512:(b + 1) * 512]))])
            if kind == 0:
                wq = dr["attn_w_qkv"][j]
                for b in range(2):
                    wspecs.append([(0, 8, 512, wv(wq[:, b * 512:(b + 1) * 512]))])
                sp_ = []
                for g in range(4):
                    for dup in range(2):
                        sp_.append(((g * 2 + dup) * 64, 8, 64, wv(wq[:, 1024 + g * 64:1024 + (g + 1) * 64]), 512))
                wspecs.append(sp_)
                wspecs.append([(0, 8, 256, wv(wq[:, 1280:1536]))])
                for b in range(2):
                    wspecs.append([(0, 8, 512, wv(dr["attn_w_o"][j][:, b * 512:(b + 1) * 512]))])
            elif kind == 1:
                wi = dr["conv_w_in"][0]
                for b in range(4):
                    wspecs.append([(0, 8, 256, wv(wi[:, b * 256:(b + 1) * 256]), 512),
                                   (256, 8, 256, wv(wi[:, 1024 + b * 256:1024 + (b + 1) * 256]), 512)])
                for b in range(2):
                    wspecs.append([(0, 8, 512, wv(dr["conv_w_out"][0][:, b * 512:(b + 1) * 512]))])
            else:
                wi = dr["lru_w_in"][0]
                for b in range(4):
                    wspecs.append([(0, 8, 512, wv(wi[:, b * 512:(b + 1) * 512]))])
                for d in range(2):
                    gw = dr["lru_gate_w"][0, d].rearrange("g n k e -> (g n k) e")
                    wspecs.append([(0, 16, 256, wv(gw))])
                for b in range(2):
                    wspecs.append([(0, 8, 512, wv(dr["lru_w_out"][0][:, b * 512:(b + 1) * 512]))])
            mb_ = 0
            for hb in range(8):
                wspecs.append([(0, 8, 512, wv(dr["mlp_w1"][li, :, hb * 512:(hb + 1) * 512]))])
                wspecs.append([(0, 4, 1024, wv(dr["mlp_w2"][li, hb * 512:(hb + 1) * 512, :]))])
                if lidx_ + 1 < len(layers):
                    nl_ = layers[lidx_ + 1]
                    for _ in range(NMOD[hb]):
                        wspecs.append([(0, 8, 512, wv(dr["mod_w"][nl_, :, mb_ * 512:(mb_ + 1) * 512]))])
                        mb_ += 1

        ws = K()
        ws.issued = 0
        ws.consumed = 0

        def w_issue(jb):
            s = jb % 4
            for spec in wspecs[jb]:
                if len(spec) == 5:
                    off, kcn, ncol, src, rowlen = spec
                    dst = SLOT[s][:, 0:kcn * rowlen].rearrange("p (k n) -> p k n", k=kcn)[:, :, off:off + ncol]
                else:
                    off, kcn, ncol, src = spec
                    dst = SLOT[s][:, off:off + kcn * ncol].rearrange("p (k n) -> p k n", k=kcn)
                p.dma("pool", dst, src, writes=[slotb[s]])

        ws.released = set()
        ws.pinned = set()

        def w_release(i):
            ws.pinned.discard(i)
            ws.released.add(i)

        def w_next(kcn, ncol, pin=False):
            i = ws.consumed
            if i - 1 >= 0 and (i - 1) not in ws.pinned:
                ws.released.add(i - 1)
            while ws.issued < min(i + 4, len(wspecs)) and (ws.issued < 4 or (ws.issued - 4) in ws.released):
                w_issue(ws.issued)
                ws.issued += 1
            assert ws.issued > i, "weight block not issued (pinned slot deadlock)"
            if pin:
                ws.pinned.add(i)
            ws.consumed += 1
            s = i % 4
            return SLOT[s][:, 0:kcn * ncol].rearrange("p (k n) -> p k n", k=kcn), slotb[s]

        p.dma("sp", VEC[:, :], dr["vecs"], writes=[b_vec])
        p.dma("sp", CONST[:, :], dr["consts"], writes=[b_const])

        def load_x_from_input():
            if first:
                p.dma("sp", X3[:, :, 0:NCTX], dr["ctxT"].rearrange("(c p) t -> p c t", p=128),
                      writes=[st_.xb[c][0] for c in range(8)], dsem=x_dsem)
                for c in range(8):
                    p.dma("sp", X3[:, c, NCTX:NT], dr["xT"][c * 128:(c + 1) * 128, :], writes=st_.xb[c][1:5], dsem=xc_dsem[c])
            else:
                for c in range(8):
                    p.dma("sp", X3[:, c, :], dr["xs_in"][:, c * NT:(c + 1) * NT], writes=st_.xb[c], dsem=xc_dsem[c])

        load_x_from_input()
        SC3 = SC[:, :].rearrange("p (k s) -> p k s", s=2)
        act(SC3[:, :, 0], VEC[:, VOFF["cctx"]:VOFF["cctx"] + 8], AF.Silu, [b_vec], [b_sc])
        act(SC3[:, :, 1], VEC[:, VOFF["c"]:VOFF["c"] + 8], AF.Silu, [b_vec], [b_sc])

        st_.par = 0

        def MODS():
            return MODS_[st_.par]

        def AB():
            return AB_[st_.par]

        def b_mods():
            return b_mods_[st_.par]

        def b_ab():
            return b_ab_[st_.par]

        def modcol(grp, c, s):
            return MODS()[:, (grp * 8 + c) * 2 + s:(grp * 8 + c) * 2 + s + 1]

        modst = K()
        modst.nb = 0

        def mod_block():
            b = modst.nb
            modst.nb += 1
            ps_t, ps_b = PS[7], psb[7]
            wt, wb = w_next(8, 512)
            for jj in range(4):
                jx = b * 4 + jj
                for kc in range(8):
                    mm(ps_t[:, 2 * jx:2 * jx + 2], wt[:, kc, jj * 128:(jj + 1) * 128], SC3[:, kc, :],
                       kc == 0, kc == 7, [wb, b_sc], [ps_b], inc=(jj == 3 and kc == 7))

        def mod_finish(li, par):
            assert modst.nb == 12
            modst.nb = 0
            ps_t, ps_b = PS[7], psb[7]
            M = MODS_[par]
            A = AB_[par]
            M3 = M[:, :].rearrange("p (j s) -> p j s", s=2)
            ps3 = ps_t[:, 0:96].rearrange("p (j s) -> p j s", s=2)
            mb = VEC[:, VOFF[f"modb{li}"]:VOFF[f"modb{li}"] + 48]
            for s in range(2):
                tt(M3[:, :, s], ps3[:, :, s], mb, ALU.add, [ps_b, b_vec], [b_mods_[par]])
            gm = VEC[:, VOFF[f"gmix{li}"]:VOFF[f"gmix{li}"] + 8]
            gl = VEC[:, VOFF[f"gmlp{li}"]:VOFF[f"gmlp{li}"] + 8]
            for s in range(2):
                stt(A[:, s * 8:s * 8 + 8], M3[:, 8:16, s], 1.0, gm, ALU.add, ALU.mult, [b_mods_[par], b_vec], [b_ab_[par]])
                stt(A[:, 16 + s * 8:16 + s * 8 + 8], M3[:, 32:40, s], 1.0, gl, ALU.add, ALU.mult, [b_mods_[par], b_vec], [b_ab_[par]])

        def norm_phase(which, tis):
            TB3 = TB[:, :].rearrange("p (c t) -> p c t", c=8)

            def stage_a1(ti, k):
                t0, n = TCH[ti]
                for c in range(8):
                    act(TB3[:, c, 0:n], X3[:, c, t0:t0 + n], AF.Square, [st_.xb[c][ti]], [b_tb])
                ps_t, ps_b = psum()
                for c in range(8):
                    mm(ps_t[:, 0:n], ONES, TB3[:, c, 0:n], c == 0, c == 7, [b_tb, b_const], [ps_b], inc=(c == 7))
                return ps_t, ps_b

            def stage_a2(ti, k, ps_t, ps_b):
                t0, n = TCH[ti]
                sd, sdb = tmp32()
                act(sd[:, 0:n], ps_t[:, 0:n], AF.Ln, [ps_b, b_misc], [sdb], bias=MISC[:, 0:1], scale=1.0 / D)
                rs, rsb = T32H[k % 2], t32hb[k % 2]
                act(rs[:, 0:n], sd[:, 0:n], AF.Exp, [sdb], [rsb], scale=-0.5)
                return rs, rsb

            def stage_b(ti, rs, rsb):
                t0, n = TCH[ti]
                s = 0 if ti == 0 else 1
                for c in range(8):
                    t_, tb_ = tmp32()
                    a_ap = AB()[:, which * 16 + s * 8 + c:which * 16 + s * 8 + c + 1]
                    stt(t_[:, 0:n], X3[:, c, t0:t0 + n], a_ap, rs[:, 0:n], ALU.mult, ALU.mult,
                        [st_.xb[c][ti], b_ab(), rsb], [tb_])
                    act(H3[:, c, t0:t0 + n], t_[:, 0:n], AF.Identity, [tb_, b_mods()], [st_.hb[c][ti]],
                        bias=modcol(0 if which == 0 else 3, c, s))

            prev = None
            for k, ti in enumerate(tis):
                pst = stage_a1(ti, k)
                if prev is not None:
                    stage_b(*prev)
                prev = (ti,) + stage_a2(ti, k, *pst)
            stage_b(*prev)

        def spill_issue(li_index):
            if li_index == 0:
                return None
            tok = None
            for c in range(8):
                tok = p.dma("sp", xs[:, c * NT:(c + 1) * NT], X3[:, c, :], reads=st_.xb[c], writes=[spill_buf])
            return tok

        def spill(tok):
            p.barrier(("pe", "act", "dve", "sp"))
            if tok is not None:
                for e in ("pe", "act", "dve", "sp", "pool"):
                    p._wait(e, tok)
            return tok

        def reload(li_index):
            p.barrier(("pe", "act", "dve", "sp"))
            st_.xb = grid(f"x{li_index}_")
            allb = [b for row in st_.xb for b in row]
            if li_index == 0:
                load_x_from_input()
            else:
                for c in range(8):
                    p.dma("sp", X3[:, c, :], xs[:, c * NT:(c + 1) * NT], reads=[spill_buf], writes=st_.xb[c], dsem=xc_dsem[c])

        def linear(nblocks, ocs_per_block, kcn, wcols, lhs_fn, rhs_fn, rhs_bufs_fn, tis, evac, psgroup=None):
            for b in range(nblocks):
                wt, wb = w_next(kcn, wcols)
                for ocl in range(ocs_per_block):
                    for ti in tis:
                        t0, n = TCH[ti]
                        ps_t, ps_b = psum(psgroup)
                        for kc in range(kcn):
                            mm(ps_t[:, 0:n], lhs_fn(wt, ocl, kc), rhs_fn(kc, t0, n), kc == 0, kc == kcn - 1,
                               [wb] + rhs_bufs_fn(kc, ti), [ps_b], inc=(kc == kcn - 1))
                        evac(b, ocl, ti, ps_t[:, 0:n], ps_b)

        def resid_evac(grp, tis_all, bias_name=None):
            def ev(b, ocl, ti, ps_ap, ps_b):
                oc = b * 4 + ocl
                t0, n = TCH[ti]
                s = 0 if ti == 0 else 1
                src = ps_ap
                rd = [ps_b]
                if bias_name is not None:
                    t_, tb_ = tmp32()
                    act(t_[:, 0:n], ps_ap, AF.Identity, [ps_b, b_vec], [tb_], bias=vcol(bias_name, oc))
                    src = t_[:, 0:n]
                    rd = [tb_]
                stt(X3[:, oc, t0:t0 + n], src, modcol(grp, oc, s), X3[:, oc, t0:t0 + n], ALU.mult, ALU.add,
                    rd + [b_mods(), st_.xb[oc][ti]], [st_.xb[oc][ti]])
            return ev

        def hb_rhs(kc, t0, n):
            return H3[:, kc, t0:t0 + n]

        def hb_bufs(kc, ti):
            return [st_.hb[kc][ti]]

        def mlp_phase(li, tis, next_li=None, next_par=None):
            HID = AUX[:, 0:4 * NT].rearrange("p (c t) -> p c t", c=4)
            hidb = [[p.buf() for _ in range(5)] for _ in range(4)]
            for hb_i in range(8):
                w1, w1b = w_next(8, 512)
                for ti in tis:
                    t0, n = TCH[ti]
                    for ocl in range(4):
                        ps_t, ps_b = psum()
                        for kc in range(8):
                            mm(ps_t[:, 0:n], w1[:, kc, ocl * 128:(ocl + 1) * 128], H3[:, kc, t0:t0 + n], kc == 0, kc == 7,
                               [w1b, st_.hb[kc][ti]], [ps_b], inc=(kc == 7))
                        t_, tb_ = tmp32()
                        act(t_[:, 0:n], ps_t[:, 0:n], AF.Relu, [ps_b], [tb_])
                        tt(HID[:, ocl, t0:t0 + n], t_[:, 0:n], t_[:, 0:n], ALU.mult, [tb_], [hidb[ocl][ti]])
                w2, w2b = w_next(4, 1024)
                for ti in tis:
                    t0, n = TCH[ti]
                    s = 0 if ti == 0 else 1
                    for oc in range(8):
                        ps_t, ps_b = psum()
                        for kc in range(4):
                            mm(ps_t[:, 0:n], w2[:, kc, oc * 128:(oc + 1) * 128], HID[:, kc, t0:t0 + n], kc == 0, kc == 3,
                               [w2b, hidb[kc][ti]], [ps_b], inc=(kc == 3))
                        stt(X3[:, oc, t0:t0 + n], ps_t[:, 0:n], modcol(5, oc, s), X3[:, oc, t0:t0 + n], ALU.mult, ALU.add,
                            [ps_b, b_mods(), st_.xb[oc][ti]], [st_.xb[oc][ti]])
                if next_li is not None:
                    for _ in range(NMOD[hb_i]):
                        mod_block()
            if next_li is not None:
                mod_finish(next_li, next_par)

        def attention(li, j_att, need_ctx, li_index):
            QT = XRb[:, 0:18432].rearrange("p (c t) -> p c t", c=8)
            KT2 = XRb[:, 18432:27648].rearrange("p (c t) -> p c t", c=4)
            ROC = XR[:, 13824:15872]
            ROS = XR[:, 15872:17920]
            VA = AUX[:, 0:18 * 576].rearrange("p (k x) -> p k x", k=18)
            b_rope = p.buf(f"rope{li}", dma=True)
            qb = [[p.buf() for _ in range(5)] for _ in range(8)]
            kb = [[p.buf() for _ in range(5)] for _ in range(4)]
            vab = [p.buf() for _ in range(18)]
            b_va_init = p.buf()
            p.dma("sp", XR[:, 13824:17920], dr["rope"], writes=[b_rope])
            vmemset(AUX[:, 0:18 * 576], 1.0, [b_va_init] + vab)
            q_tis = [0, 1, 2, 3, 4] if need_ctx else [1, 2, 3, 4]

            PS_A = [0, 1, 2, 3]
            PS_B = [4, 5]
            PS_C = [6, 7]
            pending = []

            def qk_item(ps_t, ps_b, n, ti, gain_ap, dst_ap, dst_buf):
                t0 = TCH[ti][0]
                lat = ti != 0
                state = {}

                def stage_b1():
                    sq, sqb = tmppt()
                    act(sq[:, 0:n], ps_t[:, 0:n], AF.Square, [ps_b], [sqb])
                    ss_t, ss_b = psum(PS_B)
                    mm(ss_t[:, 0:n], ONESBLK, sq[:, 0:n], True, True, [sqb, b_const], [ss_b], inc=True)
                    sd, sdb = tmp32()
                    act(sd[:, 0:n], ss_t[:, 0:n], AF.Ln, [ss_b, b_misc], [sdb], bias=MISC[:, 0:1], scale=1.0 / 64)
                    state["sd"] = (sd, sdb)

                def stage_b2():
                    sd, sdb = state["sd"]
                    rs, rsb = tmp32()
                    act(rs[:, 0:n], sd[:, 0:n], AF.Exp, [sdb], [rsb], scale=-0.5)
                    if not lat:
                        stt(dst_ap, ps_t[:, 0:n], gain_ap, rs[:, 0:n], ALU.mult, ALU.mult, [ps_b, rsb, b_vec], [dst_buf])
                    else:
                        qn, qnb = tmppt()
                        stt(qn[:, 0:n], ps_t[:, 0:n], gain_ap, rs[:, 0:n], ALU.mult, ALU.mult, [ps_b, rsb, b_vec], [qnb])
                        state["qn"] = (qn, qnb)

                def stage_c():
                    if not lat:
                        return
                    qn, qnb = state["qn"]
                    rot_t, rot_b = psum(PS_C)
                    mm(rot_t[:, 0:n], PERM, qn[:, 0:n], True, True, [qnb, b_const], [rot_b], inc=True)
                    t1, t1b = tmp32()
                    tt(t1[:, 0:n], qn[:, 0:n], ROC[:, t0 - NCTX:t0 - NCTX + n], ALU.mult, [qnb, b_rope], [t1b], eng="pool")
                    t2, t2b = tmp32()
                    tt(t2[:, 0:n], rot_t[:, 0:n], ROS[:, t0 - NCTX:t0 - NCTX + n], ALU.mult, [rot_b, b_rope], [t2b])
                    tt(dst_ap, t1[:, 0:n], t2[:, 0:n], ALU.add, [t1b, t2b], [dst_buf], eng="pool")
                return stage_b1, stage_b2, stage_c

            def pipe_push(item):
                pending.append(item)
                for back, stg in ((2, 0), (3, 1), (4, 2)):
                    if len(pending) >= back:
                        pending[-back][stg]()

            def pipe_flush():
                for extra in range(1, 4):
                    for back, stg in ((2, 0), (3, 1), (4, 2)):
                        idx_ = len(pending) + extra - back
                        if 0 <= idx_ < len(pending):
                            pending[idx_][stg]()
                pending.clear()

            for b in range(2):
                wt, wb = w_next(8, 512)
                for ocl in range(4):
                    oc = b * 4 + ocl
                    for ti in q_tis:
                        t0, n = TCH[ti]
                        ps_t, ps_b = psum(PS_A)
                        for kc in range(8):
                            mm(ps_t[:, 0:n], wt[:, kc, ocl * 128:(ocl + 1) * 128], H3[:, kc, t0:t0 + n], kc == 0, kc == 7,
                               [wb, st_.hb[kc][ti]], [ps_b], inc=(kc == 7))
                        pipe_push(qk_item(ps_t, ps_b, n, ti, vcol(f"qg{j_att}"), QT[:, oc, t0:t0 + n], qb[oc][ti]))
            wt, wb = w_next(8, 512)
            for g in range(4):
                for ti in range(5):
                    t0, n = TCH[ti]
                    ps_t, ps_b = psum(PS_A)
                    for kc in range(8):
                        mm(ps_t[:, 0:n], wt[:, kc, g * 128:(g + 1) * 128], H3[:, kc, t0:t0 + n], kc == 0, kc == 7,
                           [wb, st_.hb[kc][ti]], [ps_b], inc=(kc == 7))
                    pipe_push(qk_item(ps_t, ps_b, n, ti, vcol(f"kg{j_att}"), KT2[:, g, t0:t0 + n], kb[g][ti]))
            pipe_flush()
            wt, wb = w_next(8, 256)
            for kt in range(18):
                ti = 0 if kt < 2 else 1 + (kt - 2) // 4
                ps_t, ps_b = psum(PS_A)
                for kc in range(8):
                    mm(ps_t[:, 0:256], H3[:, kc, kt * 128:(kt + 1) * 128], wt[:, kc, :], kc == 0, kc == 7,
                       [wb, st_.hb[kc][ti]], [ps_b], inc=(kc == 7))
                dst = VA[:, kt, 64:576].rearrange("p (g x) -> p g x", x=128)[:, :, 0:64]
                src = ps_t[:, 0:256].rearrange("p (g x) -> p g x", x=64)
                act(dst, src, AF.Identity, [ps_b], [vab[kt]])

            OBANK = [[0, 1], [2, 3]]
            ptdb = [p.buf() for _ in range(4)]
            it = 0
            for jp in range(8):
                g = jp // 2
                for ti in q_tis:
                    t0, n = TCH[ti]
                    kts = list(range(18)) if ti != 0 else [0, 1]
                    ob = OBANK[it % 2]
                    it += 1
                    o_t = [PS[ob[0]], PS[ob[1]]]
                    o_b = [psb[ob[0]], psb[ob[1]]]

                    def s_stage(kt):
                        tik = 0 if kt < 2 else 1 + (kt - 2) // 4
                        di = st_.rr.get("psd", 0) % 2
                        st_.rr["psd"] = st_.rr.get("psd", 0) + 1
                        dt_ = PSD[di]
                        dbs = [psb[4 + 2 * di], psb[5 + 2 * di]]
                        for h in range(2):
                            mm(dt_[:, h * 512:h * 512 + n], KT2[h * 64:(h + 1) * 64, g, kt * 128:(kt + 1) * 128],
                               QT[h * 64:(h + 1) * 64, jp, t0:t0 + n], True, True,
                               [kb[g][tik], qb[jp][ti]], [dbs[h]], inc=(h == 1))
                        return dt_, dbs

                    def pv_stage(kt, sres):
                        dt_, dbs = sres
                        pi = st_.rr.get("ptd", 0) % 4
                        st_.rr["ptd"] = st_.rr.get("ptd", 0) + 1
                        ptd = TB[:, pi * 1024:(pi + 1) * 1024]
                        src = dt_[:, :].rearrange("p (h x) -> p h x", h=2)[:, :, 0:n]
                        dst = ptd.rearrange("p (h x) -> p h x", h=2)[:, :, 0:n]
                        act(dst, src, AF.Exp, dbs, [ptdb[pi]], scale=0.125)
                        for h in range(2):
                            if h == 0:
                                lhs = VA[:, kt, 64 + 128 * g:192 + 128 * g]
                            else:
                                lhs = VA[:, kt, 128 * g:128 + 128 * g]
                            mm(o_t[h][:, 0:n], lhs, ptd[:, h * 512:h * 512 + n], kt == kts[0], kt == kts[-1],
                               [ptdb[pi], vab[kt]], [o_b[h]], inc=True)

                    prev = s_stage(kts[0])
                    for idx, kt in enumerate(kts):
                        nxt = s_stage(kts[idx + 1]) if idx + 1 < len(kts) else None
                        pv_stage(kt, prev)
                        prev = nxt
                    for h in range(2):
                        rc, rcb = tmp32()
                        recip(rc[:, 0:n], o_t[h][:, 0:n], [o_b[h]], [rcb])
                        lo, hi = (0, 64) if h == 0 else (64, 128)
                        dlo, dhi = (64, 128) if h == 0 else (0, 64)
                        tt(H3[lo:hi, jp, t0:t0 + n], o_t[h][lo:hi, 0:n], rc[dlo:dhi, 0:n], ALU.mult,
                           [o_b[h], rcb], [st_.hb[jp][ti]])
            dbg_dump("att_xr", XR[:, :], [128, 8 * NT], F32)
            dbg_dump("att_aux", AUX[:, :], [128, 10368], BF16)
            dbg_dump("att_hb", HBt[:, :], [128, 8 * NT], BF16)
            reload(li_index)
            linear(2, 4, 8, 512, lambda wt, ocl, kc: wt[:, kc, ocl * 128:(ocl + 1) * 128], hb_rhs, hb_bufs,
                   q_tis, resid_evac(2, q_tis))

        def conformer(li, need_ctx, li_index):
            UC = XRb[:, 0:8 * 286].rearrange("p (c t) -> p c t", c=8)
            UL = XRb[:, 2288:2288 + 8 * 2078].rearrange("p (c t) -> p c t", c=8)
            DG = [XRb[:, 18912 + i * 3968:18912 + (i + 1) * 3968].rearrange("p (k m) -> p k m", k=31) for i in range(2)]
            ub = [[p.buf() for _ in range(5)] for _ in range(8)]
            upad = p.buf()
            dgb = [p.buf(), p.buf()]
            vmemset(XRb[:, 0:18912], 0.0, [upad] + [b for row in ub for b in row])
            tis = [0, 1, 2, 3, 4]

            def useg(c, ti, k, n):
                if ti == 0:
                    return UC[:, c, k:k + n]
                o = TCH[ti][0] - NCTX
                return UL[:, c, o + k:o + k + n]

            for b in range(4):
                wt, wb = w_next(8, 512)
                for cl in range(2):
                    c = b * 2 + cl
                    for ti in tis:
                        t0, n = TCH[ti]
                        pa_t, pa_b = psum()
                        for kc in range(8):
                            mm(pa_t[:, 0:n], wt[:, kc, cl * 128:(cl + 1) * 128], H3[:, kc, t0:t0 + n], kc == 0, kc == 7,
                               [wb, st_.hb[kc][ti]], [pa_b], inc=(kc == 7))
                        pg_t, pg_b = psum()
                        for kc in range(8):
                            mm(pg_t[:, 0:n], wt[:, kc, 256 + cl * 128:256 + (cl + 1) * 128], H3[:, kc, t0:t0 + n], kc == 0, kc == 7,
                               [wb, st_.hb[kc][ti]], [pg_b], inc=(kc == 7))
                        sg, sgb = tmp32()
                        act(sg[:, 0:n], pg_t[:, 0:n], AF.Sigmoid, [pg_b, b_vec], [sgb], bias=vcol("cbin", 8 + c))
                        stt(useg(c, ti, 15, n), pa_t[:, 0:n], vcol("cbin", c), sg[:, 0:n], ALU.add, ALU.mult,
                            [pa_b, sgb, b_vec, upad], [ub[c][ti]])
            vb = [[p.buf() for _ in range(5)] for _ in range(8)]
            for c in range(8):
                par = c % 2
                for k in range(31):
                    ts(DG[par][:, k, :], IDENT, vcol("cwdw", k * 8 + c), None, ALU.mult, None, [b_const, b_vec], [dgb[par]])
                for ti in tis:
                    t0, n = TCH[ti]
                    ps_t, ps_b = psum()
                    nb = [ub[c][ti]]
                    if ti > 1:
                        nb.append(ub[c][ti - 1])
                    if 1 <= ti < 4:
                        nb.append(ub[c][ti + 1])
                    for k in range(31):
                        mm(ps_t[:, 0:n], DG[par][:, k, :], useg(c, ti, k, n), k == 0, k == 30,
                           [dgb[par], upad] + nb, [ps_b], inc=(k == 30))
                    act(H3[:, c, t0:t0 + n], ps_t[:, 0:n], AF.Identity, [ps_b, b_vec],
                        [vb[c][ti], st_.hb[c][ti]], bias=vcol("cbdw", c))
            TB3 = TB[:, :].rearrange("p (c t) -> p c t", c=8)
            yb = [[p.buf() for _ in range(5)] for _ in range(8)]
            def ln_a1(ti, k):
                t0, n = TCH[ti]
                pm_t, pm_b = psum()
                for c in range(8):
                    mm(pm_t[:, 0:n], ONES, H3[:, c, t0:t0 + n], c == 0, c == 7, [vb[c][ti], b_const], [pm_b], inc=(c == 7))
                for c in range(8):
                    act(TB3[:, c, 0:n], H3[:, c, t0:t0 + n], AF.Square, [vb[c][ti]], [b_tb])
                pq_t, pq_b = psum()
                for c in range(8):
                    mm(pq_t[:, 0:n], ONES, TB3[:, c, 0:n], c == 0, c == 7, [b_tb, b_const], [pq_b], inc=(c == 7))
                mean, meanb = (T32H[1], t32hb[1]) if k % 2 == 0 else (T32X[1], t32xb[1])
                act(mean[:, 0:n], pm_t[:, 0:n], AF.Identity, [pm_b], [meanb], scale=1.0 / D)
                return mean, meanb, pq_t, pq_b

            def ln_a2(ti, k, mean, meanb, pq_t, pq_b):
                t0, n = TCH[ti]
                m2, m2b = tmp32()
                tt(m2[:, 0:n], mean[:, 0:n], mean[:, 0:n], ALU.mult, [meanb], [m2b])
                var, varb = tmp32()
                stt(var[:, 0:n], pq_t[:, 0:n], 1.0 / D, m2[:, 0:n], ALU.mult, ALU.subtract, [pq_b, m2b], [varb])
                sd, sdb = tmp32()
                act(sd[:, 0:n], var[:, 0:n], AF.Ln, [varb, b_misc], [sdb], bias=MISC[:, 0:1], scale=1.0)
                rs, rsb = (T32H[0], t32hb[0]) if k % 2 == 0 else (T32X[0], t32xb[0])
                act(rs[:, 0:n], sd[:, 0:n], AF.Exp, [sdb], [rsb], scale=-0.5)
                return mean, meanb, rs, rsb

            def ln_b(ti, mean, meanb, rs, rsb):
                t0, n = TCH[ti]
                for c in range(8):
                    t_, tb_ = tmp32()
                    tt(t_[:, 0:n], H3[:, c, t0:t0 + n], mean[:, 0:n], ALU.subtract, [vb[c][ti], meanb], [tb_])
                    tt(t_[:, 0:n], t_[:, 0:n], rs[:, 0:n], ALU.mult, [tb_, rsb], [tb_])
                    act(H3[:, c, t0:t0 + n], t_[:, 0:n], AF.Silu, [tb_, b_vec], [yb[c][ti], vb[c][ti]],
                        bias=vcol("cnb", c), scale=vcol("cng", c))

            prev = None
            for k, ti in enumerate(tis):
                a1 = ln_a1(ti, k)
                if prev is not None:
                    ln_b(*prev)
                prev = (ti,) + ln_a2(ti, k, *a1)
            ln_b(*prev)
            st_.hb = yb
            reload(li_index)
            linear(2, 4, 8, 512, lambda wt, ocl, kc: wt[:, kc, ocl * 128:(ocl + 1) * 128], hb_rhs, hb_bufs,
                   tis, resid_evac(2, tis, bias_name="cbout"))

        st_.rr["tbx"] = 0

        def tmppt32():
            i = st_.rr["tbx"] % 2
            st_.rr["tbx"] += 1
            return T32X[i], t32xb[i]

        def rglru(li, need_ctx, li_index):
            G3 = XRb[:, 0:18432].rearrange("p (c t) -> p c t", c=8)
            XL3 = XRb[:, 18432:36864].rearrange("p (c t) -> p c t", c=8)
            gb = [[p.buf() for _ in range(5)] for _ in range(8)]
            xlb = [[p.buf() for _ in range(5)] for _ in range(8)]
            tis = [0, 1, 2, 3, 4]
            for b in range(4):
                wt, wb = w_next(8, 512)
                for ocl in range(4):
                    oc = b * 4 + ocl
                    for ti in tis:
                        t0, n = TCH[ti]
                        ps_t, ps_b = psum()
                        for kc in range(8):
                            mm(ps_t[:, 0:n], wt[:, kc, ocl * 128:(ocl + 1) * 128], H3[:, kc, t0:t0 + n], kc == 0, kc == 7,
                               [wb, st_.hb[kc][ti]], [ps_b], inc=(kc == 7))
                        if oc < 8:
                            act(G3[:, oc, t0:t0 + n], ps_t[:, 0:n], AF.Gelu_apprx_tanh, [ps_b], [gb[oc][ti]])
                        else:
                            vcopy(XL3[:, oc - 8, t0:t0 + n], ps_t[:, 0:n], [ps_b], [xlb[oc - 8][ti]])
            lam = VEC[:, VOFF["llam"]:VOFF["llam"] + 16]
            b_ca = p.buf()
            act(MISC[:, 64:80], lam, AF.Exp, [b_vec], [b_ca], scale=-1.0)
            act(MISC[:, 80:96], MISC[:, 64:80], AF.Ln, [b_ca, b_misc], [b_ca], bias=MISC[:, 1:2], scale=1.0)
            ts(MISC[:, 16:32], MISC[:, 80:96], -8.0, None, ALU.mult, None, [b_ca], [b_ca])
            ts(MISC[:, 48:64], MISC[:, 80:96], -4.0, None, ALU.mult, None, [b_ca], [b_ca])
            ts(MISC[:, 96:128], VEC[:, VOFF["lgb"]:VOFF["lgb"] + 32], 0.5, None, ALU.mult, None, [b_vec, b_ca], [b_ca])
            dbg_dump("lru_xr0", XR[:, :], [128, 8 * NT], F32)
            p.barrier(("pe", "act", "dve"))
            U32 = HBf[:, 0:4608].rearrange("p (c t) -> p c t", c=2)
            HS = HBf[:, 4608:9216].rearrange("p (c t) -> p c t", c=2)
            UBF = AUX[:, 0:4608].rearrange("p (c t) -> p c t", c=2)
            XT = [AUXf[:, 2304 + i * 512:2304 + (i + 1) * 512] for i in range(5)]
            xtb = [p.buf() for _ in range(5)]
            CAR = MISC[:, 128:256]
            car_i = [0]
            rrx = [0]

            def ltmp():
                i = rrx[0] % 15
                rrx[0] += 1
                if i < 5:
                    return XT[i], xtb[i]
                if i < 11:
                    return T32[i - 5], t32b[i - 5]
                if i < 13:
                    return T32X[i - 11], t32xb[i - 11]
                return T32H[i - 13], t32hb[i - 13]

            gslots = []
            gidx = []
            for d in range(2):
                gidx.append(ws.consumed)
                gslots.append(w_next(16, 256, pin=True))
            SEGS = [(0, NCTX), (NCTX, NLAT)]
            u32b = [p.buf(), p.buf()]
            ubfb = [p.buf(), p.buf()]
            hsb = [[p.buf() for _ in range(5)] for _ in range(2)]
            for nblk in range(4):
                c0 = nblk * 2
                for d in range(2):
                    gwt, gwb = gslots[d]
                    for cl in range(2):
                        c = c0 + cl
                        xall = xlb[c]
                        for (s0, sn) in SEGS:
                            ts(U32[:, cl, s0:s0 + sn], XL3[:, c, s0:s0 + sn], vcol("lcw", (d * 4 + 3) * 8 + c), vcol("lcb", d * 8 + c),
                               ALU.mult, ALU.add, xall + [b_vec], [u32b[cl]], eng="pool")
                            for k in range(3):
                                sh = 3 - k
                                if d == 0:
                                    o_ap = U32[:, cl, s0 + sh:s0 + sn]
                                    i_ap = XL3[:, c, s0:s0 + sn - sh]
                                else:
                                    o_ap = U32[:, cl, s0:s0 + sn - sh]
                                    i_ap = XL3[:, c, s0 + sh:s0 + sn]
                                stt(o_ap, i_ap, vcol("lcw", (d * 4 + k) * 8 + c), o_ap, ALU.mult, ALU.add,
                                    xall + [b_vec, u32b[cl]], [u32b[cl]])
                        act(UBF[:, cl, :], U32[:, cl, :], AF.Identity, [u32b[cl]], [ubfb[cl]])
                    for cl in range(2):
                        c = c0 + cl
                        groups = [[0, 1, 2], [3, 4]] if d == 0 else [[0, 4, 3], [2, 1]]
                        prev_car = None
                        oi = -1
                        c4 = MISC[:, 48 + d * 8 + c:49 + d * 8 + c]
                        for grp in groups:
                            items = []
                            for ti in grp:
                                t0, n = TCH[ti]
                                gps = []
                                for gi in range(2):
                                    ps_t, ps_b = psum()
                                    for kc in range(2):
                                        mm(ps_t[:, 0:n], gwt[:, (gi * 4 + nblk) * 2 + kc, cl * 128:(cl + 1) * 128], UBF[:, kc, t0:t0 + n],
                                           kc == 0, kc == 1, [gwb, ubfb[kc]], [ps_b], inc=(kc == 1))
                                    gps.append((ps_t, ps_b))
                                r_, rb_ = ltmp()
                                act(r_[:, 0:n], gps[0][0][:, 0:n], AF.Tanh, [gps[0][1], b_ca], [rb_],
                                    bias=MISC[:, 96 + (d * 2 + 0) * 8 + c:97 + (d * 2 + 0) * 8 + c], scale=0.5)
                                a_, ab_ = ltmp()
                                act(a_[:, 0:n], r_[:, 0:n], AF.Exp, [rb_, b_ca], [ab_], bias=c4, scale=c4)
                                i_, ib_ = ltmp()
                                act(i_[:, 0:n], gps[1][0][:, 0:n], AF.Tanh, [gps[1][1], b_ca], [ib_],
                                    bias=MISC[:, 96 + (d * 2 + 1) * 8 + c:97 + (d * 2 + 1) * 8 + c], scale=0.5)
                                stt(i_[:, 0:n], i_[:, 0:n], 1.0, U32[:, cl, t0:t0 + n], ALU.add, ALU.mult, [ib_, u32b[cl]], [ib_])
                                items.append((ti, a_, ab_, i_, ib_))
                            for (ti, a_, ab_, i_, ib_) in items:
                                oi += 1
                                t0, n = TCH[ti]
                                m_, mb_ = ltmp()
                                act(m_[:, 0:n], a_[:, 0:n], AF.Square, [ab_], [mb_])
                                act(m_[:, 0:n], m_[:, 0:n], AF.Sqrt, [mb_, b_misc], [mb_], bias=MISC[:, 1:2], scale=-1.0)
                                if ti == 0:
                                    fc = 0 if d == 0 else NCTX - 1
                                    vmemset(m_[:, fc:fc + 1], 1.0, [mb_])
                                stt(i_[:, 0:n], i_[:, 0:n], 0.5, m_[:, 0:n], ALU.mult, ALU.mult, [ib_, mb_], [ib_])
                                if d == 0:
                                    init = 0.0 if oi == 0 else HS[:, cl, t0 - 1:t0]
                                    rd = [ab_, ib_] + ([hsb[cl][ti - 1]] if oi > 0 else [])
                                    p.op("dve", lambda e, o=HS[:, cl, t0:t0 + n], a=a_[:, 0:n], b=i_[:, 0:n], init=init:
                                         e.tensor_tensor_scan(out=o, data0=a, data1=b, initial=init, op0=ALU.mult, op1=ALU.add),
                                         rd, [hsb[cl][ti]])
                                else:
                                    h_, hb_ = ltmp()
                                    init = 0.0 if oi == 0 else prev_car
                                    p.op("dve", lambda e, o=h_[:, 0:n][:, ::-1], a=a_[:, 0:n][:, ::-1], b=i_[:, 0:n][:, ::-1], init=init:
                                         e.tensor_tensor_scan(out=o, data0=a, data1=b, initial=init, op0=ALU.mult, op1=ALU.add),
                                         [ab_, ib_, b_car], [hb_])
                                    ci = car_i[0] % 128
                                    car_i[0] += 1
                                    vcopy(CAR[:, ci:ci + 1], h_[:, 0:1], [hb_], [b_car])
                                    prev_car = CAR[:, ci:ci + 1]
                                    tt(h_[:, 0:n], h_[:, 0:n], HS[:, cl, t0:t0 + n], ALU.add, [hb_, hsb[cl][ti]], [hb_], eng="pool")
                                    tt(G3[:, c, t0:t0 + n], h_[:, 0:n], G3[:, c, t0:t0 + n], ALU.mult, [hb_, gb[c][ti]], [gb[c][ti]], eng="pool")
            for gi_ in gidx:
                w_release(gi_)
            dbg_dump("lru_xr1", XR[:, :], [128, 8 * NT], F32)
            p.barrier(("pe", "act", "dve"))
            yb = [[p.buf() for _ in range(5)] for _ in range(8)]
            for c in range(8):
                for ti in tis:
                    t0, n = TCH[ti]
                    if (c + ti) % 2 == 0:
                        vcopy(H3[:, c, t0:t0 + n], G3[:, c, t0:t0 + n], [gb[c][ti]], [yb[c][ti]])
                    else:
                        act(H3[:, c, t0:t0 + n], G3[:, c, t0:t0 + n], AF.Identity, [gb[c][ti]], [yb[c][ti]])
            st_.hb = yb
            reload(li_index)
            linear(2, 4, 8, 512, lambda wt, ocl, kc: wt[:, kc, ocl * 128:(ocl + 1) * 128], hb_rhs, hb_bufs,
                   tis, resid_evac(2, tis))

        vmemset(MISC[:, 0:1], EPS, [b_misc])
        vmemset(MISC[:, 1:2], 1.0, [b_misc])
        for idx, li in enumerate(layers):
            kind = li % 3
            need_ctx = li < DEPTH - 1
            st_.par = idx % 2
            if idx == 0:
                for _ in range(12):
                    mod_block()
                mod_finish(li, 0)
            st_.hb = grid(f"h{li}_")
            sp_tok = spill_issue(0 if idx == 0 else 1)
            norm_phase(0, [0, 1, 2, 3, 4])
            if DBG and idx == 0:
                p.dma("sp", dbg_mods, MODS()[:, :], reads=[b_mods()], writes=[o_buf])
                p.dma("sp", dbg_ab, AB()[:, :], reads=[b_ab()], writes=[o_buf])
                p.dma("sp", dbg_h1, HBt[:, :], reads=[b for row in st_.hb for b in row], writes=[o_buf])
            first_in_prog = (idx == 0)
            spill(sp_tok)
            lidx = 0 if first_in_prog else 1
            import os as _os
            if _os.environ.get("DBG_SKIP_MIX"):
                for _ in range({0: 6, 1: 6, 2: 8}[kind]):
                    w_next(8, 512)
                reload(lidx)
            elif kind == 0:
                attention(li, li // 3, need_ctx, lidx)
            elif kind == 1:
                conformer(li, need_ctx, lidx)
            else:
                rglru(li, need_ctx, lidx)
            tis = [0, 1, 2, 3, 4] if need_ctx else [1, 2, 3, 4]
            p.barrier(("pe", "act", "dve"))
            st_.hb = grid(f"h2{li}_")
            if _os.environ.get("DBG_SKIP_MLP"):
                for _ in range(16 + (12 if idx + 1 < len(layers) else 0)):
                    w_next(8, 512)
            else:
                if DBG and idx == 0:
                    p.dma("sp", dbg_x1, XR[:, :], reads=[b for row in st_.xb for b in row], writes=[o_buf])
                norm_phase(1, tis)
                if DBG and idx == 0:
                    p.dma("sp", dbg_h2, HBt[:, :], reads=[b for row in st_.hb for b in row], writes=[o_buf])
                if idx + 1 < len(layers):
                    mlp_phase(li, tis, layers[idx + 1], (idx + 1) % 2)
                else:
                    mlp_phase(li, tis)
        allb = [b for row in st_.xb for b in row]
        if last:
            for c in range(8):
                p.dma("sp", outT[c * 128:(c + 1) * 128, :], X3[:, c, NCTX:NT], reads=allb, writes=[o_buf])
        else:
            p.dma("sp", xs_out, XR[:, :], reads=allb, writes=[o_buf])
        p._wait("sp", o_buf.w)
        assert ws.consumed == len(wspecs), (ws.consumed, len(wspecs))
        p.emit()
    return nc


_WKEYS = ["mod_w", "mlp_w1", "mlp_w2", "attn_w_qkv", "attn_w_o", "conv_w_in", "conv_w_out",
          "lru_w_in", "lru_gate_w", "lru_w_out"]


def _common_maps(inputs):
    m = {k: np.ascontiguousarray(np.asarray(inputs[k], np.float32)) for k in _WKEYS}
    m["consts"] = _consts()
    m["rope"] = _rope_tables()
    return m


def kernel(**inputs):
    n = 8
    common = _common_maps(inputs)
    x = np.asarray(inputs["x"], np.float32)
    ctx = np.asarray(inputs["ctx"], np.float32)
    in_maps = []
    for b in range(n):
        m = dict(common)
        m["xT"] = np.ascontiguousarray(x[b].T)
        m["ctxT"] = np.ascontiguousarray(ctx[b].T)
        m["vecs"] = _pack_vecs(inputs, b)
        in_maps.append(m)
    nc = build_program((0, 1, 2, 3), True, True)
    res = run_bass_kernel_spmd(nc, in_maps, core_ids=list(range(n)))
    out = np.stack([np.ascontiguousarray(res.results[b]["outT"].T) for b in range(n)], axis=0)
    return out.astype(np.float32)
```

```python
import numpy as np
from contextlib import ExitStack
import concourse.bass as bass
import concourse.mybir as mybir
from concourse.bass_utils import run_bass_kernel_spmd
import ml_dtypes

F32 = mybir.dt.float32
BF16 = mybir.dt.bfloat16
AF = mybir.ActivationFunctionType
ALU = mybir.AluOpType

SELF_WAIT = True

NT, NCTX, NLAT, D = 2304, 256, 2048, 1024
TCH = [(0, 256), (256, 512), (768, 512), (1280, 512), (1792, 512)]
EPS = 1e-6
DEPTH = 4


class Sem:
    def __init__(self, h, name):
        self.h = h
        self.name = name
        self.count = 0


class Buf:
    __slots__ = ("name", "w", "r", "dsem")

    def __init__(self, name, dsem=None):
        self.name = name
        self.w = None
        self.r = []
        self.dsem = dsem


class Prog:
    ENG = ("pe", "act", "dve", "pool", "sp")

    def __init__(self, nc, stack):
        self.nc = nc
        self.stack = stack
        self.ops = {e: [] for e in self.ENG}
        self.esem = {}
        for e in ("pe", "act", "dve", "pool"):
            self.esem[e] = self.new_sem("s_" + e)
        self.known = {e: {} for e in self.ENG}
        self.pending_noinc = {e: False for e in self.ENG}
        self.nbuf = 0

    def new_sem(self, name):
        h = self.stack.enter_context(self.nc.semaphore(name))
        return Sem(h, name)

    def buf(self, name=None, dma=False):
        self.nbuf += 1
        name = name or f"b{self.nbuf}"
        return Buf(name, self.new_sem("d_" + name) if dma else None)

    def sb(self, name, shape, dt):
        return self.stack.enter_context(self.nc.sbuf_tensor(name, list(shape), dt))

    def ps(self, name, shape, dt=F32):
        return self.stack.enter_context(self.nc.psum_tensor(name, list(shape), dt))

    def _wait(self, eng, tok):
        if tok is None:
            return
        sem, val = tok
        if eng == "pe" and sem is self.esem["pe"]:
            return
        if (not SELF_WAIT) and eng in self.esem and sem is self.esem[eng]:
            return
        k = self.known[eng]
        if k.get(sem, 0) >= val:
            return
        k[sem] = val
        h = sem.h
        self.ops[eng].append(lambda e, h=h, val=val: e.wait_ge(h, val))

    def _deps(self, eng, reads, writes):
        for b in reads:
            self._wait(eng, b.w)
        for b in writes:
            self._wait(eng, b.w)
            for t in b.r:
                self._wait(eng, t)

    def _commit(self, tok, reads, writes):
        for b in reads:
            b.r.append(tok)
            if len(b.r) > 16:
                d = {}
                for s, v in b.r:
                    if d.get(s, 0) < v:
                        d[s] = v
                b.r = list(d.items())
        for b in writes:
            b.w = tok
            b.r = []

    def op(self, eng, fn, reads=(), writes=(), inc=True):
        self._deps(eng, reads, writes)
        sem = self.esem[eng]
        if inc:
            sem.count += 1
            val = sem.count
            h = sem.h
            self.ops[eng].append(lambda e, fn=fn, h=h: fn(e).then_inc(h, 1))
            self.pending_noinc[eng] = False
        else:
            val = sem.count + 1
            self.ops[eng].append(lambda e, fn=fn: fn(e))
            self.pending_noinc[eng] = True
        tok = (sem, val)
        self._commit(tok, reads, writes)
        return tok

    def dma(self, q, out, in_, reads=(), writes=(), dsem=None):
        self._deps(q, reads, writes)
        sem = dsem if dsem is not None else writes[0].dsem
        sem.count += 16
        val = sem.count
        h = sem.h
        self.ops[q].append(lambda e, out=out, in_=in_, h=h: e.dma_start(out=out, in_=in_).then_inc(h, 16))
        tok = (sem, val)
        self._commit(tok, reads, writes)
        return tok

    def barrier(self, engs=("pe", "act", "dve", "sp")):
        for e in ("pe", "act", "dve", "pool"):
            assert not self.pending_noinc[e]
        toks = [(self.esem[e], self.esem[e].count) for e in ("pe", "act", "dve", "pool")]
        if "pool" not in engs:
            engs = tuple(engs) + ("pool",)
        for e in engs:
            for t in toks:
                if t[1] > 0:
                    self._wait(e, t)

    def emit(self):
        for e in ("pe", "act", "dve"):
            assert not self.pending_noinc[e], f"engine {e} has trailing non-inc op"
        ops = self.ops
        with self.nc.Block() as block:
            @block.sync
            def _(eng):
                for f in ops["sp"]:
                    f(eng)

            @block.tensor
            def _(eng):
                for f in ops["pe"]:
                    f(eng)

            @block.scalar
            def _(eng):
                for f in ops["act"]:
                    f(eng)

            @block.vector
            def _(eng):
                for f in ops["dve"]:
                    f(eng)

            @block.gpsimd
            def _(eng):
                for f in ops["pool"]:
                    f(eng)


def _vec_layout():
    L = [("c", 8), ("cctx", 8)]
    for i in range(DEPTH):
        L += [(f"modb{i}", 48), (f"gmix{i}", 8), (f"gmlp{i}", 8)]
    for j in range(2):
        L += [(f"qg{j}", 1), (f"kg{j}", 1)]
    L += [("cbin", 16), ("cwdw", 248), ("cbdw", 8), ("cng", 8), ("cnb", 8), ("cbout", 8)]
    L += [("lcw", 64), ("lcb", 16), ("lgb", 32), ("llam", 16)]
    off = {}
    o = 0
    for n, k in L:
        off[n] = o
        o += k
    return off, o


VOFF, NV = _vec_layout()


def _pk(v):
    v = np.asarray(v, np.float32).reshape(-1, 128)
    return np.ascontiguousarray(v.T)


def _pack_vecs(inp, b):
    V = np.zeros((128, NV), np.float32)

    def put(name, arr):
        arr = np.asarray(arr, np.float32)
        V[:, VOFF[name]:VOFF[name] + arr.shape[1]] = arr

    put("c", _pk(inp["c"][b]))
    put("cctx", _pk(inp["c_ctx"]))
    for i in range(DEPTH):
        put(f"modb{i}", _pk(inp["mod_b"][i]))
        put(f"gmix{i}", _pk(inp["norm_mix_g"][i]))
        put(f"gmlp{i}", _pk(inp["norm_mlp_g"][i]))
    for j in range(2):
        put(f"qg{j}", np.tile(np.asarray(inp["attn_q_gain"][j], np.float32), 2)[:, None])
        put(f"kg{j}", np.tile(np.asarray(inp["attn_k_gain"][j], np.float32), 2)[:, None])
    put("cbin", _pk(inp["conv_b_in"][0]))
    wdw = np.asarray(inp["conv_w_dw"][0], np.float32).reshape(31, 8, 128).transpose(2, 0, 1).reshape(128, 248)
    put("cwdw", wdw)
    put("cbdw", _pk(inp["conv_b_dw"][0]))
    put("cng", _pk(inp["conv_norm_g"][0]))
    put("cnb", _pk(inp["conv_norm_b"][0]))
    put("cbout", _pk(inp["conv_b_out"][0]))
    lcw = np.asarray(inp["lru_conv_w"][0], np.float32).reshape(2, 4, 8, 128).transpose(3, 0, 1, 2).reshape(128, 64)
    put("lcw", lcw)
    put("lcb", np.asarray(inp["lru_conv_b"][0], np.float32).reshape(2, 8, 128).transpose(2, 0, 1).reshape(128, 16))
    put("lgb", np.asarray(inp["lru_gate_b"][0], np.float32).reshape(2, 2, 8, 128).transpose(3, 0, 1, 2).reshape(128, 32))
    put("llam", np.asarray(inp["lru_lambda"][0], np.float32).reshape(2, 8, 128).transpose(2, 0, 1).reshape(128, 16))
    return V


def _consts():
    p = np.arange(128)
    perm = (p[:, None] == (p[None, :] ^ 16)).astype(np.float32)
    onesblk = ((p[:, None] // 64) == (p[None, :] // 64)).astype(np.float32)
    ones = np.ones((128, 128), np.float32)
    ident = np.eye(128, dtype=np.float32)
    return np.concatenate([perm, onesblk, ones, ident], axis=1).astype(ml_dtypes.bfloat16)


def _rope_tables():
    t = np.arange(NLAT)
    row = (t // 64).astype(np.float64)
    col = (t % 64).astype(np.float64)
    inv = 10000.0 ** (-np.arange(16, dtype=np.float64) / 16.0)
    p = np.arange(128)
    d = p % 64
    a = d // 32
    half = (d // 16) % 2
    f = d % 16
    pos = np.where(a[:, None] == 0, row[None, :], col[None, :])
    ang = (pos.astype(np.float32) * inv.astype(np.float32)[f][:, None]).astype(np.float32)
    C = np.cos(ang).astype(np.float32)
    S = np.sin(ang).astype(np.float32) * np.where(half == 0, -1.0, 1.0)[:, None].astype(np.float32)
    return np.ascontiguousarray(np.concatenate([C, S], axis=1).astype(np.float32))


class K:
    pass


def build_program(layers=(0, 1, 2, 3), first=True, last=True):
    nc = bass.Bass("TRN2", target_bir_lowering=False)
    dr = {}

    def din(name, shape, dt=F32):
        dr[name] = nc.dram_tensor(name, list(shape), dt, kind="ExternalInput").ap()
        return dr[name]

    if first:
        din("xT", [D, NLAT])
        din("ctxT", [D, NCTX])
    else:
        din("xs_in", [128, 8 * NT])
    din("vecs", [128, NV])
    din("consts", [128, 512], BF16)
    din("rope", [128, 2 * NLAT])
    din("mod_w", [4, D, 6 * D])
    din("mlp_w1", [4, D, 4 * D])
    din("mlp_w2", [4, 4 * D, D])
    din("attn_w_qkv", [2, D, 1536])
    din("attn_w_o", [2, D, D])
    din("conv_w_in", [1, D, 2 * D])
    din("conv_w_out", [1, D, D])
    din("lru_w_in", [1, D, 2 * D])
    din("lru_gate_w", [1, 2, 2, 4, 256, 256])
    din("lru_w_out", [1, D, D])
    if last:
        outT = nc.dram_tensor("outT", [D, NLAT], F32, kind="ExternalOutput").ap()
    else:
        xs_out = nc.dram_tensor("xs_out", [128, 8 * NT], F32, kind="ExternalOutput").ap()
    xs = nc.dram_tensor("xs_scr", [128, 8 * NT], F32, kind="Internal").ap()
    import os as _os
    DBG = bool(_os.environ.get("DBG_DUMP"))
    if DBG:
        dbg_mods = nc.dram_tensor("dbg_mods", [128, 96], F32, kind="ExternalOutput").ap()
        dbg_ab = nc.dram_tensor("dbg_ab", [128, 64], F32, kind="ExternalOutput").ap()
        dbg_h1 = nc.dram_tensor("dbg_h1", [128, 8 * NT], BF16, kind="ExternalOutput").ap()
        dbg_h2 = nc.dram_tensor("dbg_h2", [128, 8 * NT], BF16, kind="ExternalOutput").ap()
        dbg_x1 = nc.dram_tensor("dbg_x1", [128, 8 * NT], F32, kind="ExternalOutput").ap()

    with ExitStack() as st:
        p = Prog(nc, st)
        XR = p.sb("XR", [128, 8 * NT], F32)
        HBt = p.sb("HB", [128, 8 * NT], BF16)
        AUX = p.sb("AUX", [128, 10368], BF16)
        SLOT = [p.sb(f"slot{i}", [128, 4096], BF16) for i in range(4)]
        VEC = p.sb("VEC", [128, NV], F32)
        CONST = p.sb("CONST", [128, 512], BF16)
        MODS_ = [p.sb(f"MODS{i}", [128, 96], F32) for i in range(2)]
        AB_ = [p.sb(f"AB{i}", [128, 64], F32) for i in range(2)]
        SC = p.sb("SC", [128, 16], BF16)
        MISC = p.sb("MISC", [128, 256], F32)
        T32 = [p.sb(f"t32_{i}", [128, 512], F32) for i in range(6)]
        TB = p.sb("TB", [128, 8 * 512], BF16)
        T32X = [p.sb(f"t32x_{i}", [128, 512], F32) for i in range(2)]
        t32xb = [p.buf(f"t32x_{i}") for i in range(2)]
        T32H = [p.sb(f"t32h_{i}", [128, 512], F32) for i in range(2)]
        t32hb = [p.buf(f"t32h_{i}") for i in range(2)]
        b_tb = p.buf("tb")
        b_car = p.buf("car")
        PT = [p.sb(f"pt{i}", [128, 512], BF16) for i in range(6)]
        PS = [p.ps(f"ps{i}", [128, 512], F32) for i in range(4)]
        PSD = [p.ps(f"psd{i}", [128, 1024], F32) for i in range(2)]
        PS = PS + [PSD[0][:, 0:512], PSD[0][:, 512:1024], PSD[1][:, 0:512], PSD[1][:, 512:1024]]
        psb = [p.buf(f"ps{i}") for i in range(8)]
        t32b = [p.buf(f"t32_{i}") for i in range(6)]
        ptb = [p.buf(f"pt{i}") for i in range(6)]
        slotb = [p.buf(f"slot{i}", dma=True) for i in range(4)]
        b_vec = p.buf("vec", dma=True)
        b_const = p.buf("const", dma=True)
        b_mods_ = [p.buf("mods0"), p.buf("mods1")]
        b_ab_ = [p.buf("ab0"), p.buf("ab1")]
        b_sc = p.buf("sc")
        b_misc = p.buf("misc")
        x_dsem = p.new_sem("d_x")
        xc_dsem = [p.new_sem(f"d_xc{c}") for c in range(8)]
        o_buf = p.buf("out", dma=True)
        spill_buf = p.buf("spill", dma=True)

        dbg_outs = {}

        def dbg_dump(name, ap, shape, dt):
            if not DBG:
                return
            t = nc.dram_tensor("dd_" + name, list(shape), dt, kind="ExternalOutput").ap()
            p.barrier(("sp",))
            p.dma("sp", t, ap, writes=[o_buf])
            for e in ("pe", "act", "dve"):
                p._wait(e, o_buf.w)

        PERM = CONST[:, 0:128]
        ONESBLK = CONST[:, 128:256]
        ONES = CONST[:, 256:384]
        IDENT = CONST[:, 384:512]

        X3 = XR[:, :].rearrange("p (c t) -> p c t", c=8)
        XRb = XR[:, :].bitcast(BF16)
        H3 = HBt[:, :].rearrange("p (c t) -> p c t", c=8)
        HBf = HBt[:, :].bitcast(F32)
        AUXf = AUX[:, :].bitcast(F32)

        def grid(name):
            return [[p.buf(f"{name}{c}_{t}") for t in range(5)] for c in range(8)]

        st_ = K()
        st_.xb = grid("x")
        st_.hb = grid("h")
        st_.rr = {"t32": 0, "pt": 0, "ps": 0}

        def vcol(name, j=0):
            o = VOFF[name] + j
            return VEC[:, o:o + 1]

        def tmp32():
            i = st_.rr["t32"] % 6
            st_.rr["t32"] += 1
            return T32[i], t32b[i]

        def tmppt():
            i = st_.rr["pt"] % 6
            st_.rr["pt"] += 1
            return PT[i], ptb[i]

        def psum(group=None):
            group = group if group is not None else list(range(7))
            key = ("ps",) + tuple(group)
            k_ = st_.rr.get(key, 0)
            st_.rr[key] = k_ + 1
            i = group[k_ % len(group)]
            return PS[i], psb[i]

        def mm(out, lhsT, rhs, start, stop, reads, writes, inc):
            p.op("pe", lambda e: e.matmul(out, lhsT, rhs, start=start, stop=stop), reads, writes, inc=inc)

        def act(out, in_, func, reads, writes, bias=None, scale=None):
            kw = {}
            if bias is not None:
                kw["bias"] = bias
            if scale is not None:
                kw["scale"] = scale
            p.op("act", lambda e: e.activation(out=out, in_=in_, func=func, **kw), reads, writes)

        def tt(out, in0, in1, op, reads, writes, eng="dve"):
            p.op(eng, lambda e: e.tensor_tensor(out=out, in0=in0, in1=in1, op=op), reads, writes)

        def ts(out, in0, s1, s2, op0, op1, reads, writes, eng="dve"):
            if s2 is None:
                p.op(eng, lambda e: e.tensor_scalar(out=out, in0=in0, scalar1=s1, scalar2=None, op0=op0), reads, writes)
            else:
                p.op(eng, lambda e: e.tensor_scalar(out=out, in0=in0, scalar1=s1, scalar2=s2, op0=op0, op1=op1), reads, writes)

        def stt(out, in0, scalar, in1, op0, op1, reads, writes, eng="dve"):
            p.op(eng, lambda e: e.scalar_tensor_tensor(out=out, in0=in0, scalar=scalar, in1=in1, op0=op0, op1=op1), reads, writes)

        def recip(out, in_, reads, writes):
            p.op("dve", lambda e: e.reciprocal(out=out, in_=in_), reads, writes)

        def vcopy(out, in_, reads, writes):
            p.op("dve", lambda e: e.tensor_copy(out=out, in_=in_), reads, writes)

        def vmemset(ap, val, writes):
            p.op("dve", lambda e: e.memset(ap, val), (), writes)

        wspecs = []

        def wv(ap2d):
            return ap2d.rearrange("(k p) n -> p k n", p=128)

        NMOD = [2, 1, 2, 1, 2, 1, 2, 1]
        for lidx_, li in enumerate(layers):
            kind = li % 3
            j = li // 3
            if lidx_ == 0:
                for b in range(12):
                    wspecs.append([(0, 8, 512, wv(dr["mod_w"][li, :, b * 512:(b + 1) * 512]))])
            if kind == 0:
                wq = dr["attn_w_qkv"][j]
                for b in range(2):
                    wspecs.append([(0, 8, 512, wv(wq[:, b * 512:(b + 1) * 512]))])
                sp_ = []
                for g in range(4):
                    for dup in range(2):
                        sp_.append(((g * 2 + dup) * 64, 8, 64, wv(wq[:, 1024 + g * 64:1024 + (g + 1) * 64]), 512))
                wspecs.append(sp_)
                wspecs.append([(0, 8, 256, wv(wq[:, 1280:1536]))])
                for b in range(2):
                    wspecs.append([(0, 8, 512, wv(dr["attn_w_o"][j][:, b * 512:(b + 1) * 512]))])
            elif kind == 1:
                wi = dr["conv_w_in"][0]
                for b in range(4):
                    wspecs.append([(0, 8, 256, wv(wi[:, b * 256:(b + 1) * 256]), 512),
                                   (256, 8, 256, wv(wi[:, 1024 + b * 256:1024 + (b + 1) * 256]), 512)])
                for b in range(2):
                    wspecs.append([(0, 8, 512, wv(dr["conv_w_out"][0][:, b * 512:(b + 1) * 512]))])
            else:
                wi = dr["lru_w_in"][0]
                for b in range(4):
                    wspecs.append([(0, 8, 512, wv(wi[:, b * 512:(b + 1) * 512]))])
                for d in range(2):
                    gw = dr["lru_gate_w"][0, d].rearrange("g n k e -> (g n k) e")
                    wspecs.append([(0, 16, 256, wv(gw))])
                for b in range(2):
                    wspecs.append([(0, 8, 512, wv(dr["lru_w_out"][0][:, b * 512:(b + 1) * 512]))])
            mb_ = 0
            for hb in range(8):
                wspecs.append([(0, 8, 512, wv(dr["mlp_w1"][li, :, hb * 512:(hb + 1) * 512]))])
                wspecs.append([(0, 4, 1024, wv(dr["mlp_w2"][li, hb * 512:(hb + 1) * 512, :]))])
                if lidx_ + 1 < len(layers):
                    nl_ = layers[lidx_ + 1]
                    for _ in range(NMOD[hb]):
                        wspecs.append([(0, 8, 512, wv(dr["mod_w"][nl_, :, mb_ * 512:(mb_ + 1) * 512]))])
                        mb_ += 1

        ws = K()
        ws.issued = 0
        ws.consumed = 0

        def w_issue(jb):
            s = jb % 4
            for spec in wspecs[jb]:
                if len(spec) == 5:
                    off, kcn, ncol, src, rowlen = spec
                    dst = SLOT[s][:, 0:kcn * rowlen].rearrange("p (k n) -> p k n", k=kcn)[:, :, off:off + ncol]
                else:
                    off, kcn, ncol, src = spec
                    dst = SLOT[s][:, off:off + kcn * ncol].rearrange("p (k n) -> p k n", k=kcn)
                p.dma("pool", dst, src, writes=[slotb[s]])

        ws.released = set()
        ws.pinned = set()

        def w_release(i):
            ws.pinned.discard(i)
            ws.released.add(i)

        def w_next(kcn, ncol, pin=False):
            i = ws.consumed
            if i - 1 >= 0 and (i - 1) not in ws.pinned:
                ws.released.add(i - 1)
            while ws.issued < min(i + 4, len(wspecs)) and (ws.issued < 4 or (ws.issued - 4) in ws.released):
                w_issue(ws.issued)
                ws.issued += 1
            assert ws.issued > i, "weight block not issued (pinned slot deadlock)"
            if pin:
                ws.pinned.add(i)
            ws.consumed += 1
            s = i % 4
            return SLOT[s][:, 0:kcn * ncol].rearrange("p (k n) -> p k n", k=kcn), slotb[s]

        p.dma("sp", VEC[:, :], dr["vecs"], writes=[b_vec])
        p.dma("sp", CONST[:, :], dr["consts"], writes=[b_const])

        def load_x_from_input():
            if first:
                p.dma("sp", X3[:, :, 0:NCTX], dr["ctxT"].rearrange("(c p) t -> p c t", p=128),
                      writes=[st_.xb[c][0] for c in range(8)], dsem=x_dsem)
                for c in range(8):
                    p.dma("sp", X3[:, c, NCTX:NT], dr["xT"][c * 128:(c + 1) * 128, :], writes=st_.xb[c][1:5], dsem=xc_dsem[c])
            else:
                for c in range(8):
                    p.dma("sp", X3[:, c, :], dr["xs_in"][:, c * NT:(c + 1) * NT], writes=st_.xb[c], dsem=xc_dsem[c])

        load_x_from_input()
        SC3 = SC[:, :].rearrange("p (k s) -> p k s", s=2)
        act(SC3[:, :, 0], VEC[:, VOFF["cctx"]:VOFF["cctx"] + 8], AF.Silu, [b_vec], [b_sc])
        act(SC3[:, :, 1], VEC[:, VOFF["c"]:VOFF["c"] + 8], AF.Silu, [b_vec], [b_sc])

        st_.par = 0

        def MODS():
            return MODS_[st_.par]

        def AB():
            return AB_[st_.par]

        def b_mods():
            return b_mods_[st_.par]

        def b_ab():
            return b_ab_[st_.par]

        def modcol(grp, c, s):
            return MODS()[:, (grp * 8 + c) * 2 + s:(grp * 8 + c) * 2 + s + 1]

        modst = K()
        modst.nb = 0

        def mod_block():
            b = modst.nb
            modst.nb += 1
            ps_t, ps_b = PS[7], psb[7]
            wt, wb = w_next(8, 512)
            for jj in range(4):
                jx = b * 4 + jj
                for kc in range(8):
                    mm(ps_t[:, 2 * jx:2 * jx + 2], wt[:, kc, jj * 128:(jj + 1) * 128], SC3[:, kc, :],
                       kc == 0, kc == 7, [wb, b_sc], [ps_b], inc=(jj == 3 and kc == 7))

        def mod_finish(li, par):
            assert modst.nb == 12
            modst.nb = 0
            ps_t, ps_b = PS[7], psb[7]
            M = MODS_[par]
            A = AB_[par]
            M3 = M[:, :].rearrange("p (j s) -> p j s", s=2)
            ps3 = ps_t[:, 0:96].rearrange("p (j s) -> p j s", s=2)
            mb = VEC[:, VOFF[f"modb{li}"]:VOFF[f"modb{li}"] + 48]
            for s in range(2):
                tt(M3[:, :, s], ps3[:, :, s], mb, ALU.add, [ps_b, b_vec], [b_mods_[par]])
            gm = VEC[:, VOFF[f"gmix{li}"]:VOFF[f"gmix{li}"] + 8]
            gl = VEC[:, VOFF[f"gmlp{li}"]:VOFF[f"gmlp{li}"] + 8]
            for s in range(2):
                stt(A[:, s * 8:s * 8 + 8], M3[:, 8:16, s], 1.0, gm, ALU.add, ALU.mult, [b_mods_[par], b_vec], [b_ab_[par]])
                stt(A[:, 16 + s * 8:16 + s * 8 + 8], M3[:, 32:40, s], 1.0, gl, ALU.add, ALU.mult, [b_mods_[par], b_vec], [b_ab_[par]])

        def norm_phase(which, tis):
            TB3 = TB[:, :].rearrange("p (c t) -> p c t", c=8)

            def stage_a1(ti, k):
                t0, n = TCH[ti]
                for c in range(8):
                    act(TB3[:, c, 0:n], X3[:, c, t0:t0 + n], AF.Square, [st_.xb[c][ti]], [b_tb])
                ps_t, ps_b = psum()
                for c in range(8):
                    mm(ps_t[:, 0:n], ONES, TB3[:, c, 0:n], c == 0, c == 7, [b_tb, b_const], [ps_b], inc=(c == 7))
                return ps_t, ps_b

            def stage_a2(ti, k, ps_t, ps_b):
                t0, n = TCH[ti]
                sd, sdb = tmp32()
                act(sd[:, 0:n], ps_t[:, 0:n], AF.Ln, [ps_b, b_misc], [sdb], bias=MISC[:, 0:1], scale=1.0 / D)
                rs, rsb = T32H[k % 2], t32hb[k % 2]
                act(rs[:, 0:n], sd[:, 0:n], AF.Exp, [sdb], [rsb], scale=-0.5)
                return rs, rsb

            def stage_b(ti, rs, rsb):
                t0, n = TCH[ti]
                s = 0 if ti == 0 else 1
                for c in range(8):
                    t_, tb_ = tmp32()
                    a_ap = AB()[:, which * 16 + s * 8 + c:which * 16 + s * 8 + c + 1]
                    stt(t_[:, 0:n], X3[:, c, t0:t0 + n], a_ap, rs[:, 0:n], ALU.mult, ALU.mult,
                        [st_.xb[c][ti], b_ab(), rsb], [tb_])
                    act(H3[:, c, t0:t0 + n], t_[:, 0:n], AF.Identity, [tb_, b_mods()], [st_.hb[c][ti]],
                        bias=modcol(0 if which == 0 else 3, c, s))

            prev = None
            for k, ti in enumerate(tis):
                pst = stage_a1(ti, k)
                if prev is not None:
                    stage_b(*prev)
                prev = (ti,) + stage_a2(ti, k, *pst)
            stage_b(*prev)

        def spill_issue(li_index):
            if li_index == 0:
                return None
            tok = None
            for c in range(8):
                tok = p.dma("sp", xs[:, c * NT:(c + 1) * NT], X3[:, c, :], reads=st_.xb[c], writes=[spill_buf])
            return tok

        def spill(tok):
            p.barrier(("pe", "act", "dve", "sp"))
            if tok is not None:
                for e in ("pe", "act", "dve", "sp", "pool"):
                    p._wait(e, tok)
            return tok

        def reload(li_index):
            p.barrier(("pe", "act", "dve", "sp"))
            st_.xb = grid(f"x{li_index}_")
            allb = [b for row in st_.xb for b in row]
            if li_index == 0:
                load_x_from_input()
            else:
                for c in range(8):
                    p.dma("sp", X3[:, c, :], xs[:, c * NT:(c + 1) * NT], reads=[spill_buf], writes=st_.xb[c], dsem=xc_dsem[c])

        def linear(nblocks, ocs_per_block, kcn, wcols, lhs_fn, rhs_fn, rhs_bufs_fn, tis, evac, psgroup=None):
            for b in range(nblocks):
                wt, wb = w_next(kcn, wcols)
                for ocl in range(ocs_per_block):
                    for ti in tis:
                        t0, n = TCH[ti]
                        ps_t, ps_b = psum(psgroup)
                        for kc in range(kcn):
                            mm(ps_t[:, 0:n], lhs_fn(wt, ocl, kc), rhs_fn(kc, t0, n), kc == 0, kc == kcn - 1,
                               [wb] + rhs_bufs_fn(kc, ti), [ps_b], inc=(kc == kcn - 1))
                        evac(b, ocl, ti, ps_t[:, 0:n], ps_b)

        def resid_evac(grp, tis_all, bias_name=None):
            def ev(b, ocl, ti, ps_ap, ps_b):
                oc = b * 4 + ocl
                t0, n = TCH[ti]
                s = 0 if ti == 0 else 1
                src = ps_ap
                rd = [ps_b]
                if bias_name is not None:
                    t_, tb_ = tmp32()
                    act(t_[:, 0:n], ps_ap, AF.Identity, [ps_b, b_vec], [tb_], bias=vcol(bias_name, oc))
                    src = t_[:, 0:n]
                    rd = [tb_]
                stt(X3[:, oc, t0:t0 + n], src, modcol(grp, oc, s), X3[:, oc, t0:t0 + n], ALU.mult, ALU.add,
                    rd + [b_mods(), st_.xb[oc][ti]], [st_.xb[oc][ti]])
            return ev

        def hb_rhs(kc, t0, n):
            return H3[:, kc, t0:t0 + n]

        def hb_bufs(kc, ti):
            return [st_.hb[kc][ti]]

        def mlp_phase(li, tis, next_li=None, next_par=None):
            HID = AUX[:, 0:4 * NT].rearrange("p (c t) -> p c t", c=4)
            hidb = [[p.buf() for _ in range(5)] for _ in range(4)]
            for hb_i in range(8):
                w1, w1b = w_next(8, 512)
                for ti in tis:
                    t0, n = TCH[ti]
                    for ocl in range(4):
                        ps_t, ps_b = psum()
                        for kc in range(8):
                            mm(ps_t[:, 0:n], w1[:, kc, ocl * 128:(ocl + 1) * 128], H3[:, kc, t0:t0 + n], kc == 0, kc == 7,
                               [w1b, st_.hb[kc][ti]], [ps_b], inc=(kc == 7))
                        t_, tb_ = tmp32()
                        act(t_[:, 0:n], ps_t[:, 0:n], AF.Relu, [ps_b], [tb_])
                        tt(HID[:, ocl, t0:t0 + n], t_[:, 0:n], t_[:, 0:n], ALU.mult, [tb_], [hidb[ocl][ti]])
                w2, w2b = w_next(4, 1024)
                for ti in tis:
                    t0, n = TCH[ti]
                    s = 0 if ti == 0 else 1
                    for oc in range(8):
                        ps_t, ps_b = psum()
                        for kc in range(4):
                            mm(ps_t[:, 0:n], w2[:, kc, oc * 128:(oc + 1) * 128], HID[:, kc, t0:t0 + n], kc == 0, kc == 3,
                               [w2b, hidb[kc][ti]], [ps_b], inc=(kc == 3))
                        stt(X3[:, oc, t0:t0 + n], ps_t[:, 0:n], modcol(5, oc, s), X3[:, oc, t0:t0 + n], ALU.mult, ALU.add,
                            [ps_b, b_mods(), st_.xb[oc][ti]], [st_.xb[oc][ti]])
                if next_li is not None:
                    for _ in range(NMOD[hb_i]):
                        mod_block()
            if next_li is not None:
                mod_finish(next_li, next_par)

        def attention(li, j_att, need_ctx, li_index):
            QT = XRb[:, 0:18432].rearrange("p (c t) -> p c t", c=8)
            KT2 = XRb[:, 18432:27648].rearrange("p (c t) -> p c t", c=4)
            ROC = XR[:, 13824:15872]
            ROS = XR[:, 15872:17920]
            VA = AUX[:, 0:18 * 576].rearrange("p (k x) -> p k x", k=18)
            b_rope = p.buf(f"rope{li}", dma=True)
            qb = [[p.buf() for _ in range(5)] for _ in range(8)]
            kb = [[p.buf() for _ in range(5)] for _ in range(4)]
            vab = [p.buf() for _ in range(18)]
            b_va_init = p.buf()
            p.dma("sp", XR[:, 13824:17920], dr["rope"], writes=[b_rope])
            vmemset(AUX[:, 0:18 * 576], 1.0, [b_va_init] + vab)
            q_tis = [0, 1, 2, 3, 4] if need_ctx else [1, 2, 3, 4]

            PS_A = [0, 1, 2, 3]
            PS_B = [4, 5]
            PS_C = [6, 7]
            pending = []

            def qk_item(ps_t, ps_b, n, ti, gain_ap, dst_ap, dst_buf):
                t0 = TCH[ti][0]
                lat = ti != 0
                state = {}

                def stage_b1():
                    sq, sqb = tmppt()
                    act(sq[:, 0:n], ps_t[:, 0:n], AF.Square, [ps_b], [sqb])
                    ss_t, ss_b = psum(PS_B)
                    mm(ss_t[:, 0:n], ONESBLK, sq[:, 0:n], True, True, [sqb, b_const], [ss_b], inc=True)
                    sd, sdb = tmp32()
                    act(sd[:, 0:n], ss_t[:, 0:n], AF.Ln, [ss_b, b_misc], [sdb], bias=MISC[:, 0:1], scale=1.0 / 64)
                    state["sd"] = (sd, sdb)

                def stage_b2():
                    sd, sdb = state["sd"]
                    rs, rsb = tmp32()
                    act(rs[:, 0:n], sd[:, 0:n], AF.Exp, [sdb], [rsb], scale=-0.5)
                    if not lat:
                        stt(dst_ap, ps_t[:, 0:n], gain_ap, rs[:, 0:n], ALU.mult, ALU.mult, [ps_b, rsb, b_vec], [dst_buf])
                    else:
                        qn, qnb = tmppt()
                        stt(qn[:, 0:n], ps_t[:, 0:n], gain_ap, rs[:, 0:n], ALU.mult, ALU.mult, [ps_b, rsb, b_vec], [qnb])
                        state["qn"] = (qn, qnb)

                def stage_c():
                    if not lat:
                        return
                    qn, qnb = state["qn"]
                    rot_t, rot_b = psum(PS_C)
                    mm(rot_t[:, 0:n], PERM, qn[:, 0:n], True, True, [qnb, b_const], [rot_b], inc=True)
                    t1, t1b = tmp32()
                    tt(t1[:, 0:n], qn[:, 0:n], ROC[:, t0 - NCTX:t0 - NCTX + n], ALU.mult, [qnb, b_rope], [t1b], eng="pool")
                    t2, t2b = tmp32()
                    tt(t2[:, 0:n], rot_t[:, 0:n], ROS[:, t0 - NCTX:t0 - NCTX + n], ALU.mult, [rot_b, b_rope], [t2b])
                    tt(dst_ap, t1[:, 0:n], t2[:, 0:n], ALU.add, [t1b, t2b], [dst_buf], eng="pool")
                return stage_b1, stage_b2, stage_c

            def pipe_push(item):
                pending.append(item)
                for back, stg in ((2, 0), (3, 1), (4, 2)):
                    if len(pending) >= back:
                        pending[-back][stg]()

            def pipe_flush():
                for extra in range(1, 4):
                    for back, stg in ((2, 0), (3, 1), (4, 2)):
                        idx_ = len(pending) + extra - back
                        if 0 <= idx_ < len(pending):
                            pending[idx_][stg]()
                pending.clear()

            for b in range(2):
                wt, wb = w_next(8, 512)
                for ocl in range(4):
                    oc = b * 4 + ocl
                    for ti in q_tis:
                        t0, n = TCH[ti]
                        ps_t, ps_b = psum(PS_A)
                        for kc in range(8):
                            mm(ps_t[:, 0:n], wt[:, kc, ocl * 128:(ocl + 1) * 128], H3[:, kc, t0:t0 + n], kc == 0, kc == 7,
                               [wb, st_.hb[kc][ti]], [ps_b], inc=(kc == 7))
                        pipe_push(qk_item(ps_t, ps_b, n, ti, vcol(f"qg{j_att}"), QT[:, oc, t0:t0 + n], qb[oc][ti]))
            wt, wb = w_next(8, 512)
            for g in range(4):
                for ti in range(5):
                    t0, n = TCH[ti]
                    ps_t, ps_b = psum(PS_A)
                    for kc in range(8):
                        mm(ps_t[:, 0:n], wt[:, kc, g * 128:(g + 1) * 128], H3[:, kc, t0:t0 + n], kc == 0, kc == 7,
                           [wb, st_.hb[kc][ti]], [ps_b], inc=(kc == 7))
                    pipe_push(qk_item(ps_t, ps_b, n, ti, vcol(f"kg{j_att}"), KT2[:, g, t0:t0 + n], kb[g][ti]))
            pipe_flush()
            wt, wb = w_next(8, 256)
            for kt in range(18):
                ti = 0 if kt < 2 else 1 + (kt - 2) // 4
                ps_t, ps_b = psum(PS_A)
                for kc in range(8):
                    mm(ps_t[:, 0:256], H3[:, kc, kt * 128:(kt + 1) * 128], wt[:, kc, :], kc == 0, kc == 7,
                       [wb, st_.hb[kc][ti]], [ps_b], inc=(kc == 7))
                dst = VA[:, kt, 64:576].rearrange("p (g x) -> p g x", x=128)[:, :, 0:64]
                src = ps_t[:, 0:256].rearrange("p (g x) -> p g x", x=64)
                act(dst, src, AF.Identity, [ps_b], [vab[kt]])

            OBANK = [[0, 1], [2, 3]]
            ptdb = [p.buf() for _ in range(4)]
            iters = [(jp, ti) for jp in range(8) for ti in q_tis]
            steps = []
            for it_i, (jp, ti) in enumerate(iters):
                kts = list(range(18)) if ti != 0 else [0, 1]
                for kt in kts:
                    steps.append((it_i, jp, ti, kt, kt == kts[0], kt == kts[-1]))

            def s_stage(stp):
                it_i, jp, ti, kt, is_first, is_last = stp
                g = jp // 2
                t0, n = TCH[ti]
                tik = 0 if kt < 2 else 1 + (kt - 2) // 4
                di = st_.rr.get("psd", 0) % 2
                st_.rr["psd"] = st_.rr.get("psd", 0) + 1
                dt_ = PSD[di]
                dbs = [psb[4 + 2 * di], psb[5 + 2 * di]]
                for h in range(2):
                    mm(dt_[:, h * 512:h * 512 + n], KT2[h * 64:(h + 1) * 64, g, kt * 128:(kt + 1) * 128],
                       QT[h * 64:(h + 1) * 64, jp, t0:t0 + n], True, True,
                       [kb[g][tik], qb[jp][ti]], [dbs[h]], inc=(h == 1))
                return dt_, dbs

            def pv_stage(stp, sres):
                it_i, jp, ti, kt, is_first, is_last = stp
                g = jp // 2
                t0, n = TCH[ti]
                ob = OBANK[it_i % 2]
                o_t = [PS[ob[0]], PS[ob[1]]]
                o_b = [psb[ob[0]], psb[ob[1]]]
                dt_, dbs = sres
                pi = st_.rr.get("ptd", 0) % 4
                st_.rr["ptd"] = st_.rr.get("ptd", 0) + 1
                ptd = TB[:, pi * 1024:(pi + 1) * 1024]
                src = dt_[:, :].rearrange("p (h x) -> p h x", h=2)[:, :, 0:n]
                dst = ptd.rearrange("p (h x) -> p h x", h=2)[:, :, 0:n]
                act(dst, src, AF.Exp, dbs, [ptdb[pi]], scale=0.125)
                for h in range(2):
                    if h == 0:
                        lhs = VA[:, kt, 64 + 128 * g:192 + 128 * g]
                    else:
                        lhs = VA[:, kt, 128 * g:128 + 128 * g]
                    mm(o_t[h][:, 0:n], lhs, ptd[:, h * 512:h * 512 + n], is_first, is_last,
                       [ptdb[pi], vab[kt]], [o_b[h]], inc=True)
                if is_last:
                    for h in range(2):
                        rc, rcb = tmp32()
                        recip(rc[:, 0:n], o_t[h][:, 0:n], [o_b[h]], [rcb])
                        lo, hi = (0, 64) if h == 0 else (64, 128)
                        dlo, dhi = (64, 128) if h == 0 else (0, 64)
                        tt(H3[lo:hi, jp, t0:t0 + n], o_t[h][lo:hi, 0:n], rc[dlo:dhi, 0:n], ALU.mult,
                           [o_b[h], rcb], [st_.hb[jp][ti]])

            prev = s_stage(steps[0])
            for idx_s, stp in enumerate(steps):
                nxt = s_stage(steps[idx_s + 1]) if idx_s + 1 < len(steps) else None
                pv_stage(stp, prev)
                prev = nxt
            dbg_dump("att_xr", XR[:, :], [128, 8 * NT], F32)
            dbg_dump("att_aux", AUX[:, :], [128, 10368], BF16)
            dbg_dump("att_hb", HBt[:, :], [128, 8 * NT], BF16)
            reload(li_index)
            linear(2, 4, 8, 512, lambda wt, ocl, kc: wt[:, kc, ocl * 128:(ocl + 1) * 128], hb_rhs, hb_bufs,
                   q_tis, resid_evac(2, q_tis))

        def conformer(li, need_ctx, li_index):
            UC = XRb[:, 0:8 * 286].rearrange("p (c t) -> p c t", c=8)
            UL = XRb[:, 2288:2288 + 8 * 2078].rearrange("p (c t) -> p c t", c=8)
            DG = [XRb[:, 18912 + i * 3968:18912 + (i + 1) * 3968].rearrange("p (k m) -> p k m", k=31) for i in range(2)]
            ub = [[p.buf() for _ in range(5)] for _ in range(8)]
            upad = p.buf()
            dgb = [p.buf(), p.buf()]
            vmemset(XRb[:, 0:18912], 0.0, [upad] + [b for row in ub for b in row])
            tis = [0, 1, 2, 3, 4]

            def useg(c, ti, k, n):
                if ti == 0:
                    return UC[:, c, k:k + n]
                o = TCH[ti][0] - NCTX
                return UL[:, c, o + k:o + k + n]

            for b in range(4):
                wt, wb = w_next(8, 512)
                for cl in range(2):
                    c = b * 2 + cl
                    for ti in tis:
                        t0, n = TCH[ti]
                        pa_t, pa_b = psum()
                        for kc in range(8):
                            mm(pa_t[:, 0:n], wt[:, kc, cl * 128:(cl + 1) * 128], H3[:, kc, t0:t0 + n], kc == 0, kc == 7,
                               [wb, st_.hb[kc][ti]], [pa_b], inc=(kc == 7))
                        pg_t, pg_b = psum()
                        for kc in range(8):
                            mm(pg_t[:, 0:n], wt[:, kc, 256 + cl * 128:256 + (cl + 1) * 128], H3[:, kc, t0:t0 + n], kc == 0, kc == 7,
                               [wb, st_.hb[kc][ti]], [pg_b], inc=(kc == 7))
                        sg, sgb = tmp32()
                        act(sg[:, 0:n], pg_t[:, 0:n], AF.Sigmoid, [pg_b, b_vec], [sgb], bias=vcol("cbin", 8 + c))
                        stt(useg(c, ti, 15, n), pa_t[:, 0:n], vcol("cbin", c), sg[:, 0:n], ALU.add, ALU.mult,
                            [pa_b, sgb, b_vec, upad], [ub[c][ti]])
            vb = [[p.buf() for _ in range(5)] for _ in range(8)]
            for c in range(8):
                par = c % 2
                for k in range(31):
                    ts(DG[par][:, k, :], IDENT, vcol("cwdw", k * 8 + c), None, ALU.mult, None, [b_const, b_vec], [dgb[par]])
                for ti in tis:
                    t0, n = TCH[ti]
                    ps_t, ps_b = psum()
                    nb = [ub[c][ti]]
                    if ti > 1:
                        nb.append(ub[c][ti - 1])
                    if 1 <= ti < 4:
                        nb.append(ub[c][ti + 1])
                    for k in range(31):
                        mm(ps_t[:, 0:n], DG[par][:, k, :], useg(c, ti, k, n), k == 0, k == 30,
                           [dgb[par], upad] + nb, [ps_b], inc=(k == 30))
                    act(H3[:, c, t0:t0 + n], ps_t[:, 0:n], AF.Identity, [ps_b, b_vec],
                        [vb[c][ti], st_.hb[c][ti]], bias=vcol("cbdw", c))
            TB3 = TB[:, :].rearrange("p (c t) -> p c t", c=8)
            yb = [[p.buf() for _ in range(5)] for _ in range(8)]
            def ln_a1(ti, k):
                t0, n = TCH[ti]
                pm_t, pm_b = psum()
                for c in range(8):
                    mm(pm_t[:, 0:n], ONES, H3[:, c, t0:t0 + n], c == 0, c == 7, [vb[c][ti], b_const], [pm_b], inc=(c == 7))
                for c in range(8):
                    act(TB3[:, c, 0:n], H3[:, c, t0:t0 + n], AF.Square, [vb[c][ti]], [b_tb])
                pq_t, pq_b = psum()
                for c in range(8):
                    mm(pq_t[:, 0:n], ONES, TB3[:, c, 0:n], c == 0, c == 7, [b_tb, b_const], [pq_b], inc=(c == 7))
                mean, meanb = (T32H[1], t32hb[1]) if k % 2 == 0 else (T32X[1], t32xb[1])
                act(mean[:, 0:n], pm_t[:, 0:n], AF.Identity, [pm_b], [meanb], scale=1.0 / D)
                return mean, meanb, pq_t, pq_b

            def ln_a2(ti, k, mean, meanb, pq_t, pq_b):
                t0, n = TCH[ti]
                m2, m2b = tmp32()
                tt(m2[:, 0:n], mean[:, 0:n], mean[:, 0:n], ALU.mult, [meanb], [m2b])
                var, varb = tmp32()
                stt(var[:, 0:n], pq_t[:, 0:n], 1.0 / D, m2[:, 0:n], ALU.mult, ALU.subtract, [pq_b, m2b], [varb])
                sd, sdb = tmp32()
                act(sd[:, 0:n], var[:, 0:n], AF.Ln, [varb, b_misc], [sdb], bias=MISC[:, 0:1], scale=1.0)
                rs, rsb = (T32H[0], t32hb[0]) if k % 2 == 0 else (T32X[0], t32xb[0])
                act(rs[:, 0:n], sd[:, 0:n], AF.Exp, [sdb], [rsb], scale=-0.5)
                return mean, meanb, rs, rsb

            def ln_b(ti, mean, meanb, rs, rsb):
                t0, n = TCH[ti]
                for c in range(8):
                    t_, tb_ = tmp32()
                    tt(t_[:, 0:n], H3[:, c, t0:t0 + n], mean[:, 0:n], ALU.subtract, [vb[c][ti], meanb], [tb_])
                    tt(t_[:, 0:n], t_[:, 0:n], rs[:, 0:n], ALU.mult, [tb_, rsb], [tb_])
                    act(H3[:, c, t0:t0 + n], t_[:, 0:n], AF.Silu, [tb_, b_vec], [yb[c][ti], vb[c][ti]],
                        bias=vcol("cnb", c), scale=vcol("cng", c))

            prev = None
            for k, ti in enumerate(tis):
                a1 = ln_a1(ti, k)
                if prev is not None:
                    ln_b(*prev)
                prev = (ti,) + ln_a2(ti, k, *a1)
            ln_b(*prev)
            st_.hb = yb
            reload(li_index)
            linear(2, 4, 8, 512, lambda wt, ocl, kc: wt[:, kc, ocl * 128:(ocl + 1) * 128], hb_rhs, hb_bufs,
                   tis, resid_evac(2, tis, bias_name="cbout"))

        st_.rr["tbx"] = 0

        def tmppt32():
            i = st_.rr["tbx"] % 2
            st_.rr["tbx"] += 1
            return T32X[i], t32xb[i]

        def rglru(li, need_ctx, li_index):
            G3 = XRb[:, 0:18432].rearrange("p (c t) -> p c t", c=8)
            XL3 = XRb[:, 18432:36864].rearrange("p (c t) -> p c t", c=8)
            gb = [[p.buf() for _ in range(5)] for _ in range(8)]
            xlb = [[p.buf() for _ in range(5)] for _ in range(8)]
            tis = [0, 1, 2, 3, 4]
            for b in range(4):
                wt, wb = w_next(8, 512)
                for ocl in range(4):
                    oc = b * 4 + ocl
                    for ti in tis:
                        t0, n = TCH[ti]
                        ps_t, ps_b = psum()
                        for kc in range(8):
                            mm(ps_t[:, 0:n], wt[:, kc, ocl * 128:(ocl + 1) * 128], H3[:, kc, t0:t0 + n], kc == 0, kc == 7,
                               [wb, st_.hb[kc][ti]], [ps_b], inc=(kc == 7))
                        if oc < 8:
                            act(G3[:, oc, t0:t0 + n], ps_t[:, 0:n], AF.Gelu_apprx_tanh, [ps_b], [gb[oc][ti]])
                        else:
                            vcopy(XL3[:, oc - 8, t0:t0 + n], ps_t[:, 0:n], [ps_b], [xlb[oc - 8][ti]])
            lam = VEC[:, VOFF["llam"]:VOFF["llam"] + 16]
            b_ca = p.buf()
            act(MISC[:, 64:80], lam, AF.Exp, [b_vec], [b_ca], scale=-1.0)
            act(MISC[:, 80:96], MISC[:, 64:80], AF.Ln, [b_ca, b_misc], [b_ca], bias=MISC[:, 1:2], scale=1.0)
            ts(MISC[:, 16:32], MISC[:, 80:96], -8.0, None, ALU.mult, None, [b_ca], [b_ca])
            ts(MISC[:, 48:64], MISC[:, 80:96], -4.0, None, ALU.mult, None, [b_ca], [b_ca])
            ts(MISC[:, 96:128], VEC[:, VOFF["lgb"]:VOFF["lgb"] + 32], 0.5, None, ALU.mult, None, [b_vec, b_ca], [b_ca])
            dbg_dump("lru_xr0", XR[:, :], [128, 8 * NT], F32)
            p.barrier(("pe", "act", "dve"))
            U32 = HBf[:, 0:4608].rearrange("p (c t) -> p c t", c=2)
            HS = HBf[:, 4608:9216].rearrange("p (c t) -> p c t", c=2)
            UBF = AUX[:, 0:4608].rearrange("p (c t) -> p c t", c=2)
            XT = [AUXf[:, 2304 + i * 512:2304 + (i + 1) * 512] for i in range(5)]
            xtb = [p.buf() for _ in range(5)]
            CAR = MISC[:, 128:256]
            car_i = [0]
            rrx = [0]

            def ltmp():
                i = rrx[0] % 15
                rrx[0] += 1
                if i < 5:
                    return XT[i], xtb[i]
                if i < 11:
                    return T32[i - 5], t32b[i - 5]
                if i < 13:
                    return T32X[i - 11], t32xb[i - 11]
                return T32H[i - 13], t32hb[i - 13]

            gslots = []
            gidx = []
            for d in range(2):
                gidx.append(ws.consumed)
                gslots.append(w_next(16, 256, pin=True))
            SEGS = [(0, NCTX), (NCTX, NLAT)]
            u32b = [p.buf(), p.buf()]
            ubfb = [p.buf(), p.buf()]
            hsb = [[p.buf() for _ in range(5)] for _ in range(2)]
            for nblk in range(4):
                c0 = nblk * 2
                for d in range(2):
                    gwt, gwb = gslots[d]
                    for cl in range(2):
                        c = c0 + cl
                        xall = xlb[c]
                        for (s0, sn) in SEGS:
                            ts(U32[:, cl, s0:s0 + sn], XL3[:, c, s0:s0 + sn], vcol("lcw", (d * 4 + 3) * 8 + c), vcol("lcb", d * 8 + c),
                               ALU.mult, ALU.add, xall + [b_vec], [u32b[cl]], eng="pool")
                            for k in range(3):
                                sh = 3 - k
                                if d == 0:
                                    o_ap = U32[:, cl, s0 + sh:s0 + sn]
                                    i_ap = XL3[:, c, s0:s0 + sn - sh]
                                else:
                                    o_ap = U32[:, cl, s0:s0 + sn - sh]
                                    i_ap = XL3[:, c, s0 + sh:s0 + sn]
                                stt(o_ap, i_ap, vcol("lcw", (d * 4 + k) * 8 + c), o_ap, ALU.mult, ALU.add,
                                    xall + [b_vec, u32b[cl]], [u32b[cl]])
                        act(UBF[:, cl, :], U32[:, cl, :], AF.Identity, [u32b[cl]], [ubfb[cl]])
                    for cl in range(2):
                        c = c0 + cl
                        groups = [[0, 1, 2], [3, 4]] if d == 0 else [[0, 4, 3], [2, 1]]
                        prev_car = None
                        oi = -1
                        c4 = MISC[:, 48 + d * 8 + c:49 + d * 8 + c]
                        for grp in groups:
                            items = []
                            for ti in grp:
                                t0, n = TCH[ti]
                                gps = []
                                for gi in range(2):
                                    ps_t, ps_b = psum()
                                    for kc in range(2):
                                        mm(ps_t[:, 0:n], gwt[:, (gi * 4 + nblk) * 2 + kc, cl * 128:(cl + 1) * 128], UBF[:, kc, t0:t0 + n],
                                           kc == 0, kc == 1, [gwb, ubfb[kc]], [ps_b], inc=(kc == 1))
                                    gps.append((ps_t, ps_b))
                                r_, rb_ = ltmp()
                                act(r_[:, 0:n], gps[0][0][:, 0:n], AF.Tanh, [gps[0][1], b_ca], [rb_],
                                    bias=MISC[:, 96 + (d * 2 + 0) * 8 + c:97 + (d * 2 + 0) * 8 + c], scale=0.5)
                                a_, ab_ = ltmp()
                                act(a_[:, 0:n], r_[:, 0:n], AF.Exp, [rb_, b_ca], [ab_], bias=c4, scale=c4)
                                i_, ib_ = ltmp()
                                act(i_[:, 0:n], gps[1][0][:, 0:n], AF.Tanh, [gps[1][1], b_ca], [ib_],
                                    bias=MISC[:, 96 + (d * 2 + 1) * 8 + c:97 + (d * 2 + 1) * 8 + c], scale=0.5)
                                stt(i_[:, 0:n], i_[:, 0:n], 1.0, U32[:, cl, t0:t0 + n], ALU.add, ALU.mult, [ib_, u32b[cl]], [ib_])
                                items.append((ti, a_, ab_, i_, ib_))
                            for (ti, a_, ab_, i_, ib_) in items:
                                oi += 1
                                t0, n = TCH[ti]
                                m_, mb_ = ltmp()
                                act(m_[:, 0:n], a_[:, 0:n], AF.Square, [ab_], [mb_])
                                act(m_[:, 0:n], m_[:, 0:n], AF.Sqrt, [mb_, b_misc], [mb_], bias=MISC[:, 1:2], scale=-1.0)
                                if ti == 0:
                                    fc = 0 if d == 0 else NCTX - 1
                                    vmemset(m_[:, fc:fc + 1], 1.0, [mb_])
                                stt(i_[:, 0:n], i_[:, 0:n], 0.5, m_[:, 0:n], ALU.mult, ALU.mult, [ib_, mb_], [ib_])
                                if d == 0:
                                    init = 0.0 if oi == 0 else HS[:, cl, t0 - 1:t0]
                                    rd = [ab_, ib_] + ([hsb[cl][ti - 1]] if oi > 0 else [])
                                    p.op("dve", lambda e, o=HS[:, cl, t0:t0 + n], a=a_[:, 0:n], b=i_[:, 0:n], init=init:
                                         e.tensor_tensor_scan(out=o, data0=a, data1=b, initial=init, op0=ALU.mult, op1=ALU.add),
                                         rd, [hsb[cl][ti]])
                                else:
                                    h_, hb_ = ltmp()
                                    init = 0.0 if oi == 0 else prev_car
                                    p.op("dve", lambda e, o=h_[:, 0:n][:, ::-1], a=a_[:, 0:n][:, ::-1], b=i_[:, 0:n][:, ::-1], init=init:
                                         e.tensor_tensor_scan(out=o, data0=a, data1=b, initial=init, op0=ALU.mult, op1=ALU.add),
                                         [ab_, ib_, b_car], [hb_])
                                    ci = car_i[0] % 128
                                    car_i[0] += 1
                                    vcopy(CAR[:, ci:ci + 1], h_[:, 0:1], [hb_], [b_car])
                                    prev_car = CAR[:, ci:ci + 1]
                                    tt(h_[:, 0:n], h_[:, 0:n], HS[:, cl, t0:t0 + n], ALU.add, [hb_, hsb[cl][ti]], [hb_], eng="pool")
                                    tt(G3[:, c, t0:t0 + n], h_[:, 0:n], G3[:, c, t0:t0 + n], ALU.mult, [hb_, gb[c][ti]], [gb[c][ti]], eng="pool")
            for gi_ in gidx:
                w_release(gi_)
            dbg_dump("lru_xr1", XR[:, :], [128, 8 * NT], F32)
            p.barrier(("pe", "act", "dve"))
            yb = [[p.buf() for _ in range(5)] for _ in range(8)]
            for c in range(8):
                for ti in tis:
                    t0, n = TCH[ti]
                    if (c + ti) % 2 == 0:
                        vcopy(H3[:, c, t0:t0 + n], G3[:, c, t0:t0 + n], [gb[c][ti]], [yb[c][ti]])
                    else:
                        act(H3[:, c, t0:t0 + n], G3[:, c, t0:t0 + n], AF.Identity, [gb[c][ti]], [yb[c][ti]])
            st_.hb = yb
            reload(li_index)
            linear(2, 4, 8, 512, lambda wt, ocl, kc: wt[:, kc, ocl * 128:(ocl + 1) * 128], hb_rhs, hb_bufs,
                   tis, resid_evac(2, tis))

        vmemset(MISC[:, 0:1], EPS, [b_misc])
        vmemset(MISC[:, 1:2], 1.0, [b_misc])
        for idx, li in enumerate(layers):
            kind = li % 3
            need_ctx = li < DEPTH - 1
            st_.par = idx % 2
            if idx == 0:
                for _ in range(12):
                    mod_block()
                mod_finish(li, 0)
            st_.hb = grid(f"h{li}_")
            sp_tok = spill_issue(0 if idx == 0 else 1)
            norm_phase(0, [0, 1, 2, 3, 4])
            if DBG and idx == 0:
                p.dma("sp", dbg_mods, MODS()[:, :], reads=[b_mods()], writes=[o_buf])
                p.dma("sp", dbg_ab, AB()[:, :], reads=[b_ab()], writes=[o_buf])
                p.dma("sp", dbg_h1, HBt[:, :], reads=[b for row in st_.hb for b in row], writes=[o_buf])
            first_in_prog = (idx == 0)
            spill(sp_tok)
            lidx = 0 if first_in_prog else 1
            import os as _os
            if _os.environ.get("DBG_SKIP_MIX"):
                for _ in range({0: 6, 1: 6, 2: 8}[kind]):
                    w_next(8, 512)
                reload(lidx)
            elif kind == 0:
                attention(li, li // 3, need_ctx, lidx)
            elif kind == 1:
                conformer(li, need_ctx, lidx)
            else:
                rglru(li, need_ctx, lidx)
            tis = [0, 1, 2, 3, 4] if need_ctx else [1, 2, 3, 4]
            p.barrier(("pe", "act", "dve"))
            st_.hb = grid(f"h2{li}_")
            if _os.environ.get("DBG_SKIP_MLP"):
                for _ in range(16 + (12 if idx + 1 < len(layers) else 0)):
                    w_next(8, 512)
            else:
                if DBG and idx == 0:
                    p.dma("sp", dbg_x1, XR[:, :], reads=[b for row in st_.xb for b in row], writes=[o_buf])
                norm_phase(1, tis)
                if DBG and idx == 0:
                    p.dma("sp", dbg_h2, HBt[:, :], reads=[b for row in st_.hb for b in row], writes=[o_buf])
                if idx + 1 < len(layers):
                    mlp_phase(li, tis, layers[idx + 1], (idx + 1) % 2)
                else:
                    mlp_phase(li, tis)
        allb = [b for row in st_.xb for b in row]
        if last:
            for c in range(8):
                p.dma("sp", outT[c * 128:(c + 1) * 128, :], X3[:, c, NCTX:NT], reads=allb, writes=[o_buf])
        else:
            p.dma("sp", xs_out, XR[:, :], reads=allb, writes=[o_buf])
        p._wait("sp", o_buf.w)
        assert ws.consumed == len(wspecs), (ws.consumed, len(wspecs))
        p.emit()
    return nc


_WKEYS = ["mod_w", "mlp_w1", "mlp_w2", "attn_w_qkv", "attn_w_o", "conv_w_in", "conv_w_out",
          "lru_w_in", "lru_gate_w", "lru_w_out"]


def _common_maps(inputs):
    m = {k: np.ascontiguousarray(np.asarray(inputs[k], np.float32)) for k in _WKEYS}
    m["consts"] = _consts()
    m["rope"] = _rope_tables()
    return m


def kernel(**inputs):
    n = 8
    common = _common_maps(inputs)
    x = np.asarray(inputs["x"], np.float32)
    ctx = np.asarray(inputs["ctx"], np.float32)
    in_maps = []
    for b in range(n):
        m = dict(common)
        m["xT"] = np.ascontiguousarray(x[b].T)
        m["ctxT"] = np.ascontiguousarray(ctx[b].T)
        m["vecs"] = _pack_vecs(inputs, b)
        in_maps.append(m)
    nc = build_program((0, 1, 2, 3), True, True)
    res = run_bass_kernel_spmd(nc, in_maps, core_ids=list(range(n)))
    out = np.stack([np.ascontiguousarray(res.results[b]["outT"].T) for b in range(n)], axis=0)
    return out.astype(np.float32)
```
